# Optimizing a Trainium2 kernel written in Bass

```python
import math
import jax, jax.numpy as jnp
from jax import lax
import numpy as np

D_MODEL = 2048
BATCH = 1
SEQ = 16384
DEPTH = 2

CTX_LEN = 256
GRID_W = 64
N_MOD = 9
D_FF = 5632
NORM_EPS = 1e-6
MLA_HEADS = 8
QK_NOPE = 128
QK_ROPE = 64
QK_DIM = QK_NOPE + QK_ROPE
V_DIM = 128
Q_LORA = 512
KV_LORA = 256
ROPE_BASE = 10000.0
Q_BLOCK = 128
S5_WIDTH = 1024
S5_GROUP = 16
S5_GROUPS = S5_WIDTH // S5_GROUP
S5_STATE = 64
DT_MIN = 1e-3
DT_MAX = 1e-1
EV_IN = Q_LORA + KV_LORA + QK_ROPE + S5_WIDTH
EV_MIX = MLA_HEADS * V_DIM + S5_WIDTH
HY_WIDTH = D_MODEL
HY_ORDER = 2
HY_SHORT = 3
HY_BANDS = 16
HY_EMB = 1 + 2 * HY_BANDS
HY_FILT_HIDDEN = 64
HY_DECAY_MIN = math.log(1e-2) / 1.5
HY_DECAY_MAX = math.log(1e-2) / 0.3

N_EVEN = (DEPTH + 1) // 2
N_ODD = DEPTH // 2

kernel_name = "hybrid_mla_s5_hyena_prefix_dit"


def rms_norm(x, g):
    xf = x.astype(jnp.float32)
    y = xf * lax.rsqrt(jnp.mean(xf * xf, axis=-1, keepdims=True) + NORM_EPS)
    return (y * g.astype(jnp.float32)).astype(x.dtype)


def modulate(h, shift, scale):
    return h * (1 + scale) + shift


def swiglu(h, w_in, w_out):
    g, u = jnp.split(h @ w_in, 2, axis=-1)
    return (jax.nn.silu(g) * u) @ w_out


def ffn_half(s, mod, k0, g_norm, w_in, w_out):
    h = modulate(rms_norm(s, g_norm), mod[k0], mod[k0 + 1])
    return s + mod[k0 + 2] * (0.5 * swiglu(h, w_in, w_out))


def axial_rope_tables(n_rows, dtype):
    row = jnp.repeat(jnp.arange(n_rows, dtype=jnp.float32), GRID_W)
    col = jnp.tile(jnp.arange(GRID_W, dtype=jnp.float32), n_rows)
    n_freq = QK_ROPE // 4
    inv_freq = ROPE_BASE ** (-jnp.arange(n_freq, dtype=jnp.float32) / n_freq)
    ang_r = row[:, None] * inv_freq
    ang_c = col[:, None] * inv_freq
    return tuple(t[:, None, :].astype(dtype) for t in
                 (jnp.cos(ang_r), jnp.sin(ang_r), jnp.cos(ang_c), jnp.sin(ang_c)))


def _rotate_pairs(x, cos, sin):
    x1, x2 = jnp.split(x, 2, axis=-1)
    return jnp.concatenate([x1 * cos - x2 * sin, x2 * cos + x1 * sin], axis=-1)


def rope_tail(t, rope):
    if rope is None:
        return t
    cos_r, sin_r, cos_c, sin_c = rope
    t_nope, t_pe = jnp.split(t, [QK_NOPE], axis=-1)
    pe_r, pe_c = jnp.split(t_pe, 2, axis=-1)
    return jnp.concatenate([t_nope, _rotate_pairs(pe_r, cos_r, sin_r), _rotate_pairs(pe_c, cos_c, sin_c)], axis=-1)


def mla_heads(q_a, kv_a, k_pe, q_a_norm_g, w_uq, kv_a_norm_g, w_ukv, q_head_g, k_head_g, rope, with_queries):
    b, n, _ = kv_a.shape
    kv = (rms_norm(kv_a, kv_a_norm_g) @ w_ukv).reshape(b, n, MLA_HEADS, QK_NOPE + V_DIM)
    k_nope, v = jnp.split(kv, [QK_NOPE], axis=-1)
    k_rot = jnp.broadcast_to(k_pe[:, :, None, :], (b, n, MLA_HEADS, QK_ROPE))
    k = rope_tail(rms_norm(jnp.concatenate([k_nope, k_rot], axis=-1), k_head_g), rope)
    if not with_queries:
        return None, k, v
    q = (rms_norm(q_a, q_a_norm_g) @ w_uq).reshape(b, n, MLA_HEADS, QK_DIM)
    q = rope_tail(rms_norm(q, q_head_g), rope)
    return q, k, v


def block_attention(q, k, v):
    b, n, h, d = q.shape
    scale = 1.0 / math.sqrt(QK_DIM)
    qb = q.reshape(b, n // Q_BLOCK, Q_BLOCK, h, d).transpose(1, 0, 2, 3, 4)

    def one_block(q_blk):
        s = jnp.einsum('bqhd,bkhd->bhqk', q_blk, k, preferred_element_type=jnp.float32) * scale
        p = jax.nn.softmax(s, axis=-1).astype(v.dtype)
        return jnp.einsum('bhqk,bkhe->bqhe', p, v)

    o = lax.map(one_block, qb)
    return o.transpose(1, 0, 2, 3, 4).reshape(b, n, h * V_DIM)


def s5_discretize(lam_re, lam_im, log_dt):
    dt = jnp.exp(log_dt)[:, None]
    mag = jnp.exp(lam_re * dt)
    a_re = mag * jnp.cos(lam_im * dt)
    a_im = mag * jnp.sin(lam_im * dt)
    den = lam_re * lam_re + lam_im * lam_im
    nr = a_re - 1.0
    k_re = (nr * lam_re + a_im * lam_im) / den
    k_im = (a_im * lam_re - nr * lam_im) / den
    return a_re, a_im, k_re, k_im


def _complex_affine_combine(e1, e2):
    a1r, a1i, b1r, b1i = e1
    a2r, a2i, b2r, b2i = e2
    return (a2r * a1r - a2i * a1i, a2r * a1i + a2i * a1r,
            a2r * b1r - a2i * b1i + b2r, a2r * b1i + a2i * b1r + b2i)


def s5_scan(u, lam_re, lam_im, log_dt, b_re, b_im, h0, reverse):
    a_re, a_im, k_re, k_im = s5_discretize(lam_re, lam_im, log_dt)
    bu_re = jnp.einsum('blgc,gpc->blgp', u, b_re)
    bu_im = jnp.einsum('blgc,gpc->blgp', u, b_im)
    x_re = k_re * bu_re - k_im * bu_im
    x_im = k_re * bu_im + k_im * bu_re
    elems = (jnp.broadcast_to(a_re, x_re.shape), jnp.broadcast_to(a_im, x_re.shape), x_re, x_im)
    p_re, p_im, h_re, h_im = lax.associative_scan(_complex_affine_combine, elems, reverse=reverse, axis=1)
    h0_re, h0_im = h0[0][:, None], h0[1][:, None]
    return h_re + p_re * h0_re - p_im * h0_im, h_im + p_re * h0_im + p_im * h0_re


def s5_readout(h_re, h_im, c_re, c_im):
    return jnp.einsum('blgp,gcp->blgc', h_re, c_re) - jnp.einsum('blgp,gcp->blgc', h_im, c_im)


def s5_glu(y, glu_w, glu_b, dtype):
    y = jax.nn.gelu(y.reshape(y.shape[0], y.shape[1], S5_WIDTH))
    return (y * jax.nn.sigmoid(y @ glu_w.astype(jnp.float32) + glu_b.astype(jnp.float32))).astype(dtype)


def s5_mixer(u_x, u_c, lam_re, lam_im, log_dt, b_re, b_im, c_re, c_im, d_skip, glu_w, glu_b, ctx_out):
    f32 = jnp.float32
    bsz = u_x.shape[0]
    gx = u_x.astype(f32).reshape(bsz, -1, S5_GROUPS, S5_GROUP)
    gc = u_c.astype(f32).reshape(bsz, -1, S5_GROUPS, S5_GROUP)
    d_g = d_skip.astype(f32).reshape(S5_GROUPS, S5_GROUP)
    zero = jnp.zeros((bsz, S5_GROUPS, S5_STATE), f32)
    y_x = d_g * gx
    y_c = d_g * gc if ctx_out else None
    for direction, reverse in ((0, False), (1, True)):
        prm = [p[direction].astype(f32) for p in (lam_re, lam_im, log_dt, b_re, b_im)]
        cr, ci = c_re[direction].astype(f32), c_im[direction].astype(f32)
        hc_re, hc_im = s5_scan(gc, *prm, (zero, zero), reverse)
        last = 0 if reverse else -1
        hx_re, hx_im = s5_scan(gx, *prm, (hc_re[:, last], hc_im[:, last]), reverse)
        y_x = y_x + s5_readout(hx_re, hx_im, cr, ci)
        if ctx_out:
            y_c = y_c + s5_readout(hc_re, hc_im, cr, ci)
    out_x = s5_glu(y_x, glu_w, glu_b, u_x.dtype)
    out_c = s5_glu(y_c, glu_w, glu_b, u_c.dtype) if ctx_out else None
    return out_x, out_c


def even_mixer(hx, hc, w_in, q_a_norm_g, w_uq, kv_a_norm_g, w_ukv, q_head_g, k_head_g,
               lam_re, lam_im, log_dt, b_re, b_im, c_re, c_im, d_skip, glu_w, glu_b, w_out, rope, ctx_out):
    cuts = [Q_LORA, Q_LORA + KV_LORA, Q_LORA + KV_LORA + QK_ROPE]
    qa_x, kva_x, kpe_x, u_x = jnp.split(hx @ w_in, cuts, axis=-1)
    qa_c, kva_c, kpe_c, u_c = jnp.split(hc @ w_in, cuts, axis=-1)
    q_x, k_x, v_x = mla_heads(qa_x, kva_x, kpe_x, q_a_norm_g, w_uq, kv_a_norm_g, w_ukv,
                              q_head_g, k_head_g, rope, True)
    q_c, k_c, v_c = mla_heads(qa_c, kva_c, kpe_c, q_a_norm_g, w_uq, kv_a_norm_g, w_ukv,
                              q_head_g, k_head_g, None, ctx_out)
    att_x = block_attention(q_x, jnp.concatenate([k_x, k_c], axis=1), jnp.concatenate([v_x, v_c], axis=1))
    ssm_x, ssm_c = s5_mixer(u_x, u_c, lam_re, lam_im, log_dt, b_re, b_im, c_re, c_im,
                            d_skip, glu_w, glu_b, ctx_out)
    out_x = jnp.concatenate([att_x, ssm_x], axis=-1) @ w_out
    if not ctx_out:
        return out_x, None
    att_c = block_attention(q_c, k_c, v_c)
    out_c = jnp.concatenate([att_c, ssm_c], axis=-1) @ w_out
    return out_x, out_c


def hyena_filter_hidden(n, w1, b1, w2, b2, w3, b3, freq):
    f32 = jnp.float32
    t = jnp.arange(n, dtype=f32) / n
    bands = jnp.linspace(1e-4, HY_BANDS - 1, HY_BANDS, dtype=f32)
    ang = (2.0 * math.pi) * t[:, None] * bands[None, :]
    feat = jnp.concatenate([t[:, None], jnp.cos(ang), -jnp.sin(ang)], axis=-1)
    fr = freq.astype(f32)
    h = jnp.sin(fr * (feat @ w1.astype(f32) + b1.astype(f32)))
    h = jnp.sin(fr * (h @ w2.astype(f32) + b2.astype(f32)))
    h = jnp.sin(fr * (h @ w3.astype(f32) + b3.astype(f32)))
    return h, t


def hyena_filter_spectrum(hid, t, w_out_o):
    f32 = jnp.float32
    h = jnp.einsum('lf,fdw->ldw', hid, w_out_o.astype(f32))
    deltas = jnp.abs(jnp.linspace(HY_DECAY_MIN, HY_DECAY_MAX, HY_WIDTH, dtype=f32))
    h = h * jnp.exp(-t[:, None, None] * deltas)
    g = jnp.concatenate([h[:, 0], jnp.zeros((1, HY_WIDTH), f32), jnp.flip(h[1:, 1], axis=0)], axis=0)
    g = g / jnp.sum(jnp.abs(g), axis=0, keepdims=True)
    return jnp.fft.rfft(g, axis=0)


def long_conv(y, g_hat):
    n = y.shape[1]
    y_hat = jnp.fft.rfft(y, n=2 * n, axis=1)
    return jnp.fft.irfft(y_hat * g_hat[None], n=2 * n, axis=1)[:, :n]


def hyena_mixer(h, w_in, conv_w, conv_b, w1, b1, w2, b2, w3, b3, freq, filt_w_out, skip, w_out):
    f32 = jnp.float32
    n = h.shape[1]
    z = h @ w_in
    ch = z.shape[-1]
    z = lax.conv_general_dilated(z, conv_w[:, None, :].astype(z.dtype), window_strides=(1,),
                                 padding=[(HY_SHORT // 2, HY_SHORT // 2)],
                                 dimension_numbers=('NWC', 'WIO', 'NWC'), feature_group_count=ch) + conv_b
    x1, x2, v = jnp.split(z, 3, axis=-1)
    hid, t = hyena_filter_hidden(n, w1, b1, w2, b2, w3, b3, freq)
    y = v.astype(f32)
    for o, gate in enumerate((x1, x2)):
        g_hat = hyena_filter_spectrum(hid, t, filt_w_out[:, o])
        y = gate.astype(f32) * (long_conv(y, g_hat) + skip[o].astype(f32) * y)
    return y.astype(h.dtype) @ w_out


def setup_inputs(seed: int = 0) -> dict:
    keys = list(jax.random.split(jax.random.key(seed), 48))

    def nrm(shape, scale):
        return scale * jax.random.normal(keys.pop(), shape, jnp.float32)

    def gain(shape):
        return 1.0 + nrm(shape, 0.02)

    d, f = D_MODEL, D_FF
    g, p, gc = S5_GROUPS, S5_STATE, S5_GROUP
    fh = HY_FILT_HIDDEN
    return {
        "x": nrm((BATCH, SEQ, d), 1.0),
        "c": nrm((BATCH, d), 1.0),
        "ctx": nrm((BATCH, CTX_LEN, d), 1.0),
        "c_ctx": nrm((d,), 1.0),
        "ada_w": nrm((DEPTH, d, N_MOD * d), 0.5 * d ** -0.5),
        "ada_b": nrm((DEPTH, N_MOD * d), 0.02),
        "norm_g": gain((DEPTH, 3, d)),
        "ffn_w_in": nrm((DEPTH, 2, d, 2 * f), d ** -0.5),
        "ffn_w_out": nrm((DEPTH, 2, f, d), f ** -0.5),
        "ev_w_in": nrm((N_EVEN, d, EV_IN), d ** -0.5),
        "mla_q_a_norm_g": gain((N_EVEN, Q_LORA)),
        "mla_w_uq": nrm((N_EVEN, Q_LORA, MLA_HEADS * QK_DIM), Q_LORA ** -0.5),
        "mla_kv_a_norm_g": gain((N_EVEN, KV_LORA)),
        "mla_w_ukv": nrm((N_EVEN, KV_LORA, MLA_HEADS * (QK_NOPE + V_DIM)), KV_LORA ** -0.5),
        "mla_q_head_g": gain((N_EVEN, QK_DIM)),
        "mla_k_head_g": gain((N_EVEN, QK_DIM)),
        "s5_lam_re": -0.5 + nrm((N_EVEN, 2, g, p), 0.01),
        "s5_lam_im": jnp.pi * jnp.arange(p, dtype=jnp.float32) + nrm((N_EVEN, 2, g, p), 0.01),
        "s5_log_dt": jax.random.uniform(keys.pop(), (N_EVEN, 2, g), jnp.float32,
                                        math.log(DT_MIN), math.log(DT_MAX)),
        "s5_b_re": nrm((N_EVEN, 2, g, p, gc), (2 * gc) ** -0.5),
        "s5_b_im": nrm((N_EVEN, 2, g, p, gc), (2 * gc) ** -0.5),
        "s5_c_re": nrm((N_EVEN, 2, g, gc, p), p ** -0.5),
        "s5_c_im": nrm((N_EVEN, 2, g, gc, p), p ** -0.5),
        "s5_d": nrm((N_EVEN, S5_WIDTH), 1.0),
        "s5_glu_w": nrm((N_EVEN, S5_WIDTH, S5_WIDTH), S5_WIDTH ** -0.5),
        "s5_glu_b": nrm((N_EVEN, S5_WIDTH), 0.02),
        "ev_w_out": nrm((N_EVEN, EV_MIX, d), EV_MIX ** -0.5),
        "hy_w_in": nrm((N_ODD, d, 3 * HY_WIDTH), d ** -0.5),
        "hy_conv_w": nrm((N_ODD, HY_SHORT, 3 * HY_WIDTH), HY_SHORT ** -0.5),
        "hy_conv_b": nrm((N_ODD, 3 * HY_WIDTH), 0.02),
        "hy_filt_w1": nrm((N_ODD, HY_EMB, fh), HY_EMB ** -0.5),
        "hy_filt_b1": nrm((N_ODD, fh), 0.02),
        "hy_filt_w2": nrm((N_ODD, fh, fh), fh ** -0.5),
        "hy_filt_b2": nrm((N_ODD, fh), 0.02),
        "hy_filt_w3": nrm((N_ODD, fh, fh), fh ** -0.5),
        "hy_filt_b3": nrm((N_ODD, fh), 0.02),
        "hy_filt_freq": gain((N_ODD, fh)),
        "hy_filt_w_out": nrm((N_ODD, fh, HY_ORDER, 2, HY_WIDTH), fh ** -0.5),
        "hy_skip": nrm((N_ODD, HY_ORDER, HY_WIDTH), 1.0),
        "hy_w_out": nrm((N_ODD, HY_WIDTH, d), HY_WIDTH ** -0.5),
    }


def reference(x, c, ctx, c_ctx, ada_w, ada_b, norm_g, ffn_w_in, ffn_w_out,
              ev_w_in, mla_q_a_norm_g, mla_w_uq, mla_kv_a_norm_g, mla_w_ukv, mla_q_head_g, mla_k_head_g,
              s5_lam_re, s5_lam_im, s5_log_dt, s5_b_re, s5_b_im, s5_c_re, s5_c_im, s5_d, s5_glu_w, s5_glu_b,
              ev_w_out, hy_w_in, hy_conv_w, hy_conv_b, hy_filt_w1, hy_filt_b1, hy_filt_w2, hy_filt_b2,
              hy_filt_w3, hy_filt_b3, hy_filt_freq, hy_filt_w_out, hy_skip, hy_w_out):
    bsz = x.shape[0]
    ROWS = x.shape[1] // GRID_W
    rope = axial_rope_tables(ROWS, x.dtype)
    silu_c = jax.nn.silu(c)
    silu_cc = jax.nn.silu(c_ctx)
    cx = ctx
    for i in range(DEPTH):
        even = i % 2 == 0
        li = i // 2
        need_after = any(j % 2 == 0 for j in range(i + 1, DEPTH))
        use_ctx = even or need_after
        mx = (silu_c @ ada_w[i] + ada_b[i]).reshape(bsz, N_MOD, D_MODEL).transpose(1, 0, 2)[:, :, None, :]
        mc = (silu_cc @ ada_w[i] + ada_b[i]).reshape(N_MOD, 1, 1, D_MODEL)
        x = ffn_half(x, mx, 0, norm_g[i, 0], ffn_w_in[i, 0], ffn_w_out[i, 0])
        if use_ctx:
            cx = ffn_half(cx, mc, 0, norm_g[i, 0], ffn_w_in[i, 0], ffn_w_out[i, 0])
        hx = modulate(rms_norm(x, norm_g[i, 1]), mx[3], mx[4])
        hc = modulate(rms_norm(cx, norm_g[i, 1]), mc[3], mc[4]) if use_ctx else None
        if even:
            ox, oc = even_mixer(hx, hc, ev_w_in[li], mla_q_a_norm_g[li], mla_w_uq[li], mla_kv_a_norm_g[li],
                                mla_w_ukv[li], mla_q_head_g[li], mla_k_head_g[li],
                                s5_lam_re[li], s5_lam_im[li], s5_log_dt[li], s5_b_re[li], s5_b_im[li],
                                s5_c_re[li], s5_c_im[li], s5_d[li], s5_glu_w[li], s5_glu_b[li],
                                ev_w_out[li], rope, need_after)
        else:
            hy = (hy_w_in[li], hy_conv_w[li], hy_conv_b[li], hy_filt_w1[li], hy_filt_b1[li], hy_filt_w2[li],
                  hy_filt_b2[li], hy_filt_w3[li], hy_filt_b3[li], hy_filt_freq[li], hy_filt_w_out[li],
                  hy_skip[li], hy_w_out[li])
            ox = hyena_mixer(hx, *hy)
            oc = hyena_mixer(hc, *hy) if need_after else None
        x = x + mx[5] * ox
        if need_after:
            cx = cx + mc[5] * oc
            cx = ffn_half(cx, mc, 6, norm_g[i, 2], ffn_w_in[i, 1], ffn_w_out[i, 1])
        x = ffn_half(x, mx, 6, norm_g[i, 2], ffn_w_in[i, 1], ffn_w_out[i, 1])
    return x
```

```python
import numpy as np
import ml_dtypes
from contextlib import ExitStack
import concourse.bass as bass
import concourse.mybir as mybir
from concourse.bass_utils import run_bass_kernel_spmd

F32 = mybir.dt.float32
BF16 = mybir.dt.bfloat16
AF = mybir.ActivationFunctionType
ALU = mybir.AluOpType
AX = mybir.AxisListType
WRITE_KEYS = ("out", "accum_out")
SEM_ROLL = 30000


class View:
    __slots__ = ("buf", "ap")

    def __init__(self, buf, ap):
        self.buf = buf
        self.ap = ap

    def __getitem__(self, idx):
        return View(self.buf, self.ap[idx])

    def rearrange(self, pat, **kw):
        return View(self.buf, self.ap.rearrange(pat, **kw))

    def bitcast(self, dt):
        return View(self.buf, self.ap.bitcast(dt))

    def unsq_bcast(self, n):
        a = self.ap
        return View(self.buf, bass.AP(a.tensor, a.offset, [list(x) for x in a.ap] + [[0, n]]))

    def pbcast(self, n):
        return View(self.buf, self.ap.partition_broadcast(n))


class Buf:
    def __init__(self, k, name, h, space):
        self.k = k
        self.name = name
        self.h = h
        self.space = space
        self.last_w = None
        self.readers = []
        self.dma_sem = None
        self.dma_cnt = 0

    def __getitem__(self, idx):
        return View(self, self.h[idx])

    def ap(self):
        return View(self, self.h.ap() if hasattr(self.h, "ap") else self.h[:])

    def bitcast_view(self, dt):
        return View(self, self.h.bitcast(dt).ap())


class Op:
    __slots__ = ("eng", "meth", "args", "kwargs", "reads", "writes", "is_dma", "tok", "has_dep", "deps", "sbuf_side", "acc")


class KB:
    def __init__(self):
        self.nc = bass.Bass("TRN2", target_bir_lowering=False)
        self.ops = []
        self.es = ExitStack()
        self.bufs = []
        self.n = 0

    def sb(self, name, shape, dt=F32):
        h = self.es.enter_context(self.nc.sbuf_tensor(name, list(shape), dt))
        b = Buf(self, name, h, "sb")
        self.bufs.append(b)
        return b

    def ps(self, name, shape, dt=F32):
        h = self.es.enter_context(self.nc.psum_tensor(name, list(shape), dt))
        b = Buf(self, name, h, "ps")
        self.bufs.append(b)
        return b

    def dram(self, name, shape, dt=F32, kind=None):
        if kind is None:
            h = self.nc.dram_tensor(name, list(shape), dt)
        else:
            h = self.nc.dram_tensor(name, list(shape), dt, kind=kind)
        b = Buf(self, name, h, "dram")
        self.bufs.append(b)
        return b

    def I(self, eng, meth, *args, reads=(), writes=(), acc=False, **kwargs):
        o = Op()
        o.eng = eng
        o.meth = meth
        o.args = args
        o.kwargs = kwargs
        rd, wr = list(reads), list(writes)
        for a in args:
            if isinstance(a, View):
                rd.append(a.buf)
        for kk, v in kwargs.items():
            if isinstance(v, View):
                (wr if kk in WRITE_KEYS else rd).append(v.buf)
        o.reads = rd
        o.writes = wr
        o.is_dma = meth == "dma_start"
        o.tok = None
        o.has_dep = False
        o.deps = None
        o.sbuf_side = None
        o.acc = acc
        if o.is_dma:
            ob, ib = kwargs["out"].buf, kwargs["in_"].buf
            o.sbuf_side = ob if ob.space != "dram" else ib
            assert o.sbuf_side.space != "dram", "dram->dram dma unsupported"
        self.ops.append(o)
        return o

    def dma(self, eng, out, in_, **kw):
        return self.I(eng, "dma_start", out=out, in_=in_, **kw)

    def mm(self, out, lhsT, rhs, start=True, stop=True, **kw):
        return self.I("tensor", "matmul", out=out, lhsT=lhsT, rhs=rhs, start=start, stop=stop, acc=not start, **kw)

    def tr(self, out, in_, ident):
        return self.I("tensor", "transpose", out=out, in_=in_, identity=ident)

    def act(self, out, in_, func, eng="scalar", **kw):
        return self.I(eng, "activation", out=out, in_=in_, func=func, **kw)

    def tt(self, eng, out, in0, in1, op):
        return self.I(eng, "tensor_tensor", out=out, in0=in0, in1=in1, op=op)

    def ts(self, eng, out, in0, s1, s2, op0, op1=None, **kw):
        if op1 is None:
            return self.I(eng, "tensor_scalar", out=out, in0=in0, scalar1=s1, scalar2=None, op0=op0, **kw)
        return self.I(eng, "tensor_scalar", out=out, in0=in0, scalar1=s1, scalar2=s2, op0=op0, op1=op1, **kw)

    def stt(self, eng, out, in0, scalar, in1, op0, op1):
        return self.I(eng, "scalar_tensor_tensor", out=out, in0=in0, scalar=scalar, in1=in1, op0=op0, op1=op1)

    def cp(self, eng, out, in_):
        if eng == "scalar":
            return self.I(eng, "copy", out=out, in_=in_)
        return self.I(eng, "tensor_copy", out=out, in_=in_)

    def memset(self, eng, out, val):
        return self.I(eng, "memset", writes=[out.buf], ap=out, constant=val)

    def emit(self, final_wait_bufs=()):
        nc = self.nc
        ops = self.ops
        for i, o in enumerate(ops):
            deps = set()
            for b in o.reads:
                if b.last_w is not None:
                    deps.add(b.last_w)
            for b in o.writes:
                if b.last_w is not None:
                    deps.add(b.last_w)
                for r in b.readers:
                    deps.add(r)
            deps.discard(i)
            fd = []
            for d in deps:
                od = ops[d]
                same = (od.eng == o.eng) and not o.is_dma and not od.is_dma
                if same:
                    raw = any((b in od.writes) for b in o.reads)
                    if o.eng == "tensor":
                        raw = False
                    if not raw:
                        continue
                fd.append(d)
            o.deps = fd
            for d in fd:
                ops[d].has_dep = True
            for b in o.writes:
                b.last_w = i
                b.readers = []
            for b in o.reads:
                if b not in o.writes:
                    b.readers.append(i)
        engs = {"tensor": nc.tensor, "vector": nc.vector, "scalar": nc.scalar, "gpsimd": nc.gpsimd, "sync": nc.sync}
        esem = {}
        ecnt = {}
        known = {e: {} for e in engs}

        def new_sem(name):
            return self.es.enter_context(nc.semaphore(name))

        nsem = [0]
        for e in engs:
            esem[e] = new_sem(f"e_{e}_0")
            ecnt[e] = 0
            nsem[0] += 1
        for i, o in enumerate(ops):
            eng = engs[o.eng]
            kn = known[o.eng]
            for d in o.deps:
                sem, val = ops[d].tok
                if kn.get(id(sem), (None, 0))[1] < val:
                    eng.wait_ge(sem, val)
                    kn[id(sem)] = (sem, val)
            args = [a.ap if isinstance(a, View) else a for a in o.args]
            kwargs = {kk: (v.ap if isinstance(v, View) else v) for kk, v in o.kwargs.items()}
            inst = getattr(eng, o.meth)(*args, **kwargs)
            if o.is_dma:
                b = o.sbuf_side
                if b.dma_sem is None:
                    b.dma_sem = new_sem(f"d_{b.name}")
                    nsem[0] += 1
                b.dma_cnt += 16
                inst.then_inc(b.dma_sem, 16)
                o.tok = (b.dma_sem, b.dma_cnt)
            elif o.has_dep:
                if ecnt[o.eng] >= SEM_ROLL:
                    esem[o.eng] = new_sem(f"e_{o.eng}_{i}")
                    ecnt[o.eng] = 0
                    nsem[0] += 1
                ecnt[o.eng] += 1
                inst.then_inc(esem[o.eng], 1)
                o.tok = (esem[o.eng], ecnt[o.eng])
        for b in self.bufs:
            if b.dma_sem is not None:
                nc.sync.wait_ge(b.dma_sem, b.dma_cnt)
        self.nsem = nsem[0]
        self.es.close()
        return nc


def bf16_np(a):
    return a.astype(ml_dtypes.bfloat16)


D = 2048
DFF = 5632
EPS = 1e-6


def alloc_common(k):
    c = {}
    c["P"] = [k.ps(f"P{i}", [128, 512], F32) for i in range(8)]
    return c


def load_ident(k, c, ident_f_d, ident_b_d):
    c["identf"] = k.sb("identf", [128, 128], F32)
    c["identb"] = k.sb("identb", [128, 128], BF16)
    c["epsc"] = k.sb("epsc", [128, 1], F32)
    k.memset("vector", c["epsc"][:, :], EPS)
    k.dma("sync", out=c["identf"][:, :], in_=ident_f_d[:, :])
    k.dma("sync", out=c["identb"][:, :], in_=ident_b_d[:, :])


def modcols_prepare(k, pfx, modc_d, normg_d, slot):
    raw = k.sb(pfx + "raw", [128, 3, 16], F32)
    ng = k.sb(pfx + "ng", [128, 16], F32)
    A = k.sb(pfx + "A", [128, 16], F32)
    G = k.sb(pfx + "G", [128, 16], F32)
    k.dma("sync", out=raw[:, :, :], in_=modc_d)
    k.dma("sync", out=ng[:, :], in_=normg_d)
    k.stt("vector", out=A[:, :], in0=raw[:, 1, :], scalar=1.0, in1=ng[:, :], op0=ALU.add, op1=ALU.mult)
    k.ts("vector", out=G[:, :], in0=raw[:, 2, :], s1=0.5, s2=None, op0=ALU.mult)
    return A, raw, G


def norm_mod_T(k, c, pfx, x_tile, tr, A, Braw, hT_view, bufs):
    junk, ss, rstd, xh = bufs["junk"], bufs["ss"], bufs["rstd"], bufs["xh"]
    P = c["P"]
    k.memset("vector", ss[:tr, :], 0.0)
    k.act(out=junk[:tr, :], in_=x_tile, func=AF.Square, accum_out=ss[:tr, :])
    k.act(out=rstd[:tr, :], in_=ss[:tr, :], func=AF.Sqrt, scale=1.0 / D, bias=c["epsc"][:tr, :])
    k.I("vector", "reciprocal", out=rstd[:tr, :], in_=rstd[:tr, :])
    k.act(out=xh[:tr, :], in_=x_tile, func=AF.Copy, scale=rstd[:tr, :])
    pb = [P[0].bitcast_view(BF16), P[1].bitcast_view(BF16)]
    for kt in range(16):
        pv = pb[kt // 8]
        k.tr(out=pv[:, (kt % 8) * 128:(kt % 8) * 128 + tr], in_=xh[:tr, kt * 128:(kt + 1) * 128], ident=c["identb"][:tr, :tr])
    for h in range(2):
        pv = pb[h].rearrange("p (k t) -> p k t", t=128)[:, :, :tr]
        a_b = A[:, h * 8:(h + 1) * 8].unsq_bcast(tr)
        b_b = Braw[:, 0, h * 8:(h + 1) * 8].unsq_bcast(tr)
        tmp = bufs["tmpT"]
        k.tt("vector", out=tmp[:, :, :tr], in0=pv, in1=a_b, op=ALU.mult)
        k.tt("gpsimd", out=hT_view[:, h * 8:(h + 1) * 8, :], in0=tmp[:, :, :tr], in1=b_b, op=ALU.add)


def ffn_stage(k, c, pfx, xin_d, xout_d, R, tr, A, Braw, G, w_in_d, w_out_d):
    P = c["P"]
    ntiles = R // tr
    bufs = c.setdefault("nm_bufs", None)
    if bufs is None:
        bufs = c["nm_bufs"] = dict(
            junk=k.sb("junk", [128, 2048], BF16), ss=k.sb("ss", [128, 1], F32), rstd=k.sb("rstd", [128, 1], F32),
            xh=k.sb("xh", [128, 2048], BF16), tmpT=k.sb("tmpT", [128, 8, 128], F32))
    if "ffn_bufs" not in c:
        c["ffn_bufs"] = dict(
            xres=k.sb("xres", [128, 4, 2048], F32), hT=k.sb("hT", [128, 16, 512], BF16), aT=k.sb("aT", [128, 44, 512], BF16),
            wg=[k.sb(f"wg{i}", [128, 16, 256], BF16) for i in range(2)],
            wo=[k.sb(f"wo{i}", [128, 44, 128], BF16) for i in range(2)],
            sg=[k.sb(f"sg{i}", [128, 512], F32) for i in range(2)],
            oTs=[k.sb(f"oTs{i}", [128, 512], F32) for i in range(2)])
    fb = c["ffn_bufs"]
    xres, hT, aT, wg, wo, sg, oTs = fb["xres"], fb["hT"], fb["aT"], fb["wg"], fb["wo"], fb["sg"], fb["oTs"]
    w_in_v = w_in_d.ap().rearrange("(kt p) c -> p kt c", p=128)
    w_out_v = w_out_d.ap().rearrange("(j p) c -> p j c", p=128)
    nblk = (ntiles + 3) // 4
    for blk in range(nblk):
        t0 = blk * 4
        nt = min(4, ntiles - t0)
        nb = nt * tr
        for t in range(nt):
            r0 = (t0 + t) * tr
            k.dma("sync", out=xres[:tr, t, :], in_=xin_d[r0:r0 + tr, :])
            norm_mod_T(k, c, pfx, xres[:tr, t, :], tr, A, Braw, hT[:, :, t * tr:(t + 1) * tr], bufs)

        def load_wg(j):
            b = wg[j % 2]
            k.dma("gpsimd", out=b[:, :, 0:128], in_=w_in_v[:, :, j * 128:(j + 1) * 128])
            k.dma("gpsimd", out=b[:, :, 128:256], in_=w_in_v[:, :, DFF + j * 128:DFF + (j + 1) * 128])

        load_wg(0)
        for j in range(44):
            if j + 1 < 44:
                load_wg(j + 1)
            b = wg[j % 2]
            gp, up = P[2 + 2 * (j % 2)], P[3 + 2 * (j % 2)]
            for kt in range(16):
                k.mm(out=gp[:, :nb], lhsT=b[:, kt, 0:128], rhs=hT[:, kt, :nb], start=(kt == 0), stop=(kt == 15))
            for kt in range(16):
                k.mm(out=up[:, :nb], lhsT=b[:, kt, 128:256], rhs=hT[:, kt, :nb], start=(kt == 0), stop=(kt == 15))
            s = sg[j % 2]
            k.act(out=s[:, :nb], in_=gp[:, :nb], func=AF.Silu)
            k.tt("vector", out=aT[:, j, :nb], in0=s[:, :nb], in1=up[:, :nb], op=ALU.mult)

        def load_wo(m):
            k.dma("gpsimd", out=wo[m % 2][:, :, :], in_=w_out_v[:, :, m * 128:(m + 1) * 128])

        load_wo(0)
        for m in range(16):
            if m + 1 < 16:
                load_wo(m + 1)
            b = wo[m % 2]
            op_ = P[6 + (m % 2)]
            for j in range(44):
                k.mm(out=op_[:, :nb], lhsT=b[:, j, :], rhs=aT[:, j, :nb], start=(j == 0), stop=(j == 43))
            o = oTs[m % 2]
            k.act(out=o[:, :nb], in_=op_[:, :nb], func=AF.Copy, scale=G[:, m:m + 1])
            tb = P[m % 2]
            for t in range(nt):
                k.tr(out=tb[:tr, t * 128:(t + 1) * 128], in_=o[:, t * tr:(t + 1) * tr], ident=c["identf"][:, :])
            k.tt("vector", out=xres[:tr, :nt, m * 128:(m + 1) * 128], in0=xres[:tr, :nt, m * 128:(m + 1) * 128],
                 in1=tb[:tr, :nt * 128].rearrange("p (t f) -> p t f", f=128), op=ALU.add)
        for t in range(nt):
            r0 = (t0 + t) * tr
            k.dma("sync", out=xout_d[r0:r0 + tr, :], in_=xres[:tr, t, :])


def ada_stage(k, c, cc_d, w_d, b_d, out_d):
    P = c["P"]
    cc = k.sb("ada_cc", [128, 16, 2], F32)
    sc = k.sb("ada_sc", [128, 16, 2], F32)
    k.dma("sync", out=cc[:, :, :], in_=cc_d[:, :, :])
    k.act(out=sc[:, :, :], in_=cc[:, :, :], func=AF.Silu)
    wb = [k.sb(f"ada_w{i}", [128, 16, 512], F32) for i in range(2)]
    bb = k.sb("ada_b", [2, 2, 2304], F32)
    ob = k.sb("ada_o", [2, 2, 2304], F32)
    for l in range(2):
        k.dma("sync", out=bb[:, l, :], in_=b_d[l:l + 1, :].pbcast(2) if False else b_d[l, :].pbcast(2))
    blocks = [(0, 512), (512, 512), (1024, 512), (1536, 512), (2048, 256)]
    it = 0
    for l in range(2):
        wv = w_d[l].rearrange("(kt p) c -> p kt c", p=128)
        for (c0, cw) in blocks:
            w = wb[it % 2]
            k.dma("sync" if it % 2 == 0 else "gpsimd", out=w[:, :, :cw], in_=wv[:, :, c0:c0 + cw])
            ps = P[it % 2]
            for kt in range(16):
                k.mm(out=ps[:2, :cw], lhsT=sc[:, kt, :], rhs=w[:, kt, :cw], start=(kt == 0), stop=(kt == 15))
            k.tt("vector", out=ob[:, l, c0:c0 + cw], in0=ps[:2, :cw], in1=bb[:, l, c0:c0 + cw], op=ALU.add)
            it += 1
        k.dma("sync", out=out_d[l], in_=ob[:, l, :])


def rstd_from_ss(k, c, out, ss, dim):
    k.act(out=out, in_=ss, func=AF.Sqrt, scale=1.0 / dim, bias=c["epsc"][:out.ap.shape[0], :])
    k.I("vector", "reciprocal", out=out, in_=out)


def rope_apply(k, eng, dst, src, cos, sin, tmp, tr, nh):
    def tv(i):
        return tmp[:tr, i, :nh * 32].rearrange("p (h a f) -> p h a f", a=2, f=16)
    x1, x2 = src[:, :, :, 0, :], src[:, :, :, 1, :]
    k.tt(eng, out=tv(0), in0=x1, in1=cos, op=ALU.mult)
    k.tt(eng, out=tv(1), in0=x2, in1=sin, op=ALU.mult)
    k.tt(eng, out=dst[:, :, :, 0, :], in0=tv(0), in1=tv(1), op=ALU.subtract)
    k.tt(eng, out=tv(2), in0=x2, in1=cos, op=ALU.mult)
    k.tt(eng, out=tv(3), in0=x1, in1=sin, op=ALU.mult)
    k.tt(eng, out=dst[:, :, :, 1, :], in0=tv(2), in1=tv(3), op=ALU.add)


def evpre_stage(k, c, xin_d, R, tr, A, Braw, w_in_d, wuq_d, wukv_d, gqa_d, gkva_d, gq_d, gk_d, cos_d, sin_d,
                qT_d, kT_d, v_d, u_d, want_q=True, stop=9):
    P = c["P"]
    if c.get("nm_bufs") is None:
        c["nm_bufs"] = dict(
            junk=k.sb("junk", [128, 2048], BF16), ss=k.sb("ss", [128, 1], F32), rstd=k.sb("rstd", [128, 1], F32),
            xh=k.sb("xh", [128, 2048], BF16), tmpT=k.sb("tmpT", [128, 8, 128], F32))
    bufs = c["nm_bufs"]
    win = k.sb("ev_win", [128, 16, 1856], BF16)
    wuq = k.sb("ev_wuq", [128, 4, 1536], BF16)
    wukv = k.sb("ev_wukv", [128, 2, 2048], BF16)
    k.dma("gpsimd", out=win[:, :, :], in_=w_in_d.ap().rearrange("(kt p) c -> p kt c", p=128))
    k.dma("gpsimd", out=wuq[:, :, :], in_=wuq_d.ap().rearrange("(kt p) c -> p kt c", p=128))
    k.dma("gpsimd", out=wukv[:, :, :], in_=wukv_d.ap().rearrange("(kt p) c -> p kt c", p=128))
    gqa = k.sb("ev_gqa", [128, 4], F32)
    gkva = k.sb("ev_gkva", [128, 2], F32)
    gq = k.sb("ev_gq", [128, 192], F32)
    gk = k.sb("ev_gk", [128, 192], F32)
    k.dma("sync", out=gqa[:, :], in_=gqa_d[:, :])
    k.dma("sync", out=gkva[:, :], in_=gkva_d[:, :])
    k.dma("sync", out=gq[:, :], in_=gq_d[:].pbcast(128))
    k.dma("sync", out=gk[:, :], in_=gk_d[:].pbcast(128))
    k.ts("vector", out=gq[:, :], in0=gq[:, :], s1=float(192 ** -0.5), s2=None, op0=ALU.mult)
    xt = [k.sb(f"ev_x{i}", [128, 2048], F32) for i in range(2)]
    hT = k.sb("ev_hT", [128, 16, 128], BF16)
    zs = k.sb("ev_zs", [128, 1856], F32)
    qan = k.sb("ev_qan", [128, 768], BF16)
    qanT = k.sb("ev_qanT", [128, 6, 128], BF16)
    sq = k.sb("ev_sq", [128, 1536], F32)
    ss8 = k.sb("ev_ss8", [128, 8], F32)
    rs8 = k.sb("ev_rs8", [128, 8], F32)
    ss1 = k.sb("ev_ss1", [128, 1], F32)
    rs1 = k.sb("ev_rs1", [128, 1], F32)
    t1 = k.sb("ev_t1", [128, 8, 192], F32)
    t2 = k.sb("ev_t2", [128, 8, 192], F32)
    qf = k.sb("ev_qf", [128, 8, 192], BF16)
    kf = k.sb("ev_kf", [128, 8, 192], BF16)
    vt = k.sb("ev_vt", [128, 8, 128], BF16)
    kpe = k.sb("ev_kpe", [128, 64], F32)
    kpr = k.sb("ev_kpr", [128, 64], F32)
    rtmp = k.sb("ev_rtmp", [128, 4, 256], F32)
    cs = k.sb("ev_cos", [128, 32], F32)
    sn = k.sb("ev_sin", [128, 32], F32)
    oT = [k.sb(f"ev_oT{i}", [128, 8, 2, 128], BF16) for i in range(2)]
    pT = P[7].bitcast_view(BF16)
    idb = c["identb"]

    def rope_views(tile, nh):
        return tile[:tr, :nh, 128:192].rearrange("p h (a b f) -> p h a b f", a=2, b=2)

    def bc_heads(tab, nh):
        a = tab[:tr, :].ap
        return View(tab, bass.AP(a.tensor, a.offset, [list(a.ap[0]), [0, nh], [16, 2], [1, 16]]))

    def heads_T(src, dst_d, it):
        o = oT[it % 2]
        for half in range(2):
            for h4 in range(4):
                h = half * 4 + h4
                k.tr(out=pT[:, h4 * 256:h4 * 256 + tr], in_=src[:tr, h, 0:128], ident=idb[:tr, :tr])
                k.tr(out=pT[:64, h4 * 256 + 128:h4 * 256 + 128 + tr], in_=src[:tr, h, 128:192], ident=idb[:tr, :tr])
            pv = pT.rearrange("p (h a t) -> p h a t", a=2, t=128)
            k.cp("vector", out=o[:, half * 4:half * 4 + 4, 0, :tr], in_=pv[:, :, 0, :tr])
            k.cp("vector", out=o[:64, half * 4:half * 4 + 4, 1, :tr], in_=pv[:64, :, 1, :tr])
        return o

    ntiles = R // tr
    for t in range(ntiles):
        r0 = t * tr
        x = xt[t % 2]
        k.dma("sync", out=x[:tr, :], in_=xin_d[r0:r0 + tr, :])
        k.dma("sync", out=cs[:tr, :], in_=cos_d[r0:r0 + tr, :])
        k.dma("sync", out=sn[:tr, :], in_=sin_d[r0:r0 + tr, :])
        norm_mod_T(k, c, "ev", x[:tr, :], tr, A, Braw, hT[:, :, :tr], bufs)
        zb = [(0, 512), (512, 512), (1024, 512), (1536, 320)]
        for bi, (c0, cw) in enumerate(zb):
            for kt in range(16):
                k.mm(out=P[bi][:tr, :cw], lhsT=hT[:, kt, :tr], rhs=win[:, kt, c0:c0 + cw], start=(kt == 0), stop=(kt == 15))
            k.cp("scalar", out=zs[:tr, c0:c0 + cw], in_=P[bi][:tr, :cw])
        k.dma("sync", out=u_d[r0:r0 + tr, :], in_=zs[:tr, 832:1856])
        if stop <= 1:
            continue
        for (c0, cw, dim) in ((0, 512, 512), (512, 256, 256)):
            k.memset("vector", ss1[:tr, :], 0.0)
            k.act(out=bufs["junk"][:tr, :cw], in_=zs[:tr, c0:c0 + cw], func=AF.Square, accum_out=ss1[:tr, :])
            rstd_from_ss(k, c, rs1[:tr, :], ss1[:tr, :], dim)
            k.act(out=qan[:tr, c0:c0 + cw], in_=zs[:tr, c0:c0 + cw], func=AF.Copy, scale=rs1[:tr, :])
        for kt in range(6):
            k.tr(out=pT[:, kt * 128:kt * 128 + tr], in_=qan[:tr, kt * 128:(kt + 1) * 128], ident=idb[:tr, :tr])
        pv6 = pT[:, :768].rearrange("p (k t) -> p k t", t=128)[:, :, :tr]
        k.tt("vector", out=qanT[:, 0:4, :tr], in0=pv6[:, 0:4, :], in1=gqa[:, :].unsq_bcast(tr), op=ALU.mult)
        k.tt("vector", out=qanT[:, 4:6, :tr], in0=pv6[:, 4:6, :], in1=gkva[:, :].unsq_bcast(tr), op=ALU.mult)
        if stop <= 2:
            continue
        for bi in range(4):
            for kt in range(2):
                k.mm(out=P[bi][:tr, :], lhsT=qanT[:, 4 + kt, :tr], rhs=wukv[:, kt, bi * 512:(bi + 1) * 512], start=(kt == 0), stop=(kt == 1))
        for bi in range(4):
            kvv = P[bi][:tr, :].rearrange("p (h d) -> p h d", d=256)
            if stop != 34:
                k.cp("vector", out=vt[:tr, bi * 2:bi * 2 + 2, :], in_=kvv[:, :, 128:256])
            k.cp("vector", out=t1[:tr, bi * 2:bi * 2 + 2, 0:128], in_=kvv[:, :, 0:128])
        if stop != 33:
            k.dma("sync", out=v_d[r0:r0 + tr, :], in_=vt[:tr, :, :].rearrange("p h d -> p (h d)"))
        if stop <= 3 or stop in (33, 34):
            continue
        k.tt("gpsimd", out=kpe[:tr, :], in0=zs[:tr, 768:832], in1=gk[:tr, 128:192], op=ALU.mult)
        kp5 = kpe[:tr, :].rearrange("p (h a b f) -> p h a b f", h=1, a=2, b=2)
        kr5 = kpr[:tr, :].rearrange("p (h a b f) -> p h a b f", h=1, a=2, b=2)
        rope_apply(k, "gpsimd", kr5, kp5, bc_heads(cs, 1), bc_heads(sn, 1), rtmp, tr, 1)
        if stop <= 4:
            continue
        k.act(out=sq[:tr, :1024].rearrange("p (h d) -> p h d", d=128), in_=t1[:tr, :, 0:128], func=AF.Square)
        k.I("vector", "tensor_reduce", out=ss8[:tr, :], in_=sq[:tr, :1024].rearrange("p (h d) -> p h d", d=128), axis=AX.X, op=ALU.add)
        k.memset("vector", ss1[:tr, :], 0.0)
        k.act(out=bufs["junk"][:tr, :64], in_=zs[:tr, 768:832], func=AF.Square, accum_out=ss1[:tr, :])
        k.ts("vector", out=ss8[:tr, :], in0=ss8[:tr, :], s1=ss1[:tr, :], s2=None, op0=ALU.add)
        rstd_from_ss(k, c, rs8[:tr, :], ss8[:tr, :], 192)
        k.tt("vector", out=t2[:tr, :, 0:128], in0=t1[:tr, :, 0:128], in1=rs8[:tr, :].unsq_bcast(128), op=ALU.mult)
        gkn = View(gk, bass.AP(gk[:tr, 0:128].ap.tensor, gk[:tr, 0:128].ap.offset, [list(gk[:tr, 0:128].ap.ap[0]), [0, 8], [1, 128]]))
        k.tt("gpsimd", out=kf[:tr, :, 0:128], in0=t2[:tr, :, 0:128], in1=gkn, op=ALU.mult)
        kprb = View(kpr, bass.AP(kpr[:tr, :].ap.tensor, kpr[:tr, :].ap.offset, [list(kpr[:tr, :].ap.ap[0]), [0, 8], [1, 64]]))
        k.tt("vector", out=kf[:tr, :, 128:192], in0=kprb, in1=rs8[:tr, :].unsq_bcast(64), op=ALU.mult)
        if stop <= 5:
            continue
        o = heads_T(kf, kT_d, 2 * t)
        k.dma("sync", out=kT_d[:, 0:128, r0:r0 + tr].rearrange("h p t -> p h t"), in_=o[:, :, 0, :tr])
        k.dma("sync", out=kT_d[:, 128:192, r0:r0 + tr].rearrange("h p t -> p h t"), in_=o[:64, :, 1, :tr])
        if want_q and stop > 6:
            for bi in range(3):
                for kt in range(4):
                    k.mm(out=P[4 + bi][:tr, :], lhsT=qanT[:, kt, :tr], rhs=wuq[:, kt, bi * 512:(bi + 1) * 512], start=(kt == 0), stop=(kt == 3))
                k.cp("scalar", out=t1[:tr, :, :].rearrange("p h d -> p (h d)")[:, bi * 512:(bi + 1) * 512], in_=P[4 + bi][:tr, :])
            k.act(out=sq[:tr, :], in_=t1[:tr, :, :].rearrange("p h d -> p (h d)"), func=AF.Square)
            k.I("vector", "tensor_reduce", out=ss8[:tr, :], in_=sq[:tr, :].rearrange("p (h d) -> p h d", d=192), axis=AX.X, op=ALU.add)
            rstd_from_ss(k, c, rs8[:tr, :], ss8[:tr, :], 192)
            k.tt("vector", out=t2[:tr, :, :], in0=t1[:tr, :, :], in1=rs8[:tr, :].unsq_bcast(192), op=ALU.mult)
            gqn = View(gq, bass.AP(gq[:tr, :].ap.tensor, gq[:tr, :].ap.offset, [list(gq[:tr, :].ap.ap[0]), [0, 8], [1, 192]]))
            k.tt("gpsimd", out=t1[:tr, :, :], in0=t2[:tr, :, :], in1=gqn, op=ALU.mult)
            k.cp("scalar", out=qf[:tr, :, 0:128], in_=t1[:tr, :, 0:128])
            rope_apply(k, "vector", rope_views(qf, 8), rope_views(t1, 8), bc_heads(cs, 8), bc_heads(sn, 8), rtmp, tr, 8)
            o = heads_T(qf, qT_d, 2 * t + 1)
            k.dma("sync", out=qT_d[:, 0:128, r0:r0 + tr].rearrange("h p t -> p h t"), in_=o[:, :, 0, :tr])
            k.dma("sync", out=qT_d[:, 128:192, r0:r0 + tr].rearrange("h p t -> p h t"), in_=o[:64, :, 1, :tr])


SEG = 256
NSEG = 65
PI = 3.141592653589793


def sincos(k, dst, src, shift, w1, w2, kint):
    k.ts("vector", out=w1, in0=src, s1=shift + 8 * PI, s2=1.0 / (2 * PI), op0=ALU.add, op1=ALU.mult)
    k.cp("vector", out=kint, in_=w1)
    k.cp("vector", out=w1, in_=kint)
    k.ts("vector", out=w2, in0=src, s1=shift + 8 * PI, s2=None, op0=ALU.add)
    k.stt("vector", out=w1, in0=w1, scalar=-2 * PI, in1=w2, op0=ALU.mult, op1=ALU.add)
    k.ts("vector", out=w2, in0=w1, s1=PI, s2=2 * PI, op0=ALU.is_gt, op1=ALU.mult)
    k.tt("vector", out=w1, in0=w1, in1=w2, op=ALU.subtract)
    k.ts("vector", out=w1, in0=w1, s1=-PI, s2=PI, op0=ALU.max, op1=ALU.min)
    k.act(out=dst, in_=w1, func=AF.Sin)


def s5_stage(k, c, U_d, lamre_d, lamim_d, logdt_d, BTre_d, BTim_d, CTre_d, CTim_d, jtab_d, Y_d, nseg=NSEG):
    P = c["P"]
    S = SEG
    sb = lambda n, s: k.sb("s5_" + n, s, F32)
    lre, lim, ldt = sb("lre", [64, 16]), sb("lim", [64, 16]), sb("ldt", [64, 16])
    BTre, BTim = sb("BTre", [16, 16, 64]), sb("BTim", [16, 16, 64])
    CTre, CTim = sb("CTre", [64, 16, 16]), sb("CTim", [64, 16, 16])
    jt = sb("jt", [64, S + 1])
    for (b, d_) in ((lre, lamre_d), (lim, lamim_d)):
        k.dma("sync", out=b[:, :], in_=d_[:, :])
    k.dma("sync", out=ldt[:, :], in_=logdt_d[:].pbcast(64))
    k.dma("sync", out=BTre[:, :, :], in_=BTre_d[:, :, :])
    k.dma("sync", out=BTim[:, :, :], in_=BTim_d[:, :, :])
    k.dma("sync", out=CTre[:, :, :], in_=CTre_d[:, :, :])
    k.dma("sync", out=CTim[:, :, :], in_=CTim_d[:, :, :])
    k.dma("sync", out=jt[:, :], in_=jtab_d[:, :])
    negpi = sb("negpi", [64, 1])
    k.memset("vector", negpi[:, :], -PI)
    dt, th, mag = sb("dt", [64, 16]), sb("th", [64, 16]), sb("mag", [64, 16])
    k.act(out=dt[:, :], in_=ldt[:, :], func=AF.Exp)
    k.tt("vector", out=th[:, :], in0=lim[:, :], in1=dt[:, :], op=ALU.mult)
    k.tt("vector", out=mag[:, :], in0=lre[:, :], in1=dt[:, :], op=ALU.mult)
    k.act(out=mag[:, :], in_=mag[:, :], func=AF.Exp)
    g = sb("g", [64, 2, 16, S])
    h = sb("h", [64, 2, 16, S])
    ang = g[:, 0, :, :]
    cosT, sinT = sb("cosT", [64, 16, S]), sb("sinT", [64, 16, S])
    tA = sb("tA", [64, 16, S])
    ki = tA.bitcast_view(mybir.dt.int32)
    jv = jt[:, 0:S].ap
    jb = View(jt, bass.AP(jv.tensor, jv.offset, [list(jv.ap[0]), [0, 16], [1, S]]))
    k.tt("vector", out=ang, in0=jb, in1=th[:, :].unsq_bcast(S), op=ALU.mult)

    sincos(k, sinT[:, :, :], ang, 0.0, h[:, 0, :, :], h[:, 1, :, :], ki[:, :, :])
    sincos(k, cosT[:, :, :], ang, PI / 2, h[:, 0, :, :], h[:, 1, :, :], ki[:, :, :])
    psc, pss, pa = sb("psc", [64, 16]), sb("pss", [64, 16]), sb("pa", [64, 16])
    pw1, pw2 = sb("pw1", [64, 16]), sb("pw2", [64, 16])
    k.ts("vector", out=pa[:, :], in0=th[:, :], s1=float(S), s2=None, op0=ALU.mult)
    sincos(k, pss[:, :], pa[:, :], 0.0, pw1[:, :], pw2[:, :], ki[:, :, 0])
    sincos(k, psc[:, :], pa[:, :], PI / 2, pw1[:, :], pw2[:, :], ki[:, :, 0])
    are, aim = sb("are", [64, 16]), sb("aim", [64, 16])
    k.tt("vector", out=are[:, :], in0=mag[:, :], in1=cosT[:, :, 1], op=ALU.mult)
    k.tt("vector", out=aim[:, :], in0=mag[:, :], in1=sinT[:, :, 1], op=ALU.mult)
    den, w1, w2 = sb("den", [64, 16]), sb("w1", [64, 16]), sb("w2", [64, 16])
    kre, kim, nr = sb("kre", [64, 16]), sb("kim", [64, 16]), sb("nr", [64, 16])
    k.tt("vector", out=w1[:, :], in0=lre[:, :], in1=lre[:, :], op=ALU.mult)
    k.tt("vector", out=w2[:, :], in0=lim[:, :], in1=lim[:, :], op=ALU.mult)
    k.tt("vector", out=den[:, :], in0=w1[:, :], in1=w2[:, :], op=ALU.add)
    k.I("vector", "reciprocal", out=den[:, :], in_=den[:, :])
    k.ts("vector", out=nr[:, :], in0=are[:, :], s1=-1.0, s2=None, op0=ALU.add)
    k.tt("vector", out=w1[:, :], in0=nr[:, :], in1=lre[:, :], op=ALU.mult)
    k.tt("vector", out=w2[:, :], in0=aim[:, :], in1=lim[:, :], op=ALU.mult)
    k.tt("vector", out=kre[:, :], in0=w1[:, :], in1=w2[:, :], op=ALU.add)
    k.tt("vector", out=kre[:, :], in0=kre[:, :], in1=den[:, :], op=ALU.mult)
    k.tt("vector", out=w1[:, :], in0=aim[:, :], in1=lre[:, :], op=ALU.mult)
    k.tt("vector", out=w2[:, :], in0=nr[:, :], in1=lim[:, :], op=ALU.mult)
    k.tt("vector", out=kim[:, :], in0=w1[:, :], in1=w2[:, :], op=ALU.subtract)
    k.tt("vector", out=kim[:, :], in0=kim[:, :], in1=den[:, :], op=ALU.mult)
    PhR, PhI, magT = sb("PhR", [64, 16, S]), sb("PhI", [64, 16, S]), sb("magT", [64, 16, S])
    tB = h[:, 0, :, :]
    cS, sS = cosT[:, :, :], sinT[:, :, :]
    k.tt("vector", out=tA[:, :, :], in0=cS, in1=kre[:, :].unsq_bcast(S), op=ALU.mult)
    k.tt("gpsimd", out=tB, in0=sS, in1=kim[:, :].unsq_bcast(S), op=ALU.mult)
    k.tt("vector", out=PhR[:, :, :], in0=tA[:, :, :], in1=tB, op=ALU.add)
    k.tt("vector", out=tA[:, :, :], in0=cS, in1=kim[:, :].unsq_bcast(S), op=ALU.mult)
    k.tt("gpsimd", out=tB, in0=sS, in1=kre[:, :].unsq_bcast(S), op=ALU.mult)
    k.tt("vector", out=PhI[:, :, :], in0=tA[:, :, :], in1=tB, op=ALU.subtract)
    k.memset("vector", magT[:, :, :], 1.0)
    k.tt("vector", out=magT[:, :, :], in0=magT[:, :, :], in1=mag[:, :].unsq_bcast(S), op=ALU.mult)
    nCTim = sb("nCTim", [64, 16, 16])
    k.ts("vector", out=nCTim[:, :, :], in0=CTim[:, :, :], s1=-1.0, s2=None, op0=ALU.mult)
    car = sb("car", [64, 2, 16])
    k.memset("vector", car[:, :, :], 0.0)
    m = [sb(f"m{i}", [64, 2, S]) for i in range(2)]
    t4 = [sb("t4_0", [64, 4, S])] * 2
    Ub = [sb("U0", [16, 16, S])] * 2
    Yb = [sb("Y0", [16, 8, S])] * 2
    cw = sb("cw", [64, 6, 16])
    Yv = Y_d.ap().rearrange("l c t -> c l t")
    for sg in range(nseg):
        U = Ub[sg % 2]
        k.dma("sync", out=U[:, :, :], in_=U_d[:, :, sg * S:(sg + 1) * S])
        for l in range(16):
            ps = P[l % 4]
            k.mm(out=ps[:64, 0:S], lhsT=BTre[:, l, :], rhs=U[:, l, :])
            k.mm(out=ps[:64, S:2 * S], lhsT=BTim[:, l, :], rhs=U[:, l, :])
            bre, bim = ps[:64, 0:S], ps[:64, S:2 * S]
            tt_ = t4[l % 2]
            mm_ = m[l % 2]
            k.tt("vector", out=tt_[:, 0, :], in0=bre, in1=PhR[:, l, :], op=ALU.mult)
            k.tt("vector", out=tt_[:, 1, :], in0=bim, in1=PhI[:, l, :], op=ALU.mult)
            k.tt("vector", out=tt_[:, 2, :], in0=bre, in1=PhI[:, l, :], op=ALU.mult)
            k.tt("vector", out=tt_[:, 3, :], in0=bim, in1=PhR[:, l, :], op=ALU.mult)
            k.tt("gpsimd", out=mm_[:, 0, :], in0=tt_[:, 0, :], in1=tt_[:, 1, :], op=ALU.subtract)
            k.tt("gpsimd", out=mm_[:, 1, :], in0=tt_[:, 2, :], in1=tt_[:, 3, :], op=ALU.add)
            for ri in range(2):
                k.I("vector", "tensor_tensor_scan", out=g[:, ri, l, :], data0=magT[:, l, :], data1=mm_[:, ri, :],
                    initial=car[:, ri, l:l + 1], op0=ALU.mult, op1=ALU.add)
        gl_re, gl_im = g[:, 0, :, S - 1], g[:, 1, :, S - 1]
        pc, psn = psc[:, :], pss[:, :]
        k.tt("vector", out=cw[:, 0, :], in0=gl_re, in1=pc, op=ALU.mult)
        k.tt("vector", out=cw[:, 1, :], in0=gl_im, in1=psn, op=ALU.mult)
        k.tt("vector", out=cw[:, 2, :], in0=gl_re, in1=psn, op=ALU.mult)
        k.tt("vector", out=cw[:, 3, :], in0=gl_im, in1=pc, op=ALU.mult)
        k.tt("vector", out=car[:, 0, :], in0=cw[:, 0, :], in1=cw[:, 1, :], op=ALU.subtract)
        k.tt("vector", out=car[:, 1, :], in0=cw[:, 2, :], in1=cw[:, 3, :], op=ALU.add)
        if sg == 0:
            continue
        k.tt("gpsimd", out=h[:, 0, :, :], in0=g[:, 0, :, :], in1=cS, op=ALU.mult)
        k.tt("gpsimd", out=tA[:, :, :], in0=g[:, 1, :, :], in1=sS, op=ALU.mult)
        k.tt("vector", out=h[:, 0, :, :], in0=h[:, 0, :, :], in1=tA[:, :, :], op=ALU.subtract)
        k.tt("gpsimd", out=h[:, 1, :, :], in0=g[:, 0, :, :], in1=sS, op=ALU.mult)
        k.tt("gpsimd", out=tA[:, :, :], in0=g[:, 1, :, :], in1=cS, op=ALU.mult)
        k.tt("vector", out=h[:, 1, :, :], in0=h[:, 1, :, :], in1=tA[:, :, :], op=ALU.add)
        Y = Yb[sg % 2]
        for l in range(16):
            ps = P[4 + (l // 2) % 4]
            o = ps[:16, (l % 2) * S:(l % 2 + 1) * S]
            k.mm(out=o, lhsT=CTre[:, l, :], rhs=h[:, 0, l, :], start=True, stop=False)
            k.mm(out=o, lhsT=nCTim[:, l, :], rhs=h[:, 1, l, :], start=False, stop=True)
            if l % 2 == 1:
                l8 = (l - 1) % 8
                k.cp("scalar", out=Y[:, l8:l8 + 2, :].rearrange("p l t -> p (l t)"), in_=ps[:16, :])
            if l % 8 == 7:
                k.dma("sync", out=Yv[:, l - 7:l + 1, (sg - 1) * S:sg * S], in_=Y[:, :, :])


def attn_stage(k, c, qT_d, kT_d, v_d, attT_d, R=2048, NK=16640, nheads=8):
    P = c["P"]
    nkt = NK // 128
    ka = k.sb("at_ka", [128, NK], BF16)
    kb = k.sb("at_kb", [64, NK], BF16)
    vv = k.sb("at_v", [128, nkt, 128], BF16)
    qa = k.sb("at_qa", [128, R], BF16)
    qb = k.sb("at_qb", [64, R], BF16)
    ones = k.sb("at_ones", [128, 128], BF16)
    k.memset("vector", ones[:, :], 1.0)
    PT = [k.sb(f"at_PT{i}", [128, 512], BF16) for i in range(3)]
    rden = k.sb("at_rden", [128, 512], F32)
    oT = [k.sb(f"at_oT{i}", [128, 512], BF16) for i in range(2)]
    it = 0
    for h in range(nheads):
        k.dma("sync", out=ka[:, :], in_=kT_d[h, 0:128, :])
        k.dma("sync", out=kb[:, :], in_=kT_d[h, 128:192, :])
        vsrc = v_d[:, h * 128:(h + 1) * 128].rearrange("(t p) e -> p t e", p=128)
        hk = nkt // 2
        k.dma("gpsimd", out=vv[:, :hk, :], in_=vsrc[:, :hk, :])
        k.dma("gpsimd", out=vv[:, hk:, :], in_=vsrc[:, hk:, :])
        k.dma("sync", out=qa[:, :], in_=qT_d[h, 0:128, :])
        k.dma("sync", out=qb[:, :], in_=qT_d[h, 128:192, :])
        for qb_i in range(R // 512):
            qs = slice(qb_i * 512, (qb_i + 1) * 512)
            Ops, Dps = P[4 + (qb_i % 2)], P[6 + (qb_i % 2)]
            for kt in range(nkt):
                S = P[it % 3]
                pt = PT[it % 3]
                ks = slice(kt * 128, (kt + 1) * 128)
                k.mm(out=S[:, :], lhsT=ka[:, ks], rhs=qa[:, qs], start=True, stop=False)
                k.mm(out=S[:, :], lhsT=kb[:, ks], rhs=qb[:, qs], start=False, stop=True)
                k.act(out=pt[:, :], in_=S[:, :], func=AF.Exp)
                k.mm(out=Ops[:, :], lhsT=vv[:, kt, :], rhs=pt[:, :], start=(kt == 0), stop=(kt == nkt - 1))
                k.mm(out=Dps[:, :], lhsT=ones[:, :], rhs=pt[:, :], start=(kt == 0), stop=(kt == nkt - 1))
                it += 1
            k.I("vector", "reciprocal", out=rden[:, :], in_=Dps[:, :])
            o = oT[qb_i % 2]
            k.tt("vector", out=o[:, :], in0=Ops[:, :], in1=rden[:, :], op=ALU.mult)
            k.dma("sync", out=attT_d[h * 128:(h + 1) * 128, qs], in_=o[:, :])


def evout_stage(k, c, xa_d, u_d, yf_d, yb_d, attT_d, d_d, gluw_d, glub_d, wout_d, gate_d, xm_d, R=2048):
    P = c["P"]
    idb = c["identb"]
    wout = k.sb("eo_wout", [128, 16, 2048], BF16)
    gluw = k.sb("eo_gluw", [128, 8, 1024], BF16)
    k.dma("gpsimd", out=wout[:, :, :], in_=wout_d.ap().rearrange("(kt p) c -> p kt c", p=128))
    k.dma("gpsimd", out=gluw[:, :, :], in_=gluw_d.ap().rearrange("(kt p) c -> p kt c", p=128))
    drow, brow, grow = k.sb("eo_drow", [128, 1024], F32), k.sb("eo_brow", [128, 1024], F32), k.sb("eo_grow", [128, 2048], F32)
    k.dma("sync", out=drow[:, :], in_=d_d[:].pbcast(128))
    k.dma("sync", out=brow[:, :], in_=glub_d[:].pbcast(128))
    k.dma("sync", out=grow[:, :], in_=gate_d[:].pbcast(128))
    ut, yft, ybt = k.sb("eo_u", [128, 1024], F32), k.sb("eo_yf", [128, 1024], F32), k.sb("eo_yb", [128, 1024], F32)
    y, w1, ge = k.sb("eo_y", [128, 1024], F32), k.sb("eo_w1", [128, 1024], F32), k.sb("eo_ge", [128, 1024], F32)
    geb, ssb = k.sb("eo_geb", [128, 1024], BF16), k.sb("eo_ssb", [128, 1024], BF16)
    geT, ssT = k.sb("eo_geT", [128, 8, 128], BF16), k.sb("eo_ssT", [128, 8, 128], BF16)
    att = k.sb("eo_att", [128, 8, 128], BF16)
    xt = k.sb("eo_x", [128, 2048], F32)
    pT = P[7].bitcast_view(BF16)
    attv = attT_d.ap().rearrange("(kt p) t -> p kt t", p=128)
    for t in range(R // 128):
        rs = slice(t * 128, (t + 1) * 128)
        k.dma("sync", out=ut[:, :], in_=u_d[rs, :])
        k.dma("sync", out=yft[:, :], in_=yf_d[rs, :])
        k.dma("sync", out=ybt[:, :], in_=yb_d[rs, :])
        k.dma("sync", out=xt[:, :], in_=xa_d[rs, :])
        k.dma("sync", out=att[:, :, :], in_=attv[:, :, rs])
        k.tt("vector", out=y[:, :], in0=ut[:, :], in1=drow[:, :], op=ALU.mult)
        k.tt("gpsimd", out=w1[:, :], in0=yft[:, :], in1=ybt[:, :], op=ALU.add)
        k.tt("vector", out=y[:, :], in0=y[:, :], in1=w1[:, :], op=ALU.add)
        k.tt("gpsimd", out=w1[:, :], in0=y[:, :], in1=y[:, :], op=ALU.mult)
        k.ts("vector", out=w1[:, :], in0=w1[:, :], s1=0.044715, s2=1.0, op0=ALU.mult, op1=ALU.add)
        k.tt("vector", out=w1[:, :], in0=w1[:, :], in1=y[:, :], op=ALU.mult)
        k.act(out=w1[:, :], in_=w1[:, :], func=AF.Sigmoid, scale=1.5957691216057308)
        k.tt("vector", out=ge[:, :], in0=y[:, :], in1=w1[:, :], op=ALU.mult)
        k.cp("gpsimd", out=geb[:, :], in_=ge[:, :])
        for kt in range(8):
            k.tr(out=pT[:, kt * 128:(kt + 1) * 128], in_=geb[:, kt * 128:(kt + 1) * 128], ident=idb[:, :])
        k.cp("vector", out=geT[:, :, :].rearrange("p k t -> p (k t)"), in_=pT[:, :])
        for bi in range(2):
            for kt in range(8):
                k.mm(out=P[bi][:, :], lhsT=geT[:, kt, :], rhs=gluw[:, kt, bi * 512:(bi + 1) * 512], start=(kt == 0), stop=(kt == 7))
            k.tt("vector", out=w1[:, bi * 512:(bi + 1) * 512], in0=P[bi][:, :], in1=brow[:, bi * 512:(bi + 1) * 512], op=ALU.add)
        k.act(out=w1[:, :], in_=w1[:, :], func=AF.Sigmoid)
        k.tt("vector", out=ssb[:, :], in0=ge[:, :], in1=w1[:, :], op=ALU.mult)
        for kt in range(8):
            k.tr(out=pT[:, kt * 128:(kt + 1) * 128], in_=ssb[:, kt * 128:(kt + 1) * 128], ident=idb[:, :])
        k.cp("vector", out=ssT[:, :, :].rearrange("p k t -> p (k t)"), in_=pT[:, :])
        for bi in range(4):
            ps = P[2 + bi]
            for kt in range(16):
                lt = att[:, kt, :] if kt < 8 else ssT[:, kt - 8, :]
                k.mm(out=ps[:, :], lhsT=lt, rhs=wout[:, kt, bi * 512:(bi + 1) * 512], start=(kt == 0), stop=(kt == 15))
            cs_ = slice(bi * 512, (bi + 1) * 512)
            k.tt("vector", out=y[:, 0:512], in0=ps[:, :], in1=grow[:, cs_], op=ALU.mult)
            k.tt("vector", out=xt[:, cs_], in0=xt[:, cs_], in1=y[:, 0:512], op=ALU.add)
        k.dma("sync", out=xm_d[rs, :], in_=xt[:, :])


def hypre_stage(k, c, xh_d, hmask_d, A, Braw, w_in_d, cw_d, cb_d, zc_d):
    P = c["P"]
    if c.get("nm_bufs") is None:
        c["nm_bufs"] = dict(
            junk=k.sb("junk", [128, 2048], BF16), ss=k.sb("ss", [128, 1], F32), rstd=k.sb("rstd", [128, 1], F32),
            xh=k.sb("xh", [128, 2048], BF16), tmpT=k.sb("tmpT", [128, 8, 128], F32))
    bufs = c["nm_bufs"]
    hT = k.sb("hp_hT", [128, 16, 2050], BF16)
    hTh = k.sb("hp_hTh", [128, 16, 2], BF16)
    hm = k.sb("hp_hm", [128, 2], F32)
    cw = k.sb("hp_cw", [128, 48, 3], F32)
    cb = k.sb("hp_cb", [128, 48], F32)
    k.dma("sync", out=hm[:, :], in_=hmask_d[:, :])
    k.dma("sync", out=cw[:, :, :], in_=cw_d[:, :, :])
    k.dma("sync", out=cb[:, :], in_=cb_d[:, :])
    xt = [k.sb(f"hp_x{i}", [128, 2048], F32) for i in range(2)]
    for t in range(16):
        x = xt[t % 2]
        k.dma("sync", out=x[:, :], in_=xh_d[1 + t * 128:1 + (t + 1) * 128, :])
        norm_mod_T(k, c, "hp", x[:, :], 128, A, Braw, hT[:, :, 1 + t * 128:1 + (t + 1) * 128], bufs)
    xhalo = k.sb("hp_xhalo", [2, 2048], F32)
    k.dma("sync", out=xhalo[0:1, :], in_=xh_d[0:1, :])
    k.dma("sync", out=xhalo[1:2, :], in_=xh_d[2049:2050, :])
    norm_mod_T(k, c, "hp", xhalo[:2, :], 2, A, Braw, hTh[:, :, :], bufs)
    k.ts("vector", out=hT[:, :, 0:1], in0=hTh[:, :, 0:1], s1=hm[:, 0:1], s2=None, op0=ALU.mult)
    k.ts("vector", out=hT[:, :, 2049:2050], in0=hTh[:, :, 1:2], s1=hm[:, 1:2], s2=None, op0=ALU.mult)
    wv = w_in_d.ap().rearrange("(kt p) c -> p kt c", p=128)
    wj = [k.sb(f"hp_w{i}", [128, 16, 128], BF16) for i in range(2)]
    osb = [k.sb(f"hp_o{i}", [128, 512], F32) for i in range(2)]
    k.dma("gpsimd", out=wj[0][:, :, :], in_=wv[:, :, 0:128])
    it = 0
    for j in range(48):
        if j + 1 < 48:
            k.dma("gpsimd", out=wj[(j + 1) % 2][:, :, :], in_=wv[:, :, (j + 1) * 128:(j + 2) * 128])
        w = wj[j % 2]
        for b in range(4):
            zp = P[b]
            zh = P[4 + b]
            for kt in range(16):
                k.mm(out=zp[:, :], lhsT=w[:, kt, :], rhs=hT[:, kt, 1 + 512 * b:1 + 512 * (b + 1)], start=(kt == 0), stop=(kt == 15))
            for kt in range(16):
                k.mm(out=zh[:, 0:2], lhsT=w[:, kt, :], rhs=hT[:, kt, 512 * b:512 * b + 514:513], start=(kt == 0), stop=(kt == 15))
            o = osb[it % 2]
            w0, w1, w2 = cw[:, j, 0:1], cw[:, j, 1:2], cw[:, j, 2:3]
            k.ts("vector", out=o[:, :], in0=zp[:, :], s1=w1, s2=cb[:, j:j + 1], op0=ALU.mult, op1=ALU.add)
            k.stt("vector", out=o[:, 1:512], in0=zp[:, 0:511], scalar=w0, in1=o[:, 1:512], op0=ALU.mult, op1=ALU.add)
            k.stt("vector", out=o[:, 0:511], in0=zp[:, 1:512], scalar=w2, in1=o[:, 0:511], op0=ALU.mult, op1=ALU.add)
            k.stt("vector", out=o[:, 0:1], in0=zh[:, 0:1], scalar=w0, in1=o[:, 0:1], op0=ALU.mult, op1=ALU.add)
            k.stt("vector", out=o[:, 511:512], in0=zh[:, 1:2], scalar=w2, in1=o[:, 511:512], op0=ALU.mult, op1=ALU.add)
            k.dma("sync", out=zc_d[j * 128:(j + 1) * 128, 512 * b:512 * (b + 1)], in_=o[:, :])
            it += 1


def projres_stage(k, c, xin_d, yT_d, wout_d, gate_d, xm_d, R=2048):
    P = c["P"]
    wout = k.sb("pr_wout", [128, 16, 2048], BF16)
    yT = k.sb("pr_yT", [128, 16, R], BF16)
    k.dma("gpsimd", out=wout[:, :, :], in_=wout_d.ap().rearrange("(kt p) c -> p kt c", p=128))
    yv = yT_d.ap().rearrange("(kt p) t -> p kt t", p=128)
    for kt in range(16):
        k.dma("gpsimd", out=yT[:, kt, :], in_=yv[:, kt, :])
    grow = k.sb("pr_grow", [128, 2048], F32)
    k.dma("sync", out=grow[:, :], in_=gate_d[:].pbcast(128))
    xt = [k.sb(f"pr_x{i}", [128, 2048], F32) for i in range(2)]
    tmp = k.sb("pr_tmp", [128, 512], F32)
    for t in range(R // 128):
        rs = slice(t * 128, (t + 1) * 128)
        x = xt[t % 2]
        k.dma("sync", out=x[:, :], in_=xin_d[rs, :])
        for bi in range(4):
            ps = P[(t % 2) * 4 + bi]
            for kt in range(16):
                k.mm(out=ps[:, :], lhsT=yT[:, kt, rs], rhs=wout[:, kt, bi * 512:(bi + 1) * 512], start=(kt == 0), stop=(kt == 15))
            cs_ = slice(bi * 512, (bi + 1) * 512)
            k.tt("vector", out=tmp[:, :], in0=ps[:, :], in1=grow[:, cs_], op=ALU.mult)
            k.tt("vector", out=x[:, cs_], in0=x[:, cs_], in1=tmp[:, :], op=ALU.add)
        k.dma("sync", out=xm_d[rs, :], in_=x[:, :])


NFFT = 32768
LSEQ = 16384
CB = 16


def hy_consts():
    f8 = np.float64
    ts = np.arange(64, dtype=f8)[:, None]
    kf = np.arange(128, dtype=f8)[None, :]
    a1 = 2 * np.pi * ts * kf / 128
    F1cat = np.concatenate([np.cos(a1), -np.sin(a1)], axis=1)
    tf = np.arange(256, dtype=f8)[:, None]
    aw = 2 * np.pi * tf * kf / NFFT
    Wre = np.cos(aw).reshape(2, 128, 128).transpose(1, 0, 2)
    Wim = (-np.sin(aw)).reshape(2, 128, 128).transpose(1, 0, 2)
    ks = np.arange(256, dtype=f8)[None, :]
    a2 = 2 * np.pi * tf * ks / 256
    Cos = np.cos(a2).reshape(2, 128, 256).transpose(1, 0, 2)
    Sin = np.sin(a2).reshape(2, 128, 256).transpose(1, 0, 2)
    a2t = a2.T
    cA1 = np.concatenate([np.cos(a2t), np.sin(a2t)], axis=1).reshape(2, 128, 512).transpose(1, 0, 2)
    cA2 = np.concatenate([-np.sin(a2t), np.cos(a2t)], axis=1).reshape(2, 128, 512).transpose(1, 0, 2)
    awt = aw.T
    WTre, WTim = np.cos(awt), np.sin(awt)
    a1t = a1.T
    C1 = np.cos(a1t) / NFFT
    S1n = -np.sin(a1t) / NFFT
    b = lambda a: np.ascontiguousarray(a).astype(ml_dtypes.bfloat16)
    f = lambda a: np.ascontiguousarray(a, dtype=np.float32)
    return dict(F1cat=b(F1cat), Wre=f(Wre), Wim=f(Wim), Cos=b(Cos), Sin=b(Sin), NSin=b(-Sin), cA1=b(cA1), cA2=b(cA2),
                WTre=f(WTre), WTim=f(WTim), C1=b(C1), S1n=b(S1n))


HY_CONST_SHAPES = dict(F1cat=([64, 256], BF16), Wre=([128, 2, 128], F32), Wim=([128, 2, 128], F32), Cos=([128, 2, 256], BF16),
                       Sin=([128, 2, 256], BF16), NSin=([128, 2, 256], BF16), cA1=([128, 2, 512], BF16), cA2=([128, 2, 512], BF16),
                       WTre=([128, 256], F32), WTim=([128, 256], F32), C1=([128, 64], BF16), S1n=([128, 64], BF16))


def hy_load_consts(k, cd):
    out = {}
    for n, (shp, dt) in HY_CONST_SHAPES.items():
        b = k.sb("hc_" + n, shp, dt)
        src = cd[n]
        k.dma("sync", out=b.ap(), in_=src.ap())
        out[n] = b
    return out


def fft_stageA(k, c, hc, ybf, nch, Ab_re, Ab_im, tw, it0=0):
    P = c["P"]
    it = it0
    for cp in range(nch // 2):
        for half in range(2):
            ps = P[it % 2]
            for ci in range(2):
                ch = cp * 2 + ci
                k.mm(out=ps[:, ci * 256:(ci + 1) * 256], lhsT=ybf[:, ch, half * 128:(half + 1) * 128], rhs=hc["F1cat"][:, :])
            pv = ps[:, :].rearrange("p (c r f) -> p c r f", c=2, r=2)
            Are, Aim = pv[:, :, 0, :], pv[:, :, 1, :]
            wre = hc["Wre"][:, half, :].ap
            wim = hc["Wim"][:, half, :].ap
            wre_b = View(hc["Wre"], bass.AP(wre.tensor, wre.offset, [list(wre.ap[0]), [0, 2], [1, 128]]))
            wim_b = View(hc["Wim"], bass.AP(wim.tensor, wim.offset, [list(wim.ap[0]), [0, 2], [1, 128]]))
            t = tw[it % 2]
            tv = lambda i: t[:, i, 0:256].rearrange("p (c f) -> p c f", c=2)
            k.tt("vector", out=tv(0), in0=Are, in1=wre_b, op=ALU.mult)
            k.tt("vector", out=tv(1), in0=Aim, in1=wim_b, op=ALU.mult)
            k.tt("vector", out=tv(2), in0=Are, in1=wim_b, op=ALU.mult)
            k.tt("vector", out=tv(3), in0=Aim, in1=wre_b, op=ALU.mult)
            k.tt("gpsimd", out=Ab_re[:, half, cp * 2:cp * 2 + 2, :], in0=tv(0), in1=tv(1), op=ALU.subtract)
            k.tt("gpsimd", out=Ab_im[:, half, cp * 2:cp * 2 + 2, :], in0=tv(2), in1=tv(3), op=ALU.add)
            it += 1
    return it


def fft_stageB_block(k, c, hc, Ab_re, Ab_im, hk, blk, want_re=True, want_im=True):
    P = c["P"]
    cs = slice(blk * 4, blk * 4 + 4)
    ksl = slice(hk * 128, (hk + 1) * 128)
    Xre, Xim = P[2 + (blk % 2) * 2], P[3 + (blk % 2) * 2]
    if want_re:
        n = 0
        for ht in range(2):
            for (lt, rb) in ((hc["Cos"], Ab_re), (hc["Sin"], Ab_im)):
                k.mm(out=Xre[:, :], lhsT=lt[:, ht, ksl], rhs=rb[:, ht, cs, :].rearrange("p c f -> p (c f)"), start=(n == 0), stop=(n == 3))
                n += 1
    if want_im:
        n = 0
        for ht in range(2):
            for (lt, rb) in ((hc["NSin"], Ab_re), (hc["Cos"], Ab_im)):
                k.mm(out=Xim[:, :], lhsT=lt[:, ht, ksl], rhs=rb[:, ht, cs, :].rearrange("p c f -> p (c f)"), start=(n == 0), stop=(n == 3))
                n += 1
    return Xre, Xim


def hyconv_stage(k, c, cd, x1_d, x2_d, v_d, skip_d, Gre_d, Gim_d, y2_d, ncb=16):
    P = c["P"]
    hc = hy_load_consts(k, cd)
    sk = k.sb("hv_sk", [64, 2, 256], F32)
    for o in range(2):
        k.dma("sync", out=sk[:, o, :], in_=skip_d[o, :].pbcast(64))
    yf = k.sb("hv_yf", [64, CB, 256], F32)
    gt = k.sb("hv_gt", [64, CB, 256], F32)
    ybf = k.sb("hv_ybf", [64, CB, 256], BF16)
    Ab_re, Ab_im = k.sb("hv_Abre", [128, 2, CB, 128], BF16), k.sb("hv_Abim", [128, 2, CB, 128], BF16)
    Zb_re, Zb_im = k.sb("hv_Zbre", [128, 2, CB, 128], BF16), k.sb("hv_Zbim", [128, 2, CB, 128], BF16)
    Bb_re, Bb_im = k.sb("hv_Bbre", [128, CB, 256], BF16), k.sb("hv_Bbim", [128, CB, 256], BF16)
    tw = [k.sb(f"hv_tw{i}", [128, 4, 512], F32) for i in range(2)]
    Gt = [(k.sb(f"hv_Gre{i}", [128, 512], F32), k.sb(f"hv_Gim{i}", [128, 512], F32)) for i in range(2)]
    wsk = k.sb("hv_wsk", [64, 2, 256], F32)
    w2 = k.sb("hv_w2", [64, 2, 256], F32)
    gates = (x1_d, x2_d)
    it = 0
    for cb in range(ncb):
        chs = slice(cb * CB, (cb + 1) * CB)
        k.dma("sync", out=yf[:, :, :], in_=v_d[chs, :].rearrange("c (s f) -> s c f", s=64))
        for o in range(2):
            k.dma("sync", out=gt[:, :, :], in_=gates[o][chs, :].rearrange("c (s f) -> s c f", s=64))
            k.cp("scalar", out=ybf[:, :, :].rearrange("p c f -> p (c f)"), in_=yf[:, :, :].rearrange("p c f -> p (c f)"))
            it = fft_stageA(k, c, hc, ybf, CB, Ab_re, Ab_im, tw, it)
            for hk in range(2):
                for blk in range(CB // 4):
                    Xre, Xim = fft_stageB_block(k, c, hc, Ab_re, Ab_im, hk, blk)
                    gre, gim = Gt[it % 2]
                    c0 = (cb * CB + blk * 4) * 128
                    k.dma("sync", out=gre[:, :], in_=Gre_d[o, hk, :, c0:c0 + 512])
                    k.dma("sync", out=gim[:, :], in_=Gim_d[o, hk, :, c0:c0 + 512])
                    t = tw[it % 2]
                    k.tt("vector", out=t[:, 0, :], in0=Xre[:, :], in1=gre[:, :], op=ALU.mult)
                    k.tt("vector", out=t[:, 1, :], in0=Xim[:, :], in1=gim[:, :], op=ALU.mult)
                    k.tt("vector", out=t[:, 2, :], in0=Xre[:, :], in1=gim[:, :], op=ALU.mult)
                    k.tt("vector", out=t[:, 3, :], in0=Xim[:, :], in1=gre[:, :], op=ALU.mult)
                    zs = slice(blk * 4, blk * 4 + 4)
                    k.tt("gpsimd", out=Zb_re[:, hk, zs, :].rearrange("p c f -> p (c f)"), in0=t[:, 0, :], in1=t[:, 1, :], op=ALU.subtract)
                    k.tt("gpsimd", out=Zb_im[:, hk, zs, :].rearrange("p c f -> p (c f)"), in0=t[:, 2, :], in1=t[:, 3, :], op=ALU.add)
                    it += 1
            for ch in range(CB):
                ps = P[6 + (ch % 2)]
                n = 0
                for hk in range(2):
                    for (zb, rt) in ((Zb_re, hc["cA1"]), (Zb_im, hc["cA2"])):
                        k.mm(out=ps[:, :], lhsT=zb[:, hk, ch, :], rhs=rt[:, hk, :], start=(n == 0), stop=(n == 3))
                        n += 1
                Bre, Bim = ps[:, 0:256], ps[:, 256:512]
                t = tw[it % 2]
                k.tt("vector", out=t[:, 0, 0:256], in0=Bre, in1=hc["WTre"][:, :], op=ALU.mult)
                k.tt("vector", out=t[:, 1, 0:256], in0=Bim, in1=hc["WTim"][:, :], op=ALU.mult)
                k.tt("vector", out=t[:, 2, 0:256], in0=Bre, in1=hc["WTim"][:, :], op=ALU.mult)
                k.tt("vector", out=t[:, 3, 0:256], in0=Bim, in1=hc["WTre"][:, :], op=ALU.mult)
                k.tt("gpsimd", out=Bb_re[:, ch, :], in0=t[:, 0, 0:256], in1=t[:, 1, 0:256], op=ALU.subtract)
                k.tt("gpsimd", out=Bb_im[:, ch, :], in0=t[:, 2, 0:256], in1=t[:, 3, 0:256], op=ALU.add)
                it += 1
            for pr in range(CB // 2):
                ps = P[pr % 2]
                cs2 = slice(pr * 2, pr * 2 + 2)
                k.mm(out=ps[:64, :], lhsT=hc["C1"][:, :], rhs=Bb_re[:, cs2, :].rearrange("p c f -> p (c f)"), start=True, stop=False)
                k.mm(out=ps[:64, :], lhsT=hc["S1n"][:, :], rhs=Bb_im[:, cs2, :].rearrange("p c f -> p (c f)"), start=False, stop=True)
                skb = sk[:, o, cb * CB + pr * 2:cb * CB + pr * 2 + 2].unsq_bcast(256)
                k.tt("gpsimd", out=wsk[:, :, :], in0=yf[:, cs2, :], in1=skb, op=ALU.mult)
                k.tt("vector", out=w2[:, :, :], in0=ps[:64, :].rearrange("p (c f) -> p c f", c=2), in1=wsk[:, :, :], op=ALU.add)
                k.tt("gpsimd", out=yf[:, cs2, :], in0=w2[:, :, :], in1=gt[:, cs2, :], op=ALU.mult)
        k.dma("sync", out=y2_d[chs, :].rearrange("c (s f) -> s c f", s=64), in_=yf[:, :, :])


def hy_filt_consts(ci):
    import math
    L = LSEQ
    t = np.arange(L, dtype=np.float32) / np.float32(L)
    bands = np.linspace(1e-4, 15, 16, dtype=np.float32)
    ang = (np.float32(2.0 * math.pi) * t[:, None] * bands[None, :]).astype(np.float32)
    feat = np.concatenate([t[:, None], np.cos(ang), -np.sin(ang)], axis=-1).astype(np.float32)
    dmin, dmax = math.log(1e-2) / 1.5, math.log(1e-2) / 0.3
    deltas = np.abs(np.linspace(dmin, dmax, 2048, dtype=np.float32))[ci * 256:(ci + 1) * 256].astype(np.float64)
    drow = np.broadcast_to(deltas[None, :], (64, 256))
    E1 = np.exp(-(256.0 * np.arange(64, dtype=np.float64)[:, None] / L) * deltas[None, :])
    return dict(featT=np.ascontiguousarray(feat.T), drow=np.ascontiguousarray(drow, dtype=np.float32), E1=np.ascontiguousarray(E1, dtype=np.float32))


def hyfilt_stage(k, c, cd, featT_d, w1_d, w2_d, w3_d, bf_d, wout_d, drow_d, E1_d, Gre_d, Gim_d, nfb=16, norders=2):
    P = c["P"]
    hc = {}
    for n in ("F1cat", "Wre", "Wim", "Cos", "Sin", "NSin"):
        shp, dt = HY_CONST_SHAPES[n]
        hc[n] = k.sb("hc_" + n, shp, dt)
        k.dma("sync", out=hc[n].ap(), in_=cd[n].ap())
    sb = lambda n, s, dt=F32: k.sb("hf_" + n, s, dt)
    w1s, w2s, w3s, bf = sb("w1", [33, 64]), sb("w2", [64, 64]), sb("w3", [64, 64]), sb("bf", [64, 4])
    k.dma("sync", out=w1s[:, :], in_=w1_d[:, :])
    k.dma("sync", out=w2s[:, :], in_=w2_d[:, :])
    k.dma("sync", out=w3s[:, :], in_=w3_d[:, :])
    k.dma("sync", out=bf[:, :], in_=bf_d[:, :])
    frb = sb("frb", [64, 3])
    for l in range(3):
        k.tt("vector", out=frb[:, l:l + 1], in0=bf[:, l:l + 1], in1=bf[:, 3:4], op=ALU.mult)
    drow, E1 = sb("drow", [64, 256]), sb("E1", [64, 256])
    k.dma("sync", out=drow[:, :], in_=drow_d[:, :])
    k.dma("sync", out=E1[:, :], in_=E1_d[:, :])
    hidT = sb("hidT", [64, LSEQ])
    ft = [sb(f"ft{i}", [33, 512]) for i in range(2)]
    a_, s1_, s2_ = sb("a", [64, 512]), sb("s1", [64, 512]), sb("s2", [64, 512])
    ki = k.sb("hf_ki", [64, 512], mybir.dt.int32)
    hA, hB = sb("hA", [64, 512]), sb("hB", [64, 512])
    ws = (w1s, w2s, w3s)
    for blk in range(LSEQ // 512):
        f = ft[blk % 2]
        k.dma("sync", out=f[:, :], in_=featT_d[:, blk * 512:(blk + 1) * 512])
        src = f[:, :]
        dsts = (hA[:, :], hB[:, :], hidT[:, blk * 512:(blk + 1) * 512])
        for l in range(3):
            ps = P[(blk * 3 + l) % 2]
            k.mm(out=ps[:64, :], lhsT=ws[l][:, :], rhs=src)
            k.ts("vector", out=a_[:, :], in0=ps[:64, :], s1=bf[:, 3:4], s2=frb[:, l:l + 1], op0=ALU.mult, op1=ALU.add)
            sincos(k, dsts[l], a_[:, :], 0.0, s1_[:, :], s2_[:, :], ki[:, :])
            src = dsts[l]
    ones = sb("ones", [64, 128])
    k.memset("vector", ones[:, :], 1.0)
    Wo = [sb(f"Wo{i}", [64, 2, 16]) for i in range(2)]
    e2 = [sb(f"e2_{i}", [64, 8, 32]) for i in range(2)]
    r8 = [sb(f"r8_{i}", [64, 8, 32]) for i in range(2)]
    Hf = sb("Hf", [64, 2, 16, 256])
    Hsd = k.sb("hf_Hsd", [64, 2, 16, 256], BF16)
    absum, s2n, rn = sb("absum", [64, 32]), sb("s2n", [64, 16]), sb("rn", [128, 16])
    Ab_re, Ab_im = k.sb("hf_Abre", [128, 2, 16, 128], BF16), k.sb("hf_Abim", [128, 2, 16, 128], BF16)
    tw = [sb(f"tw{i}", [128, 4, 512]) for i in range(2)]
    go = [sb(f"go{i}", [128, 512]) for i in range(2)]
    it = 0
    ig = 0
    for o in range(norders):
        for fb in range(nfb):
            chs = slice(fb * 16, (fb + 1) * 16)
            wo = Wo[fb % 2]
            k.dma("sync", out=wo[:, :, :], in_=wout_d[:, o, :, chs])
            dv = drow[:, chs].ap
            dr_b = View(drow, bass.AP(dv.tensor, dv.offset, [list(dv.ap[0]), [0, 2], [1, 16]]))
            ev = E1[:, chs].ap
            E1_b = View(E1, bass.AP(ev.tensor, ev.offset, [list(ev.ap[0]), [0, 2], [1, 16], [0, 8]]))
            wv = wo[:, :, :].rearrange("p d c -> p (d c)").ap
            wo_b = View(wo, bass.AP(wv.tensor, wv.offset, [list(wv.ap[0]), [0, 8], [1, 32]]))
            for tg in range(32):
                eb, rb = e2[tg % 2], r8[tg % 2]
                for j in range(8):
                    tfv = tg * 8 + j
                    k.act(out=eb[:, j, :].rearrange("p (d c) -> p d c", d=2), in_=dr_b, func=AF.Exp, scale=-float(tfv) / LSEQ)
                k.tt("vector", out=rb[:, :, :], in0=eb[:, :, :], in1=wo_b, op=ALU.mult)
                ps = P[2 + tg % 2]
                for j in range(8):
                    tfv = tg * 8 + j
                    k.mm(out=ps[:64, j * 32:(j + 1) * 32], lhsT=hidT[:, tfv:LSEQ:256], rhs=rb[:, j, :])
                k.tt("vector", out=Hf[:, :, :, tg * 8:(tg + 1) * 8], in0=ps[:64, 0:256].rearrange("p (j d c) -> p d c j", j=8, d=2),
                     in1=E1_b, op=ALU.mult)
            k.memset("vector", Hf[0:1, 1, :, 0:1], 0.0)
            k.I("vector", "tensor_reduce", out=absum[:, :], in_=Hf[:, :, :, :].rearrange("p d c f -> p (d c) f"), axis=AX.X, op=ALU.add,
                apply_absolute_value=True)
            k.tt("vector", out=s2n[:, :], in0=absum[:, 0:16], in1=absum[:, 16:32], op=ALU.add)
            psn = P[4]
            k.mm(out=psn[:, 0:16], lhsT=ones[:, :], rhs=s2n[:, :])
            k.I("vector", "reciprocal", out=rn[:, :], in_=psn[:, 0:16])
            k.tt("gpsimd", out=Hsd[:, 0, :, :], in0=Hf[:, 0, :, :], in1=Hf[:, 1, :, :], op=ALU.add)
            k.tt("gpsimd", out=Hsd[:, 1, :, :], in0=Hf[:, 0, :, :], in1=Hf[:, 1, :, :], op=ALU.subtract)
            for sd in range(2):
                it = fft_stageA(k, c, hc, Hsd[:, sd, :, :], 16, Ab_re, Ab_im, tw, it)
                for hk in range(2):
                    for blk in range(4):
                        Xre, Xim = fft_stageB_block(k, c, hc, Ab_re, Ab_im, hk, blk, want_re=(sd == 0), want_im=(sd == 1))
                        X = Xre if sd == 0 else Xim
                        g_ = go[ig % 2]
                        k.tt("vector", out=g_[:, :].rearrange("p (c f) -> p c f", c=4), in0=X[:, :].rearrange("p (c f) -> p c f", c=4),
                             in1=rn[:, blk * 4:blk * 4 + 4].unsq_bcast(128), op=ALU.mult)
                        c0 = (fb * 16 + blk * 4) * 128
                        dst = Gre_d if sd == 0 else Gim_d
                        k.dma("sync", out=dst[o, hk, :, c0:c0 + 512], in_=g_[:, :])
                        ig += 1


def _cols(v, n=16):
    return np.ascontiguousarray(np.asarray(v, np.float32).reshape(n, 128).T)


def _modc(m, k0):
    return np.ascontiguousarray(np.stack([_cols(m[k0]), _cols(m[k0 + 1]), _cols(m[k0 + 2])], axis=1).astype(np.float32))


def _rope_tabs(n_tokens):
    t = np.arange(n_tokens)
    row = (t // 64).astype(np.float32)
    col = (t % 64).astype(np.float32)
    inv = (10000.0 ** (-np.arange(16, dtype=np.float32) / 16)).astype(np.float32)
    ar = row[:, None] * inv
    ac = col[:, None] * inv
    cos = np.concatenate([np.cos(ar), np.cos(ac)], axis=1).astype(np.float32)
    sin = np.concatenate([np.sin(ar), np.sin(ac)], axis=1).astype(np.float32)
    return cos, sin


_IDF = np.eye(128, dtype=np.float32)
_IDB = _IDF.astype(ml_dtypes.bfloat16)
_PROGS = {}
_DBG = {}


def _prog(key, fn):
    if key not in _PROGS:
        _PROGS[key] = fn()
    return _PROGS[key]


def _run(nc, maps):
    return run_bass_kernel_spmd(nc, maps, core_ids=list(range(8))).results


def _build_ada():
    k = KB()
    cc = k.dram("cc", [128, 16, 2], F32, kind="ExternalInput")
    w = k.dram("w", [2, 2048, 2304], F32, kind="ExternalInput")
    b = k.dram("b", [2, 2304], F32, kind="ExternalInput")
    out = k.dram("out", [2, 2, 2304], F32, kind="ExternalOutput")
    c = alloc_common(k)
    ada_stage(k, c, cc, w, b, out)
    return k.emit()


def _build_ffn(R):
    k = KB()
    xin = k.dram("xin", [R, D], F32, kind="ExternalInput")
    xout = k.dram("xout", [R, D], F32, kind="ExternalOutput")
    modc = k.dram("modc", [128, 3, 16], F32, kind="ExternalInput")
    normg = k.dram("normg", [128, 16], F32, kind="ExternalInput")
    w_in = k.dram("w_in", [D, 2 * DFF], F32, kind="ExternalInput")
    w_out = k.dram("w_out", [DFF, D], F32, kind="ExternalInput")
    idf = k.dram("idf", [128, 128], F32, kind="ExternalInput")
    idb = k.dram("idb", [128, 128], BF16, kind="ExternalInput")
    c = alloc_common(k)
    load_ident(k, c, idf, idb)
    A, Braw, G = modcols_prepare(k, "m0", modc[:, :, :], normg[:, :], 0)
    ffn_stage(k, c, "f0", xin, xout, R, 128, A, Braw, G, w_in, w_out)
    return k.emit()


def _build_evpre(R):
    k = KB()
    di = lambda n, s, dt=F32: k.dram(n, s, dt, kind="ExternalInput")
    do = lambda n, s, dt=F32: k.dram(n, s, dt, kind="ExternalOutput")
    xin = di("xin", [R, D]); modc = di("modc", [128, 3, 16]); normg = di("normg", [128, 16])
    w_in = di("w_in", [D, 1856]); wuq = di("wuq", [512, 1536]); wukv = di("wukv", [256, 2048])
    gqa = di("gqa", [128, 4]); gkva = di("gkva", [128, 2]); gq = di("gq", [192]); gk = di("gk", [192])
    cos = di("cos", [R, 32]); sin = di("sin", [R, 32])
    idf = di("idf", [128, 128]); idb = di("idb", [128, 128], BF16)
    qT = do("qT", [8, 192, R], BF16); kT = do("kT", [8, 192, R], BF16); v = do("v", [R, 1024], BF16); u = do("u", [R, 1024])
    c = alloc_common(k)
    load_ident(k, c, idf, idb)
    A, Braw, G = modcols_prepare(k, "m1", modc[:, :, :], normg[:, :], 0)
    evpre_stage(k, c, xin, R, 128, A, Braw, w_in, wuq, wukv, gqa, gkva, gq, gk, cos, sin, qT, kT, v, u, True)
    return k.emit()


def _build_attn():
    k = KB()
    di = lambda n, s, dt=F32: k.dram(n, s, dt, kind="ExternalInput")
    qT = di("qT", [8, 192, 2048], BF16); kT = di("kT", [8, 192, 16640], BF16); v = di("v", [16640, 1024], BF16)
    attT = k.dram("attT", [1024, 2048], BF16, kind="ExternalOutput")
    c = alloc_common(k)
    attn_stage(k, c, qT, kT, v, attT, 2048, 16640, 8)
    return k.emit()


def _build_s5():
    k = KB()
    di = lambda n, s, dt=F32: k.dram(n, s, dt, kind="ExternalInput")
    U = di("U", [16, 16, 16640]); lre = di("lre", [64, 16]); lim = di("lim", [64, 16]); ldt = di("ldt", [16])
    BTre = di("BTre", [16, 16, 64]); BTim = di("BTim", [16, 16, 64]); CTre = di("CTre", [64, 16, 16]); CTim = di("CTim", [64, 16, 16])
    jt = di("jt", [64, SEG + 1])
    Y = k.dram("Y", [16, 16, 16384], F32, kind="ExternalOutput")
    c = alloc_common(k)
    s5_stage(k, c, U, lre, lim, ldt, BTre, BTim, CTre, CTim, jt, Y, NSEG)
    return k.emit()


def _build_evout():
    k = KB()
    R = 2048
    di = lambda n, s, dt=F32: k.dram(n, s, dt, kind="ExternalInput")
    xa = di("xa", [R, D]); u = di("u", [R, 1024]); yf = di("yf", [R, 1024]); yb = di("yb", [R, 1024]); attT = di("attT", [1024, R], BF16)
    dd = di("dd", [1024]); gluw = di("gluw", [1024, 1024]); glub = di("glub", [1024]); wout = di("wout", [2048, 2048]); gate = di("gate", [2048])
    idf = di("idf", [128, 128]); idb = di("idb", [128, 128], BF16)
    xm = k.dram("xm", [R, D], F32, kind="ExternalOutput")
    c = alloc_common(k)
    load_ident(k, c, idf, idb)
    evout_stage(k, c, xa, u, yf, yb, attT, dd, gluw, glub, wout, gate, xm, R)
    return k.emit()


def _build_hypre():
    k = KB()
    di = lambda n, s, dt=F32: k.dram(n, s, dt, kind="ExternalInput")
    xh = di("xhin", [2050, D]); hmask = di("hmask", [128, 2]); modc = di("modc", [128, 3, 16]); normg = di("normg", [128, 16])
    w_in = di("w_in", [D, 6144]); cw = di("cw", [128, 48, 3]); cb = di("cb", [128, 48])
    idf = di("idf", [128, 128]); idb = di("idb", [128, 128], BF16)
    zc = k.dram("zc", [6144, 2048], F32, kind="ExternalOutput")
    c = alloc_common(k)
    load_ident(k, c, idf, idb)
    A, Braw, G = modcols_prepare(k, "m1", modc[:, :, :], normg[:, :], 0)
    hypre_stage(k, c, xh, hmask, A, Braw, w_in, cw, cb, zc)
    return k.emit()


_FILT_CONSTS = ("F1cat", "Wre", "Wim", "Cos", "Sin", "NSin")


def _build_hyfilt():
    k = KB()
    di = lambda n, s, dt=F32: k.dram(n, s, dt, kind="ExternalInput")
    cd = {n: di("k_" + n, HY_CONST_SHAPES[n][0], HY_CONST_SHAPES[n][1]) for n in _FILT_CONSTS}
    featT = di("featT", [33, LSEQ]); w1 = di("w1", [33, 64]); w2 = di("w2", [64, 64]); w3 = di("w3", [64, 64]); bf = di("bf", [64, 4])
    wout = di("wout", [64, 2, 2, 256]); drow = di("drow", [64, 256]); E1 = di("E1", [64, 256])
    Gre = k.dram("Gre", [2, 2, 128, 256 * 128], F32, kind="ExternalOutput")
    Gim = k.dram("Gim", [2, 2, 128, 256 * 128], F32, kind="ExternalOutput")
    c = alloc_common(k)
    hyfilt_stage(k, c, cd, featT, w1, w2, w3, bf, wout, drow, E1, Gre, Gim, 16, 2)
    return k.emit()


def _build_hyconv():
    k = KB()
    di = lambda n, s, dt=F32: k.dram(n, s, dt, kind="ExternalInput")
    cd = {n: di("k_" + n, shp, dt) for n, (shp, dt) in HY_CONST_SHAPES.items()}
    x1 = di("x1", [256, LSEQ]); x2 = di("x2", [256, LSEQ]); v = di("v", [256, LSEQ]); skip = di("skip", [2, 256])
    Gre = di("Gre", [2, 2, 128, 256 * 128]); Gim = di("Gim", [2, 2, 128, 256 * 128])
    y2 = k.dram("y2", [256, LSEQ], F32, kind="ExternalOutput")
    c = alloc_common(k)
    hyconv_stage(k, c, cd, x1, x2, v, skip, Gre, Gim, y2, 16)
    return k.emit()


def _build_projres():
    k = KB()
    di = lambda n, s, dt=F32: k.dram(n, s, dt, kind="ExternalInput")
    xin = di("xin", [2048, D]); yT = di("yT", [2048, 2048]); wout = di("wout", [2048, 2048]); gate = di("gate", [2048])
    xm = k.dram("xm", [2048, D], F32, kind="ExternalOutput")
    c = alloc_common(k)
    projres_stage(k, c, xin, yT, wout, gate, xm, 2048)
    return k.emit()


def _hyena_mixer(inp, xs, m1):
    x_full = np.concatenate(xs, axis=0)
    xp = np.concatenate([np.zeros((1, 2048), np.float32), x_full, np.zeros((1, 2048), np.float32)], axis=0)
    cw = np.ascontiguousarray(inp["hy_conv_w"][0].reshape(3, 48, 128).transpose(2, 1, 0))
    cb = np.ascontiguousarray(inp["hy_conv_b"][0].reshape(48, 128).T)
    maps = []
    for ci in range(8):
        hm = np.ones((128, 2), np.float32)
        if ci == 0:
            hm[:, 0] = 0
        if ci == 7:
            hm[:, 1] = 0
        maps.append(dict(xhin=np.ascontiguousarray(xp[ci * 2048:ci * 2048 + 2050]), hmask=hm, modc=_modc(m1, 3), normg=_cols(inp["norm_g"][1, 1]),
                         w_in=np.ascontiguousarray(inp["hy_w_in"][0]), cw=cw, cb=cb, idf=_IDF, idb=_IDB))
    rz = _run(_prog("hypre", _build_hypre), maps)
    z_all = np.concatenate([rz[ci]["zc"] for ci in range(8)], axis=1)
    consts = hy_consts()
    bf = np.ascontiguousarray(np.stack([inp["hy_filt_b1"][0], inp["hy_filt_b2"][0], inp["hy_filt_b3"][0], inp["hy_filt_freq"][0]], axis=1).astype(np.float32))
    maps = []
    for ci in range(8):
        fc = hy_filt_consts(ci)
        m = {"k_" + n: consts[n] for n in _FILT_CONSTS}
        m.update(featT=fc["featT"], drow=fc["drow"], E1=fc["E1"], w1=np.ascontiguousarray(inp["hy_filt_w1"][0]), w2=np.ascontiguousarray(inp["hy_filt_w2"][0]),
                 w3=np.ascontiguousarray(inp["hy_filt_w3"][0]), bf=bf, wout=np.ascontiguousarray(inp["hy_filt_w_out"][0][:, :, :, ci * 256:(ci + 1) * 256]))
        maps.append(m)
    rf = _run(_prog("hyfilt", _build_hyfilt), maps)
    maps = []
    for ci in range(8):
        cs = slice(ci * 256, (ci + 1) * 256)
        m = {"k_" + n: v for n, v in consts.items()}
        m.update(x1=np.ascontiguousarray(z_all[0:2048][cs]), x2=np.ascontiguousarray(z_all[2048:4096][cs]), v=np.ascontiguousarray(z_all[4096:6144][cs]),
                 skip=np.ascontiguousarray(inp["hy_skip"][0][:, cs]), Gre=rf[ci]["Gre"], Gim=rf[ci]["Gim"])
        maps.append(m)
    ry = _run(_prog("hyconv", _build_hyconv), maps)
    y_all = np.concatenate([ry[ci]["y2"] for ci in range(8)], axis=0)
    prc = dict(wout=np.ascontiguousarray(inp["hy_w_out"][0]), gate=np.ascontiguousarray(m1[5]))
    rp = _run(_prog("projres", _build_projres), [dict(prc, xin=np.ascontiguousarray(xs[ci]), yT=np.ascontiguousarray(y_all[:, ci * 2048:(ci + 1) * 2048]))
                                                  for ci in range(8)])
    return [rp[ci]["xm"] for ci in range(8)]


def _s5_inmaps(inp, u_x, u_c):
    jt = np.broadcast_to(np.arange(SEG + 1, dtype=np.float32)[None, :], (64, SEG + 1)).copy()
    maps = []
    seq_f = np.concatenate([u_c, u_x], axis=0)
    seq_b = np.concatenate([u_x, u_c], axis=0)[::-1]
    for ci in range(8):
        gs = slice(ci * 8, ci * 8 + 8)
        U = np.empty((16, 16, 16640), np.float32)
        for di_, seq in enumerate((seq_f, seq_b)):
            blk = seq[:, ci * 128:(ci + 1) * 128].reshape(16640, 8, 16)
            U[:, di_ * 8:(di_ + 1) * 8, :] = blk.transpose(2, 1, 0)

        def lanes(a):
            return a[:, gs].reshape(16, *a.shape[2:])
        m = dict(U=U, lre=lanes(inp["s5_lam_re"][0]).T, lim=lanes(inp["s5_lam_im"][0]).T, ldt=lanes(inp["s5_log_dt"][0]),
                 BTre=lanes(inp["s5_b_re"][0]).transpose(2, 0, 1), BTim=lanes(inp["s5_b_im"][0]).transpose(2, 0, 1),
                 CTre=lanes(inp["s5_c_re"][0]).transpose(2, 0, 1), CTim=lanes(inp["s5_c_im"][0]).transpose(2, 0, 1), jt=jt)
        maps.append({kk: np.ascontiguousarray(v, dtype=np.float32) for kk, v in m.items()})
    return maps


def _ffn_launch(x_rows_per_core, m, k0, normg, w_in, w_out):
    R = x_rows_per_core[0].shape[0]
    nc = _prog(("ffn", R), lambda: _build_ffn(R))
    common = dict(modc=_modc(m, k0), normg=_cols(normg), w_in=np.ascontiguousarray(w_in), w_out=np.ascontiguousarray(w_out), idf=_IDF, idb=_IDB)
    res = _run(nc, [dict(common, xin=np.ascontiguousarray(xr)) for xr in x_rows_per_core])
    return [r["xout"] for r in res]


def kernel(**inp):
    inp = {kk: np.asarray(v) for kk, v in inp.items()}
    x = inp["x"][0]
    ctx = inp["ctx"][0]
    cvec = np.stack([inp["c"][0], inp["c_ctx"]], axis=1)
    cc = np.ascontiguousarray(cvec.reshape(16, 128, 2).transpose(1, 0, 2))
    nc = _prog("ada", _build_ada)
    res = _run(nc, [dict(cc=cc, w=np.ascontiguousarray(inp["ada_w"][:, :, ci * 2304:(ci + 1) * 2304]),
                         b=np.ascontiguousarray(inp["ada_b"][:, ci * 2304:(ci + 1) * 2304])) for ci in range(8)])
    mods = np.concatenate([res[ci]["out"] for ci in range(8)], axis=2)
    mx = [mods[l, 0].reshape(9, 2048) for l in range(2)]
    mc = [mods[l, 1].reshape(9, 2048) for l in range(2)]
    xs = [x[ci * 2048:(ci + 1) * 2048] for ci in range(8)]
    cs = [ctx[(ci % 2) * 128:(ci % 2) * 128 + 128] for ci in range(8)]
    xs = _ffn_launch(xs, mx[0], 0, inp["norm_g"][0, 0], inp["ffn_w_in"][0, 0], inp["ffn_w_out"][0, 0])
    cs = _ffn_launch(cs, mc[0], 0, inp["norm_g"][0, 0], inp["ffn_w_in"][0, 0], inp["ffn_w_out"][0, 0])
    cos, sin = _rope_tabs(16384)
    evc = dict(normg=_cols(inp["norm_g"][0, 1]), w_in=inp["ev_w_in"][0], wuq=inp["mla_w_uq"][0], wukv=inp["mla_w_ukv"][0],
               gqa=_cols(inp["mla_q_a_norm_g"][0], 4), gkva=_cols(inp["mla_kv_a_norm_g"][0], 2), gq=inp["mla_q_head_g"][0], gk=inp["mla_k_head_g"][0],
               idf=_IDF, idb=_IDB)
    evc = {kk: np.ascontiguousarray(v) for kk, v in evc.items()}
    nc = _prog(("evpre", 2048), lambda: _build_evpre(2048))
    rx = _run(nc, [dict(evc, modc=_modc(mx[0], 3), xin=np.ascontiguousarray(xs[ci]), cos=np.ascontiguousarray(cos[ci * 2048:(ci + 1) * 2048]),
                        sin=np.ascontiguousarray(sin[ci * 2048:(ci + 1) * 2048])) for ci in range(8)])
    nc = _prog(("evpre", 128), lambda: _build_evpre(128))
    one = np.ones((128, 32), np.float32)
    zero = np.zeros((128, 32), np.float32)
    rc = _run(nc, [dict(evc, modc=_modc(mc[0], 3), xin=np.ascontiguousarray(cs[ci]), cos=one, sin=zero) for ci in range(8)])
    kT_all = np.ascontiguousarray(np.concatenate([rx[ci]["kT"] for ci in range(8)] + [rc[0]["kT"], rc[1]["kT"]], axis=2))
    v_all = np.ascontiguousarray(np.concatenate([rx[ci]["v"] for ci in range(8)] + [rc[0]["v"], rc[1]["v"]], axis=0))
    u_x = np.concatenate([rx[ci]["u"] for ci in range(8)], axis=0)
    u_c = np.concatenate([rc[0]["u"], rc[1]["u"]], axis=0)
    nc = _prog("attn", _build_attn)
    ra = _run(nc, [dict(qT=rx[ci]["qT"], kT=kT_all, v=v_all) for ci in range(8)])
    nc = _prog("s5", _build_s5)
    rs = _run(nc, _s5_inmaps(inp, u_x, u_c))
    Yf = np.empty((16384, 1024), np.float32)
    Yb = np.empty((16384, 1024), np.float32)
    for ci in range(8):
        Y = rs[ci]["Y"]
        Yf[:, ci * 128:(ci + 1) * 128] = Y[0:8].transpose(2, 0, 1).reshape(16384, 128)
        Yb[:, ci * 128:(ci + 1) * 128] = Y[8:16, :, ::-1].transpose(2, 0, 1).reshape(16384, 128)
    nc = _prog("evout", _build_evout)
    eoc = dict(dd=inp["s5_d"][0], gluw=inp["s5_glu_w"][0], glub=inp["s5_glu_b"][0], wout=inp["ev_w_out"][0], gate=mx[0][5], idf=_IDF, idb=_IDB)
    eoc = {kk: np.ascontiguousarray(v) for kk, v in eoc.items()}
    ro = _run(nc, [dict(eoc, xa=np.ascontiguousarray(xs[ci]), u=np.ascontiguousarray(u_x[ci * 2048:(ci + 1) * 2048]),
                        yf=np.ascontiguousarray(Yf[ci * 2048:(ci + 1) * 2048]), yb=np.ascontiguousarray(Yb[ci * 2048:(ci + 1) * 2048]),
                        attT=ra[ci]["attT"]) for ci in range(8)])
    xs = [ro[ci]["xm"] for ci in range(8)]
    _DBG["x_m0"] = xs
    xs = _ffn_launch(xs, mx[0], 6, inp["norm_g"][0, 2], inp["ffn_w_in"][0, 1], inp["ffn_w_out"][0, 1])
    _DBG["x_b0"] = xs
    xs = _ffn_launch(xs, mx[1], 0, inp["norm_g"][1, 0], inp["ffn_w_in"][1, 0], inp["ffn_w_out"][1, 0])
    _DBG["x_a1"] = xs
    xs = _hyena_mixer(inp, xs, mx[1])
    _DBG["x_m1"] = xs
    xs = _ffn_launch(xs, mx[1], 6, inp["norm_g"][1, 2], inp["ffn_w_in"][1, 1], inp["ffn_w_out"][1, 1])
    return np.concatenate(xs, axis=0)[None].astype(np.float32)
```

```python
import numpy as np
import ml_dtypes
from contextlib import ExitStack
import concourse.bass as bass
import concourse.mybir as mybir
from concourse.bass_utils import run_bass_kernel_spmd

F32 = mybir.dt.float32
BF16 = mybir.dt.bfloat16
AF = mybir.ActivationFunctionType
ALU = mybir.AluOpType
AX = mybir.AxisListType
WRITE_KEYS = ("out", "accum_out")
SEM_ROLL = 30000


class View:
    __slots__ = ("buf", "ap")

    def __init__(self, buf, ap):
        self.buf = buf
        self.ap = ap

    def __getitem__(self, idx):
        return View(self.buf, self.ap[idx])

    def rearrange(self, pat, **kw):
        return View(self.buf, self.ap.rearrange(pat, **kw))

    def bitcast(self, dt):
        return View(self.buf, self.ap.bitcast(dt))

    def unsq_bcast(self, n):
        a = self.ap
        return View(self.buf, bass.AP(a.tensor, a.offset, [list(x) for x in a.ap] + [[0, n]]))

    def pbcast(self, n):
        return View(self.buf, self.ap.partition_broadcast(n))


class Buf:
    def __init__(self, k, name, h, space):
        self.k = k
        self.name = name
        self.h = h
        self.space = space
        self.last_w = None
        self.readers = []
        self.dma_sem = None
        self.dma_cnt = 0

    def __getitem__(self, idx):
        return View(self, self.h[idx])

    def ap(self):
        return View(self, self.h.ap() if hasattr(self.h, "ap") else self.h[:])

    def bitcast_view(self, dt):
        return View(self, self.h.bitcast(dt).ap())


class Op:
    __slots__ = ("eng", "meth", "args", "kwargs", "reads", "writes", "is_dma", "tok", "has_dep", "deps", "sbuf_side", "acc")


class KB:
    def __init__(self):
        self.nc = bass.Bass("TRN2", target_bir_lowering=False)
        self.ops = []
        self.es = ExitStack()
        self.bufs = []
        self.n = 0

    def sb(self, name, shape, dt=F32):
        h = self.es.enter_context(self.nc.sbuf_tensor(name, list(shape), dt))
        b = Buf(self, name, h, "sb")
        self.bufs.append(b)
        return b

    def ps(self, name, shape, dt=F32):
        h = self.es.enter_context(self.nc.psum_tensor(name, list(shape), dt))
        b = Buf(self, name, h, "ps")
        self.bufs.append(b)
        return b

    def dram(self, name, shape, dt=F32, kind=None):
        if kind is None:
            h = self.nc.dram_tensor(name, list(shape), dt)
        else:
            h = self.nc.dram_tensor(name, list(shape), dt, kind=kind)
        b = Buf(self, name, h, "dram")
        self.bufs.append(b)
        return b

    def I(self, eng, meth, *args, reads=(), writes=(), acc=False, **kwargs):
        o = Op()
        o.eng = eng
        o.meth = meth
        o.args = args
        o.kwargs = kwargs
        rd, wr = list(reads), list(writes)
        for a in args:
            if isinstance(a, View):
                rd.append(a.buf)
        for kk, v in kwargs.items():
            if isinstance(v, View):
                (wr if kk in WRITE_KEYS else rd).append(v.buf)
        o.reads = rd
        o.writes = wr
        o.is_dma = meth == "dma_start"
        o.tok = None
        o.has_dep = False
        o.deps = None
        o.sbuf_side = None
        o.acc = acc
        if o.is_dma:
            ob, ib = kwargs["out"].buf, kwargs["in_"].buf
            o.sbuf_side = ob if ob.space != "dram" else ib
            assert o.sbuf_side.space != "dram", "dram->dram dma unsupported"
        self.ops.append(o)
        return o

    def dma(self, eng, out, in_, **kw):
        return self.I(eng, "dma_start", out=out, in_=in_, **kw)

    def mm(self, out, lhsT, rhs, start=True, stop=True, **kw):
        return self.I("tensor", "matmul", out=out, lhsT=lhsT, rhs=rhs, start=start, stop=stop, acc=not start, **kw)

    def tr(self, out, in_, ident):
        return self.I("tensor", "transpose", out=out, in_=in_, identity=ident)

    def act(self, out, in_, func, eng="scalar", **kw):
        return self.I(eng, "activation", out=out, in_=in_, func=func, **kw)

    def tt(self, eng, out, in0, in1, op):
        return self.I(eng, "tensor_tensor", out=out, in0=in0, in1=in1, op=op)

    def ts(self, eng, out, in0, s1, s2, op0, op1=None, **kw):
        if op1 is None:
            return self.I(eng, "tensor_scalar", out=out, in0=in0, scalar1=s1, scalar2=None, op0=op0, **kw)
        return self.I(eng, "tensor_scalar", out=out, in0=in0, scalar1=s1, scalar2=s2, op0=op0, op1=op1, **kw)

    def stt(self, eng, out, in0, scalar, in1, op0, op1):
        return self.I(eng, "scalar_tensor_tensor", out=out, in0=in0, scalar=scalar, in1=in1, op0=op0, op1=op1)

    def cp(self, eng, out, in_):
        if eng == "scalar":
            return self.I(eng, "copy", out=out, in_=in_)
        return self.I(eng, "tensor_copy", out=out, in_=in_)

    def memset(self, eng, out, val):
        return self.I(eng, "memset", writes=[out.buf], ap=out, constant=val)

    def emit(self, final_wait_bufs=()):
        nc = self.nc
        ops = self.ops
        for i, o in enumerate(ops):
            deps = set()
            for b in o.reads:
                if b.last_w is not None:
                    deps.add(b.last_w)
            for b in o.writes:
                if b.last_w is not None:
                    deps.add(b.last_w)
                for r in b.readers:
                    deps.add(r)
            deps.discard(i)
            fd = []
            for d in deps:
                od = ops[d]
                same = (od.eng == o.eng) and not o.is_dma and not od.is_dma
                if same:
                    raw = any((b in od.writes) for b in o.reads)
                    if o.eng == "tensor":
                        raw = False
                    if not raw:
                        continue
                fd.append(d)
            o.deps = fd
            for d in fd:
                ops[d].has_dep = True
            for b in o.writes:
                b.last_w = i
                b.readers = []
            for b in o.reads:
                if b not in o.writes:
                    b.readers.append(i)
        engs = {"tensor": nc.tensor, "vector": nc.vector, "scalar": nc.scalar, "gpsimd": nc.gpsimd, "sync": nc.sync}
        esem = {}
        ecnt = {}
        known = {e: {} for e in engs}

        def new_sem(name):
            return self.es.enter_context(nc.semaphore(name))

        nsem = [0]
        for e in engs:
            esem[e] = new_sem(f"e_{e}_0")
            ecnt[e] = 0
            nsem[0] += 1
        for i, o in enumerate(ops):
            eng = engs[o.eng]
            kn = known[o.eng]
            for d in o.deps:
                sem, val = ops[d].tok
                if kn.get(id(sem), (None, 0))[1] < val:
                    eng.wait_ge(sem, val)
                    kn[id(sem)] = (sem, val)
            args = [a.ap if isinstance(a, View) else a for a in o.args]
            kwargs = {kk: (v.ap if isinstance(v, View) else v) for kk, v in o.kwargs.items()}
            inst = getattr(eng, o.meth)(*args, **kwargs)
            if o.is_dma:
                b = o.sbuf_side
                if b.dma_sem is None:
                    b.dma_sem = new_sem(f"d_{b.name}")
                    nsem[0] += 1
                b.dma_cnt += 16
                inst.then_inc(b.dma_sem, 16)
                o.tok = (b.dma_sem, b.dma_cnt)
            elif o.has_dep:
                if ecnt[o.eng] >= SEM_ROLL:
                    esem[o.eng] = new_sem(f"e_{o.eng}_{i}")
                    ecnt[o.eng] = 0
                    nsem[0] += 1
                ecnt[o.eng] += 1
                inst.then_inc(esem[o.eng], 1)
                o.tok = (esem[o.eng], ecnt[o.eng])
        for b in self.bufs:
            if b.dma_sem is not None:
                nc.sync.wait_ge(b.dma_sem, b.dma_cnt)
        self.nsem = nsem[0]
        self.es.close()
        return nc


def bf16_np(a):
    return a.astype(ml_dtypes.bfloat16)


D = 2048
DFF = 5632
EPS = 1e-6


def alloc_common(k):
    c = {}
    c["P"] = [k.ps(f"P{i}", [128, 512], F32) for i in range(8)]
    return c


def load_ident(k, c, ident_f_d, ident_b_d):
    c["identf"] = k.sb("identf", [128, 128], F32)
    c["identb"] = k.sb("identb", [128, 128], BF16)
    c["epsc"] = k.sb("epsc", [128, 1], F32)
    k.memset("vector", c["epsc"][:, :], EPS)
    k.dma("sync", out=c["identf"][:, :], in_=ident_f_d[:, :])
    k.dma("sync", out=c["identb"][:, :], in_=ident_b_d[:, :])


def modcols_prepare(k, pfx, modc_d, normg_d, slot):
    raw = k.sb(pfx + "raw", [128, 3, 16], F32)
    ng = k.sb(pfx + "ng", [128, 16], F32)
    A = k.sb(pfx + "A", [128, 16], F32)
    G = k.sb(pfx + "G", [128, 16], F32)
    k.dma("sync", out=raw[:, :, :], in_=modc_d)
    k.dma("sync", out=ng[:, :], in_=normg_d)
    k.stt("vector", out=A[:, :], in0=raw[:, 1, :], scalar=1.0, in1=ng[:, :], op0=ALU.add, op1=ALU.mult)
    k.ts("vector", out=G[:, :], in0=raw[:, 2, :], s1=0.5, s2=None, op0=ALU.mult)
    return A, raw, G


def norm_mod_T(k, c, pfx, x_tile, tr, A, Braw, hT_view, bufs):
    junk, ss, rstd, xh = bufs["junk"], bufs["ss"], bufs["rstd"], bufs["xh"]
    P = c["P"]
    k.memset("vector", ss[:tr, :], 0.0)
    k.act(out=junk[:tr, :], in_=x_tile, func=AF.Square, accum_out=ss[:tr, :])
    k.act(out=rstd[:tr, :], in_=ss[:tr, :], func=AF.Sqrt, scale=1.0 / D, bias=c["epsc"][:tr, :])
    k.I("vector", "reciprocal", out=rstd[:tr, :], in_=rstd[:tr, :])
    k.act(out=xh[:tr, :], in_=x_tile, func=AF.Copy, scale=rstd[:tr, :])
    pb = [P[0].bitcast_view(BF16), P[1].bitcast_view(BF16)]
    for kt in range(16):
        pv = pb[kt // 8]
        k.tr(out=pv[:, (kt % 8) * 128:(kt % 8) * 128 + tr], in_=xh[:tr, kt * 128:(kt + 1) * 128], ident=c["identb"][:tr, :tr])
    for h in range(2):
        pv = pb[h].rearrange("p (k t) -> p k t", t=128)[:, :, :tr]
        a_b = A[:, h * 8:(h + 1) * 8].unsq_bcast(tr)
        b_b = Braw[:, 0, h * 8:(h + 1) * 8].unsq_bcast(tr)
        tmp = bufs["tmpT"]
        k.tt("vector", out=tmp[:, :, :tr], in0=pv, in1=a_b, op=ALU.mult)
        k.tt("gpsimd", out=hT_view[:, h * 8:(h + 1) * 8, :], in0=tmp[:, :, :tr], in1=b_b, op=ALU.add)


def ffn_stage(k, c, pfx, xin_d, xout_d, R, tr, A, Braw, G, w_in_d, w_out_d):
    P = c["P"]
    ntiles = R // tr
    bufs = c.setdefault("nm_bufs", None)
    if bufs is None:
        bufs = c["nm_bufs"] = dict(
            junk=k.sb("junk", [128, 2048], BF16), ss=k.sb("ss", [128, 1], F32), rstd=k.sb("rstd", [128, 1], F32),
            xh=k.sb("xh", [128, 2048], BF16), tmpT=k.sb("tmpT", [128, 8, 128], F32))
    if "ffn_bufs" not in c:
        c["ffn_bufs"] = dict(
            xres=k.sb("xres", [128, 4, 2048], F32), hT=k.sb("hT", [128, 16, 512], BF16), aT=k.sb("aT", [128, 44, 512], BF16),
            wg=[k.sb(f"wg{i}", [128, 16, 256], BF16) for i in range(2)],
            wo=[k.sb(f"wo{i}", [128, 44, 128], BF16) for i in range(2)],
            sg=[k.sb(f"sg{i}", [128, 512], F32) for i in range(2)],
            oTs=[k.sb(f"oTs{i}", [128, 512], F32) for i in range(2)])
    fb = c["ffn_bufs"]
    xres, hT, aT, wg, wo, sg, oTs = fb["xres"], fb["hT"], fb["aT"], fb["wg"], fb["wo"], fb["sg"], fb["oTs"]
    w_in_v = w_in_d.ap().rearrange("(kt p) c -> p kt c", p=128)
    w_out_v = w_out_d.ap().rearrange("(j p) c -> p j c", p=128)
    nblk = (ntiles + 3) // 4
    for blk in range(nblk):
        t0 = blk * 4
        nt = min(4, ntiles - t0)
        nb = nt * tr
        for t in range(nt):
            r0 = (t0 + t) * tr
            k.dma("sync", out=xres[:tr, t, :], in_=xin_d[r0:r0 + tr, :])
            norm_mod_T(k, c, pfx, xres[:tr, t, :], tr, A, Braw, hT[:, :, t * tr:(t + 1) * tr], bufs)

        def load_wg(j):
            b = wg[j % 2]
            k.dma("gpsimd", out=b[:, :, 0:128], in_=w_in_v[:, :, j * 128:(j + 1) * 128])
            k.dma("gpsimd", out=b[:, :, 128:256], in_=w_in_v[:, :, DFF + j * 128:DFF + (j + 1) * 128])

        load_wg(0)
        for j in range(44):
            if j + 1 < 44:
                load_wg(j + 1)
            b = wg[j % 2]
            gp, up = P[2 + 2 * (j % 2)], P[3 + 2 * (j % 2)]
            for kt in range(16):
                k.mm(out=gp[:, :nb], lhsT=b[:, kt, 0:128], rhs=hT[:, kt, :nb], start=(kt == 0), stop=(kt == 15))
            for kt in range(16):
                k.mm(out=up[:, :nb], lhsT=b[:, kt, 128:256], rhs=hT[:, kt, :nb], start=(kt == 0), stop=(kt == 15))
            s = sg[j % 2]
            k.act(out=s[:, :nb], in_=gp[:, :nb], func=AF.Silu)
            k.tt("vector", out=aT[:, j, :nb], in0=s[:, :nb], in1=up[:, :nb], op=ALU.mult)

        def load_wo(m):
            k.dma("gpsimd", out=wo[m % 2][:, :, :], in_=w_out_v[:, :, m * 128:(m + 1) * 128])

        load_wo(0)
        for m in range(16):
            if m + 1 < 16:
                load_wo(m + 1)
            b = wo[m % 2]
            op_ = P[6 + (m % 2)]
            for j in range(44):
                k.mm(out=op_[:, :nb], lhsT=b[:, j, :], rhs=aT[:, j, :nb], start=(j == 0), stop=(j == 43))
            o = oTs[m % 2]
            k.act(out=o[:, :nb], in_=op_[:, :nb], func=AF.Copy, scale=G[:, m:m + 1])
            tb = P[m % 2]
            for t in range(nt):
                k.tr(out=tb[:tr, t * 128:(t + 1) * 128], in_=o[:, t * tr:(t + 1) * tr], ident=c["identf"][:, :])
            k.tt("vector", out=xres[:tr, :nt, m * 128:(m + 1) * 128], in0=xres[:tr, :nt, m * 128:(m + 1) * 128],
                 in1=tb[:tr, :nt * 128].rearrange("p (t f) -> p t f", f=128), op=ALU.add)
        for t in range(nt):
            r0 = (t0 + t) * tr
            k.dma("sync", out=xout_d[r0:r0 + tr, :], in_=xres[:tr, t, :])


def ada_stage(k, c, cc_d, w_d, b_d, out_d):
    P = c["P"]
    cc = k.sb("ada_cc", [128, 16, 2], F32)
    sc = k.sb("ada_sc", [128, 16, 2], F32)
    k.dma("sync", out=cc[:, :, :], in_=cc_d[:, :, :])
    k.act(out=sc[:, :, :], in_=cc[:, :, :], func=AF.Silu)
    wb = [k.sb(f"ada_w{i}", [128, 16, 512], F32) for i in range(2)]
    bb = k.sb("ada_b", [2, 2, 2304], F32)
    ob = k.sb("ada_o", [2, 2, 2304], F32)
    for l in range(2):
        k.dma("sync", out=bb[:, l, :], in_=b_d[l:l + 1, :].pbcast(2) if False else b_d[l, :].pbcast(2))
    blocks = [(0, 512), (512, 512), (1024, 512), (1536, 512), (2048, 256)]
    it = 0
    for l in range(2):
        wv = w_d[l].rearrange("(kt p) c -> p kt c", p=128)
        for (c0, cw) in blocks:
            w = wb[it % 2]
            k.dma("sync" if it % 2 == 0 else "gpsimd", out=w[:, :, :cw], in_=wv[:, :, c0:c0 + cw])
            ps = P[it % 2]
            for kt in range(16):
                k.mm(out=ps[:2, :cw], lhsT=sc[:, kt, :], rhs=w[:, kt, :cw], start=(kt == 0), stop=(kt == 15))
            k.tt("vector", out=ob[:, l, c0:c0 + cw], in0=ps[:2, :cw], in1=bb[:, l, c0:c0 + cw], op=ALU.add)
            it += 1
        k.dma("sync", out=out_d[l], in_=ob[:, l, :])


def rstd_from_ss(k, c, out, ss, dim):
    k.act(out=out, in_=ss, func=AF.Sqrt, scale=1.0 / dim, bias=c["epsc"][:out.ap.shape[0], :])
    k.I("vector", "reciprocal", out=out, in_=out)


def rope_apply(k, eng, dst, src, cos, sin, tmp, tr, nh):
    def tv(i):
        return tmp[:tr, i, :nh * 32].rearrange("p (h a f) -> p h a f", a=2, f=16)
    x1, x2 = src[:, :, :, 0, :], src[:, :, :, 1, :]
    k.tt(eng, out=tv(0), in0=x1, in1=cos, op=ALU.mult)
    k.tt(eng, out=tv(1), in0=x2, in1=sin, op=ALU.mult)
    k.tt(eng, out=dst[:, :, :, 0, :], in0=tv(0), in1=tv(1), op=ALU.subtract)
    k.tt(eng, out=tv(2), in0=x2, in1=cos, op=ALU.mult)
    k.tt(eng, out=tv(3), in0=x1, in1=sin, op=ALU.mult)
    k.tt(eng, out=dst[:, :, :, 1, :], in0=tv(2), in1=tv(3), op=ALU.add)


def evpre_stage(k, c, xin_d, R, tr, A, Braw, w_in_d, wuq_d, wukv_d, gqa_d, gkva_d, gq_d, gk_d, cos_d, sin_d,
                qT_d, kT_d, v_d, u_d, want_q=True, stop=9):
    P = c["P"]
    if c.get("nm_bufs") is None:
        c["nm_bufs"] = dict(
            junk=k.sb("junk", [128, 2048], BF16), ss=k.sb("ss", [128, 1], F32), rstd=k.sb("rstd", [128, 1], F32),
            xh=k.sb("xh", [128, 2048], BF16), tmpT=k.sb("tmpT", [128, 8, 128], F32))
    bufs = c["nm_bufs"]
    win = k.sb("ev_win", [128, 16, 1856], BF16)
    wuq = k.sb("ev_wuq", [128, 4, 1536], BF16)
    wukv = k.sb("ev_wukv", [128, 2, 2048], BF16)
    k.dma("gpsimd", out=win[:, :, :], in_=w_in_d.ap().rearrange("(kt p) c -> p kt c", p=128))
    k.dma("gpsimd", out=wuq[:, :, :], in_=wuq_d.ap().rearrange("(kt p) c -> p kt c", p=128))
    k.dma("gpsimd", out=wukv[:, :, :], in_=wukv_d.ap().rearrange("(kt p) c -> p kt c", p=128))
    gqa = k.sb("ev_gqa", [128, 4], F32)
    gkva = k.sb("ev_gkva", [128, 2], F32)
    gq = k.sb("ev_gq", [128, 192], F32)
    gk = k.sb("ev_gk", [128, 192], F32)
    k.dma("sync", out=gqa[:, :], in_=gqa_d[:, :])
    k.dma("sync", out=gkva[:, :], in_=gkva_d[:, :])
    k.dma("sync", out=gq[:, :], in_=gq_d[:].pbcast(128))
    k.dma("sync", out=gk[:, :], in_=gk_d[:].pbcast(128))
    k.ts("vector", out=gq[:, :], in0=gq[:, :], s1=float(192 ** -0.5), s2=None, op0=ALU.mult)
    xt = [k.sb(f"ev_x{i}", [128, 2048], F32) for i in range(2)]
    hT = k.sb("ev_hT", [128, 16, 128], BF16)
    zs = k.sb("ev_zs", [128, 1856], F32)
    qan = k.sb("ev_qan", [128, 768], BF16)
    qanT = k.sb("ev_qanT", [128, 6, 128], BF16)
    sq = k.sb("ev_sq", [128, 1536], F32)
    ss8 = k.sb("ev_ss8", [128, 8], F32)
    rs8 = k.sb("ev_rs8", [128, 8], F32)
    ss1 = k.sb("ev_ss1", [128, 1], F32)
    rs1 = k.sb("ev_rs1", [128, 1], F32)
    t1 = k.sb("ev_t1", [128, 8, 192], F32)
    t2 = k.sb("ev_t2", [128, 8, 192], F32)
    qf = k.sb("ev_qf", [128, 8, 192], BF16)
    kf = k.sb("ev_kf", [128, 8, 192], BF16)
    vt = k.sb("ev_vt", [128, 8, 128], BF16)
    kpe = k.sb("ev_kpe", [128, 64], F32)
    kpr = k.sb("ev_kpr", [128, 64], F32)
    rtmp = k.sb("ev_rtmp", [128, 4, 256], F32)
    cs = k.sb("ev_cos", [128, 32], F32)
    sn = k.sb("ev_sin", [128, 32], F32)
    oT = [k.sb(f"ev_oT{i}", [128, 8, 2, 128], BF16) for i in range(2)]
    pT = P[7].bitcast_view(BF16)
    idb = c["identb"]

    def rope_views(tile, nh):
        return tile[:tr, :nh, 128:192].rearrange("p h (a b f) -> p h a b f", a=2, b=2)

    def bc_heads(tab, nh):
        a = tab[:tr, :].ap
        return View(tab, bass.AP(a.tensor, a.offset, [list(a.ap[0]), [0, nh], [16, 2], [1, 16]]))

    def heads_T(src, dst_d, it):
        o = oT[it % 2]
        for half in range(2):
            for h4 in range(4):
                h = half * 4 + h4
                k.tr(out=pT[:, h4 * 256:h4 * 256 + tr], in_=src[:tr, h, 0:128], ident=idb[:tr, :tr])
                k.tr(out=pT[:64, h4 * 256 + 128:h4 * 256 + 128 + tr], in_=src[:tr, h, 128:192], ident=idb[:tr, :tr])
            pv = pT.rearrange("p (h a t) -> p h a t", a=2, t=128)
            k.cp("vector", out=o[:, half * 4:half * 4 + 4, 0, :tr], in_=pv[:, :, 0, :tr])
            k.cp("vector", out=o[:64, half * 4:half * 4 + 4, 1, :tr], in_=pv[:64, :, 1, :tr])
        return o

    ntiles = R // tr
    for t in range(ntiles):
        r0 = t * tr
        x = xt[t % 2]
        k.dma("sync", out=x[:tr, :], in_=xin_d[r0:r0 + tr, :])
        k.dma("sync", out=cs[:tr, :], in_=cos_d[r0:r0 + tr, :])
        k.dma("sync", out=sn[:tr, :], in_=sin_d[r0:r0 + tr, :])
        norm_mod_T(k, c, "ev", x[:tr, :], tr, A, Braw, hT[:, :, :tr], bufs)
        zb = [(0, 512), (512, 512), (1024, 512), (1536, 320)]
        for bi, (c0, cw) in enumerate(zb):
            for kt in range(16):
                k.mm(out=P[bi][:tr, :cw], lhsT=hT[:, kt, :tr], rhs=win[:, kt, c0:c0 + cw], start=(kt == 0), stop=(kt == 15))
            k.cp("scalar", out=zs[:tr, c0:c0 + cw], in_=P[bi][:tr, :cw])
        k.dma("sync", out=u_d[r0:r0 + tr, :], in_=zs[:tr, 832:1856])
        if stop <= 1:
            continue
        for (c0, cw, dim) in ((0, 512, 512), (512, 256, 256)):
            k.memset("vector", ss1[:tr, :], 0.0)
            k.act(out=bufs["junk"][:tr, :cw], in_=zs[:tr, c0:c0 + cw], func=AF.Square, accum_out=ss1[:tr, :])
            rstd_from_ss(k, c, rs1[:tr, :], ss1[:tr, :], dim)
            k.act(out=qan[:tr, c0:c0 + cw], in_=zs[:tr, c0:c0 + cw], func=AF.Copy, scale=rs1[:tr, :])
        for kt in range(6):
            k.tr(out=pT[:, kt * 128:kt * 128 + tr], in_=qan[:tr, kt * 128:(kt + 1) * 128], ident=idb[:tr, :tr])
        pv6 = pT[:, :768].rearrange("p (k t) -> p k t", t=128)[:, :, :tr]
        k.tt("vector", out=qanT[:, 0:4, :tr], in0=pv6[:, 0:4, :], in1=gqa[:, :].unsq_bcast(tr), op=ALU.mult)
        k.tt("vector", out=qanT[:, 4:6, :tr], in0=pv6[:, 4:6, :], in1=gkva[:, :].unsq_bcast(tr), op=ALU.mult)
        if stop <= 2:
            continue
        for bi in range(4):
            for kt in range(2):
                k.mm(out=P[bi][:tr, :], lhsT=qanT[:, 4 + kt, :tr], rhs=wukv[:, kt, bi * 512:(bi + 1) * 512], start=(kt == 0), stop=(kt == 1))
        for bi in range(4):
            kvv = P[bi][:tr, :].rearrange("p (h d) -> p h d", d=256)
            if stop != 34:
                k.cp("vector", out=vt[:tr, bi * 2:bi * 2 + 2, :], in_=kvv[:, :, 128:256])
            k.cp("vector", out=t1[:tr, bi * 2:bi * 2 + 2, 0:128], in_=kvv[:, :, 0:128])
        if stop != 33:
            k.dma("sync", out=v_d[r0:r0 + tr, :], in_=vt[:tr, :, :].rearrange("p h d -> p (h d)"))
        if stop <= 3 or stop in (33, 34):
            continue
        k.tt("gpsimd", out=kpe[:tr, :], in0=zs[:tr, 768:832], in1=gk[:tr, 128:192], op=ALU.mult)
        kp5 = kpe[:tr, :].rearrange("p (h a b f) -> p h a b f", h=1, a=2, b=2)
        kr5 = kpr[:tr, :].rearrange("p (h a b f) -> p h a b f", h=1, a=2, b=2)
        rope_apply(k, "gpsimd", kr5, kp5, bc_heads(cs, 1), bc_heads(sn, 1), rtmp, tr, 1)
        if stop <= 4:
            continue
        k.act(out=sq[:tr, :1024].rearrange("p (h d) -> p h d", d=128), in_=t1[:tr, :, 0:128], func=AF.Square)
        k.I("vector", "tensor_reduce", out=ss8[:tr, :], in_=sq[:tr, :1024].rearrange("p (h d) -> p h d", d=128), axis=AX.X, op=ALU.add)
        k.memset("vector", ss1[:tr, :], 0.0)
        k.act(out=bufs["junk"][:tr, :64], in_=zs[:tr, 768:832], func=AF.Square, accum_out=ss1[:tr, :])
        k.ts("vector", out=ss8[:tr, :], in0=ss8[:tr, :], s1=ss1[:tr, :], s2=None, op0=ALU.add)
        rstd_from_ss(k, c, rs8[:tr, :], ss8[:tr, :], 192)
        k.tt("vector", out=t2[:tr, :, 0:128], in0=t1[:tr, :, 0:128], in1=rs8[:tr, :].unsq_bcast(128), op=ALU.mult)
        gkn = View(gk, bass.AP(gk[:tr, 0:128].ap.tensor, gk[:tr, 0:128].ap.offset, [list(gk[:tr, 0:128].ap.ap[0]), [0, 8], [1, 128]]))
        k.tt("gpsimd", out=kf[:tr, :, 0:128], in0=t2[:tr, :, 0:128], in1=gkn, op=ALU.mult)
        kprb = View(kpr, bass.AP(kpr[:tr, :].ap.tensor, kpr[:tr, :].ap.offset, [list(kpr[:tr, :].ap.ap[0]), [0, 8], [1, 64]]))
        k.tt("vector", out=kf[:tr, :, 128:192], in0=kprb, in1=rs8[:tr, :].unsq_bcast(64), op=ALU.mult)
        if stop <= 5:
            continue
        o = heads_T(kf, kT_d, 2 * t)
        k.dma("sync", out=kT_d[:, 0:128, r0:r0 + tr].rearrange("h p t -> p h t"), in_=o[:, :, 0, :tr])
        k.dma("sync", out=kT_d[:, 128:192, r0:r0 + tr].rearrange("h p t -> p h t"), in_=o[:64, :, 1, :tr])
        if want_q and stop > 6:
            for bi in range(3):
                for kt in range(4):
                    k.mm(out=P[4 + bi][:tr, :], lhsT=qanT[:, kt, :tr], rhs=wuq[:, kt, bi * 512:(bi + 1) * 512], start=(kt == 0), stop=(kt == 3))
                k.cp("scalar", out=t1[:tr, :, :].rearrange("p h d -> p (h d)")[:, bi * 512:(bi + 1) * 512], in_=P[4 + bi][:tr, :])
            k.act(out=sq[:tr, :], in_=t1[:tr, :, :].rearrange("p h d -> p (h d)"), func=AF.Square)
            k.I("vector", "tensor_reduce", out=ss8[:tr, :], in_=sq[:tr, :].rearrange("p (h d) -> p h d", d=192), axis=AX.X, op=ALU.add)
            rstd_from_ss(k, c, rs8[:tr, :], ss8[:tr, :], 192)
            k.tt("vector", out=t2[:tr, :, :], in0=t1[:tr, :, :], in1=rs8[:tr, :].unsq_bcast(192), op=ALU.mult)
            gqn = View(gq, bass.AP(gq[:tr, :].ap.tensor, gq[:tr, :].ap.offset, [list(gq[:tr, :].ap.ap[0]), [0, 8], [1, 192]]))
            k.tt("gpsimd", out=t1[:tr, :, :], in0=t2[:tr, :, :], in1=gqn, op=ALU.mult)
            k.cp("scalar", out=qf[:tr, :, 0:128], in_=t1[:tr, :, 0:128])
            rope_apply(k, "vector", rope_views(qf, 8), rope_views(t1, 8), bc_heads(cs, 8), bc_heads(sn, 8), rtmp, tr, 8)
            o = heads_T(qf, qT_d, 2 * t + 1)
            k.dma("sync", out=qT_d[:, 0:128, r0:r0 + tr].rearrange("h p t -> p h t"), in_=o[:, :, 0, :tr])
            k.dma("sync", out=qT_d[:, 128:192, r0:r0 + tr].rearrange("h p t -> p h t"), in_=o[:64, :, 1, :tr])


SEG = 256
NSEG = 65
PI = 3.141592653589793


def sincos(k, dst, src, shift, w1, w2, kint):
    k.ts("vector", out=w1, in0=src, s1=shift + 8 * PI, s2=1.0 / (2 * PI), op0=ALU.add, op1=ALU.mult)
    k.cp("vector", out=kint, in_=w1)
    k.cp("vector", out=w1, in_=kint)
    k.ts("vector", out=w2, in0=src, s1=shift + 8 * PI, s2=None, op0=ALU.add)
    k.stt("vector", out=w1, in0=w1, scalar=-2 * PI, in1=w2, op0=ALU.mult, op1=ALU.add)
    k.ts("vector", out=w2, in0=w1, s1=PI, s2=2 * PI, op0=ALU.is_gt, op1=ALU.mult)
    k.tt("vector", out=w1, in0=w1, in1=w2, op=ALU.subtract)
    k.ts("vector", out=w1, in0=w1, s1=-PI, s2=PI, op0=ALU.max, op1=ALU.min)
    k.act(out=dst, in_=w1, func=AF.Sin)


def s5_stage(k, c, U_d, lamre_d, lamim_d, logdt_d, BTre_d, BTim_d, CTre_d, CTim_d, jtab_d, Y_d, nseg=NSEG):
    P = c["P"]
    S = SEG
    sb = lambda n, s: k.sb("s5_" + n, s, F32)
    lre, lim, ldt = sb("lre", [64, 16]), sb("lim", [64, 16]), sb("ldt", [64, 16])
    BTre, BTim = sb("BTre", [16, 16, 64]), sb("BTim", [16, 16, 64])
    CTre, CTim = sb("CTre", [64, 16, 16]), sb("CTim", [64, 16, 16])
    jt = sb("jt", [64, S + 1])
    for (b, d_) in ((lre, lamre_d), (lim, lamim_d)):
        k.dma("sync", out=b[:, :], in_=d_[:, :])
    k.dma("sync", out=ldt[:, :], in_=logdt_d[:].pbcast(64))
    k.dma("sync", out=BTre[:, :, :], in_=BTre_d[:, :, :])
    k.dma("sync", out=BTim[:, :, :], in_=BTim_d[:, :, :])
    k.dma("sync", out=CTre[:, :, :], in_=CTre_d[:, :, :])
    k.dma("sync", out=CTim[:, :, :], in_=CTim_d[:, :, :])
    k.dma("sync", out=jt[:, :], in_=jtab_d[:, :])
    negpi = sb("negpi", [64, 1])
    k.memset("vector", negpi[:, :], -PI)
    dt, th, mag = sb("dt", [64, 16]), sb("th", [64, 16]), sb("mag", [64, 16])
    k.act(out=dt[:, :], in_=ldt[:, :], func=AF.Exp)
    k.tt("vector", out=th[:, :], in0=lim[:, :], in1=dt[:, :], op=ALU.mult)
    k.tt("vector", out=mag[:, :], in0=lre[:, :], in1=dt[:, :], op=ALU.mult)
    k.act(out=mag[:, :], in_=mag[:, :], func=AF.Exp)
    g = sb("g", [64, 2, 16, S])
    h = sb("h", [64, 2, 16, S])
    ang = g[:, 0, :, :]
    cosT, sinT = sb("cosT", [64, 16, S]), sb("sinT", [64, 16, S])
    tA = sb("tA", [64, 16, S])
    ki = tA.bitcast_view(mybir.dt.int32)
    jv = jt[:, 0:S].ap
    jb = View(jt, bass.AP(jv.tensor, jv.offset, [list(jv.ap[0]), [0, 16], [1, S]]))
    k.tt("vector", out=ang, in0=jb, in1=th[:, :].unsq_bcast(S), op=ALU.mult)

    sincos(k, sinT[:, :, :], ang, 0.0, h[:, 0, :, :], h[:, 1, :, :], ki[:, :, :])
    sincos(k, cosT[:, :, :], ang, PI / 2, h[:, 0, :, :], h[:, 1, :, :], ki[:, :, :])
    psc, pss, pa = sb("psc", [64, 16]), sb("pss", [64, 16]), sb("pa", [64, 16])
    pw1, pw2 = sb("pw1", [64, 16]), sb("pw2", [64, 16])
    k.ts("vector", out=pa[:, :], in0=th[:, :], s1=float(S), s2=None, op0=ALU.mult)
    sincos(k, pss[:, :], pa[:, :], 0.0, pw1[:, :], pw2[:, :], ki[:, :, 0])
    sincos(k, psc[:, :], pa[:, :], PI / 2, pw1[:, :], pw2[:, :], ki[:, :, 0])
    are, aim = sb("are", [64, 16]), sb("aim", [64, 16])
    k.tt("vector", out=are[:, :], in0=mag[:, :], in1=cosT[:, :, 1], op=ALU.mult)
    k.tt("vector", out=aim[:, :], in0=mag[:, :], in1=sinT[:, :, 1], op=ALU.mult)
    den, w1, w2 = sb("den", [64, 16]), sb("w1", [64, 16]), sb("w2", [64, 16])
    kre, kim, nr = sb("kre", [64, 16]), sb("kim", [64, 16]), sb("nr", [64, 16])
    k.tt("vector", out=w1[:, :], in0=lre[:, :], in1=lre[:, :], op=ALU.mult)
    k.tt("vector", out=w2[:, :], in0=lim[:, :], in1=lim[:, :], op=ALU.mult)
    k.tt("vector", out=den[:, :], in0=w1[:, :], in1=w2[:, :], op=ALU.add)
    k.I("vector", "reciprocal", out=den[:, :], in_=den[:, :])
    k.ts("vector", out=nr[:, :], in0=are[:, :], s1=-1.0, s2=None, op0=ALU.add)
    k.tt("vector", out=w1[:, :], in0=nr[:, :], in1=lre[:, :], op=ALU.mult)
    k.tt("vector", out=w2[:, :], in0=aim[:, :], in1=lim[:, :], op=ALU.mult)
    k.tt("vector", out=kre[:, :], in0=w1[:, :], in1=w2[:, :], op=ALU.add)
    k.tt("vector", out=kre[:, :], in0=kre[:, :], in1=den[:, :], op=ALU.mult)
    k.tt("vector", out=w1[:, :], in0=aim[:, :], in1=lre[:, :], op=ALU.mult)
    k.tt("vector", out=w2[:, :], in0=nr[:, :], in1=lim[:, :], op=ALU.mult)
    k.tt("vector", out=kim[:, :], in0=w1[:, :], in1=w2[:, :], op=ALU.subtract)
    k.tt("vector", out=kim[:, :], in0=kim[:, :], in1=den[:, :], op=ALU.mult)
    PhR, PhI, magT = sb("PhR", [64, 16, S]), sb("PhI", [64, 16, S]), sb("magT", [64, 16, S])
    tB = h[:, 0, :, :]
    cS, sS = cosT[:, :, :], sinT[:, :, :]
    k.tt("vector", out=tA[:, :, :], in0=cS, in1=kre[:, :].unsq_bcast(S), op=ALU.mult)
    k.tt("gpsimd", out=tB, in0=sS, in1=kim[:, :].unsq_bcast(S), op=ALU.mult)
    k.tt("vector", out=PhR[:, :, :], in0=tA[:, :, :], in1=tB, op=ALU.add)
    k.tt("vector", out=tA[:, :, :], in0=cS, in1=kim[:, :].unsq_bcast(S), op=ALU.mult)
    k.tt("gpsimd", out=tB, in0=sS, in1=kre[:, :].unsq_bcast(S), op=ALU.mult)
    k.tt("vector", out=PhI[:, :, :], in0=tA[:, :, :], in1=tB, op=ALU.subtract)
    k.memset("vector", magT[:, :, :], 1.0)
    k.tt("vector", out=magT[:, :, :], in0=magT[:, :, :], in1=mag[:, :].unsq_bcast(S), op=ALU.mult)
    nCTim = sb("nCTim", [64, 16, 16])
    k.ts("vector", out=nCTim[:, :, :], in0=CTim[:, :, :], s1=-1.0, s2=None, op0=ALU.mult)
    car = sb("car", [64, 2, 16])
    k.memset("vector", car[:, :, :], 0.0)
    m = [sb(f"m{i}", [64, 2, S]) for i in range(2)]
    t4 = [sb(f"t4_{i}", [64, 4, S]) for i in range(2)]
    Ub = [sb("U0", [16, 16, S])] * 2
    Yb = [sb("Y0", [16, 4, S])] * 2
    cw = sb("cw", [64, 6, 16])
    Yv = Y_d.ap().rearrange("l c t -> c l t")
    for sg in range(nseg):
        U = Ub[sg % 2]
        k.dma("sync", out=U[:, :, :], in_=U_d[:, :, sg * S:(sg + 1) * S])
        def modulate_lane(l):
            ps = P[l % 4]
            k.mm(out=ps[:64, 0:S], lhsT=BTre[:, l, :], rhs=U[:, l, :])
            k.mm(out=ps[:64, S:2 * S], lhsT=BTim[:, l, :], rhs=U[:, l, :])
            bre, bim = ps[:64, 0:S], ps[:64, S:2 * S]
            tt_ = t4[l % 2]
            mm_ = m[l % 2]
            k.tt("vector", out=tt_[:, 0, :], in0=bre, in1=PhR[:, l, :], op=ALU.mult)
            k.tt("vector", out=tt_[:, 1, :], in0=bim, in1=PhI[:, l, :], op=ALU.mult)
            k.tt("vector", out=tt_[:, 2, :], in0=bre, in1=PhI[:, l, :], op=ALU.mult)
            k.tt("vector", out=tt_[:, 3, :], in0=bim, in1=PhR[:, l, :], op=ALU.mult)
            k.tt("gpsimd", out=mm_[:, 0, :], in0=tt_[:, 0, :], in1=tt_[:, 1, :], op=ALU.subtract)
            k.tt("gpsimd", out=mm_[:, 1, :], in0=tt_[:, 2, :], in1=tt_[:, 3, :], op=ALU.add)

        modulate_lane(0)
        for l in range(16):
            if l + 1 < 16:
                modulate_lane(l + 1)
            mm_ = m[l % 2]
            for ri in range(2):
                k.I("vector", "tensor_tensor_scan", out=g[:, ri, l, :], data0=magT[:, l, :], data1=mm_[:, ri, :],
                    initial=car[:, ri, l:l + 1], op0=ALU.mult, op1=ALU.add)
        gl_re, gl_im = g[:, 0, :, S - 1], g[:, 1, :, S - 1]
        pc, psn = psc[:, :], pss[:, :]
        k.tt("vector", out=cw[:, 0, :], in0=gl_re, in1=pc, op=ALU.mult)
        k.tt("vector", out=cw[:, 1, :], in0=gl_im, in1=psn, op=ALU.mult)
        k.tt("vector", out=cw[:, 2, :], in0=gl_re, in1=psn, op=ALU.mult)
        k.tt("vector", out=cw[:, 3, :], in0=gl_im, in1=pc, op=ALU.mult)
        k.tt("vector", out=car[:, 0, :], in0=cw[:, 0, :], in1=cw[:, 1, :], op=ALU.subtract)
        k.tt("vector", out=car[:, 1, :], in0=cw[:, 2, :], in1=cw[:, 3, :], op=ALU.add)
        if sg == 0:
            continue
        k.tt("gpsimd", out=h[:, 0, :, :], in0=g[:, 0, :, :], in1=cS, op=ALU.mult)
        k.tt("vector", out=tA[:, :, :], in0=g[:, 1, :, :], in1=sS, op=ALU.mult)
        k.tt("vector", out=h[:, 0, :, :], in0=h[:, 0, :, :], in1=tA[:, :, :], op=ALU.subtract)
        k.tt("gpsimd", out=h[:, 1, :, :], in0=g[:, 0, :, :], in1=sS, op=ALU.mult)
        k.tt("vector", out=tA[:, :, :], in0=g[:, 1, :, :], in1=cS, op=ALU.mult)
        k.tt("vector", out=h[:, 1, :, :], in0=h[:, 1, :, :], in1=tA[:, :, :], op=ALU.add)
        Y = Yb[sg % 2]
        for l in range(16):
            ps = P[4 + (l // 2) % 4]
            o = ps[:16, (l % 2) * S:(l % 2 + 1) * S]
            k.mm(out=o, lhsT=CTre[:, l, :], rhs=h[:, 0, l, :], start=True, stop=False)
            k.mm(out=o, lhsT=nCTim[:, l, :], rhs=h[:, 1, l, :], start=False, stop=True)
            if l % 2 == 1:
                l8 = (l - 1) % 4
                k.cp("scalar", out=Y[:, l8:l8 + 2, :].rearrange("p l t -> p (l t)"), in_=ps[:16, :])
            if l % 4 == 3:
                k.dma("sync", out=Yv[:, l - 3:l + 1, (sg - 1) * S:sg * S], in_=Y[:, :, :])


def attn_stage(k, c, qT_d, kT_d, v_d, attT_d, R=2048, NK=16640, nheads=8):
    P = c["P"]
    nkt = NK // 128
    ka = k.sb("at_ka", [128, NK], BF16)
    kb = k.sb("at_kb", [64, NK], BF16)
    vv = k.sb("at_v", [128, nkt, 128], BF16)
    qa = k.sb("at_qa", [128, R], BF16)
    qb = k.sb("at_qb", [64, R], BF16)
    ones = k.sb("at_ones", [128, 128], BF16)
    k.memset("vector", ones[:, :], 1.0)
    PT = [k.sb(f"at_PT{i}", [128, 512], BF16) for i in range(3)]
    rden = k.sb("at_rden", [128, 512], F32)
    oT = [k.sb(f"at_oT{i}", [128, 512], BF16) for i in range(2)]
    it = 0
    for h in range(nheads):
        k.dma("sync", out=ka[:, :], in_=kT_d[h, 0:128, :])
        k.dma("sync", out=kb[:, :], in_=kT_d[h, 128:192, :])
        vsrc = v_d[:, h * 128:(h + 1) * 128].rearrange("(t p) e -> p t e", p=128)
        hk = nkt // 2
        k.dma("gpsimd", out=vv[:, :hk, :], in_=vsrc[:, :hk, :])
        k.dma("gpsimd", out=vv[:, hk:, :], in_=vsrc[:, hk:, :])
        k.dma("sync", out=qa[:, :], in_=qT_d[h, 0:128, :])
        k.dma("sync", out=qb[:, :], in_=qT_d[h, 128:192, :])
        for qb_i in range(R // 512):
            qs = slice(qb_i * 512, (qb_i + 1) * 512)
            Ops, Dps = P[4 + (qb_i % 2)], P[6 + (qb_i % 2)]
            def scores(kt_, it_):
                S_ = P[it_ % 3]
                ks = slice(kt_ * 128, (kt_ + 1) * 128)
                k.mm(out=S_[:, :], lhsT=ka[:, ks], rhs=qa[:, qs], start=True, stop=False)
                k.mm(out=S_[:, :], lhsT=kb[:, ks], rhs=qb[:, qs], start=False, stop=True)

            scores(0, it)
            for kt in range(nkt):
                S = P[it % 3]
                pt = PT[it % 3]
                if kt + 1 < nkt:
                    scores(kt + 1, it + 1)
                k.act(out=pt[:, :], in_=S[:, :], func=AF.Exp)
                k.mm(out=Ops[:, :], lhsT=vv[:, kt, :], rhs=pt[:, :], start=(kt == 0), stop=(kt == nkt - 1))
                k.mm(out=Dps[:, :], lhsT=ones[:, :], rhs=pt[:, :], start=(kt == 0), stop=(kt == nkt - 1))
                it += 1
            k.I("vector", "reciprocal", out=rden[:, :], in_=Dps[:, :])
            o = oT[qb_i % 2]
            k.tt("vector", out=o[:, :], in0=Ops[:, :], in1=rden[:, :], op=ALU.mult)
            k.dma("sync", out=attT_d[h * 128:(h + 1) * 128, qs], in_=o[:, :])


def evout_stage(k, c, xa_d, u_d, yf_d, yb_d, attT_d, d_d, gluw_d, glub_d, wout_d, gate_d, xm_d, R=2048):
    P = c["P"]
    idb = c["identb"]
    wout = k.sb("eo_wout", [128, 16, 2048], BF16)
    gluw = k.sb("eo_gluw", [128, 8, 1024], BF16)
    k.dma("gpsimd", out=wout[:, :, :], in_=wout_d.ap().rearrange("(kt p) c -> p kt c", p=128))
    k.dma("gpsimd", out=gluw[:, :, :], in_=gluw_d.ap().rearrange("(kt p) c -> p kt c", p=128))
    drow, brow, grow = k.sb("eo_drow", [128, 1024], F32), k.sb("eo_brow", [128, 1024], F32), k.sb("eo_grow", [128, 2048], F32)
    k.dma("sync", out=drow[:, :], in_=d_d[:].pbcast(128))
    k.dma("sync", out=brow[:, :], in_=glub_d[:].pbcast(128))
    k.dma("sync", out=grow[:, :], in_=gate_d[:].pbcast(128))
    ut, yft, ybt = k.sb("eo_u", [128, 1024], F32), k.sb("eo_yf", [128, 1024], F32), k.sb("eo_yb", [128, 1024], F32)
    y, w1, ge = k.sb("eo_y", [128, 1024], F32), k.sb("eo_w1", [128, 1024], F32), k.sb("eo_ge", [128, 1024], F32)
    geb, ssb = k.sb("eo_geb", [128, 1024], BF16), k.sb("eo_ssb", [128, 1024], BF16)
    geT, ssT = k.sb("eo_geT", [128, 8, 128], BF16), k.sb("eo_ssT", [128, 8, 128], BF16)
    att = k.sb("eo_att", [128, 8, 128], BF16)
    xt = k.sb("eo_x", [128, 2048], F32)
    pT = P[7].bitcast_view(BF16)
    attv = attT_d.ap().rearrange("(kt p) t -> p kt t", p=128)
    for t in range(R // 128):
        rs = slice(t * 128, (t + 1) * 128)
        k.dma("sync", out=ut[:, :], in_=u_d[rs, :])
        k.dma("sync", out=yft[:, :], in_=yf_d[rs, :])
        k.dma("sync", out=ybt[:, :], in_=yb_d[rs, :])
        k.dma("sync", out=xt[:, :], in_=xa_d[rs, :])
        k.dma("sync", out=att[:, :, :], in_=attv[:, :, rs])
        k.tt("vector", out=y[:, :], in0=ut[:, :], in1=drow[:, :], op=ALU.mult)
        k.tt("gpsimd", out=w1[:, :], in0=yft[:, :], in1=ybt[:, :], op=ALU.add)
        k.tt("vector", out=y[:, :], in0=y[:, :], in1=w1[:, :], op=ALU.add)
        k.tt("gpsimd", out=w1[:, :], in0=y[:, :], in1=y[:, :], op=ALU.mult)
        k.ts("vector", out=w1[:, :], in0=w1[:, :], s1=0.044715, s2=1.0, op0=ALU.mult, op1=ALU.add)
        k.tt("vector", out=w1[:, :], in0=w1[:, :], in1=y[:, :], op=ALU.mult)
        k.act(out=w1[:, :], in_=w1[:, :], func=AF.Sigmoid, scale=1.5957691216057308)
        k.tt("vector", out=ge[:, :], in0=y[:, :], in1=w1[:, :], op=ALU.mult)
        k.cp("gpsimd", out=geb[:, :], in_=ge[:, :])
        for kt in range(8):
            k.tr(out=pT[:, kt * 128:(kt + 1) * 128], in_=geb[:, kt * 128:(kt + 1) * 128], ident=idb[:, :])
        k.cp("vector", out=geT[:, :, :].rearrange("p k t -> p (k t)"), in_=pT[:, :])
        for bi in range(2):
            for kt in range(8):
                k.mm(out=P[bi][:, :], lhsT=geT[:, kt, :], rhs=gluw[:, kt, bi * 512:(bi + 1) * 512], start=(kt == 0), stop=(kt == 7))
            k.tt("vector", out=w1[:, bi * 512:(bi + 1) * 512], in0=P[bi][:, :], in1=brow[:, bi * 512:(bi + 1) * 512], op=ALU.add)
        k.act(out=w1[:, :], in_=w1[:, :], func=AF.Sigmoid)
        k.tt("vector", out=ssb[:, :], in0=ge[:, :], in1=w1[:, :], op=ALU.mult)
        for kt in range(8):
            k.tr(out=pT[:, kt * 128:(kt + 1) * 128], in_=ssb[:, kt * 128:(kt + 1) * 128], ident=idb[:, :])
        k.cp("vector", out=ssT[:, :, :].rearrange("p k t -> p (k t)"), in_=pT[:, :])
        for bi in range(4):
            ps = P[2 + bi]
            for kt in range(16):
                lt = att[:, kt, :] if kt < 8 else ssT[:, kt - 8, :]
                k.mm(out=ps[:, :], lhsT=lt, rhs=wout[:, kt, bi * 512:(bi + 1) * 512], start=(kt == 0), stop=(kt == 15))
            cs_ = slice(bi * 512, (bi + 1) * 512)
            k.tt("vector", out=y[:, 0:512], in0=ps[:, :], in1=grow[:, cs_], op=ALU.mult)
            k.tt("vector", out=xt[:, cs_], in0=xt[:, cs_], in1=y[:, 0:512], op=ALU.add)
        k.dma("sync", out=xm_d[rs, :], in_=xt[:, :])


def hypre_stage(k, c, xh_d, hmask_d, A, Braw, w_in_d, cw_d, cb_d, zc_d):
    P = c["P"]
    if c.get("nm_bufs") is None:
        c["nm_bufs"] = dict(
            junk=k.sb("junk", [128, 2048], BF16), ss=k.sb("ss", [128, 1], F32), rstd=k.sb("rstd", [128, 1], F32),
            xh=k.sb("xh", [128, 2048], BF16), tmpT=k.sb("tmpT", [128, 8, 128], F32))
    bufs = c["nm_bufs"]
    hT = k.sb("hp_hT", [128, 16, 2050], BF16)
    hTh = k.sb("hp_hTh", [128, 16, 2], BF16)
    hm = k.sb("hp_hm", [128, 2], F32)
    cw = k.sb("hp_cw", [128, 48, 3], F32)
    cb = k.sb("hp_cb", [128, 48], F32)
    k.dma("sync", out=hm[:, :], in_=hmask_d[:, :])
    k.dma("sync", out=cw[:, :, :], in_=cw_d[:, :, :])
    k.dma("sync", out=cb[:, :], in_=cb_d[:, :])
    xt = [k.sb(f"hp_x{i}", [128, 2048], F32) for i in range(2)]
    for t in range(16):
        x = xt[t % 2]
        k.dma("sync", out=x[:, :], in_=xh_d[1 + t * 128:1 + (t + 1) * 128, :])
        norm_mod_T(k, c, "hp", x[:, :], 128, A, Braw, hT[:, :, 1 + t * 128:1 + (t + 1) * 128], bufs)
    xhalo = k.sb("hp_xhalo", [2, 2048], F32)
    k.dma("sync", out=xhalo[0:1, :], in_=xh_d[0:1, :])
    k.dma("sync", out=xhalo[1:2, :], in_=xh_d[2049:2050, :])
    norm_mod_T(k, c, "hp", xhalo[:2, :], 2, A, Braw, hTh[:, :, :], bufs)
    k.ts("vector", out=hT[:, :, 0:1], in0=hTh[:, :, 0:1], s1=hm[:, 0:1], s2=None, op0=ALU.mult)
    k.ts("vector", out=hT[:, :, 2049:2050], in0=hTh[:, :, 1:2], s1=hm[:, 1:2], s2=None, op0=ALU.mult)
    wv = w_in_d.ap().rearrange("(kt p) c -> p kt c", p=128)
    wj = [k.sb(f"hp_w{i}", [128, 16, 128], BF16) for i in range(2)]
    osb = [k.sb(f"hp_o{i}", [128, 512], F32) for i in range(2)]
    k.dma("gpsimd", out=wj[0][:, :, :], in_=wv[:, :, 0:128])
    it = 0
    for j in range(48):
        if j + 1 < 48:
            k.dma("gpsimd", out=wj[(j + 1) % 2][:, :, :], in_=wv[:, :, (j + 1) * 128:(j + 2) * 128])
        w = wj[j % 2]
        for b in range(4):
            zp = P[b]
            zh = P[4 + b]
            for kt in range(16):
                k.mm(out=zp[:, :], lhsT=w[:, kt, :], rhs=hT[:, kt, 1 + 512 * b:1 + 512 * (b + 1)], start=(kt == 0), stop=(kt == 15))
            for kt in range(16):
                k.mm(out=zh[:, 0:2], lhsT=w[:, kt, :], rhs=hT[:, kt, 512 * b:512 * b + 514:513], start=(kt == 0), stop=(kt == 15))
            o = osb[it % 2]
            w0, w1, w2 = cw[:, j, 0:1], cw[:, j, 1:2], cw[:, j, 2:3]
            k.ts("vector", out=o[:, :], in0=zp[:, :], s1=w1, s2=cb[:, j:j + 1], op0=ALU.mult, op1=ALU.add)
            k.stt("vector", out=o[:, 1:512], in0=zp[:, 0:511], scalar=w0, in1=o[:, 1:512], op0=ALU.mult, op1=ALU.add)
            k.stt("vector", out=o[:, 0:511], in0=zp[:, 1:512], scalar=w2, in1=o[:, 0:511], op0=ALU.mult, op1=ALU.add)
            k.stt("vector", out=o[:, 0:1], in0=zh[:, 0:1], scalar=w0, in1=o[:, 0:1], op0=ALU.mult, op1=ALU.add)
            k.stt("vector", out=o[:, 511:512], in0=zh[:, 1:2], scalar=w2, in1=o[:, 511:512], op0=ALU.mult, op1=ALU.add)
            k.dma("sync", out=zc_d[j * 128:(j + 1) * 128, 512 * b:512 * (b + 1)], in_=o[:, :])
            it += 1


def projres_stage(k, c, xin_d, yT_d, wout_d, gate_d, xm_d, R=2048):
    P = c["P"]
    wout = k.sb("pr_wout", [128, 16, 2048], BF16)
    yT = k.sb("pr_yT", [128, 16, R], BF16)
    k.dma("gpsimd", out=wout[:, :, :], in_=wout_d.ap().rearrange("(kt p) c -> p kt c", p=128))
    yv = yT_d.ap().rearrange("(kt p) t -> p kt t", p=128)
    for kt in range(16):
        k.dma("gpsimd", out=yT[:, kt, :], in_=yv[:, kt, :])
    grow = k.sb("pr_grow", [128, 2048], F32)
    k.dma("sync", out=grow[:, :], in_=gate_d[:].pbcast(128))
    xt = [k.sb(f"pr_x{i}", [128, 2048], F32) for i in range(2)]
    tmp = k.sb("pr_tmp", [128, 512], F32)
    for t in range(R // 128):
        rs = slice(t * 128, (t + 1) * 128)
        x = xt[t % 2]
        k.dma("sync", out=x[:, :], in_=xin_d[rs, :])
        for bi in range(4):
            ps = P[(t % 2) * 4 + bi]
            for kt in range(16):
                k.mm(out=ps[:, :], lhsT=yT[:, kt, rs], rhs=wout[:, kt, bi * 512:(bi + 1) * 512], start=(kt == 0), stop=(kt == 15))
            cs_ = slice(bi * 512, (bi + 1) * 512)
            k.tt("vector", out=tmp[:, :], in0=ps[:, :], in1=grow[:, cs_], op=ALU.mult)
            k.tt("vector", out=x[:, cs_], in0=x[:, cs_], in1=tmp[:, :], op=ALU.add)
        k.dma("sync", out=xm_d[rs, :], in_=x[:, :])


NFFT = 32768
LSEQ = 16384
CB = 16


def hy_consts():
    f8 = np.float64
    ts = np.arange(64, dtype=f8)[:, None]
    kf = np.arange(128, dtype=f8)[None, :]
    a1 = 2 * np.pi * ts * kf / 128
    F1cat = np.concatenate([np.cos(a1), -np.sin(a1)], axis=1)
    tf = np.arange(256, dtype=f8)[:, None]
    aw = 2 * np.pi * tf * kf / NFFT
    Wre = np.cos(aw).reshape(2, 128, 128).transpose(1, 0, 2)
    Wim = (-np.sin(aw)).reshape(2, 128, 128).transpose(1, 0, 2)
    ks = np.arange(256, dtype=f8)[None, :]
    a2 = 2 * np.pi * tf * ks / 256
    Cos = np.cos(a2).reshape(2, 128, 256).transpose(1, 0, 2)
    Sin = np.sin(a2).reshape(2, 128, 256).transpose(1, 0, 2)
    a2t = a2.T
    cA1 = np.concatenate([np.cos(a2t), np.sin(a2t)], axis=1).reshape(2, 128, 512).transpose(1, 0, 2)
    cA2 = np.concatenate([-np.sin(a2t), np.cos(a2t)], axis=1).reshape(2, 128, 512).transpose(1, 0, 2)
    awt = aw.T
    WTre, WTim = np.cos(awt), np.sin(awt)
    a1t = a1.T
    C1 = np.cos(a1t) / NFFT
    S1n = -np.sin(a1t) / NFFT
    b = lambda a: np.ascontiguousarray(a).astype(ml_dtypes.bfloat16)
    f = lambda a: np.ascontiguousarray(a, dtype=np.float32)
    return dict(F1cat=b(F1cat), Wre=f(Wre), Wim=f(Wim), Cos=b(Cos), Sin=b(Sin), NSin=b(-Sin), cA1=b(cA1), cA2=b(cA2),
                WTre=f(WTre), WTim=f(WTim), C1=b(C1), S1n=b(S1n))


HY_CONST_SHAPES = dict(F1cat=([64, 256], BF16), Wre=([128, 2, 128], F32), Wim=([128, 2, 128], F32), Cos=([128, 2, 256], BF16),
                       Sin=([128, 2, 256], BF16), NSin=([128, 2, 256], BF16), cA1=([128, 2, 512], BF16), cA2=([128, 2, 512], BF16),
                       WTre=([128, 256], F32), WTim=([128, 256], F32), C1=([128, 64], BF16), S1n=([128, 64], BF16))


def hy_load_consts(k, cd):
    out = {}
    for n, (shp, dt) in HY_CONST_SHAPES.items():
        b = k.sb("hc_" + n, shp, dt)
        src = cd[n]
        k.dma("sync", out=b.ap(), in_=src.ap())
        out[n] = b
    return out


def fft_stageA(k, c, hc, ybf, nch, Ab_re, Ab_im, tw, it0=0):
    P = c["P"]
    it = it0
    for cp in range(nch // 2):
        for half in range(2):
            ps = P[it % 2]
            for ci in range(2):
                ch = cp * 2 + ci
                k.mm(out=ps[:, ci * 256:(ci + 1) * 256], lhsT=ybf[:, ch, half * 128:(half + 1) * 128], rhs=hc["F1cat"][:, :])
            pv = ps[:, :].rearrange("p (c r f) -> p c r f", c=2, r=2)
            Are, Aim = pv[:, :, 0, :], pv[:, :, 1, :]
            wre = hc["Wre"][:, half, :].ap
            wim = hc["Wim"][:, half, :].ap
            wre_b = View(hc["Wre"], bass.AP(wre.tensor, wre.offset, [list(wre.ap[0]), [0, 2], [1, 128]]))
            wim_b = View(hc["Wim"], bass.AP(wim.tensor, wim.offset, [list(wim.ap[0]), [0, 2], [1, 128]]))
            t = tw[it % 2]
            tv = lambda i: t[:, i, 0:256].rearrange("p (c f) -> p c f", c=2)
            k.tt("vector", out=tv(0), in0=Are, in1=wre_b, op=ALU.mult)
            k.tt("vector", out=tv(1), in0=Aim, in1=wim_b, op=ALU.mult)
            k.tt("vector", out=tv(2), in0=Are, in1=wim_b, op=ALU.mult)
            k.tt("vector", out=tv(3), in0=Aim, in1=wre_b, op=ALU.mult)
            k.tt("gpsimd", out=Ab_re[:, half, cp * 2:cp * 2 + 2, :], in0=tv(0), in1=tv(1), op=ALU.subtract)
            k.tt("gpsimd", out=Ab_im[:, half, cp * 2:cp * 2 + 2, :], in0=tv(2), in1=tv(3), op=ALU.add)
            it += 1
    return it


def fft_stageB_block(k, c, hc, Ab_re, Ab_im, hk, blk, want_re=True, want_im=True):
    P = c["P"]
    cs = slice(blk * 4, blk * 4 + 4)
    ksl = slice(hk * 128, (hk + 1) * 128)
    Xre, Xim = P[2 + (blk % 2) * 2], P[3 + (blk % 2) * 2]
    if want_re:
        n = 0
        for ht in range(2):
            for (lt, rb) in ((hc["Cos"], Ab_re), (hc["Sin"], Ab_im)):
                k.mm(out=Xre[:, :], lhsT=lt[:, ht, ksl], rhs=rb[:, ht, cs, :].rearrange("p c f -> p (c f)"), start=(n == 0), stop=(n == 3))
                n += 1
    if want_im:
        n = 0
        for ht in range(2):
            for (lt, rb) in ((hc["NSin"], Ab_re), (hc["Cos"], Ab_im)):
                k.mm(out=Xim[:, :], lhsT=lt[:, ht, ksl], rhs=rb[:, ht, cs, :].rearrange("p c f -> p (c f)"), start=(n == 0), stop=(n == 3))
                n += 1
    return Xre, Xim


def hyconv_stage(k, c, cd, x1_d, x2_d, v_d, skip_d, Gre_d, Gim_d, y2_d, ncb=16):
    P = c["P"]
    hc = hy_load_consts(k, cd)
    sk = k.sb("hv_sk", [64, 2, 256], F32)
    for o in range(2):
        k.dma("sync", out=sk[:, o, :], in_=skip_d[o, :].pbcast(64))
    yf = k.sb("hv_yf", [64, CB, 256], F32)
    gt = k.sb("hv_gt", [64, CB, 256], F32)
    ybf = k.sb("hv_ybf", [64, CB, 256], BF16)
    Ab_re, Ab_im = k.sb("hv_Abre", [128, 2, CB, 128], BF16), k.sb("hv_Abim", [128, 2, CB, 128], BF16)
    Zb_re, Zb_im = k.sb("hv_Zbre", [128, 2, CB, 128], BF16), k.sb("hv_Zbim", [128, 2, CB, 128], BF16)
    Bb_re, Bb_im = k.sb("hv_Bbre", [128, CB, 256], BF16), k.sb("hv_Bbim", [128, CB, 256], BF16)
    tw = [k.sb(f"hv_tw{i}", [128, 4, 512], F32) for i in range(2)]
    Gt = [(k.sb(f"hv_Gre{i}", [128, 512], F32), k.sb(f"hv_Gim{i}", [128, 512], F32)) for i in range(2)]
    wsk = k.sb("hv_wsk", [64, 2, 256], F32)
    w2 = k.sb("hv_w2", [64, 2, 256], F32)
    gates = (x1_d, x2_d)
    it = 0
    for cb in range(ncb):
        chs = slice(cb * CB, (cb + 1) * CB)
        k.dma("sync", out=yf[:, :, :], in_=v_d[chs, :].rearrange("c (s f) -> s c f", s=64))
        for o in range(2):
            k.dma("sync", out=gt[:, :, :], in_=gates[o][chs, :].rearrange("c (s f) -> s c f", s=64))
            k.cp("scalar", out=ybf[:, :, :].rearrange("p c f -> p (c f)"), in_=yf[:, :, :].rearrange("p c f -> p (c f)"))
            it = fft_stageA(k, c, hc, ybf, CB, Ab_re, Ab_im, tw, it)
            for hk in range(2):
                for blk in range(CB // 4):
                    Xre, Xim = fft_stageB_block(k, c, hc, Ab_re, Ab_im, hk, blk)
                    gre, gim = Gt[it % 2]
                    c0 = (cb * CB + blk * 4) * 128
                    k.dma("sync", out=gre[:, :], in_=Gre_d[o, hk, :, c0:c0 + 512])
                    k.dma("sync", out=gim[:, :], in_=Gim_d[o, hk, :, c0:c0 + 512])
                    t = tw[it % 2]
                    k.tt("vector", out=t[:, 0, :], in0=Xre[:, :], in1=gre[:, :], op=ALU.mult)
                    k.tt("vector", out=t[:, 1, :], in0=Xim[:, :], in1=gim[:, :], op=ALU.mult)
                    k.tt("vector", out=t[:, 2, :], in0=Xre[:, :], in1=gim[:, :], op=ALU.mult)
                    k.tt("vector", out=t[:, 3, :], in0=Xim[:, :], in1=gre[:, :], op=ALU.mult)
                    zs = slice(blk * 4, blk * 4 + 4)
                    k.tt("gpsimd", out=Zb_re[:, hk, zs, :].rearrange("p c f -> p (c f)"), in0=t[:, 0, :], in1=t[:, 1, :], op=ALU.subtract)
                    k.tt("gpsimd", out=Zb_im[:, hk, zs, :].rearrange("p c f -> p (c f)"), in0=t[:, 2, :], in1=t[:, 3, :], op=ALU.add)
                    it += 1
            for ch in range(CB):
                ps = P[6 + (ch % 2)]
                n = 0
                for hk in range(2):
                    for (zb, rt) in ((Zb_re, hc["cA1"]), (Zb_im, hc["cA2"])):
                        k.mm(out=ps[:, :], lhsT=zb[:, hk, ch, :], rhs=rt[:, hk, :], start=(n == 0), stop=(n == 3))
                        n += 1
                Bre, Bim = ps[:, 0:256], ps[:, 256:512]
                t = tw[it % 2]
                k.tt("vector", out=t[:, 0, 0:256], in0=Bre, in1=hc["WTre"][:, :], op=ALU.mult)
                k.tt("vector", out=t[:, 1, 0:256], in0=Bim, in1=hc["WTim"][:, :], op=ALU.mult)
                k.tt("vector", out=t[:, 2, 0:256], in0=Bre, in1=hc["WTim"][:, :], op=ALU.mult)
                k.tt("vector", out=t[:, 3, 0:256], in0=Bim, in1=hc["WTre"][:, :], op=ALU.mult)
                k.tt("gpsimd", out=Bb_re[:, ch, :], in0=t[:, 0, 0:256], in1=t[:, 1, 0:256], op=ALU.subtract)
                k.tt("gpsimd", out=Bb_im[:, ch, :], in0=t[:, 2, 0:256], in1=t[:, 3, 0:256], op=ALU.add)
                it += 1
            for pr in range(CB // 2):
                ps = P[pr % 2]
                cs2 = slice(pr * 2, pr * 2 + 2)
                k.mm(out=ps[:64, :], lhsT=hc["C1"][:, :], rhs=Bb_re[:, cs2, :].rearrange("p c f -> p (c f)"), start=True, stop=False)
                k.mm(out=ps[:64, :], lhsT=hc["S1n"][:, :], rhs=Bb_im[:, cs2, :].rearrange("p c f -> p (c f)"), start=False, stop=True)
                skb = sk[:, o, cb * CB + pr * 2:cb * CB + pr * 2 + 2].unsq_bcast(256)
                k.tt("gpsimd", out=wsk[:, :, :], in0=yf[:, cs2, :], in1=skb, op=ALU.mult)
                k.tt("vector", out=w2[:, :, :], in0=ps[:64, :].rearrange("p (c f) -> p c f", c=2), in1=wsk[:, :, :], op=ALU.add)
                k.tt("gpsimd", out=yf[:, cs2, :], in0=w2[:, :, :], in1=gt[:, cs2, :], op=ALU.mult)
        k.dma("sync", out=y2_d[chs, :].rearrange("c (s f) -> s c f", s=64), in_=yf[:, :, :])


def hy_filt_consts(ci):
    import math
    L = LSEQ
    t = np.arange(L, dtype=np.float32) / np.float32(L)
    bands = np.linspace(1e-4, 15, 16, dtype=np.float32)
    ang = (np.float32(2.0 * math.pi) * t[:, None] * bands[None, :]).astype(np.float32)
    feat = np.concatenate([t[:, None], np.cos(ang), -np.sin(ang)], axis=-1).astype(np.float32)
    dmin, dmax = math.log(1e-2) / 1.5, math.log(1e-2) / 0.3
    deltas = np.abs(np.linspace(dmin, dmax, 2048, dtype=np.float32))[ci * 256:(ci + 1) * 256].astype(np.float64)
    drow = np.broadcast_to(deltas[None, :], (64, 256))
    E1 = np.exp(-(256.0 * np.arange(64, dtype=np.float64)[:, None] / L) * deltas[None, :])
    tfrow = np.broadcast_to(np.arange(256, dtype=np.float32)[None, :], (64, 256))
    return dict(featT=np.ascontiguousarray(feat.T), drow=np.ascontiguousarray(drow, dtype=np.float32), E1=np.ascontiguousarray(E1, dtype=np.float32),
                tfrow=np.ascontiguousarray(tfrow))


def hyfilt_stage(k, c, cd, featT_d, w1_d, w2_d, w3_d, bf_d, wout_d, drow_d, E1_d, tfrow_d, Gre_d, Gim_d, nfb=16, norders=2):
    P = c["P"]
    hc = {}
    for n in ("F1cat", "Wre", "Wim", "Cos", "Sin", "NSin"):
        shp, dt = HY_CONST_SHAPES[n]
        hc[n] = k.sb("hc_" + n, shp, dt)
        k.dma("sync", out=hc[n].ap(), in_=cd[n].ap())
    sb = lambda n, s, dt=F32: k.sb("hf_" + n, s, dt)
    w1s, w2s, w3s, bf = sb("w1", [33, 64]), sb("w2", [64, 64]), sb("w3", [64, 64]), sb("bf", [64, 4])
    k.dma("sync", out=w1s[:, :], in_=w1_d[:, :])
    k.dma("sync", out=w2s[:, :], in_=w2_d[:, :])
    k.dma("sync", out=w3s[:, :], in_=w3_d[:, :])
    k.dma("sync", out=bf[:, :], in_=bf_d[:, :])
    frb = sb("frb", [64, 3])
    for l in range(3):
        k.tt("vector", out=frb[:, l:l + 1], in0=bf[:, l:l + 1], in1=bf[:, 3:4], op=ALU.mult)
    drow, E1 = sb("drow", [64, 256]), sb("E1", [64, 256])
    k.dma("sync", out=drow[:, :], in_=drow_d[:, :])
    k.dma("sync", out=E1[:, :], in_=E1_d[:, :])
    hidT = sb("hidT", [64, LSEQ])
    ft = [sb(f"ft{i}", [33, 512]) for i in range(2)]
    a_, s1_, s2_ = sb("a", [64, 512]), sb("s1", [64, 512]), sb("s2", [64, 512])
    ki = k.sb("hf_ki", [64, 512], mybir.dt.int32)
    hA, hB = sb("hA", [64, 512]), sb("hB", [64, 512])
    ws = (w1s, w2s, w3s)
    for blk in range(LSEQ // 512):
        f = ft[blk % 2]
        k.dma("sync", out=f[:, :], in_=featT_d[:, blk * 512:(blk + 1) * 512])
        src = f[:, :]
        dsts = (hA[:, :], hB[:, :], hidT[:, blk * 512:(blk + 1) * 512])
        for l in range(3):
            ps = P[(blk * 3 + l) % 2]
            k.mm(out=ps[:64, :], lhsT=ws[l][:, :], rhs=src)
            k.ts("vector", out=a_[:, :], in0=ps[:64, :], s1=bf[:, 3:4], s2=frb[:, l:l + 1], op0=ALU.mult, op1=ALU.add)
            sincos(k, dsts[l], a_[:, :], 0.0, s1_[:, :], s2_[:, :], ki[:, :])
            src = dsts[l]
    ones = sb("ones", [64, 128])
    k.memset("vector", ones[:, :], 1.0)
    Wo = [sb(f"Wo{i}", [64, 2, 16]) for i in range(2)]
    Wn = sb("Wn", [64, 16, 256])
    tfrow = sb("tfrow", [64, 256])
    k.dma("sync", out=tfrow[:, :], in_=tfrow_d[:, :])
    Hf = sb("Hf", [64, 2, 16, 256])
    Hsd = k.sb("hf_Hsd", [64, 2, 16, 256], BF16)
    absum, s2n, rn = sb("absum", [64, 32]), sb("s2n", [64, 16]), sb("rn", [128, 16])
    Ab_re, Ab_im = k.sb("hf_Abre", [128, 2, 16, 128], BF16), k.sb("hf_Abim", [128, 2, 16, 128], BF16)
    tw = [sb(f"tw{i}", [128, 4, 512]) for i in range(2)]
    go = [sb(f"go{i}", [128, 512]) for i in range(2)]
    it = 0
    ig = 0
    for o in range(norders):
        for fb in range(nfb):
            chs = slice(fb * 16, (fb + 1) * 16)
            wo = Wo[fb % 2]
            k.dma("sync", out=wo[:, :, :], in_=wout_d[:, o, :, chs])
            dv = drow[:, chs].ap
            dr_b = View(drow, bass.AP(dv.tensor, dv.offset, [list(dv.ap[0]), [1, 16], [0, 256]]))
            tv_ = tfrow[:, :].ap
            tf_b = View(tfrow, bass.AP(tv_.tensor, tv_.offset, [list(tv_.ap[0]), [0, 16], [1, 256]]))
            k.tt("gpsimd", out=Wn[:, :, :], in0=dr_b, in1=tf_b, op=ALU.mult)
            k.act(out=Wn[:, :, :], in_=Wn[:, :, :], func=AF.Exp, scale=-1.0 / LSEQ)
            k.tt("gpsimd", out=Wn[:, :, :], in0=Wn[:, :, :], in1=E1[:, chs].unsq_bcast(256), op=ALU.mult)
            wflat = wo[:, :, :].rearrange("p d c -> p (d c)")
            for tg in range(32):
                ps = P[2 + tg % 2]
                for j in range(8):
                    tfv = tg * 8 + j
                    k.mm(out=ps[:64, j * 32:(j + 1) * 32], lhsT=hidT[:, tfv:LSEQ:256], rhs=wflat)
                wv_ = Wn[:, :, tg * 8:(tg + 1) * 8].ap
                Wn_b = View(Wn, bass.AP(wv_.tensor, wv_.offset, [list(wv_.ap[0]), [0, 2], [256, 16], [1, 8]]))
                k.tt("vector", out=Hf[:, :, :, tg * 8:(tg + 1) * 8], in0=ps[:64, 0:256].rearrange("p (j d c) -> p d c j", j=8, d=2),
                     in1=Wn_b, op=ALU.mult)
            k.memset("vector", Hf[0:1, 1, :, 0:1], 0.0)
            k.I("vector", "tensor_reduce", out=absum[:, :], in_=Hf[:, :, :, :].rearrange("p d c f -> p (d c) f"), axis=AX.X, op=ALU.add,
                apply_absolute_value=True)
            k.tt("vector", out=s2n[:, :], in0=absum[:, 0:16], in1=absum[:, 16:32], op=ALU.add)
            psn = P[4]
            k.mm(out=psn[:, 0:16], lhsT=ones[:, :], rhs=s2n[:, :])
            k.I("vector", "reciprocal", out=rn[:, :], in_=psn[:, 0:16])
            k.tt("gpsimd", out=Hsd[:, 0, :, :], in0=Hf[:, 0, :, :], in1=Hf[:, 1, :, :], op=ALU.add)
            k.tt("gpsimd", out=Hsd[:, 1, :, :], in0=Hf[:, 0, :, :], in1=Hf[:, 1, :, :], op=ALU.subtract)
            for sd in range(2):
                it = fft_stageA(k, c, hc, Hsd[:, sd, :, :], 16, Ab_re, Ab_im, tw, it)
                for hk in range(2):
                    for blk in range(4):
                        Xre, Xim = fft_stageB_block(k, c, hc, Ab_re, Ab_im, hk, blk, want_re=(sd == 0), want_im=(sd == 1))
                        X = Xre if sd == 0 else Xim
                        g_ = go[ig % 2]
                        k.tt("vector", out=g_[:, :].rearrange("p (c f) -> p c f", c=4), in0=X[:, :].rearrange("p (c f) -> p c f", c=4),
                             in1=rn[:, blk * 4:blk * 4 + 4].unsq_bcast(128), op=ALU.mult)
                        c0 = (fb * 16 + blk * 4) * 128
                        dst = Gre_d if sd == 0 else Gim_d
                        k.dma("sync", out=dst[o, hk, :, c0:c0 + 512], in_=g_[:, :])
                        ig += 1


def _cols(v, n=16):
    return np.ascontiguousarray(np.asarray(v, np.float32).reshape(n, 128).T)


def _modc(m, k0):
    return np.ascontiguousarray(np.stack([_cols(m[k0]), _cols(m[k0 + 1]), _cols(m[k0 + 2])], axis=1).astype(np.float32))


def _rope_tabs(n_tokens):
    t = np.arange(n_tokens)
    row = (t // 64).astype(np.float32)
    col = (t % 64).astype(np.float32)
    inv = (10000.0 ** (-np.arange(16, dtype=np.float32) / 16)).astype(np.float32)
    ar = row[:, None] * inv
    ac = col[:, None] * inv
    cos = np.concatenate([np.cos(ar), np.cos(ac)], axis=1).astype(np.float32)
    sin = np.concatenate([np.sin(ar), np.sin(ac)], axis=1).astype(np.float32)
    return cos, sin


_IDF = np.eye(128, dtype=np.float32)
_IDB = _IDF.astype(ml_dtypes.bfloat16)
_PROGS = {}
_DBG = {}


def _prog(key, fn):
    if key not in _PROGS:
        _PROGS[key] = fn()
    return _PROGS[key]


def _run(nc, maps):
    return run_bass_kernel_spmd(nc, maps, core_ids=list(range(8))).results


def _build_ada():
    k = KB()
    cc = k.dram("cc", [128, 16, 2], F32, kind="ExternalInput")
    w = k.dram("w", [2, 2048, 2304], F32, kind="ExternalInput")
    b = k.dram("b", [2, 2304], F32, kind="ExternalInput")
    out = k.dram("out", [2, 2, 2304], F32, kind="ExternalOutput")
    c = alloc_common(k)
    ada_stage(k, c, cc, w, b, out)
    return k.emit()


def _build_ffn(R):
    k = KB()
    xin = k.dram("xin", [R, D], F32, kind="ExternalInput")
    xout = k.dram("xout", [R, D], F32, kind="ExternalOutput")
    modc = k.dram("modc", [128, 3, 16], F32, kind="ExternalInput")
    normg = k.dram("normg", [128, 16], F32, kind="ExternalInput")
    w_in = k.dram("w_in", [D, 2 * DFF], F32, kind="ExternalInput")
    w_out = k.dram("w_out", [DFF, D], F32, kind="ExternalInput")
    idf = k.dram("idf", [128, 128], F32, kind="ExternalInput")
    idb = k.dram("idb", [128, 128], BF16, kind="ExternalInput")
    c = alloc_common(k)
    load_ident(k, c, idf, idb)
    A, Braw, G = modcols_prepare(k, "m0", modc[:, :, :], normg[:, :], 0)
    ffn_stage(k, c, "f0", xin, xout, R, 128, A, Braw, G, w_in, w_out)
    return k.emit()


def _build_evpre(R):
    k = KB()
    di = lambda n, s, dt=F32: k.dram(n, s, dt, kind="ExternalInput")
    do = lambda n, s, dt=F32: k.dram(n, s, dt, kind="ExternalOutput")
    xin = di("xin", [R, D]); modc = di("modc", [128, 3, 16]); normg = di("normg", [128, 16])
    w_in = di("w_in", [D, 1856]); wuq = di("wuq", [512, 1536]); wukv = di("wukv", [256, 2048])
    gqa = di("gqa", [128, 4]); gkva = di("gkva", [128, 2]); gq = di("gq", [192]); gk = di("gk", [192])
    cos = di("cos", [R, 32]); sin = di("sin", [R, 32])
    idf = di("idf", [128, 128]); idb = di("idb", [128, 128], BF16)
    qT = do("qT", [8, 192, R], BF16); kT = do("kT", [8, 192, R], BF16); v = do("v", [R, 1024], BF16); u = do("u", [R, 1024])
    c = alloc_common(k)
    load_ident(k, c, idf, idb)
    A, Braw, G = modcols_prepare(k, "m1", modc[:, :, :], normg[:, :], 0)
    evpre_stage(k, c, xin, R, 128, A, Braw, w_in, wuq, wukv, gqa, gkva, gq, gk, cos, sin, qT, kT, v, u, True)
    return k.emit()


def _build_attn():
    k = KB()
    di = lambda n, s, dt=F32: k.dram(n, s, dt, kind="ExternalInput")
    qT = di("qT", [8, 192, 2048], BF16); kT = di("kT", [8, 192, 16640], BF16); v = di("v", [16640, 1024], BF16)
    attT = k.dram("attT", [1024, 2048], BF16, kind="ExternalOutput")
    c = alloc_common(k)
    attn_stage(k, c, qT, kT, v, attT, 2048, 16640, 8)
    return k.emit()


def _build_s5():
    k = KB()
    di = lambda n, s, dt=F32: k.dram(n, s, dt, kind="ExternalInput")
    U = di("U", [16, 16, 16640]); lre = di("lre", [64, 16]); lim = di("lim", [64, 16]); ldt = di("ldt", [16])
    BTre = di("BTre", [16, 16, 64]); BTim = di("BTim", [16, 16, 64]); CTre = di("CTre", [64, 16, 16]); CTim = di("CTim", [64, 16, 16])
    jt = di("jt", [64, SEG + 1])
    Y = k.dram("Y", [16, 16, 16384], F32, kind="ExternalOutput")
    c = alloc_common(k)
    s5_stage(k, c, U, lre, lim, ldt, BTre, BTim, CTre, CTim, jt, Y, NSEG)
    return k.emit()


def _build_evout():
    k = KB()
    R = 2048
    di = lambda n, s, dt=F32: k.dram(n, s, dt, kind="ExternalInput")
    xa = di("xa", [R, D]); u = di("u", [R, 1024]); yf = di("yf", [R, 1024]); yb = di("yb", [R, 1024]); attT = di("attT", [1024, R], BF16)
    dd = di("dd", [1024]); gluw = di("gluw", [1024, 1024]); glub = di("glub", [1024]); wout = di("wout", [2048, 2048]); gate = di("gate", [2048])
    idf = di("idf", [128, 128]); idb = di("idb", [128, 128], BF16)
    xm = k.dram("xm", [R, D], F32, kind="ExternalOutput")
    c = alloc_common(k)
    load_ident(k, c, idf, idb)
    evout_stage(k, c, xa, u, yf, yb, attT, dd, gluw, glub, wout, gate, xm, R)
    return k.emit()


def _build_hypre():
    k = KB()
    di = lambda n, s, dt=F32: k.dram(n, s, dt, kind="ExternalInput")
    xh = di("xhin", [2050, D]); hmask = di("hmask", [128, 2]); modc = di("modc", [128, 3, 16]); normg = di("normg", [128, 16])
    w_in = di("w_in", [D, 6144]); cw = di("cw", [128, 48, 3]); cb = di("cb", [128, 48])
    idf = di("idf", [128, 128]); idb = di("idb", [128, 128], BF16)
    zc = k.dram("zc", [6144, 2048], F32, kind="ExternalOutput")
    c = alloc_common(k)
    load_ident(k, c, idf, idb)
    A, Braw, G = modcols_prepare(k, "m1", modc[:, :, :], normg[:, :], 0)
    hypre_stage(k, c, xh, hmask, A, Braw, w_in, cw, cb, zc)
    return k.emit()


_FILT_CONSTS = ("F1cat", "Wre", "Wim", "Cos", "Sin", "NSin")


def _build_hyfilt():
    k = KB()
    di = lambda n, s, dt=F32: k.dram(n, s, dt, kind="ExternalInput")
    cd = {n: di("k_" + n, HY_CONST_SHAPES[n][0], HY_CONST_SHAPES[n][1]) for n in _FILT_CONSTS}
    featT = di("featT", [33, LSEQ]); w1 = di("w1", [33, 64]); w2 = di("w2", [64, 64]); w3 = di("w3", [64, 64]); bf = di("bf", [64, 4])
    wout = di("wout", [64, 2, 2, 256]); drow = di("drow", [64, 256]); E1 = di("E1", [64, 256]); tfrow = di("tfrow", [64, 256])
    Gre = k.dram("Gre", [2, 2, 128, 256 * 128], F32, kind="ExternalOutput")
    Gim = k.dram("Gim", [2, 2, 128, 256 * 128], F32, kind="ExternalOutput")
    c = alloc_common(k)
    hyfilt_stage(k, c, cd, featT, w1, w2, w3, bf, wout, drow, E1, tfrow, Gre, Gim, 16, 2)
    return k.emit()


def _build_hyconv():
    k = KB()
    di = lambda n, s, dt=F32: k.dram(n, s, dt, kind="ExternalInput")
    cd = {n: di("k_" + n, shp, dt) for n, (shp, dt) in HY_CONST_SHAPES.items()}
    x1 = di("x1", [256, LSEQ]); x2 = di("x2", [256, LSEQ]); v = di("v", [256, LSEQ]); skip = di("skip", [2, 256])
    Gre = di("Gre", [2, 2, 128, 256 * 128]); Gim = di("Gim", [2, 2, 128, 256 * 128])
    y2 = k.dram("y2", [256, LSEQ], F32, kind="ExternalOutput")
    c = alloc_common(k)
    hyconv_stage(k, c, cd, x1, x2, v, skip, Gre, Gim, y2, 16)
    return k.emit()


def _build_projres():
    k = KB()
    di = lambda n, s, dt=F32: k.dram(n, s, dt, kind="ExternalInput")
    xin = di("xin", [2048, D]); yT = di("yT", [2048, 2048]); wout = di("wout", [2048, 2048]); gate = di("gate", [2048])
    xm = k.dram("xm", [2048, D], F32, kind="ExternalOutput")
    c = alloc_common(k)
    projres_stage(k, c, xin, yT, wout, gate, xm, 2048)
    return k.emit()


def _hyena_mixer(inp, xs, m1):
    x_full = np.concatenate(xs, axis=0)
    xp = np.concatenate([np.zeros((1, 2048), np.float32), x_full, np.zeros((1, 2048), np.float32)], axis=0)
    cw = np.ascontiguousarray(inp["hy_conv_w"][0].reshape(3, 48, 128).transpose(2, 1, 0))
    cb = np.ascontiguousarray(inp["hy_conv_b"][0].reshape(48, 128).T)
    maps = []
    for ci in range(8):
        hm = np.ones((128, 2), np.float32)
        if ci == 0:
            hm[:, 0] = 0
        if ci == 7:
            hm[:, 1] = 0
        maps.append(dict(xhin=np.ascontiguousarray(xp[ci * 2048:ci * 2048 + 2050]), hmask=hm, modc=_modc(m1, 3), normg=_cols(inp["norm_g"][1, 1]),
                         w_in=np.ascontiguousarray(inp["hy_w_in"][0]), cw=cw, cb=cb, idf=_IDF, idb=_IDB))
    rz = _run(_prog("hypre", _build_hypre), maps)
    z_all = np.concatenate([rz[ci]["zc"] for ci in range(8)], axis=1)
    consts = hy_consts()
    bf = np.ascontiguousarray(np.stack([inp["hy_filt_b1"][0], inp["hy_filt_b2"][0], inp["hy_filt_b3"][0], inp["hy_filt_freq"][0]], axis=1).astype(np.float32))
    maps = []
    for ci in range(8):
        fc = hy_filt_consts(ci)
        m = {"k_" + n: consts[n] for n in _FILT_CONSTS}
        m.update(featT=fc["featT"], drow=fc["drow"], E1=fc["E1"], tfrow=fc["tfrow"], w1=np.ascontiguousarray(inp["hy_filt_w1"][0]), w2=np.ascontiguousarray(inp["hy_filt_w2"][0]),
                 w3=np.ascontiguousarray(inp["hy_filt_w3"][0]), bf=bf, wout=np.ascontiguousarray(inp["hy_filt_w_out"][0][:, :, :, ci * 256:(ci + 1) * 256]))
        maps.append(m)
    rf = _run(_prog("hyfilt", _build_hyfilt), maps)
    maps = []
    for ci in range(8):
        cs = slice(ci * 256, (ci + 1) * 256)
        m = {"k_" + n: v for n, v in consts.items()}
        m.update(x1=np.ascontiguousarray(z_all[0:2048][cs]), x2=np.ascontiguousarray(z_all[2048:4096][cs]), v=np.ascontiguousarray(z_all[4096:6144][cs]),
                 skip=np.ascontiguousarray(inp["hy_skip"][0][:, cs]), Gre=rf[ci]["Gre"], Gim=rf[ci]["Gim"])
        maps.append(m)
    ry = _run(_prog("hyconv", _build_hyconv), maps)
    y_all = np.concatenate([ry[ci]["y2"] for ci in range(8)], axis=0)
    prc = dict(wout=np.ascontiguousarray(inp["hy_w_out"][0]), gate=np.ascontiguousarray(m1[5]))
    rp = _run(_prog("projres", _build_projres), [dict(prc, xin=np.ascontiguousarray(xs[ci]), yT=np.ascontiguousarray(y_all[:, ci * 2048:(ci + 1) * 2048]))
                                                  for ci in range(8)])
    return [rp[ci]["xm"] for ci in range(8)]


def _s5_inmaps(inp, u_x, u_c):
    jt = np.broadcast_to(np.arange(SEG + 1, dtype=np.float32)[None, :], (64, SEG + 1)).copy()
    maps = []
    seq_f = np.concatenate([u_c, u_x], axis=0)
    seq_b = np.concatenate([u_x, u_c], axis=0)[::-1]
    for ci in range(8):
        gs = slice(ci * 8, ci * 8 + 8)
        U = np.empty((16, 16, 16640), np.float32)
        for di_, seq in enumerate((seq_f, seq_b)):
            blk = seq[:, ci * 128:(ci + 1) * 128].reshape(16640, 8, 16)
            U[:, di_ * 8:(di_ + 1) * 8, :] = blk.transpose(2, 1, 0)

        def lanes(a):
            return a[:, gs].reshape(16, *a.shape[2:])
        m = dict(U=U, lre=lanes(inp["s5_lam_re"][0]).T, lim=lanes(inp["s5_lam_im"][0]).T, ldt=lanes(inp["s5_log_dt"][0]),
                 BTre=lanes(inp["s5_b_re"][0]).transpose(2, 0, 1), BTim=lanes(inp["s5_b_im"][0]).transpose(2, 0, 1),
                 CTre=lanes(inp["s5_c_re"][0]).transpose(2, 0, 1), CTim=lanes(inp["s5_c_im"][0]).transpose(2, 0, 1), jt=jt)
        maps.append({kk: np.ascontiguousarray(v, dtype=np.float32) for kk, v in m.items()})
    return maps


def _ffn_launch(x_rows_per_core, m, k0, normg, w_in, w_out):
    R = x_rows_per_core[0].shape[0]
    nc = _prog(("ffn", R), lambda: _build_ffn(R))
    common = dict(modc=_modc(m, k0), normg=_cols(normg), w_in=np.ascontiguousarray(w_in), w_out=np.ascontiguousarray(w_out), idf=_IDF, idb=_IDB)
    res = _run(nc, [dict(common, xin=np.ascontiguousarray(xr)) for xr in x_rows_per_core])
    return [r["xout"] for r in res]


def kernel(**inp):
    inp = {kk: np.asarray(v) for kk, v in inp.items()}
    x = inp["x"][0]
    ctx = inp["ctx"][0]
    cvec = np.stack([inp["c"][0], inp["c_ctx"]], axis=1)
    cc = np.ascontiguousarray(cvec.reshape(16, 128, 2).transpose(1, 0, 2))
    nc = _prog("ada", _build_ada)
    res = _run(nc, [dict(cc=cc, w=np.ascontiguousarray(inp["ada_w"][:, :, ci * 2304:(ci + 1) * 2304]),
                         b=np.ascontiguousarray(inp["ada_b"][:, ci * 2304:(ci + 1) * 2304])) for ci in range(8)])
    mods = np.concatenate([res[ci]["out"] for ci in range(8)], axis=2)
    mx = [mods[l, 0].reshape(9, 2048) for l in range(2)]
    mc = [mods[l, 1].reshape(9, 2048) for l in range(2)]
    xs = [x[ci * 2048:(ci + 1) * 2048] for ci in range(8)]
    cs = [ctx[(ci % 2) * 128:(ci % 2) * 128 + 128] for ci in range(8)]
    xs = _ffn_launch(xs, mx[0], 0, inp["norm_g"][0, 0], inp["ffn_w_in"][0, 0], inp["ffn_w_out"][0, 0])
    cs = _ffn_launch(cs, mc[0], 0, inp["norm_g"][0, 0], inp["ffn_w_in"][0, 0], inp["ffn_w_out"][0, 0])
    cos, sin = _rope_tabs(16384)
    evc = dict(normg=_cols(inp["norm_g"][0, 1]), w_in=inp["ev_w_in"][0], wuq=inp["mla_w_uq"][0], wukv=inp["mla_w_ukv"][0],
               gqa=_cols(inp["mla_q_a_norm_g"][0], 4), gkva=_cols(inp["mla_kv_a_norm_g"][0], 2), gq=inp["mla_q_head_g"][0], gk=inp["mla_k_head_g"][0],
               idf=_IDF, idb=_IDB)
    evc = {kk: np.ascontiguousarray(v) for kk, v in evc.items()}
    nc = _prog(("evpre", 2048), lambda: _build_evpre(2048))
    rx = _run(nc, [dict(evc, modc=_modc(mx[0], 3), xin=np.ascontiguousarray(xs[ci]), cos=np.ascontiguousarray(cos[ci * 2048:(ci + 1) * 2048]),
                        sin=np.ascontiguousarray(sin[ci * 2048:(ci + 1) * 2048])) for ci in range(8)])
    nc = _prog(("evpre", 128), lambda: _build_evpre(128))
    one = np.ones((128, 32), np.float32)
    zero = np.zeros((128, 32), np.float32)
    rc = _run(nc, [dict(evc, modc=_modc(mc[0], 3), xin=np.ascontiguousarray(cs[ci]), cos=one, sin=zero) for ci in range(8)])
    kT_all = np.ascontiguousarray(np.concatenate([rx[ci]["kT"] for ci in range(8)] + [rc[0]["kT"], rc[1]["kT"]], axis=2))
    v_all = np.ascontiguousarray(np.concatenate([rx[ci]["v"] for ci in range(8)] + [rc[0]["v"], rc[1]["v"]], axis=0))
    u_x = np.concatenate([rx[ci]["u"] for ci in range(8)], axis=0)
    u_c = np.concatenate([rc[0]["u"], rc[1]["u"]], axis=0)
    nc = _prog("attn", _build_attn)
    ra = _run(nc, [dict(qT=rx[ci]["qT"], kT=kT_all, v=v_all) for ci in range(8)])
    nc = _prog("s5", _build_s5)
    rs = _run(nc, _s5_inmaps(inp, u_x, u_c))
    Yf = np.empty((16384, 1024), np.float32)
    Yb = np.empty((16384, 1024), np.float32)
    for ci in range(8):
        Y = rs[ci]["Y"]
        Yf[:, ci * 128:(ci + 1) * 128] = Y[0:8].transpose(2, 0, 1).reshape(16384, 128)
        Yb[:, ci * 128:(ci + 1) * 128] = Y[8:16, :, ::-1].transpose(2, 0, 1).reshape(16384, 128)
    nc = _prog("evout", _build_evout)
    eoc = dict(dd=inp["s5_d"][0], gluw=inp["s5_glu_w"][0], glub=inp["s5_glu_b"][0], wout=inp["ev_w_out"][0], gate=mx[0][5], idf=_IDF, idb=_IDB)
    eoc = {kk: np.ascontiguousarray(v) for kk, v in eoc.items()}
    ro = _run(nc, [dict(eoc, xa=np.ascontiguousarray(xs[ci]), u=np.ascontiguousarray(u_x[ci * 2048:(ci + 1) * 2048]),
                        yf=np.ascontiguousarray(Yf[ci * 2048:(ci + 1) * 2048]), yb=np.ascontiguousarray(Yb[ci * 2048:(ci + 1) * 2048]),
                        attT=ra[ci]["attT"]) for ci in range(8)])
    xs = [ro[ci]["xm"] for ci in range(8)]
    _DBG["x_m0"] = xs
    xs = _ffn_launch(xs, mx[0], 6, inp["norm_g"][0, 2], inp["ffn_w_in"][0, 1], inp["ffn_w_out"][0, 1])
    _DBG["x_b0"] = xs
    xs = _ffn_launch(xs, mx[1], 0, inp["norm_g"][1, 0], inp["ffn_w_in"][1, 0], inp["ffn_w_out"][1, 0])
    _DBG["x_a1"] = xs
    xs = _hyena_mixer(inp, xs, mx[1])
    _DBG["x_m1"] = xs
    xs = _ffn_launch(xs, mx[1], 6, inp["norm_g"][1, 2], inp["ffn_w_in"][1, 1], inp["ffn_w_out"][1, 1])
    return np.concatenate(xs, axis=0)[None].astype(np.float32)
```

```python
import numpy as np
import ml_dtypes
from contextlib import ExitStack
import concourse.bass as bass
import concourse.mybir as mybir
from concourse.bass_utils import run_bass_kernel_spmd

F32 = mybir.dt.float32
BF16 = mybir.dt.bfloat16
AF = mybir.ActivationFunctionType
ALU = mybir.AluOpType
AX = mybir.AxisListType
WRITE_KEYS = ("out", "accum_out")
SEM_ROLL = 30000


class View:
    __slots__ = ("buf", "ap")

    def __init__(self, buf, ap):
        self.buf = buf
        self.ap = ap

    def __getitem__(self, idx):
        return View(self.buf, self.ap[idx])

    def rearrange(self, pat, **kw):
        return View(self.buf, self.ap.rearrange(pat, **kw))

    def bitcast(self, dt):
        return View(self.buf, self.ap.bitcast(dt))

    def unsq_bcast(self, n):
        a = self.ap
        return View(self.buf, bass.AP(a.tensor, a.offset, [list(x) for x in a.ap] + [[0, n]]))

    def pbcast(self, n):
        return View(self.buf, self.ap.partition_broadcast(n))


class Buf:
    def __init__(self, k, name, h, space):
        self.k = k
        self.name = name
        self.h = h
        self.space = space
        self.last_w = None
        self.readers = []
        self.dma_sem = None
        self.dma_cnt = 0

    def __getitem__(self, idx):
        return View(self, self.h[idx])

    def ap(self):
        return View(self, self.h.ap() if hasattr(self.h, "ap") else self.h[:])

    def bitcast_view(self, dt):
        return View(self, self.h.bitcast(dt).ap())


class Op:
    __slots__ = ("eng", "meth", "args", "kwargs", "reads", "writes", "is_dma", "tok", "has_dep", "deps", "sbuf_side", "acc")


class KB:
    def __init__(self):
        self.nc = bass.Bass("TRN2", target_bir_lowering=False)
        self.ops = []
        self.es = ExitStack()
        self.bufs = []
        self.n = 0

    def sb(self, name, shape, dt=F32):
        h = self.es.enter_context(self.nc.sbuf_tensor(name, list(shape), dt))
        b = Buf(self, name, h, "sb")
        self.bufs.append(b)
        return b

    def ps(self, name, shape, dt=F32):
        h = self.es.enter_context(self.nc.psum_tensor(name, list(shape), dt))
        b = Buf(self, name, h, "ps")
        self.bufs.append(b)
        return b

    def dram(self, name, shape, dt=F32, kind=None):
        if kind is None:
            h = self.nc.dram_tensor(name, list(shape), dt)
        else:
            h = self.nc.dram_tensor(name, list(shape), dt, kind=kind)
        b = Buf(self, name, h, "dram")
        self.bufs.append(b)
        return b

    def I(self, eng, meth, *args, reads=(), writes=(), acc=False, **kwargs):
        o = Op()
        o.eng = eng
        o.meth = meth
        o.args = args
        o.kwargs = kwargs
        rd, wr = list(reads), list(writes)
        for a in args:
            if isinstance(a, View):
                rd.append(a.buf)
        for kk, v in kwargs.items():
            if isinstance(v, View):
                (wr if kk in WRITE_KEYS else rd).append(v.buf)
        o.reads = rd
        o.writes = wr
        o.is_dma = meth == "dma_start"
        o.tok = None
        o.has_dep = False
        o.deps = None
        o.sbuf_side = None
        o.acc = acc
        if o.is_dma:
            ob, ib = kwargs["out"].buf, kwargs["in_"].buf
            o.sbuf_side = ob if ob.space != "dram" else ib
            assert o.sbuf_side.space != "dram", "dram->dram dma unsupported"
        self.ops.append(o)
        return o

    def dma(self, eng, out, in_, **kw):
        return self.I(eng, "dma_start", out=out, in_=in_, **kw)

    def mm(self, out, lhsT, rhs, start=True, stop=True, **kw):
        return self.I("tensor", "matmul", out=out, lhsT=lhsT, rhs=rhs, start=start, stop=stop, acc=not start, **kw)

    def tr(self, out, in_, ident):
        return self.I("tensor", "transpose", out=out, in_=in_, identity=ident)

    def act(self, out, in_, func, eng="scalar", **kw):
        return self.I(eng, "activation", out=out, in_=in_, func=func, **kw)

    def tt(self, eng, out, in0, in1, op):
        return self.I(eng, "tensor_tensor", out=out, in0=in0, in1=in1, op=op)

    def ts(self, eng, out, in0, s1, s2, op0, op1=None, **kw):
        if op1 is None:
            return self.I(eng, "tensor_scalar", out=out, in0=in0, scalar1=s1, scalar2=None, op0=op0, **kw)
        return self.I(eng, "tensor_scalar", out=out, in0=in0, scalar1=s1, scalar2=s2, op0=op0, op1=op1, **kw)

    def stt(self, eng, out, in0, scalar, in1, op0, op1):
        return self.I(eng, "scalar_tensor_tensor", out=out, in0=in0, scalar=scalar, in1=in1, op0=op0, op1=op1)

    def cp(self, eng, out, in_):
        if eng == "scalar":
            return self.I(eng, "copy", out=out, in_=in_)
        return self.I(eng, "tensor_copy", out=out, in_=in_)

    def memset(self, eng, out, val):
        return self.I(eng, "memset", writes=[out.buf], ap=out, constant=val)

    def emit(self, final_wait_bufs=()):
        nc = self.nc
        ops = self.ops
        for i, o in enumerate(ops):
            deps = set()
            for b in o.reads:
                if b.last_w is not None:
                    deps.add(b.last_w)
            for b in o.writes:
                if b.last_w is not None:
                    deps.add(b.last_w)
                for r in b.readers:
                    deps.add(r)
            deps.discard(i)
            fd = []
            for d in deps:
                od = ops[d]
                same = (od.eng == o.eng) and not o.is_dma and not od.is_dma
                if same:
                    raw = any((b in od.writes) for b in o.reads)
                    if o.eng == "tensor":
                        raw = False
                    if not raw:
                        continue
                fd.append(d)
            o.deps = fd
            for d in fd:
                ops[d].has_dep = True
            for b in o.writes:
                b.last_w = i
                b.readers = []
            for b in o.reads:
                if b not in o.writes:
                    b.readers.append(i)
        engs = {"tensor": nc.tensor, "vector": nc.vector, "scalar": nc.scalar, "gpsimd": nc.gpsimd, "sync": nc.sync}
        esem = {}
        ecnt = {}
        known = {e: {} for e in engs}

        def new_sem(name):
            return self.es.enter_context(nc.semaphore(name))

        nsem = [0]
        for e in engs:
            esem[e] = new_sem(f"e_{e}_0")
            ecnt[e] = 0
            nsem[0] += 1
        for i, o in enumerate(ops):
            eng = engs[o.eng]
            kn = known[o.eng]
            for d in o.deps:
                sem, val = ops[d].tok
                if kn.get(id(sem), (None, 0))[1] < val:
                    eng.wait_ge(sem, val)
                    kn[id(sem)] = (sem, val)
            args = [a.ap if isinstance(a, View) else a for a in o.args]
            kwargs = {kk: (v.ap if isinstance(v, View) else v) for kk, v in o.kwargs.items()}
            inst = getattr(eng, o.meth)(*args, **kwargs)
            if o.is_dma:
                b = o.sbuf_side
                if b.dma_sem is None:
                    b.dma_sem = new_sem(f"d_{b.name}")
                    nsem[0] += 1
                b.dma_cnt += 16
                inst.then_inc(b.dma_sem, 16)
                o.tok = (b.dma_sem, b.dma_cnt)
            elif o.has_dep:
                if ecnt[o.eng] >= SEM_ROLL:
                    esem[o.eng] = new_sem(f"e_{o.eng}_{i}")
                    ecnt[o.eng] = 0
                    nsem[0] += 1
                ecnt[o.eng] += 1
                inst.then_inc(esem[o.eng], 1)
                o.tok = (esem[o.eng], ecnt[o.eng])
        for b in self.bufs:
            if b.dma_sem is not None:
                nc.sync.wait_ge(b.dma_sem, b.dma_cnt)
        self.nsem = nsem[0]
        self.es.close()
        return nc


def bf16_np(a):
    return a.astype(ml_dtypes.bfloat16)


D = 2048
DFF = 5632
EPS = 1e-6


def alloc_common(k):
    c = {}
    c["P"] = [k.ps(f"P{i}", [128, 512], F32) for i in range(8)]
    return c


def load_ident(k, c, ident_f_d, ident_b_d):
    c["identf"] = k.sb("identf", [128, 128], F32)
    c["identb"] = k.sb("identb", [128, 128], BF16)
    c["epsc"] = k.sb("epsc", [128, 1], F32)
    k.memset("vector", c["epsc"][:, :], EPS)
    k.dma("sync", out=c["identf"][:, :], in_=ident_f_d[:, :])
    k.dma("sync", out=c["identb"][:, :], in_=ident_b_d[:, :])


def modcols_prepare(k, pfx, modc_d, normg_d, slot):
    raw = k.sb(pfx + "raw", [128, 3, 16], F32)
    ng = k.sb(pfx + "ng", [128, 16], F32)
    A = k.sb(pfx + "A", [128, 16], F32)
    G = k.sb(pfx + "G", [128, 16], F32)
    k.dma("sync", out=raw[:, :, :], in_=modc_d)
    k.dma("sync", out=ng[:, :], in_=normg_d)
    k.stt("vector", out=A[:, :], in0=raw[:, 1, :], scalar=1.0, in1=ng[:, :], op0=ALU.add, op1=ALU.mult)
    k.ts("vector", out=G[:, :], in0=raw[:, 2, :], s1=0.5, s2=None, op0=ALU.mult)
    return A, raw, G


def norm_mod_T(k, c, pfx, x_tile, tr, A, Braw, hT_view, bufs):
    junk, ss, rstd, xh = bufs["junk"], bufs["ss"], bufs["rstd"], bufs["xh"]
    P = c["P"]
    k.memset("vector", ss[:tr, :], 0.0)
    k.act(out=junk[:tr, :], in_=x_tile, func=AF.Square, accum_out=ss[:tr, :])
    k.act(out=rstd[:tr, :], in_=ss[:tr, :], func=AF.Sqrt, scale=1.0 / D, bias=c["epsc"][:tr, :])
    k.I("vector", "reciprocal", out=rstd[:tr, :], in_=rstd[:tr, :])
    k.act(out=xh[:tr, :], in_=x_tile, func=AF.Copy, scale=rstd[:tr, :])
    pb = [P[0].bitcast_view(BF16), P[1].bitcast_view(BF16)]
    for kt in range(16):
        pv = pb[kt // 8]
        k.tr(out=pv[:, (kt % 8) * 128:(kt % 8) * 128 + tr], in_=xh[:tr, kt * 128:(kt + 1) * 128], ident=c["identb"][:tr, :tr])
    for h in range(2):
        pv = pb[h].rearrange("p (k t) -> p k t", t=128)[:, :, :tr]
        a_b = A[:, h * 8:(h + 1) * 8].unsq_bcast(tr)
        b_b = Braw[:, 0, h * 8:(h + 1) * 8].unsq_bcast(tr)
        tmp = bufs["tmpT"]
        k.tt("vector", out=tmp[:, :, :tr], in0=pv, in1=a_b, op=ALU.mult)
        k.tt("gpsimd", out=hT_view[:, h * 8:(h + 1) * 8, :], in0=tmp[:, :, :tr], in1=b_b, op=ALU.add)


def ffn_stage(k, c, pfx, xin_d, xout_d, R, tr, A, Braw, G, w_in_d, w_out_d):
    P = c["P"]
    ntiles = R // tr
    bufs = c.setdefault("nm_bufs", None)
    if bufs is None:
        bufs = c["nm_bufs"] = dict(
            junk=k.sb("junk", [128, 2048], BF16), ss=k.sb("ss", [128, 1], F32), rstd=k.sb("rstd", [128, 1], F32),
            xh=k.sb("xh", [128, 2048], BF16), tmpT=k.sb("tmpT", [128, 8, 128], F32))
    if "ffn_bufs" not in c:
        c["ffn_bufs"] = dict(
            xres=k.sb("xres", [128, 4, 2048], F32), hT=k.sb("hT", [128, 16, 512], BF16), aT=k.sb("aT", [128, 44, 512], BF16),
            wg=[k.sb(f"wg{i}", [128, 16, 256], BF16) for i in range(2)],
            wo=[k.sb(f"wo{i}", [128, 44, 128], BF16) for i in range(2)],
            sg=[k.sb(f"sg{i}", [128, 512], F32) for i in range(2)],
            oTs=[k.sb(f"oTs{i}", [128, 512], F32) for i in range(2)])
    fb = c["ffn_bufs"]
    xres, hT, aT, wg, wo, sg, oTs = fb["xres"], fb["hT"], fb["aT"], fb["wg"], fb["wo"], fb["sg"], fb["oTs"]
    w_in_v = w_in_d.ap().rearrange("(kt p) c -> p kt c", p=128)
    w_out_v = w_out_d.ap().rearrange("(j p) c -> p j c", p=128)
    nblk = (ntiles + 3) // 4
    for blk in range(nblk):
        t0 = blk * 4
        nt = min(4, ntiles - t0)
        nb = nt * tr
        for t in range(nt):
            r0 = (t0 + t) * tr
            k.dma("sync", out=xres[:tr, t, :], in_=xin_d[r0:r0 + tr, :])
            norm_mod_T(k, c, pfx, xres[:tr, t, :], tr, A, Braw, hT[:, :, t * tr:(t + 1) * tr], bufs)

        def load_wg(j):
            b = wg[j % 2]
            k.dma("gpsimd", out=b[:, :, 0:128], in_=w_in_v[:, :, j * 128:(j + 1) * 128])
            k.dma("gpsimd", out=b[:, :, 128:256], in_=w_in_v[:, :, DFF + j * 128:DFF + (j + 1) * 128])

        load_wg(0)
        for j in range(44):
            if j + 1 < 44:
                load_wg(j + 1)
            b = wg[j % 2]
            gp, up = P[2 + 2 * (j % 2)], P[3 + 2 * (j % 2)]
            for kt in range(16):
                k.mm(out=gp[:, :nb], lhsT=b[:, kt, 0:128], rhs=hT[:, kt, :nb], start=(kt == 0), stop=(kt == 15))
            for kt in range(16):
                k.mm(out=up[:, :nb], lhsT=b[:, kt, 128:256], rhs=hT[:, kt, :nb], start=(kt == 0), stop=(kt == 15))
            s = sg[j % 2]
            k.act(out=s[:, :nb], in_=gp[:, :nb], func=AF.Silu)
            k.tt("vector", out=aT[:, j, :nb], in0=s[:, :nb], in1=up[:, :nb], op=ALU.mult)

        def load_wo(m):
            k.dma("gpsimd", out=wo[m % 2][:, :, :], in_=w_out_v[:, :, m * 128:(m + 1) * 128])

        load_wo(0)
        for m in range(16):
            if m + 1 < 16:
                load_wo(m + 1)
            b = wo[m % 2]
            op_ = P[6 + (m % 2)]
            for j in range(44):
                k.mm(out=op_[:, :nb], lhsT=b[:, j, :], rhs=aT[:, j, :nb], start=(j == 0), stop=(j == 43))
            o = oTs[m % 2]
            k.act(out=o[:, :nb], in_=op_[:, :nb], func=AF.Copy, scale=G[:, m:m + 1])
            tb = P[m % 2]
            for t in range(nt):
                k.tr(out=tb[:tr, t * 128:(t + 1) * 128], in_=o[:, t * tr:(t + 1) * tr], ident=c["identf"][:, :])
            k.tt("vector", out=xres[:tr, :nt, m * 128:(m + 1) * 128], in0=xres[:tr, :nt, m * 128:(m + 1) * 128],
                 in1=tb[:tr, :nt * 128].rearrange("p (t f) -> p t f", f=128), op=ALU.add)
        for t in range(nt):
            r0 = (t0 + t) * tr
            k.dma("sync", out=xout_d[r0:r0 + tr, :], in_=xres[:tr, t, :])


def ada_stage(k, c, cc_d, w_d, b_d, out_d):
    P = c["P"]
    cc = k.sb("ada_cc", [128, 16, 2], F32)
    sc = k.sb("ada_sc", [128, 16, 2], F32)
    k.dma("sync", out=cc[:, :, :], in_=cc_d[:, :, :])
    k.act(out=sc[:, :, :], in_=cc[:, :, :], func=AF.Silu)
    wb = [k.sb(f"ada_w{i}", [128, 16, 512], F32) for i in range(2)]
    bb = k.sb("ada_b", [2, 2, 2304], F32)
    ob = k.sb("ada_o", [2, 2, 2304], F32)
    for l in range(2):
        k.dma("sync", out=bb[:, l, :], in_=b_d[l:l + 1, :].pbcast(2) if False else b_d[l, :].pbcast(2))
    blocks = [(0, 512), (512, 512), (1024, 512), (1536, 512), (2048, 256)]
    it = 0
    for l in range(2):
        wv = w_d[l].rearrange("(kt p) c -> p kt c", p=128)
        for (c0, cw) in blocks:
            w = wb[it % 2]
            k.dma("sync" if it % 2 == 0 else "gpsimd", out=w[:, :, :cw], in_=wv[:, :, c0:c0 + cw])
            ps = P[it % 2]
            for kt in range(16):
                k.mm(out=ps[:2, :cw], lhsT=sc[:, kt, :], rhs=w[:, kt, :cw], start=(kt == 0), stop=(kt == 15))
            k.tt("vector", out=ob[:, l, c0:c0 + cw], in0=ps[:2, :cw], in1=bb[:, l, c0:c0 + cw], op=ALU.add)
            it += 1
        k.dma("sync", out=out_d[l], in_=ob[:, l, :])


def rstd_from_ss(k, c, out, ss, dim):
    k.act(out=out, in_=ss, func=AF.Sqrt, scale=1.0 / dim, bias=c["epsc"][:out.ap.shape[0], :])
    k.I("vector", "reciprocal", out=out, in_=out)


def rope_apply(k, eng, dst, src, cos, sin, tmp, tr, nh):
    def tv(i):
        return tmp[:tr, i, :nh * 32].rearrange("p (h a f) -> p h a f", a=2, f=16)
    x1, x2 = src[:, :, :, 0, :], src[:, :, :, 1, :]
    k.tt(eng, out=tv(0), in0=x1, in1=cos, op=ALU.mult)
    k.tt(eng, out=tv(1), in0=x2, in1=sin, op=ALU.mult)
    k.tt(eng, out=dst[:, :, :, 0, :], in0=tv(0), in1=tv(1), op=ALU.subtract)
    k.tt(eng, out=tv(2), in0=x2, in1=cos, op=ALU.mult)
    k.tt(eng, out=tv(3), in0=x1, in1=sin, op=ALU.mult)
    k.tt(eng, out=dst[:, :, :, 1, :], in0=tv(2), in1=tv(3), op=ALU.add)


def evpre_stage(k, c, xin_d, R, tr, A, Braw, w_in_d, wuq_d, wukv_d, gqa_d, gkva_d, gq_d, gk_d, cos_d, sin_d,
                qT_d, kT_d, v_d, u_d, want_q=True, stop=9):
    P = c["P"]
    if c.get("nm_bufs") is None:
        c["nm_bufs"] = dict(
            junk=k.sb("junk", [128, 2048], BF16), ss=k.sb("ss", [128, 1], F32), rstd=k.sb("rstd", [128, 1], F32),
            xh=k.sb("xh", [128, 2048], BF16), tmpT=k.sb("tmpT", [128, 8, 128], F32))
    bufs = c["nm_bufs"]
    win = k.sb("ev_win", [128, 16, 1856], BF16)
    wuq = k.sb("ev_wuq", [128, 4, 1536], BF16)
    wukv = k.sb("ev_wukv", [128, 2, 2048], BF16)
    k.dma("gpsimd", out=win[:, :, :], in_=w_in_d.ap().rearrange("(kt p) c -> p kt c", p=128))
    k.dma("gpsimd", out=wuq[:, :, :], in_=wuq_d.ap().rearrange("(kt p) c -> p kt c", p=128))
    k.dma("gpsimd", out=wukv[:, :, :], in_=wukv_d.ap().rearrange("(kt p) c -> p kt c", p=128))
    gqa = k.sb("ev_gqa", [128, 4], F32)
    gkva = k.sb("ev_gkva", [128, 2], F32)
    gq = k.sb("ev_gq", [128, 192], F32)
    gk = k.sb("ev_gk", [128, 192], F32)
    k.dma("sync", out=gqa[:, :], in_=gqa_d[:, :])
    k.dma("sync", out=gkva[:, :], in_=gkva_d[:, :])
    k.dma("sync", out=gq[:, :], in_=gq_d[:].pbcast(128))
    k.dma("sync", out=gk[:, :], in_=gk_d[:].pbcast(128))
    k.ts("vector", out=gq[:, :], in0=gq[:, :], s1=float(192 ** -0.5), s2=None, op0=ALU.mult)
    xt = [k.sb(f"ev_x{i}", [128, 2048], F32) for i in range(2)]
    hT = k.sb("ev_hT", [128, 16, 128], BF16)
    zs = k.sb("ev_zs", [128, 1856], F32)
    qan = k.sb("ev_qan", [128, 768], BF16)
    qanT = k.sb("ev_qanT", [128, 6, 128], BF16)
    sq = k.sb("ev_sq", [128, 1536], F32)
    ss8 = k.sb("ev_ss8", [128, 8], F32)
    rs8 = k.sb("ev_rs8", [128, 8], F32)
    ss1 = k.sb("ev_ss1", [128, 1], F32)
    rs1 = k.sb("ev_rs1", [128, 1], F32)
    t1 = k.sb("ev_t1", [128, 8, 192], F32)
    t2 = k.sb("ev_t2", [128, 8, 192], F32)
    qf = k.sb("ev_qf", [128, 8, 192], BF16)
    kf = k.sb("ev_kf", [128, 8, 192], BF16)
    vt = k.sb("ev_vt", [128, 8, 128], BF16)
    kpe = k.sb("ev_kpe", [128, 64], F32)
    kpr = k.sb("ev_kpr", [128, 64], F32)
    rtmp = k.sb("ev_rtmp", [128, 4, 256], F32)
    cs = k.sb("ev_cos", [128, 32], F32)
    sn = k.sb("ev_sin", [128, 32], F32)
    oT = [k.sb(f"ev_oT{i}", [128, 8, 2, 128], BF16) for i in range(2)]
    pT = P[7].bitcast_view(BF16)
    idb = c["identb"]

    def rope_views(tile, nh):
        return tile[:tr, :nh, 128:192].rearrange("p h (a b f) -> p h a b f", a=2, b=2)

    def bc_heads(tab, nh):
        a = tab[:tr, :].ap
        return View(tab, bass.AP(a.tensor, a.offset, [list(a.ap[0]), [0, nh], [16, 2], [1, 16]]))

    def heads_T(src, dst_d, it):
        o = oT[it % 2]
        for half in range(2):
            for h4 in range(4):
                h = half * 4 + h4
                k.tr(out=pT[:, h4 * 256:h4 * 256 + tr], in_=src[:tr, h, 0:128], ident=idb[:tr, :tr])
                k.tr(out=pT[:64, h4 * 256 + 128:h4 * 256 + 128 + tr], in_=src[:tr, h, 128:192], ident=idb[:tr, :tr])
            pv = pT.rearrange("p (h a t) -> p h a t", a=2, t=128)
            k.cp("vector", out=o[:, half * 4:half * 4 + 4, 0, :tr], in_=pv[:, :, 0, :tr])
            k.cp("vector", out=o[:64, half * 4:half * 4 + 4, 1, :tr], in_=pv[:64, :, 1, :tr])
        return o

    ntiles = R // tr
    for t in range(ntiles):
        r0 = t * tr
        x = xt[t % 2]
        k.dma("sync", out=x[:tr, :], in_=xin_d[r0:r0 + tr, :])
        k.dma("sync", out=cs[:tr, :], in_=cos_d[r0:r0 + tr, :])
        k.dma("sync", out=sn[:tr, :], in_=sin_d[r0:r0 + tr, :])
        norm_mod_T(k, c, "ev", x[:tr, :], tr, A, Braw, hT[:, :, :tr], bufs)
        zb = [(0, 512), (512, 512), (1024, 512), (1536, 320)]
        for bi, (c0, cw) in enumerate(zb):
            for kt in range(16):
                k.mm(out=P[bi][:tr, :cw], lhsT=hT[:, kt, :tr], rhs=win[:, kt, c0:c0 + cw], start=(kt == 0), stop=(kt == 15))
            k.cp("scalar", out=zs[:tr, c0:c0 + cw], in_=P[bi][:tr, :cw])
        k.dma("sync", out=u_d[r0:r0 + tr, :], in_=zs[:tr, 832:1856])
        if stop <= 1:
            continue
        for (c0, cw, dim) in ((0, 512, 512), (512, 256, 256)):
            k.memset("vector", ss1[:tr, :], 0.0)
            k.act(out=bufs["junk"][:tr, :cw], in_=zs[:tr, c0:c0 + cw], func=AF.Square, accum_out=ss1[:tr, :])
            rstd_from_ss(k, c, rs1[:tr, :], ss1[:tr, :], dim)
            k.act(out=qan[:tr, c0:c0 + cw], in_=zs[:tr, c0:c0 + cw], func=AF.Copy, scale=rs1[:tr, :])
        for kt in range(6):
            k.tr(out=pT[:, kt * 128:kt * 128 + tr], in_=qan[:tr, kt * 128:(kt + 1) * 128], ident=idb[:tr, :tr])
        pv6 = pT[:, :768].rearrange("p (k t) -> p k t", t=128)[:, :, :tr]
        k.tt("vector", out=qanT[:, 0:4, :tr], in0=pv6[:, 0:4, :], in1=gqa[:, :].unsq_bcast(tr), op=ALU.mult)
        k.tt("vector", out=qanT[:, 4:6, :tr], in0=pv6[:, 4:6, :], in1=gkva[:, :].unsq_bcast(tr), op=ALU.mult)
        if stop <= 2:
            continue
        for bi in range(4):
            for kt in range(2):
                k.mm(out=P[bi][:tr, :], lhsT=qanT[:, 4 + kt, :tr], rhs=wukv[:, kt, bi * 512:(bi + 1) * 512], start=(kt == 0), stop=(kt == 1))
        for bi in range(4):
            kvv = P[bi][:tr, :].rearrange("p (h d) -> p h d", d=256)
            if stop != 34:
                k.cp("vector", out=vt[:tr, bi * 2:bi * 2 + 2, :], in_=kvv[:, :, 128:256])
            k.cp("vector", out=t1[:tr, bi * 2:bi * 2 + 2, 0:128], in_=kvv[:, :, 0:128])
        if stop != 33:
            k.dma("sync", out=v_d[r0:r0 + tr, :], in_=vt[:tr, :, :].rearrange("p h d -> p (h d)"))
        if stop <= 3 or stop in (33, 34):
            continue
        k.tt("gpsimd", out=kpe[:tr, :], in0=zs[:tr, 768:832], in1=gk[:tr, 128:192], op=ALU.mult)
        kp5 = kpe[:tr, :].rearrange("p (h a b f) -> p h a b f", h=1, a=2, b=2)
        kr5 = kpr[:tr, :].rearrange("p (h a b f) -> p h a b f", h=1, a=2, b=2)
        rope_apply(k, "gpsimd", kr5, kp5, bc_heads(cs, 1), bc_heads(sn, 1), rtmp, tr, 1)
        if stop <= 4:
            continue
        k.act(out=sq[:tr, :1024].rearrange("p (h d) -> p h d", d=128), in_=t1[:tr, :, 0:128], func=AF.Square)
        k.I("vector", "tensor_reduce", out=ss8[:tr, :], in_=sq[:tr, :1024].rearrange("p (h d) -> p h d", d=128), axis=AX.X, op=ALU.add)
        k.memset("vector", ss1[:tr, :], 0.0)
        k.act(out=bufs["junk"][:tr, :64], in_=zs[:tr, 768:832], func=AF.Square, accum_out=ss1[:tr, :])
        k.ts("vector", out=ss8[:tr, :], in0=ss8[:tr, :], s1=ss1[:tr, :], s2=None, op0=ALU.add)
        rstd_from_ss(k, c, rs8[:tr, :], ss8[:tr, :], 192)
        k.tt("vector", out=t2[:tr, :, 0:128], in0=t1[:tr, :, 0:128], in1=rs8[:tr, :].unsq_bcast(128), op=ALU.mult)
        gkn = View(gk, bass.AP(gk[:tr, 0:128].ap.tensor, gk[:tr, 0:128].ap.offset, [list(gk[:tr, 0:128].ap.ap[0]), [0, 8], [1, 128]]))
        k.tt("gpsimd", out=kf[:tr, :, 0:128], in0=t2[:tr, :, 0:128], in1=gkn, op=ALU.mult)
        kprb = View(kpr, bass.AP(kpr[:tr, :].ap.tensor, kpr[:tr, :].ap.offset, [list(kpr[:tr, :].ap.ap[0]), [0, 8], [1, 64]]))
        k.tt("vector", out=kf[:tr, :, 128:192], in0=kprb, in1=rs8[:tr, :].unsq_bcast(64), op=ALU.mult)
        if stop <= 5:
            continue
        o = heads_T(kf, kT_d, 2 * t)
        k.dma("sync", out=kT_d[:, 0:128, r0:r0 + tr].rearrange("h p t -> p h t"), in_=o[:, :, 0, :tr])
        k.dma("sync", out=kT_d[:, 128:192, r0:r0 + tr].rearrange("h p t -> p h t"), in_=o[:64, :, 1, :tr])
        if want_q and stop > 6:
            for bi in range(3):
                for kt in range(4):
                    k.mm(out=P[4 + bi][:tr, :], lhsT=qanT[:, kt, :tr], rhs=wuq[:, kt, bi * 512:(bi + 1) * 512], start=(kt == 0), stop=(kt == 3))
                k.cp("scalar", out=t1[:tr, :, :].rearrange("p h d -> p (h d)")[:, bi * 512:(bi + 1) * 512], in_=P[4 + bi][:tr, :])
            k.act(out=sq[:tr, :], in_=t1[:tr, :, :].rearrange("p h d -> p (h d)"), func=AF.Square)
            k.I("vector", "tensor_reduce", out=ss8[:tr, :], in_=sq[:tr, :].rearrange("p (h d) -> p h d", d=192), axis=AX.X, op=ALU.add)
            rstd_from_ss(k, c, rs8[:tr, :], ss8[:tr, :], 192)
            k.tt("vector", out=t2[:tr, :, :], in0=t1[:tr, :, :], in1=rs8[:tr, :].unsq_bcast(192), op=ALU.mult)
            gqn = View(gq, bass.AP(gq[:tr, :].ap.tensor, gq[:tr, :].ap.offset, [list(gq[:tr, :].ap.ap[0]), [0, 8], [1, 192]]))
            k.tt("gpsimd", out=t1[:tr, :, :], in0=t2[:tr, :, :], in1=gqn, op=ALU.mult)
            k.cp("scalar", out=qf[:tr, :, 0:128], in_=t1[:tr, :, 0:128])
            rope_apply(k, "vector", rope_views(qf, 8), rope_views(t1, 8), bc_heads(cs, 8), bc_heads(sn, 8), rtmp, tr, 8)
            o = heads_T(qf, qT_d, 2 * t + 1)
            k.dma("sync", out=qT_d[:, 0:128, r0:r0 + tr].rearrange("h p t -> p h t"), in_=o[:, :, 0, :tr])
            k.dma("sync", out=qT_d[:, 128:192, r0:r0 + tr].rearrange("h p t -> p h t"), in_=o[:64, :, 1, :tr])


SEG = 256
NSEG = 65
PI = 3.141592653589793


def sincos(k, dst, src, shift, w1, w2, kint):
    k.ts("vector", out=w1, in0=src, s1=shift + 8 * PI, s2=1.0 / (2 * PI), op0=ALU.add, op1=ALU.mult)
    k.cp("vector", out=kint, in_=w1)
    k.cp("vector", out=w1, in_=kint)
    k.ts("vector", out=w2, in0=src, s1=shift + 8 * PI, s2=None, op0=ALU.add)
    k.stt("vector", out=w1, in0=w1, scalar=-2 * PI, in1=w2, op0=ALU.mult, op1=ALU.add)
    k.ts("vector", out=w2, in0=w1, s1=PI, s2=2 * PI, op0=ALU.is_gt, op1=ALU.mult)
    k.tt("vector", out=w1, in0=w1, in1=w2, op=ALU.subtract)
    k.ts("vector", out=w1, in0=w1, s1=-PI, s2=PI, op0=ALU.max, op1=ALU.min)
    k.act(out=dst, in_=w1, func=AF.Sin)


def s5_stage(k, c, U_d, lamre_d, lamim_d, logdt_d, BTre_d, BTim_d, CTre_d, CTim_d, jtab_d, Y_d, nseg=NSEG):
    P = c["P"]
    S = SEG
    sb = lambda n, s: k.sb("s5_" + n, s, F32)
    lre, lim, ldt = sb("lre", [64, 16]), sb("lim", [64, 16]), sb("ldt", [64, 16])
    BTre, BTim = sb("BTre", [16, 16, 64]), sb("BTim", [16, 16, 64])
    CTre, CTim = sb("CTre", [64, 16, 16]), sb("CTim", [64, 16, 16])
    jt = sb("jt", [64, S + 1])
    for (b, d_) in ((lre, lamre_d), (lim, lamim_d)):
        k.dma("sync", out=b[:, :], in_=d_[:, :])
    k.dma("sync", out=ldt[:, :], in_=logdt_d[:].pbcast(64))
    k.dma("sync", out=BTre[:, :, :], in_=BTre_d[:, :, :])
    k.dma("sync", out=BTim[:, :, :], in_=BTim_d[:, :, :])
    k.dma("sync", out=CTre[:, :, :], in_=CTre_d[:, :, :])
    k.dma("sync", out=CTim[:, :, :], in_=CTim_d[:, :, :])
    k.dma("sync", out=jt[:, :], in_=jtab_d[:, :])
    negpi = sb("negpi", [64, 1])
    k.memset("vector", negpi[:, :], -PI)
    dt, th, mag = sb("dt", [64, 16]), sb("th", [64, 16]), sb("mag", [64, 16])
    k.act(out=dt[:, :], in_=ldt[:, :], func=AF.Exp)
    k.tt("vector", out=th[:, :], in0=lim[:, :], in1=dt[:, :], op=ALU.mult)
    k.tt("vector", out=mag[:, :], in0=lre[:, :], in1=dt[:, :], op=ALU.mult)
    k.act(out=mag[:, :], in_=mag[:, :], func=AF.Exp)
    g = sb("g", [64, 2, 16, S])
    h = sb("h", [64, 2, 16, S])
    ang = g[:, 0, :, :]
    cosT, sinT = sb("cosT", [64, 16, S]), sb("sinT", [64, 16, S])
    tA = sb("tA", [64, 16, S])
    ki = tA.bitcast_view(mybir.dt.int32)
    jv = jt[:, 0:S].ap
    jb = View(jt, bass.AP(jv.tensor, jv.offset, [list(jv.ap[0]), [0, 16], [1, S]]))
    k.tt("vector", out=ang, in0=jb, in1=th[:, :].unsq_bcast(S), op=ALU.mult)

    sincos(k, sinT[:, :, :], ang, 0.0, h[:, 0, :, :], h[:, 1, :, :], ki[:, :, :])
    sincos(k, cosT[:, :, :], ang, PI / 2, h[:, 0, :, :], h[:, 1, :, :], ki[:, :, :])
    psc, pss, pa = sb("psc", [64, 16]), sb("pss", [64, 16]), sb("pa", [64, 16])
    pw1, pw2 = sb("pw1", [64, 16]), sb("pw2", [64, 16])
    k.ts("vector", out=pa[:, :], in0=th[:, :], s1=float(S), s2=None, op0=ALU.mult)
    sincos(k, pss[:, :], pa[:, :], 0.0, pw1[:, :], pw2[:, :], ki[:, :, 0])
    sincos(k, psc[:, :], pa[:, :], PI / 2, pw1[:, :], pw2[:, :], ki[:, :, 0])
    are, aim = sb("are", [64, 16]), sb("aim", [64, 16])
    k.tt("vector", out=are[:, :], in0=mag[:, :], in1=cosT[:, :, 1], op=ALU.mult)
    k.tt("vector", out=aim[:, :], in0=mag[:, :], in1=sinT[:, :, 1], op=ALU.mult)
    den, w1, w2 = sb("den", [64, 16]), sb("w1", [64, 16]), sb("w2", [64, 16])
    kre, kim, nr = sb("kre", [64, 16]), sb("kim", [64, 16]), sb("nr", [64, 16])
    k.tt("vector", out=w1[:, :], in0=lre[:, :], in1=lre[:, :], op=ALU.mult)
    k.tt("vector", out=w2[:, :], in0=lim[:, :], in1=lim[:, :], op=ALU.mult)
    k.tt("vector", out=den[:, :], in0=w1[:, :], in1=w2[:, :], op=ALU.add)
    k.I("vector", "reciprocal", out=den[:, :], in_=den[:, :])
    k.ts("vector", out=nr[:, :], in0=are[:, :], s1=-1.0, s2=None, op0=ALU.add)
    k.tt("vector", out=w1[:, :], in0=nr[:, :], in1=lre[:, :], op=ALU.mult)
    k.tt("vector", out=w2[:, :], in0=aim[:, :], in1=lim[:, :], op=ALU.mult)
    k.tt("vector", out=kre[:, :], in0=w1[:, :], in1=w2[:, :], op=ALU.add)
    k.tt("vector", out=kre[:, :], in0=kre[:, :], in1=den[:, :], op=ALU.mult)
    k.tt("vector", out=w1[:, :], in0=aim[:, :], in1=lre[:, :], op=ALU.mult)
    k.tt("vector", out=w2[:, :], in0=nr[:, :], in1=lim[:, :], op=ALU.mult)
    k.tt("vector", out=kim[:, :], in0=w1[:, :], in1=w2[:, :], op=ALU.subtract)
    k.tt("vector", out=kim[:, :], in0=kim[:, :], in1=den[:, :], op=ALU.mult)
    PhR, PhI, magT = sb("PhR", [64, 16, S]), sb("PhI", [64, 16, S]), sb("magT", [64, 16, S])
    tB = h[:, 0, :, :]
    cS, sS = cosT[:, :, :], sinT[:, :, :]
    k.tt("vector", out=tA[:, :, :], in0=cS, in1=kre[:, :].unsq_bcast(S), op=ALU.mult)
    k.tt("gpsimd", out=tB, in0=sS, in1=kim[:, :].unsq_bcast(S), op=ALU.mult)
    k.tt("vector", out=PhR[:, :, :], in0=tA[:, :, :], in1=tB, op=ALU.add)
    k.tt("vector", out=tA[:, :, :], in0=cS, in1=kim[:, :].unsq_bcast(S), op=ALU.mult)
    k.tt("gpsimd", out=tB, in0=sS, in1=kre[:, :].unsq_bcast(S), op=ALU.mult)
    k.tt("vector", out=PhI[:, :, :], in0=tA[:, :, :], in1=tB, op=ALU.subtract)
    k.memset("vector", magT[:, :, :], 1.0)
    k.tt("vector", out=magT[:, :, :], in0=magT[:, :, :], in1=mag[:, :].unsq_bcast(S), op=ALU.mult)
    nCTim = sb("nCTim", [64, 16, 16])
    k.ts("vector", out=nCTim[:, :, :], in0=CTim[:, :, :], s1=-1.0, s2=None, op0=ALU.mult)
    car = sb("car", [64, 2, 16])
    k.memset("vector", car[:, :, :], 0.0)
    m = [sb(f"m{i}", [64, 2, S]) for i in range(2)]
    t4 = [sb(f"t4_{i}", [64, 4, S]) for i in range(2)]
    Ub = [sb("U0", [16, 16, S])] * 2
    Yb = [sb("Y0", [16, 4, S])] * 2
    cw = sb("cw", [64, 6, 16])
    Yv = Y_d.ap().rearrange("l c t -> c l t")
    for sg in range(nseg):
        U = Ub[sg % 2]
        k.dma("sync", out=U[:, :, :], in_=U_d[:, :, sg * S:(sg + 1) * S])
        def modulate_lane(l):
            ps = P[l % 4]
            k.mm(out=ps[:64, 0:S], lhsT=BTre[:, l, :], rhs=U[:, l, :])
            k.mm(out=ps[:64, S:2 * S], lhsT=BTim[:, l, :], rhs=U[:, l, :])
            bre, bim = ps[:64, 0:S], ps[:64, S:2 * S]
            tt_ = t4[l % 2]
            mm_ = m[l % 2]
            k.tt("vector", out=tt_[:, 0, :], in0=bre, in1=PhR[:, l, :], op=ALU.mult)
            k.tt("vector", out=tt_[:, 1, :], in0=bim, in1=PhI[:, l, :], op=ALU.mult)
            k.tt("vector", out=tt_[:, 2, :], in0=bre, in1=PhI[:, l, :], op=ALU.mult)
            k.tt("vector", out=tt_[:, 3, :], in0=bim, in1=PhR[:, l, :], op=ALU.mult)
            k.tt("gpsimd", out=mm_[:, 0, :], in0=tt_[:, 0, :], in1=tt_[:, 1, :], op=ALU.subtract)
            k.tt("gpsimd", out=mm_[:, 1, :], in0=tt_[:, 2, :], in1=tt_[:, 3, :], op=ALU.add)

        modulate_lane(0)
        for l in range(16):
            if l + 1 < 16:
                modulate_lane(l + 1)
            mm_ = m[l % 2]
            for ri in range(2):
                k.I("vector", "tensor_tensor_scan", out=g[:, ri, l, :], data0=magT[:, l, :], data1=mm_[:, ri, :],
                    initial=car[:, ri, l:l + 1], op0=ALU.mult, op1=ALU.add)
        gl_re, gl_im = g[:, 0, :, S - 1], g[:, 1, :, S - 1]
        pc, psn = psc[:, :], pss[:, :]
        k.tt("vector", out=cw[:, 0, :], in0=gl_re, in1=pc, op=ALU.mult)
        k.tt("vector", out=cw[:, 1, :], in0=gl_im, in1=psn, op=ALU.mult)
        k.tt("vector", out=cw[:, 2, :], in0=gl_re, in1=psn, op=ALU.mult)
        k.tt("vector", out=cw[:, 3, :], in0=gl_im, in1=pc, op=ALU.mult)
        k.tt("vector", out=car[:, 0, :], in0=cw[:, 0, :], in1=cw[:, 1, :], op=ALU.subtract)
        k.tt("vector", out=car[:, 1, :], in0=cw[:, 2, :], in1=cw[:, 3, :], op=ALU.add)
        if sg == 0:
            continue
        k.tt("gpsimd", out=h[:, 0, :, :], in0=g[:, 0, :, :], in1=cS, op=ALU.mult)
        k.tt("vector", out=tA[:, :, :], in0=g[:, 1, :, :], in1=sS, op=ALU.mult)
        k.tt("vector", out=h[:, 0, :, :], in0=h[:, 0, :, :], in1=tA[:, :, :], op=ALU.subtract)
        k.tt("gpsimd", out=h[:, 1, :, :], in0=g[:, 0, :, :], in1=sS, op=ALU.mult)
        k.tt("vector", out=tA[:, :, :], in0=g[:, 1, :, :], in1=cS, op=ALU.mult)
        k.tt("vector", out=h[:, 1, :, :], in0=h[:, 1, :, :], in1=tA[:, :, :], op=ALU.add)
        Y = Yb[sg % 2]
        for l in range(16):
            ps = P[4 + (l // 2) % 4]
            o = ps[:16, (l % 2) * S:(l % 2 + 1) * S]
            k.mm(out=o, lhsT=CTre[:, l, :], rhs=h[:, 0, l, :], start=True, stop=False)
            k.mm(out=o, lhsT=nCTim[:, l, :], rhs=h[:, 1, l, :], start=False, stop=True)
            if l % 2 == 1:
                l8 = (l - 1) % 4
                k.cp("scalar", out=Y[:, l8:l8 + 2, :].rearrange("p l t -> p (l t)"), in_=ps[:16, :])
            if l % 4 == 3:
                k.dma("sync", out=Yv[:, l - 3:l + 1, (sg - 1) * S:sg * S], in_=Y[:, :, :])


def attn_stage(k, c, qT_d, kT_d, v_d, attT_d, R=2048, NK=16640, nheads=8):
    P = c["P"]
    nkt = NK // 128
    ka = k.sb("at_ka", [128, NK], BF16)
    kb = k.sb("at_kb", [64, NK], BF16)
    vv = k.sb("at_v", [128, nkt, 128], BF16)
    qa = k.sb("at_qa", [128, R], BF16)
    qb = k.sb("at_qb", [64, R], BF16)
    ones = k.sb("at_ones", [128, 128], BF16)
    k.memset("vector", ones[:, :], 1.0)
    PT = [k.sb(f"at_PT{i}", [128, 512], BF16) for i in range(3)]
    rden = k.sb("at_rden", [128, 512], F32)
    oT = [k.sb(f"at_oT{i}", [128, 512], BF16) for i in range(2)]
    it = 0
    for h in range(nheads):
        k.dma("sync", out=ka[:, :], in_=kT_d[h, 0:128, :])
        k.dma("sync", out=kb[:, :], in_=kT_d[h, 128:192, :])
        vsrc = v_d[:, h * 128:(h + 1) * 128].rearrange("(t p) e -> p t e", p=128)
        hk = nkt // 2
        k.dma("gpsimd", out=vv[:, :hk, :], in_=vsrc[:, :hk, :])
        k.dma("gpsimd", out=vv[:, hk:, :], in_=vsrc[:, hk:, :])
        k.dma("sync", out=qa[:, :], in_=qT_d[h, 0:128, :])
        k.dma("sync", out=qb[:, :], in_=qT_d[h, 128:192, :])
        for qb_i in range(R // 512):
            qs = slice(qb_i * 512, (qb_i + 1) * 512)
            Ops, Dps = P[4 + (qb_i % 2)], P[6 + (qb_i % 2)]
            def scores(kt_, it_):
                S_ = P[it_ % 3]
                ks = slice(kt_ * 128, (kt_ + 1) * 128)
                k.mm(out=S_[:, :], lhsT=ka[:, ks], rhs=qa[:, qs], start=True, stop=False)
                k.mm(out=S_[:, :], lhsT=kb[:, ks], rhs=qb[:, qs], start=False, stop=True)

            scores(0, it)
            for kt in range(nkt):
                S = P[it % 3]
                pt = PT[it % 3]
                if kt + 1 < nkt:
                    scores(kt + 1, it + 1)
                k.act(out=pt[:, :], in_=S[:, :], func=AF.Exp)
                k.mm(out=Ops[:, :], lhsT=vv[:, kt, :], rhs=pt[:, :], start=(kt == 0), stop=(kt == nkt - 1))
                k.mm(out=Dps[:, :], lhsT=ones[:, :], rhs=pt[:, :], start=(kt == 0), stop=(kt == nkt - 1))
                it += 1
            k.I("vector", "reciprocal", out=rden[:, :], in_=Dps[:, :])
            o = oT[qb_i % 2]
            k.tt("vector", out=o[:, :], in0=Ops[:, :], in1=rden[:, :], op=ALU.mult)
            k.dma("sync", out=attT_d[h * 128:(h + 1) * 128, qs], in_=o[:, :])


def evout_stage(k, c, xa_d, u_d, yf_d, yb_d, attT_d, d_d, gluw_d, glub_d, wout_d, gate_d, xm_d, R=2048):
    P = c["P"]
    idb = c["identb"]
    wout = k.sb("eo_wout", [128, 16, 2048], BF16)
    gluw = k.sb("eo_gluw", [128, 8, 1024], BF16)
    k.dma("gpsimd", out=wout[:, :, :], in_=wout_d.ap().rearrange("(kt p) c -> p kt c", p=128))
    k.dma("gpsimd", out=gluw[:, :, :], in_=gluw_d.ap().rearrange("(kt p) c -> p kt c", p=128))
    drow, brow, grow = k.sb("eo_drow", [128, 1024], F32), k.sb("eo_brow", [128, 1024], F32), k.sb("eo_grow", [128, 2048], F32)
    k.dma("sync", out=drow[:, :], in_=d_d[:].pbcast(128))
    k.dma("sync", out=brow[:, :], in_=glub_d[:].pbcast(128))
    k.dma("sync", out=grow[:, :], in_=gate_d[:].pbcast(128))
    ut, yft, ybt = k.sb("eo_u", [128, 1024], F32), k.sb("eo_yf", [128, 1024], F32), k.sb("eo_yb", [128, 1024], F32)
    y, w1, ge = k.sb("eo_y", [128, 1024], F32), k.sb("eo_w1", [128, 1024], F32), k.sb("eo_ge", [128, 1024], F32)
    geb, ssb = k.sb("eo_geb", [128, 1024], BF16), k.sb("eo_ssb", [128, 1024], BF16)
    geT, ssT = k.sb("eo_geT", [128, 8, 128], BF16), k.sb("eo_ssT", [128, 8, 128], BF16)
    att = k.sb("eo_att", [128, 8, 128], BF16)
    xt = k.sb("eo_x", [128, 2048], F32)
    pT = P[7].bitcast_view(BF16)
    attv = attT_d.ap().rearrange("(kt p) t -> p kt t", p=128)
    for t in range(R // 128):
        rs = slice(t * 128, (t + 1) * 128)
        k.dma("sync", out=ut[:, :], in_=u_d[rs, :])
        k.dma("sync", out=yft[:, :], in_=yf_d[rs, :])
        k.dma("sync", out=ybt[:, :], in_=yb_d[rs, :])
        k.dma("sync", out=xt[:, :], in_=xa_d[rs, :])
        k.dma("sync", out=att[:, :, :], in_=attv[:, :, rs])
        k.tt("vector", out=y[:, :], in0=ut[:, :], in1=drow[:, :], op=ALU.mult)
        k.tt("gpsimd", out=w1[:, :], in0=yft[:, :], in1=ybt[:, :], op=ALU.add)
        k.tt("vector", out=y[:, :], in0=y[:, :], in1=w1[:, :], op=ALU.add)
        k.tt("gpsimd", out=w1[:, :], in0=y[:, :], in1=y[:, :], op=ALU.mult)
        k.ts("vector", out=w1[:, :], in0=w1[:, :], s1=0.044715, s2=1.0, op0=ALU.mult, op1=ALU.add)
        k.tt("vector", out=w1[:, :], in0=w1[:, :], in1=y[:, :], op=ALU.mult)
        k.act(out=w1[:, :], in_=w1[:, :], func=AF.Sigmoid, scale=1.5957691216057308)
        k.tt("vector", out=ge[:, :], in0=y[:, :], in1=w1[:, :], op=ALU.mult)
        k.cp("gpsimd", out=geb[:, :], in_=ge[:, :])
        for kt in range(8):
            k.tr(out=pT[:, kt * 128:(kt + 1) * 128], in_=geb[:, kt * 128:(kt + 1) * 128], ident=idb[:, :])
        k.cp("vector", out=geT[:, :, :].rearrange("p k t -> p (k t)"), in_=pT[:, :])
        for bi in range(2):
            for kt in range(8):
                k.mm(out=P[bi][:, :], lhsT=geT[:, kt, :], rhs=gluw[:, kt, bi * 512:(bi + 1) * 512], start=(kt == 0), stop=(kt == 7))
            k.tt("vector", out=w1[:, bi * 512:(bi + 1) * 512], in0=P[bi][:, :], in1=brow[:, bi * 512:(bi + 1) * 512], op=ALU.add)
        k.act(out=w1[:, :], in_=w1[:, :], func=AF.Sigmoid)
        k.tt("vector", out=ssb[:, :], in0=ge[:, :], in1=w1[:, :], op=ALU.mult)
        for kt in range(8):
            k.tr(out=pT[:, kt * 128:(kt + 1) * 128], in_=ssb[:, kt * 128:(kt + 1) * 128], ident=idb[:, :])
        k.cp("vector", out=ssT[:, :, :].rearrange("p k t -> p (k t)"), in_=pT[:, :])
        for bi in range(4):
            ps = P[2 + bi]
            for kt in range(16):
                lt = att[:, kt, :] if kt < 8 else ssT[:, kt - 8, :]
                k.mm(out=ps[:, :], lhsT=lt, rhs=wout[:, kt, bi * 512:(bi + 1) * 512], start=(kt == 0), stop=(kt == 15))
            cs_ = slice(bi * 512, (bi + 1) * 512)
            k.tt("vector", out=y[:, 0:512], in0=ps[:, :], in1=grow[:, cs_], op=ALU.mult)
            k.tt("vector", out=xt[:, cs_], in0=xt[:, cs_], in1=y[:, 0:512], op=ALU.add)
        k.dma("sync", out=xm_d[rs, :], in_=xt[:, :])


def hypre_stage(k, c, xh_d, hmask_d, A, Braw, w_in_d, cw_d, cb_d, zc_d):
    P = c["P"]
    if c.get("nm_bufs") is None:
        c["nm_bufs"] = dict(
            junk=k.sb("junk", [128, 2048], BF16), ss=k.sb("ss", [128, 1], F32), rstd=k.sb("rstd", [128, 1], F32),
            xh=k.sb("xh", [128, 2048], BF16), tmpT=k.sb("tmpT", [128, 8, 128], F32))
    bufs = c["nm_bufs"]
    hT = k.sb("hp_hT", [128, 16, 2050], BF16)
    hTh = k.sb("hp_hTh", [128, 16, 2], BF16)
    hm = k.sb("hp_hm", [128, 2], F32)
    cw = k.sb("hp_cw", [128, 48, 3], F32)
    cb = k.sb("hp_cb", [128, 48], F32)
    k.dma("sync", out=hm[:, :], in_=hmask_d[:, :])
    k.dma("sync", out=cw[:, :, :], in_=cw_d[:, :, :])
    k.dma("sync", out=cb[:, :], in_=cb_d[:, :])
    xt = [k.sb(f"hp_x{i}", [128, 2048], F32) for i in range(2)]
    for t in range(16):
        x = xt[t % 2]
        k.dma("sync", out=x[:, :], in_=xh_d[1 + t * 128:1 + (t + 1) * 128, :])
        norm_mod_T(k, c, "hp", x[:, :], 128, A, Braw, hT[:, :, 1 + t * 128:1 + (t + 1) * 128], bufs)
    xhalo = k.sb("hp_xhalo", [2, 2048], F32)
    k.dma("sync", out=xhalo[0:1, :], in_=xh_d[0:1, :])
    k.dma("sync", out=xhalo[1:2, :], in_=xh_d[2049:2050, :])
    norm_mod_T(k, c, "hp", xhalo[:2, :], 2, A, Braw, hTh[:, :, :], bufs)
    k.ts("vector", out=hT[:, :, 0:1], in0=hTh[:, :, 0:1], s1=hm[:, 0:1], s2=None, op0=ALU.mult)
    k.ts("vector", out=hT[:, :, 2049:2050], in0=hTh[:, :, 1:2], s1=hm[:, 1:2], s2=None, op0=ALU.mult)
    wv = w_in_d.ap().rearrange("(kt p) c -> p kt c", p=128)
    wj = [k.sb(f"hp_w{i}", [128, 16, 128], BF16) for i in range(2)]
    osb = [k.sb(f"hp_o{i}", [128, 512], F32) for i in range(2)]
    k.dma("gpsimd", out=wj[0][:, :, :], in_=wv[:, :, 0:128])
    it = 0
    for j in range(48):
        if j + 1 < 48:
            k.dma("gpsimd", out=wj[(j + 1) % 2][:, :, :], in_=wv[:, :, (j + 1) * 128:(j + 2) * 128])
        w = wj[j % 2]
        for b in range(4):
            zp = P[b]
            zh = P[4 + b]
            for kt in range(16):
                k.mm(out=zp[:, :], lhsT=w[:, kt, :], rhs=hT[:, kt, 1 + 512 * b:1 + 512 * (b + 1)], start=(kt == 0), stop=(kt == 15))
            for kt in range(16):
                k.mm(out=zh[:, 0:2], lhsT=w[:, kt, :], rhs=hT[:, kt, 512 * b:512 * b + 514:513], start=(kt == 0), stop=(kt == 15))
            o = osb[it % 2]
            w0, w1, w2 = cw[:, j, 0:1], cw[:, j, 1:2], cw[:, j, 2:3]
            k.ts("vector", out=o[:, :], in0=zp[:, :], s1=w1, s2=cb[:, j:j + 1], op0=ALU.mult, op1=ALU.add)
            k.stt("vector", out=o[:, 1:512], in0=zp[:, 0:511], scalar=w0, in1=o[:, 1:512], op0=ALU.mult, op1=ALU.add)
            k.stt("vector", out=o[:, 0:511], in0=zp[:, 1:512], scalar=w2, in1=o[:, 0:511], op0=ALU.mult, op1=ALU.add)
            k.stt("vector", out=o[:, 0:1], in0=zh[:, 0:1], scalar=w0, in1=o[:, 0:1], op0=ALU.mult, op1=ALU.add)
            k.stt("vector", out=o[:, 511:512], in0=zh[:, 1:2], scalar=w2, in1=o[:, 511:512], op0=ALU.mult, op1=ALU.add)
            k.dma("sync", out=zc_d[j * 128:(j + 1) * 128, 512 * b:512 * (b + 1)], in_=o[:, :])
            it += 1


def projres_stage(k, c, xin_d, yT_d, wout_d, gate_d, xm_d, R=2048):
    P = c["P"]
    wout = k.sb("pr_wout", [128, 16, 2048], BF16)
    yT = k.sb("pr_yT", [128, 16, R], BF16)
    k.dma("gpsimd", out=wout[:, :, :], in_=wout_d.ap().rearrange("(kt p) c -> p kt c", p=128))
    yv = yT_d.ap().rearrange("(kt p) t -> p kt t", p=128)
    for kt in range(16):
        k.dma("gpsimd", out=yT[:, kt, :], in_=yv[:, kt, :])
    grow = k.sb("pr_grow", [128, 2048], F32)
    k.dma("sync", out=grow[:, :], in_=gate_d[:].pbcast(128))
    xt = [k.sb(f"pr_x{i}", [128, 2048], F32) for i in range(2)]
    tmp = k.sb("pr_tmp", [128, 512], F32)
    for t in range(R // 128):
        rs = slice(t * 128, (t + 1) * 128)
        x = xt[t % 2]
        k.dma("sync", out=x[:, :], in_=xin_d[rs, :])
        for bi in range(4):
            ps = P[(t % 2) * 4 + bi]
            for kt in range(16):
                k.mm(out=ps[:, :], lhsT=yT[:, kt, rs], rhs=wout[:, kt, bi * 512:(bi + 1) * 512], start=(kt == 0), stop=(kt == 15))
            cs_ = slice(bi * 512, (bi + 1) * 512)
            k.tt("vector", out=tmp[:, :], in0=ps[:, :], in1=grow[:, cs_], op=ALU.mult)
            k.tt("vector", out=x[:, cs_], in0=x[:, cs_], in1=tmp[:, :], op=ALU.add)
        k.dma("sync", out=xm_d[rs, :], in_=x[:, :])


NFFT = 32768
LSEQ = 16384
CB = 16


def hy_consts():
    f8 = np.float64
    ts = np.arange(64, dtype=f8)[:, None]
    kf = np.arange(128, dtype=f8)[None, :]
    a1 = 2 * np.pi * ts * kf / 128
    F1cat = np.concatenate([np.cos(a1), -np.sin(a1)], axis=1)
    tf = np.arange(256, dtype=f8)[:, None]
    aw = 2 * np.pi * tf * kf / NFFT
    Wre = np.cos(aw).reshape(2, 128, 128).transpose(1, 0, 2)
    Wim = (-np.sin(aw)).reshape(2, 128, 128).transpose(1, 0, 2)
    ks = np.arange(256, dtype=f8)[None, :]
    a2 = 2 * np.pi * tf * ks / 256
    Cos = np.cos(a2).reshape(2, 128, 256).transpose(1, 0, 2)
    Sin = np.sin(a2).reshape(2, 128, 256).transpose(1, 0, 2)
    a2t = a2.T
    cA1 = np.concatenate([np.cos(a2t), np.sin(a2t)], axis=1).reshape(2, 128, 512).transpose(1, 0, 2)
    cA2 = np.concatenate([-np.sin(a2t), np.cos(a2t)], axis=1).reshape(2, 128, 512).transpose(1, 0, 2)
    awt = aw.T
    WTre, WTim = np.cos(awt), np.sin(awt)
    a1t = a1.T
    C1 = np.cos(a1t) / NFFT
    S1n = -np.sin(a1t) / NFFT
    b = lambda a: np.ascontiguousarray(a).astype(ml_dtypes.bfloat16)
    f = lambda a: np.ascontiguousarray(a, dtype=np.float32)
    return dict(F1cat=b(F1cat), Wre=f(Wre), Wim=f(Wim), Cos=b(Cos), Sin=b(Sin), NSin=b(-Sin), cA1=b(cA1), cA2=b(cA2),
                WTre=f(WTre), WTim=f(WTim), C1=b(C1), S1n=b(S1n))


HY_CONST_SHAPES = dict(F1cat=([64, 256], BF16), Wre=([128, 2, 128], F32), Wim=([128, 2, 128], F32), Cos=([128, 2, 256], BF16),
                       Sin=([128, 2, 256], BF16), NSin=([128, 2, 256], BF16), cA1=([128, 2, 512], BF16), cA2=([128, 2, 512], BF16),
                       WTre=([128, 256], F32), WTim=([128, 256], F32), C1=([128, 64], BF16), S1n=([128, 64], BF16))


def hy_load_consts(k, cd):
    out = {}
    for n, (shp, dt) in HY_CONST_SHAPES.items():
        b = k.sb("hc_" + n, shp, dt)
        src = cd[n]
        k.dma("sync", out=b.ap(), in_=src.ap())
        out[n] = b
    return out


def fft_stageA(k, c, hc, ybf, nch, Ab_re, Ab_im, tw, it0=0):
    P = c["P"]
    it = it0
    for cp in range(nch // 2):
        for half in range(2):
            ps = P[it % 2]
            for ci in range(2):
                ch = cp * 2 + ci
                k.mm(out=ps[:, ci * 256:(ci + 1) * 256], lhsT=ybf[:, ch, half * 128:(half + 1) * 128], rhs=hc["F1cat"][:, :])
            pv = ps[:, :].rearrange("p (c r f) -> p c r f", c=2, r=2)
            Are, Aim = pv[:, :, 0, :], pv[:, :, 1, :]
            wre = hc["Wre"][:, half, :].ap
            wim = hc["Wim"][:, half, :].ap
            wre_b = View(hc["Wre"], bass.AP(wre.tensor, wre.offset, [list(wre.ap[0]), [0, 2], [1, 128]]))
            wim_b = View(hc["Wim"], bass.AP(wim.tensor, wim.offset, [list(wim.ap[0]), [0, 2], [1, 128]]))
            t = tw[it % 2]
            tv = lambda i: t[:, i, 0:256].rearrange("p (c f) -> p c f", c=2)
            k.tt("vector", out=tv(0), in0=Are, in1=wre_b, op=ALU.mult)
            k.tt("vector", out=tv(1), in0=Aim, in1=wim_b, op=ALU.mult)
            k.tt("vector", out=tv(2), in0=Are, in1=wim_b, op=ALU.mult)
            k.tt("vector", out=tv(3), in0=Aim, in1=wre_b, op=ALU.mult)
            k.tt("gpsimd", out=Ab_re[:, half, cp * 2:cp * 2 + 2, :], in0=tv(0), in1=tv(1), op=ALU.subtract)
            k.tt("gpsimd", out=Ab_im[:, half, cp * 2:cp * 2 + 2, :], in0=tv(2), in1=tv(3), op=ALU.add)
            it += 1
    return it


def fft_stageB_block(k, c, hc, Ab_re, Ab_im, hk, blk, want_re=True, want_im=True):
    P = c["P"]
    cs = slice(blk * 4, blk * 4 + 4)
    ksl = slice(hk * 128, (hk + 1) * 128)
    Xre, Xim = P[2 + (blk % 2) * 2], P[3 + (blk % 2) * 2]
    if want_re:
        n = 0
        for ht in range(2):
            for (lt, rb) in ((hc["Cos"], Ab_re), (hc["Sin"], Ab_im)):
                k.mm(out=Xre[:, :], lhsT=lt[:, ht, ksl], rhs=rb[:, ht, cs, :].rearrange("p c f -> p (c f)"), start=(n == 0), stop=(n == 3))
                n += 1
    if want_im:
        n = 0
        for ht in range(2):
            for (lt, rb) in ((hc["NSin"], Ab_re), (hc["Cos"], Ab_im)):
                k.mm(out=Xim[:, :], lhsT=lt[:, ht, ksl], rhs=rb[:, ht, cs, :].rearrange("p c f -> p (c f)"), start=(n == 0), stop=(n == 3))
                n += 1
    return Xre, Xim


def hyconv_stage(k, c, cd, x1_d, x2_d, v_d, skip_d, Gre_d, Gim_d, y2_d, ncb=16):
    P = c["P"]
    hc = hy_load_consts(k, cd)
    sk = k.sb("hv_sk", [64, 2, 256], F32)
    for o in range(2):
        k.dma("sync", out=sk[:, o, :], in_=skip_d[o, :].pbcast(64))
    yf = k.sb("hv_yf", [64, CB, 256], F32)
    gt = k.sb("hv_gt", [64, CB, 256], F32)
    ybf = k.sb("hv_ybf", [64, CB, 256], BF16)
    Ab_re, Ab_im = k.sb("hv_Abre", [128, 2, CB, 128], BF16), k.sb("hv_Abim", [128, 2, CB, 128], BF16)
    Zb_re, Zb_im = k.sb("hv_Zbre", [128, 2, CB, 128], BF16), k.sb("hv_Zbim", [128, 2, CB, 128], BF16)
    Bb_re, Bb_im = k.sb("hv_Bbre", [128, CB, 256], BF16), k.sb("hv_Bbim", [128, CB, 256], BF16)
    tw = [k.sb(f"hv_tw{i}", [128, 4, 512], F32) for i in range(2)]
    Gt = [(k.sb(f"hv_Gre{i}", [128, 512], F32), k.sb(f"hv_Gim{i}", [128, 512], F32)) for i in range(2)]
    wsk = k.sb("hv_wsk", [64, 2, 256], F32)
    w2 = k.sb("hv_w2", [64, 2, 256], F32)
    gates = (x1_d, x2_d)
    it = 0
    for cb in range(ncb):
        chs = slice(cb * CB, (cb + 1) * CB)
        k.dma("sync", out=yf[:, :, :], in_=v_d[chs, :].rearrange("c (s f) -> s c f", s=64))
        for o in range(2):
            k.dma("sync", out=gt[:, :, :], in_=gates[o][chs, :].rearrange("c (s f) -> s c f", s=64))
            k.cp("scalar", out=ybf[:, :, :].rearrange("p c f -> p (c f)"), in_=yf[:, :, :].rearrange("p c f -> p (c f)"))
            it = fft_stageA(k, c, hc, ybf, CB, Ab_re, Ab_im, tw, it)
            for hk in range(2):
                for blk in range(CB // 4):
                    Xre, Xim = fft_stageB_block(k, c, hc, Ab_re, Ab_im, hk, blk)
                    gre, gim = Gt[it % 2]
                    c0 = (cb * CB + blk * 4) * 128
                    k.dma("sync", out=gre[:, :], in_=Gre_d[o, hk, :, c0:c0 + 512])
                    k.dma("sync", out=gim[:, :], in_=Gim_d[o, hk, :, c0:c0 + 512])
                    t = tw[it % 2]
                    k.tt("vector", out=t[:, 0, :], in0=Xre[:, :], in1=gre[:, :], op=ALU.mult)
                    k.tt("vector", out=t[:, 1, :], in0=Xim[:, :], in1=gim[:, :], op=ALU.mult)
                    k.tt("vector", out=t[:, 2, :], in0=Xre[:, :], in1=gim[:, :], op=ALU.mult)
                    k.tt("vector", out=t[:, 3, :], in0=Xim[:, :], in1=gre[:, :], op=ALU.mult)
                    zs = slice(blk * 4, blk * 4 + 4)
                    k.tt("gpsimd", out=Zb_re[:, hk, zs, :].rearrange("p c f -> p (c f)"), in0=t[:, 0, :], in1=t[:, 1, :], op=ALU.subtract)
                    k.tt("gpsimd", out=Zb_im[:, hk, zs, :].rearrange("p c f -> p (c f)"), in0=t[:, 2, :], in1=t[:, 3, :], op=ALU.add)
                    it += 1
            for ch in range(CB):
                ps = P[6 + (ch % 2)]
                n = 0
                for hk in range(2):
                    for (zb, rt) in ((Zb_re, hc["cA1"]), (Zb_im, hc["cA2"])):
                        k.mm(out=ps[:, :], lhsT=zb[:, hk, ch, :], rhs=rt[:, hk, :], start=(n == 0), stop=(n == 3))
                        n += 1
                Bre, Bim = ps[:, 0:256], ps[:, 256:512]
                t = tw[it % 2]
                k.tt("vector", out=t[:, 0, 0:256], in0=Bre, in1=hc["WTre"][:, :], op=ALU.mult)
                k.tt("vector", out=t[:, 1, 0:256], in0=Bim, in1=hc["WTim"][:, :], op=ALU.mult)
                k.tt("vector", out=t[:, 2, 0:256], in0=Bre, in1=hc["WTim"][:, :], op=ALU.mult)
                k.tt("vector", out=t[:, 3, 0:256], in0=Bim, in1=hc["WTre"][:, :], op=ALU.mult)
                k.tt("gpsimd", out=Bb_re[:, ch, :], in0=t[:, 0, 0:256], in1=t[:, 1, 0:256], op=ALU.subtract)
                k.tt("gpsimd", out=Bb_im[:, ch, :], in0=t[:, 2, 0:256], in1=t[:, 3, 0:256], op=ALU.add)
                it += 1
            for pr in range(CB // 2):
                ps = P[pr % 2]
                cs2 = slice(pr * 2, pr * 2 + 2)
                k.mm(out=ps[:64, :], lhsT=hc["C1"][:, :], rhs=Bb_re[:, cs2, :].rearrange("p c f -> p (c f)"), start=True, stop=False)
                k.mm(out=ps[:64, :], lhsT=hc["S1n"][:, :], rhs=Bb_im[:, cs2, :].rearrange("p c f -> p (c f)"), start=False, stop=True)
                skb = sk[:, o, cb * CB + pr * 2:cb * CB + pr * 2 + 2].unsq_bcast(256)
                k.tt("gpsimd", out=wsk[:, :, :], in0=yf[:, cs2, :], in1=skb, op=ALU.mult)
                k.tt("vector", out=w2[:, :, :], in0=ps[:64, :].rearrange("p (c f) -> p c f", c=2), in1=wsk[:, :, :], op=ALU.add)
                k.tt("gpsimd", out=yf[:, cs2, :], in0=w2[:, :, :], in1=gt[:, cs2, :], op=ALU.mult)
        k.dma("sync", out=y2_d[chs, :].rearrange("c (s f) -> s c f", s=64), in_=yf[:, :, :])


def hy_filt_consts(ci):
    import math
    L = LSEQ
    t = np.arange(L, dtype=np.float32) / np.float32(L)
    bands = np.linspace(1e-4, 15, 16, dtype=np.float32)
    ang = (np.float32(2.0 * math.pi) * t[:, None] * bands[None, :]).astype(np.float32)
    feat = np.concatenate([t[:, None], np.cos(ang), -np.sin(ang)], axis=-1).astype(np.float32)
    dmin, dmax = math.log(1e-2) / 1.5, math.log(1e-2) / 0.3
    deltas = np.abs(np.linspace(dmin, dmax, 2048, dtype=np.float32))[ci * 256:(ci + 1) * 256].astype(np.float64)
    drow = np.broadcast_to(deltas[None, :], (64, 256))
    E1 = np.exp(-(256.0 * np.arange(64, dtype=np.float64)[:, None] / L) * deltas[None, :])
    tfrow = np.broadcast_to(np.arange(256, dtype=np.float32)[None, :], (64, 256))
    return dict(featT=np.ascontiguousarray(feat.T), drow=np.ascontiguousarray(drow, dtype=np.float32), E1=np.ascontiguousarray(E1, dtype=np.float32),
                tfrow=np.ascontiguousarray(tfrow))


def hyfilt_stage(k, c, cd, featT_d, w1_d, w2_d, w3_d, bf_d, wout_d, drow_d, E1_d, tfrow_d, Gre_d, Gim_d, nfb=16, norders=2):
    P = c["P"]
    hc = {}
    for n in ("F1cat", "Wre", "Wim", "Cos", "Sin", "NSin"):
        shp, dt = HY_CONST_SHAPES[n]
        hc[n] = k.sb("hc_" + n, shp, dt)
        k.dma("sync", out=hc[n].ap(), in_=cd[n].ap())
    sb = lambda n, s, dt=F32: k.sb("hf_" + n, s, dt)
    w1s, w2s, w3s, bf = sb("w1", [33, 64]), sb("w2", [64, 64]), sb("w3", [64, 64]), sb("bf", [64, 4])
    k.dma("sync", out=w1s[:, :], in_=w1_d[:, :])
    k.dma("sync", out=w2s[:, :], in_=w2_d[:, :])
    k.dma("sync", out=w3s[:, :], in_=w3_d[:, :])
    k.dma("sync", out=bf[:, :], in_=bf_d[:, :])
    frb = sb("frb", [64, 3])
    for l in range(3):
        k.tt("vector", out=frb[:, l:l + 1], in0=bf[:, l:l + 1], in1=bf[:, 3:4], op=ALU.mult)
    drow, E1 = sb("drow", [64, 256]), sb("E1", [64, 256])
    k.dma("sync", out=drow[:, :], in_=drow_d[:, :])
    k.dma("sync", out=E1[:, :], in_=E1_d[:, :])
    hidT = sb("hidT", [64, LSEQ])
    ft = [sb(f"ft{i}", [33, 512]) for i in range(2)]
    a_, s1_, s2_ = sb("a", [64, 512]), sb("s1", [64, 512]), sb("s2", [64, 512])
    ki = k.sb("hf_ki", [64, 512], mybir.dt.int32)
    hA, hB = sb("hA", [64, 512]), sb("hB", [64, 512])
    ws = (w1s, w2s, w3s)
    for blk in range(LSEQ // 512):
        f = ft[blk % 2]
        k.dma("sync", out=f[:, :], in_=featT_d[:, blk * 512:(blk + 1) * 512])
        src = f[:, :]
        dsts = (hA[:, :], hB[:, :], hidT[:, blk * 512:(blk + 1) * 512])
        for l in range(3):
            ps = P[(blk * 3 + l) % 2]
            k.mm(out=ps[:64, :], lhsT=ws[l][:, :], rhs=src)
            k.ts("vector", out=a_[:, :], in0=ps[:64, :], s1=bf[:, 3:4], s2=frb[:, l:l + 1], op0=ALU.mult, op1=ALU.add)
            sincos(k, dsts[l], a_[:, :], 0.0, s1_[:, :], s2_[:, :], ki[:, :])
            src = dsts[l]
    ones = sb("ones", [64, 128])
    k.memset("vector", ones[:, :], 1.0)
    Wo = [sb(f"Wo{i}", [64, 2, 16]) for i in range(2)]
    Wn = sb("Wn", [64, 16, 256])
    tfrow = sb("tfrow", [64, 256])
    k.dma("sync", out=tfrow[:, :], in_=tfrow_d[:, :])
    Hf = sb("Hf", [64, 2, 16, 256])
    Hsd = k.sb("hf_Hsd", [64, 2, 16, 256], BF16)
    absum, s2n, rn = sb("absum", [64, 32]), sb("s2n", [64, 16]), sb("rn", [128, 16])
    Ab_re, Ab_im = k.sb("hf_Abre", [128, 2, 16, 128], BF16), k.sb("hf_Abim", [128, 2, 16, 128], BF16)
    tw = [sb(f"tw{i}", [128, 4, 512]) for i in range(2)]
    go = [sb(f"go{i}", [128, 512]) for i in range(2)]
    it = 0
    ig = 0
    for o in range(norders):
        for fb in range(nfb):
            chs = slice(fb * 16, (fb + 1) * 16)
            wo = Wo[fb % 2]
            k.dma("sync", out=wo[:, :, :], in_=wout_d[:, o, :, chs])
            dv = drow[:, chs].ap
            dr_b = View(drow, bass.AP(dv.tensor, dv.offset, [list(dv.ap[0]), [1, 16], [0, 256]]))
            tv_ = tfrow[:, :].ap
            tf_b = View(tfrow, bass.AP(tv_.tensor, tv_.offset, [list(tv_.ap[0]), [0, 16], [1, 256]]))
            k.tt("gpsimd", out=Wn[:, :, :], in0=dr_b, in1=tf_b, op=ALU.mult)
            k.act(out=Wn[:, :, :], in_=Wn[:, :, :], func=AF.Exp, scale=-1.0 / LSEQ)
            k.tt("gpsimd", out=Wn[:, :, :], in0=Wn[:, :, :], in1=E1[:, chs].unsq_bcast(256), op=ALU.mult)
            wflat = wo[:, :, :].rearrange("p d c -> p (d c)")
            for tg in range(32):
                ps = P[2 + tg % 2]
                for j in range(8):
                    tfv = tg * 8 + j
                    k.mm(out=ps[:64, j * 32:(j + 1) * 32], lhsT=hidT[:, tfv:LSEQ:256], rhs=wflat)
                wv_ = Wn[:, :, tg * 8:(tg + 1) * 8].ap
                Wn_b = View(Wn, bass.AP(wv_.tensor, wv_.offset, [list(wv_.ap[0]), [0, 2], [256, 16], [1, 8]]))
                k.tt("vector", out=Hf[:, :, :, tg * 8:(tg + 1) * 8], in0=ps[:64, 0:256].rearrange("p (j d c) -> p d c j", j=8, d=2),
                     in1=Wn_b, op=ALU.mult)
            k.memset("vector", Hf[0:1, 1, :, 0:1], 0.0)
            k.I("vector", "tensor_reduce", out=absum[:, :], in_=Hf[:, :, :, :].rearrange("p d c f -> p (d c) f"), axis=AX.X, op=ALU.add,
                apply_absolute_value=True)
            k.tt("vector", out=s2n[:, :], in0=absum[:, 0:16], in1=absum[:, 16:32], op=ALU.add)
            psn = P[4]
            k.mm(out=psn[:, 0:16], lhsT=ones[:, :], rhs=s2n[:, :])
            k.I("vector", "reciprocal", out=rn[:, :], in_=psn[:, 0:16])
            k.tt("gpsimd", out=Hsd[:, 0, :, :], in0=Hf[:, 0, :, :], in1=Hf[:, 1, :, :], op=ALU.add)
            k.tt("gpsimd", out=Hsd[:, 1, :, :], in0=Hf[:, 0, :, :], in1=Hf[:, 1, :, :], op=ALU.subtract)
            for sd in range(2):
                it = fft_stageA(k, c, hc, Hsd[:, sd, :, :], 16, Ab_re, Ab_im, tw, it)
                for hk in range(2):
                    for blk in range(4):
                        Xre, Xim = fft_stageB_block(k, c, hc, Ab_re, Ab_im, hk, blk, want_re=(sd == 0), want_im=(sd == 1))
                        X = Xre if sd == 0 else Xim
                        g_ = go[ig % 2]
                        k.tt("vector", out=g_[:, :].rearrange("p (c f) -> p c f", c=4), in0=X[:, :].rearrange("p (c f) -> p c f", c=4),
                             in1=rn[:, blk * 4:blk * 4 + 4].unsq_bcast(128), op=ALU.mult)
                        c0 = (fb * 16 + blk * 4) * 128
                        dst = Gre_d if sd == 0 else Gim_d
                        k.dma("sync", out=dst[o, hk, :, c0:c0 + 512], in_=g_[:, :])
                        ig += 1


def wprep_stage(k, c, win_d, wout_d, winb_d, woutb_d, nf=4):
    bufs = [k.sb(f"wp_b{i}", [128, 5632], BF16) for i in range(3)]
    it = 0
    for f in range(nf):
        for (src, dst, n) in ((win_d, winb_d, 22528), (wout_d, woutb_d, 11264)):
            for c0 in range(0, n, 5632):
                b = bufs[it % 3]
                k.dma("gpsimd", out=b[:, :], in_=src[f, :, c0:c0 + 5632])
                k.dma("sync", out=dst[f, :, c0:c0 + 5632], in_=b[:, :])
                it += 1


def _cols(v, n=16):
    return np.ascontiguousarray(np.asarray(v, np.float32).reshape(n, 128).T)


def _modc(m, k0):
    return np.ascontiguousarray(np.stack([_cols(m[k0]), _cols(m[k0 + 1]), _cols(m[k0 + 2])], axis=1).astype(np.float32))


def _rope_tabs(n_tokens):
    t = np.arange(n_tokens)
    row = (t // 64).astype(np.float32)
    col = (t % 64).astype(np.float32)
    inv = (10000.0 ** (-np.arange(16, dtype=np.float32) / 16)).astype(np.float32)
    ar = row[:, None] * inv
    ac = col[:, None] * inv
    cos = np.concatenate([np.cos(ar), np.cos(ac)], axis=1).astype(np.float32)
    sin = np.concatenate([np.sin(ar), np.sin(ac)], axis=1).astype(np.float32)
    return cos, sin


_IDF = np.eye(128, dtype=np.float32)
_IDB = _IDF.astype(ml_dtypes.bfloat16)
_PROGS = {}
_DBG = {}


def _prog(key, fn):
    if key not in _PROGS:
        _PROGS[key] = fn()
    return _PROGS[key]


def _run(nc, maps):
    return run_bass_kernel_spmd(nc, maps, core_ids=list(range(8))).results


def _build_ada():
    k = KB()
    cc = k.dram("cc", [128, 16, 2], F32, kind="ExternalInput")
    w = k.dram("w", [2, 2048, 2304], F32, kind="ExternalInput")
    b = k.dram("b", [2, 2304], F32, kind="ExternalInput")
    out = k.dram("out", [2, 2, 2304], F32, kind="ExternalOutput")
    c = alloc_common(k)
    ada_stage(k, c, cc, w, b, out)
    return k.emit()


def _build_ffn(R):
    k = KB()
    xin = k.dram("xin", [R, D], F32, kind="ExternalInput")
    xout = k.dram("xout", [R, D], F32, kind="ExternalOutput")
    modc = k.dram("modc", [128, 3, 16], F32, kind="ExternalInput")
    normg = k.dram("normg", [128, 16], F32, kind="ExternalInput")
    w_in = k.dram("w_in", [D, 2 * DFF], BF16, kind="ExternalInput")
    w_out = k.dram("w_out", [DFF, D], BF16, kind="ExternalInput")
    idf = k.dram("idf", [128, 128], F32, kind="ExternalInput")
    idb = k.dram("idb", [128, 128], BF16, kind="ExternalInput")
    c = alloc_common(k)
    load_ident(k, c, idf, idb)
    A, Braw, G = modcols_prepare(k, "m0", modc[:, :, :], normg[:, :], 0)
    ffn_stage(k, c, "f0", xin, xout, R, 128, A, Braw, G, w_in, w_out)
    return k.emit()


def _build_evpre(R):
    k = KB()
    di = lambda n, s, dt=F32: k.dram(n, s, dt, kind="ExternalInput")
    do = lambda n, s, dt=F32: k.dram(n, s, dt, kind="ExternalOutput")
    xin = di("xin", [R, D]); modc = di("modc", [128, 3, 16]); normg = di("normg", [128, 16])
    w_in = di("w_in", [D, 1856]); wuq = di("wuq", [512, 1536]); wukv = di("wukv", [256, 2048])
    gqa = di("gqa", [128, 4]); gkva = di("gkva", [128, 2]); gq = di("gq", [192]); gk = di("gk", [192])
    cos = di("cos", [R, 32]); sin = di("sin", [R, 32])
    idf = di("idf", [128, 128]); idb = di("idb", [128, 128], BF16)
    qT = do("qT", [8, 192, R], BF16); kT = do("kT", [8, 192, R], BF16); v = do("v", [R, 1024], BF16); u = do("u", [R, 1024])
    c = alloc_common(k)
    load_ident(k, c, idf, idb)
    A, Braw, G = modcols_prepare(k, "m1", modc[:, :, :], normg[:, :], 0)
    evpre_stage(k, c, xin, R, 128, A, Braw, w_in, wuq, wukv, gqa, gkva, gq, gk, cos, sin, qT, kT, v, u, True)
    return k.emit()


def _build_attn():
    k = KB()
    di = lambda n, s, dt=F32: k.dram(n, s, dt, kind="ExternalInput")
    qT = di("qT", [8, 192, 2048], BF16); kT = di("kT", [8, 192, 16640], BF16); v = di("v", [16640, 1024], BF16)
    attT = k.dram("attT", [1024, 2048], BF16, kind="ExternalOutput")
    c = alloc_common(k)
    attn_stage(k, c, qT, kT, v, attT, 2048, 16640, 8)
    return k.emit()


def _build_s5():
    k = KB()
    di = lambda n, s, dt=F32: k.dram(n, s, dt, kind="ExternalInput")
    U = di("U", [16, 16, 16640]); lre = di("lre", [64, 16]); lim = di("lim", [64, 16]); ldt = di("ldt", [16])
    BTre = di("BTre", [16, 16, 64]); BTim = di("BTim", [16, 16, 64]); CTre = di("CTre", [64, 16, 16]); CTim = di("CTim", [64, 16, 16])
    jt = di("jt", [64, SEG + 1])
    Y = k.dram("Y", [16, 16, 16384], F32, kind="ExternalOutput")
    c = alloc_common(k)
    s5_stage(k, c, U, lre, lim, ldt, BTre, BTim, CTre, CTim, jt, Y, NSEG)
    return k.emit()


def _build_evout():
    k = KB()
    R = 2048
    di = lambda n, s, dt=F32: k.dram(n, s, dt, kind="ExternalInput")
    xa = di("xa", [R, D]); u = di("u", [R, 1024]); yf = di("yf", [R, 1024]); yb = di("yb", [R, 1024]); attT = di("attT", [1024, R], BF16)
    dd = di("dd", [1024]); gluw = di("gluw", [1024, 1024]); glub = di("glub", [1024]); wout = di("wout", [2048, 2048]); gate = di("gate", [2048])
    idf = di("idf", [128, 128]); idb = di("idb", [128, 128], BF16)
    xm = k.dram("xm", [R, D], F32, kind="ExternalOutput")
    c = alloc_common(k)
    load_ident(k, c, idf, idb)
    evout_stage(k, c, xa, u, yf, yb, attT, dd, gluw, glub, wout, gate, xm, R)
    return k.emit()


def _build_hypre():
    k = KB()
    di = lambda n, s, dt=F32: k.dram(n, s, dt, kind="ExternalInput")
    xh = di("xhin", [2050, D]); hmask = di("hmask", [128, 2]); modc = di("modc", [128, 3, 16]); normg = di("normg", [128, 16])
    w_in = di("w_in", [D, 6144]); cw = di("cw", [128, 48, 3]); cb = di("cb", [128, 48])
    idf = di("idf", [128, 128]); idb = di("idb", [128, 128], BF16)
    zc = k.dram("zc", [6144, 2048], F32, kind="ExternalOutput")
    c = alloc_common(k)
    load_ident(k, c, idf, idb)
    A, Braw, G = modcols_prepare(k, "m1", modc[:, :, :], normg[:, :], 0)
    hypre_stage(k, c, xh, hmask, A, Braw, w_in, cw, cb, zc)
    return k.emit()


_FILT_CONSTS = ("F1cat", "Wre", "Wim", "Cos", "Sin", "NSin")


def _build_hyfilt():
    k = KB()
    di = lambda n, s, dt=F32: k.dram(n, s, dt, kind="ExternalInput")
    cd = {n: di("k_" + n, HY_CONST_SHAPES[n][0], HY_CONST_SHAPES[n][1]) for n in _FILT_CONSTS}
    featT = di("featT", [33, LSEQ]); w1 = di("w1", [33, 64]); w2 = di("w2", [64, 64]); w3 = di("w3", [64, 64]); bf = di("bf", [64, 4])
    wout = di("wout", [64, 2, 2, 256]); drow = di("drow", [64, 256]); E1 = di("E1", [64, 256]); tfrow = di("tfrow", [64, 256])
    Gre = k.dram("Gre", [2, 2, 128, 256 * 128], F32, kind="ExternalOutput")
    Gim = k.dram("Gim", [2, 2, 128, 256 * 128], F32, kind="ExternalOutput")
    c = alloc_common(k)
    hyfilt_stage(k, c, cd, featT, w1, w2, w3, bf, wout, drow, E1, tfrow, Gre, Gim, 16, 2)
    return k.emit()


def _build_hyconv():
    k = KB()
    di = lambda n, s, dt=F32: k.dram(n, s, dt, kind="ExternalInput")
    cd = {n: di("k_" + n, shp, dt) for n, (shp, dt) in HY_CONST_SHAPES.items()}
    x1 = di("x1", [256, LSEQ]); x2 = di("x2", [256, LSEQ]); v = di("v", [256, LSEQ]); skip = di("skip", [2, 256])
    Gre = di("Gre", [2, 2, 128, 256 * 128]); Gim = di("Gim", [2, 2, 128, 256 * 128])
    y2 = k.dram("y2", [256, LSEQ], F32, kind="ExternalOutput")
    c = alloc_common(k)
    hyconv_stage(k, c, cd, x1, x2, v, skip, Gre, Gim, y2, 16)
    return k.emit()


def _build_projres():
    k = KB()
    di = lambda n, s, dt=F32: k.dram(n, s, dt, kind="ExternalInput")
    xin = di("xin", [2048, D]); yT = di("yT", [2048, 2048]); wout = di("wout", [2048, 2048]); gate = di("gate", [2048])
    xm = k.dram("xm", [2048, D], F32, kind="ExternalOutput")
    c = alloc_common(k)
    projres_stage(k, c, xin, yT, wout, gate, xm, 2048)
    return k.emit()


def _hyena_mixer(inp, xs, m1):
    x_full = np.concatenate(xs, axis=0)
    xp = np.concatenate([np.zeros((1, 2048), np.float32), x_full, np.zeros((1, 2048), np.float32)], axis=0)
    cw = np.ascontiguousarray(inp["hy_conv_w"][0].reshape(3, 48, 128).transpose(2, 1, 0))
    cb = np.ascontiguousarray(inp["hy_conv_b"][0].reshape(48, 128).T)
    maps = []
    for ci in range(8):
        hm = np.ones((128, 2), np.float32)
        if ci == 0:
            hm[:, 0] = 0
        if ci == 7:
            hm[:, 1] = 0
        maps.append(dict(xhin=np.ascontiguousarray(xp[ci * 2048:ci * 2048 + 2050]), hmask=hm, modc=_modc(m1, 3), normg=_cols(inp["norm_g"][1, 1]),
                         w_in=np.ascontiguousarray(inp["hy_w_in"][0]), cw=cw, cb=cb, idf=_IDF, idb=_IDB))
    rz = _run(_prog("hypre", _build_hypre), maps)
    z_all = np.concatenate([rz[ci]["zc"] for ci in range(8)], axis=1)
    consts = hy_consts()
    bf = np.ascontiguousarray(np.stack([inp["hy_filt_b1"][0], inp["hy_filt_b2"][0], inp["hy_filt_b3"][0], inp["hy_filt_freq"][0]], axis=1).astype(np.float32))
    maps = []
    for ci in range(8):
        fc = hy_filt_consts(ci)
        m = {"k_" + n: consts[n] for n in _FILT_CONSTS}
        m.update(featT=fc["featT"], drow=fc["drow"], E1=fc["E1"], tfrow=fc["tfrow"], w1=np.ascontiguousarray(inp["hy_filt_w1"][0]), w2=np.ascontiguousarray(inp["hy_filt_w2"][0]),
                 w3=np.ascontiguousarray(inp["hy_filt_w3"][0]), bf=bf, wout=np.ascontiguousarray(inp["hy_filt_w_out"][0][:, :, :, ci * 256:(ci + 1) * 256]))
        maps.append(m)
    rf = _run(_prog("hyfilt", _build_hyfilt), maps)
    maps = []
    for ci in range(8):
        cs = slice(ci * 256, (ci + 1) * 256)
        m = {"k_" + n: v for n, v in consts.items()}
        m.update(x1=np.ascontiguousarray(z_all[0:2048][cs]), x2=np.ascontiguousarray(z_all[2048:4096][cs]), v=np.ascontiguousarray(z_all[4096:6144][cs]),
                 skip=np.ascontiguousarray(inp["hy_skip"][0][:, cs]), Gre=rf[ci]["Gre"], Gim=rf[ci]["Gim"])
        maps.append(m)
    ry = _run(_prog("hyconv", _build_hyconv), maps)
    y_all = np.concatenate([ry[ci]["y2"] for ci in range(8)], axis=0)
    prc = dict(wout=np.ascontiguousarray(inp["hy_w_out"][0]), gate=np.ascontiguousarray(m1[5]))
    rp = _run(_prog("projres", _build_projres), [dict(prc, xin=np.ascontiguousarray(xs[ci]), yT=np.ascontiguousarray(y_all[:, ci * 2048:(ci + 1) * 2048]))
                                                  for ci in range(8)])
    return [rp[ci]["xm"] for ci in range(8)]


def _build_wprep():
    k = KB()
    win = k.dram("win", [4, 128, 22528], F32, kind="ExternalInput")
    wout = k.dram("wout", [4, 128, 11264], F32, kind="ExternalInput")
    winb = k.dram("winb", [4, 128, 22528], BF16, kind="ExternalOutput")
    woutb = k.dram("woutb", [4, 128, 11264], BF16, kind="ExternalOutput")
    c = {}
    wprep_stage(k, c, win, wout, winb, woutb, 4)
    return k.emit()


def _prep_ffn_weights(inp):
    win = inp["ffn_w_in"].reshape(4, 8, 128, 22528)
    wout = inp["ffn_w_out"].reshape(4, 8, 128, 11264)
    res = _run(_prog("wprep", _build_wprep), [dict(win=np.ascontiguousarray(win[:, ci]), wout=np.ascontiguousarray(wout[:, ci])) for ci in range(8)])
    winb = np.stack([res[ci]["winb"] for ci in range(8)], axis=1).reshape(2, 2, 2048, 11264)
    woutb = np.stack([res[ci]["woutb"] for ci in range(8)], axis=1).reshape(2, 2, 5632, 2048)
    return winb, woutb


def _s5_inmaps(inp, u_x, u_c):
    jt = np.broadcast_to(np.arange(SEG + 1, dtype=np.float32)[None, :], (64, SEG + 1)).copy()
    maps = []
    seq_f = np.concatenate([u_c, u_x], axis=0)
    seq_b = np.concatenate([u_x, u_c], axis=0)[::-1]
    for ci in range(8):
        gs = slice(ci * 8, ci * 8 + 8)
        U = np.empty((16, 16, 16640), np.float32)
        for di_, seq in enumerate((seq_f, seq_b)):
            blk = seq[:, ci * 128:(ci + 1) * 128].reshape(16640, 8, 16)
            U[:, di_ * 8:(di_ + 1) * 8, :] = blk.transpose(2, 1, 0)

        def lanes(a):
            return a[:, gs].reshape(16, *a.shape[2:])
        m = dict(U=U, lre=lanes(inp["s5_lam_re"][0]).T, lim=lanes(inp["s5_lam_im"][0]).T, ldt=lanes(inp["s5_log_dt"][0]),
                 BTre=lanes(inp["s5_b_re"][0]).transpose(2, 0, 1), BTim=lanes(inp["s5_b_im"][0]).transpose(2, 0, 1),
                 CTre=lanes(inp["s5_c_re"][0]).transpose(2, 0, 1), CTim=lanes(inp["s5_c_im"][0]).transpose(2, 0, 1), jt=jt)
        maps.append({kk: np.ascontiguousarray(v, dtype=np.float32) for kk, v in m.items()})
    return maps


def _ffn_launch(x_rows_per_core, m, k0, normg, w_in, w_out):
    R = x_rows_per_core[0].shape[0]
    nc = _prog(("ffn", R), lambda: _build_ffn(R))
    common = dict(modc=_modc(m, k0), normg=_cols(normg), w_in=np.ascontiguousarray(w_in), w_out=np.ascontiguousarray(w_out), idf=_IDF, idb=_IDB)
    res = _run(nc, [dict(common, xin=np.ascontiguousarray(xr)) for xr in x_rows_per_core])
    return [r["xout"] for r in res]


def kernel(**inp):
    inp = {kk: np.asarray(v) for kk, v in inp.items()}
    x = inp["x"][0]
    ctx = inp["ctx"][0]
    cvec = np.stack([inp["c"][0], inp["c_ctx"]], axis=1)
    cc = np.ascontiguousarray(cvec.reshape(16, 128, 2).transpose(1, 0, 2))
    nc = _prog("ada", _build_ada)
    res = _run(nc, [dict(cc=cc, w=np.ascontiguousarray(inp["ada_w"][:, :, ci * 2304:(ci + 1) * 2304]),
                         b=np.ascontiguousarray(inp["ada_b"][:, ci * 2304:(ci + 1) * 2304])) for ci in range(8)])
    mods = np.concatenate([res[ci]["out"] for ci in range(8)], axis=2)
    mx = [mods[l, 0].reshape(9, 2048) for l in range(2)]
    mc = [mods[l, 1].reshape(9, 2048) for l in range(2)]
    xs = [x[ci * 2048:(ci + 1) * 2048] for ci in range(8)]
    cs = [ctx[(ci % 2) * 128:(ci % 2) * 128 + 128] for ci in range(8)]
    winb, woutb = _prep_ffn_weights(inp)
    xs = _ffn_launch(xs, mx[0], 0, inp["norm_g"][0, 0], winb[0, 0], woutb[0, 0])
    cs = _ffn_launch(cs, mc[0], 0, inp["norm_g"][0, 0], winb[0, 0], woutb[0, 0])
    cos, sin = _rope_tabs(16384)
    evc = dict(normg=_cols(inp["norm_g"][0, 1]), w_in=inp["ev_w_in"][0], wuq=inp["mla_w_uq"][0], wukv=inp["mla_w_ukv"][0],
               gqa=_cols(inp["mla_q_a_norm_g"][0], 4), gkva=_cols(inp["mla_kv_a_norm_g"][0], 2), gq=inp["mla_q_head_g"][0], gk=inp["mla_k_head_g"][0],
               idf=_IDF, idb=_IDB)
    evc = {kk: np.ascontiguousarray(v) for kk, v in evc.items()}
    nc = _prog(("evpre", 2048), lambda: _build_evpre(2048))
    rx = _run(nc, [dict(evc, modc=_modc(mx[0], 3), xin=np.ascontiguousarray(xs[ci]), cos=np.ascontiguousarray(cos[ci * 2048:(ci + 1) * 2048]),
                        sin=np.ascontiguousarray(sin[ci * 2048:(ci + 1) * 2048])) for ci in range(8)])
    nc = _prog(("evpre", 128), lambda: _build_evpre(128))
    one = np.ones((128, 32), np.float32)
    zero = np.zeros((128, 32), np.float32)
    rc = _run(nc, [dict(evc, modc=_modc(mc[0], 3), xin=np.ascontiguousarray(cs[ci]), cos=one, sin=zero) for ci in range(8)])
    kT_all = np.ascontiguousarray(np.concatenate([rx[ci]["kT"] for ci in range(8)] + [rc[0]["kT"], rc[1]["kT"]], axis=2))
    v_all = np.ascontiguousarray(np.concatenate([rx[ci]["v"] for ci in range(8)] + [rc[0]["v"], rc[1]["v"]], axis=0))
    u_x = np.concatenate([rx[ci]["u"] for ci in range(8)], axis=0)
    u_c = np.concatenate([rc[0]["u"], rc[1]["u"]], axis=0)
    nc = _prog("attn", _build_attn)
    ra = _run(nc, [dict(qT=rx[ci]["qT"], kT=kT_all, v=v_all) for ci in range(8)])
    nc = _prog("s5", _build_s5)
    rs = _run(nc, _s5_inmaps(inp, u_x, u_c))
    Yf = np.empty((16384, 1024), np.float32)
    Yb = np.empty((16384, 1024), np.float32)
    for ci in range(8):
        Y = rs[ci]["Y"]
        Yf[:, ci * 128:(ci + 1) * 128] = Y[0:8].transpose(2, 0, 1).reshape(16384, 128)
        Yb[:, ci * 128:(ci + 1) * 128] = Y[8:16, :, ::-1].transpose(2, 0, 1).reshape(16384, 128)
    nc = _prog("evout", _build_evout)
    eoc = dict(dd=inp["s5_d"][0], gluw=inp["s5_glu_w"][0], glub=inp["s5_glu_b"][0], wout=inp["ev_w_out"][0], gate=mx[0][5], idf=_IDF, idb=_IDB)
    eoc = {kk: np.ascontiguousarray(v) for kk, v in eoc.items()}
    ro = _run(nc, [dict(eoc, xa=np.ascontiguousarray(xs[ci]), u=np.ascontiguousarray(u_x[ci * 2048:(ci + 1) * 2048]),
                        yf=np.ascontiguousarray(Yf[ci * 2048:(ci + 1) * 2048]), yb=np.ascontiguousarray(Yb[ci * 2048:(ci + 1) * 2048]),
                        attT=ra[ci]["attT"]) for ci in range(8)])
    xs = [ro[ci]["xm"] for ci in range(8)]
    _DBG["x_m0"] = xs
    xs = _ffn_launch(xs, mx[0], 6, inp["norm_g"][0, 2], winb[0, 1], woutb[0, 1])
    _DBG["x_b0"] = xs
    xs = _ffn_launch(xs, mx[1], 0, inp["norm_g"][1, 0], winb[1, 0], woutb[1, 0])
    _DBG["x_a1"] = xs
    xs = _hyena_mixer(inp, xs, mx[1])
    _DBG["x_m1"] = xs
    xs = _ffn_launch(xs, mx[1], 6, inp["norm_g"][1, 2], winb[1, 1], woutb[1, 1])
    return np.concatenate(xs, axis=0)[None].astype(np.float32)
```

```python
import numpy as np
import ml_dtypes
from contextlib import ExitStack
import concourse.bass as bass
import concourse.mybir as mybir
from concourse.bass_utils import run_bass_kernel_spmd

F32 = mybir.dt.float32
BF16 = mybir.dt.bfloat16
AF = mybir.ActivationFunctionType
ALU = mybir.AluOpType
AX = mybir.AxisListType
WRITE_KEYS = ("out", "accum_out")
SEM_ROLL = 30000


class View:
    __slots__ = ("buf", "ap")

    def __init__(self, buf, ap):
        self.buf = buf
        self.ap = ap

    def __getitem__(self, idx):
        return View(self.buf, self.ap[idx])

    def rearrange(self, pat, **kw):
        return View(self.buf, self.ap.rearrange(pat, **kw))

    def bitcast(self, dt):
        return View(self.buf, self.ap.bitcast(dt))

    def unsq_bcast(self, n):
        a = self.ap
        return View(self.buf, bass.AP(a.tensor, a.offset, [list(x) for x in a.ap] + [[0, n]]))

    def pbcast(self, n):
        return View(self.buf, self.ap.partition_broadcast(n))


class Buf:
    def __init__(self, k, name, h, space):
        self.k = k
        self.name = name
        self.h = h
        self.space = space
        self.last_w = None
        self.readers = []
        self.dma_sem = None
        self.dma_cnt = 0

    def __getitem__(self, idx):
        return View(self, self.h[idx])

    def ap(self):
        return View(self, self.h.ap() if hasattr(self.h, "ap") else self.h[:])

    def bitcast_view(self, dt):
        return View(self, self.h.bitcast(dt).ap())


class Op:
    __slots__ = ("eng", "meth", "args", "kwargs", "reads", "writes", "is_dma", "tok", "has_dep", "deps", "sbuf_side", "acc")


class KB:
    def __init__(self):
        self.nc = bass.Bass("TRN2", target_bir_lowering=False)
        self.ops = []
        self.es = ExitStack()
        self.bufs = []
        self.n = 0

    def sb(self, name, shape, dt=F32):
        h = self.es.enter_context(self.nc.sbuf_tensor(name, list(shape), dt))
        b = Buf(self, name, h, "sb")
        self.bufs.append(b)
        return b

    def ps(self, name, shape, dt=F32):
        h = self.es.enter_context(self.nc.psum_tensor(name, list(shape), dt))
        b = Buf(self, name, h, "ps")
        self.bufs.append(b)
        return b

    def dram(self, name, shape, dt=F32, kind=None):
        if kind is None:
            h = self.nc.dram_tensor(name, list(shape), dt)
        else:
            h = self.nc.dram_tensor(name, list(shape), dt, kind=kind)
        b = Buf(self, name, h, "dram")
        self.bufs.append(b)
        return b

    def I(self, eng, meth, *args, reads=(), writes=(), acc=False, **kwargs):
        o = Op()
        o.eng = eng
        o.meth = meth
        o.args = args
        o.kwargs = kwargs
        rd, wr = list(reads), list(writes)
        for a in args:
            if isinstance(a, View):
                rd.append(a.buf)
        for kk, v in kwargs.items():
            if isinstance(v, View):
                (wr if kk in WRITE_KEYS else rd).append(v.buf)
        o.reads = rd
        o.writes = wr
        o.is_dma = meth == "dma_start"
        o.tok = None
        o.has_dep = False
        o.deps = None
        o.sbuf_side = None
        o.acc = acc
        if o.is_dma:
            ob, ib = kwargs["out"].buf, kwargs["in_"].buf
            o.sbuf_side = ob if ob.space != "dram" else ib
            assert o.sbuf_side.space != "dram", "dram->dram dma unsupported"
        self.ops.append(o)
        return o

    def dma(self, eng, out, in_, **kw):
        return self.I(eng, "dma_start", out=out, in_=in_, **kw)

    def mm(self, out, lhsT, rhs, start=True, stop=True, **kw):
        return self.I("tensor", "matmul", out=out, lhsT=lhsT, rhs=rhs, start=start, stop=stop, acc=not start, **kw)

    def tr(self, out, in_, ident):
        return self.I("tensor", "transpose", out=out, in_=in_, identity=ident)

    def act(self, out, in_, func, eng="scalar", **kw):
        return self.I(eng, "activation", out=out, in_=in_, func=func, **kw)

    def tt(self, eng, out, in0, in1, op):
        return self.I(eng, "tensor_tensor", out=out, in0=in0, in1=in1, op=op)

    def ts(self, eng, out, in0, s1, s2, op0, op1=None, **kw):
        if op1 is None:
            return self.I(eng, "tensor_scalar", out=out, in0=in0, scalar1=s1, scalar2=None, op0=op0, **kw)
        return self.I(eng, "tensor_scalar", out=out, in0=in0, scalar1=s1, scalar2=s2, op0=op0, op1=op1, **kw)

    def stt(self, eng, out, in0, scalar, in1, op0, op1):
        return self.I(eng, "scalar_tensor_tensor", out=out, in0=in0, scalar=scalar, in1=in1, op0=op0, op1=op1)

    def cp(self, eng, out, in_):
        if eng == "scalar":
            return self.I(eng, "copy", out=out, in_=in_)
        return self.I(eng, "tensor_copy", out=out, in_=in_)

    def memset(self, eng, out, val):
        return self.I(eng, "memset", writes=[out.buf], ap=out, constant=val)

    def emit(self, final_wait_bufs=()):
        nc = self.nc
        ops = self.ops
        for i, o in enumerate(ops):
            deps = set()
            for b in o.reads:
                if b.last_w is not None:
                    deps.add(b.last_w)
            for b in o.writes:
                if b.last_w is not None:
                    deps.add(b.last_w)
                for r in b.readers:
                    deps.add(r)
            deps.discard(i)
            fd = []
            for d in deps:
                od = ops[d]
                same = (od.eng == o.eng) and not o.is_dma and not od.is_dma
                if same:
                    raw = any((b in od.writes) for b in o.reads)
                    if o.eng == "tensor":
                        raw = False
                    if not raw:
                        continue
                fd.append(d)
            o.deps = fd
            for d in fd:
                ops[d].has_dep = True
            for b in o.writes:
                b.last_w = i
                b.readers = []
            for b in o.reads:
                if b not in o.writes:
                    b.readers.append(i)
        engs = {"tensor": nc.tensor, "vector": nc.vector, "scalar": nc.scalar, "gpsimd": nc.gpsimd, "sync": nc.sync}
        esem = {}
        ecnt = {}
        known = {e: {} for e in engs}

        def new_sem(name):
            return self.es.enter_context(nc.semaphore(name))

        nsem = [0]
        for e in engs:
            esem[e] = new_sem(f"e_{e}_0")
            ecnt[e] = 0
            nsem[0] += 1
        for i, o in enumerate(ops):
            eng = engs[o.eng]
            kn = known[o.eng]
            for d in o.deps:
                sem, val = ops[d].tok
                if kn.get(id(sem), (None, 0))[1] < val:
                    eng.wait_ge(sem, val)
                    kn[id(sem)] = (sem, val)
            args = [a.ap if isinstance(a, View) else a for a in o.args]
            kwargs = {kk: (v.ap if isinstance(v, View) else v) for kk, v in o.kwargs.items()}
            inst = getattr(eng, o.meth)(*args, **kwargs)
            if o.is_dma:
                b = o.sbuf_side
                if b.dma_sem is None:
                    b.dma_sem = new_sem(f"d_{b.name}")
                    nsem[0] += 1
                b.dma_cnt += 16
                inst.then_inc(b.dma_sem, 16)
                o.tok = (b.dma_sem, b.dma_cnt)
            elif o.has_dep:
                if ecnt[o.eng] >= SEM_ROLL:
                    esem[o.eng] = new_sem(f"e_{o.eng}_{i}")
                    ecnt[o.eng] = 0
                    nsem[0] += 1
                ecnt[o.eng] += 1
                inst.then_inc(esem[o.eng], 1)
                o.tok = (esem[o.eng], ecnt[o.eng])
        for b in self.bufs:
            if b.dma_sem is not None:
                nc.sync.wait_ge(b.dma_sem, b.dma_cnt)
        self.nsem = nsem[0]
        self.es.close()
        return nc


def bf16_np(a):
    return a.astype(ml_dtypes.bfloat16)


D = 2048
DFF = 5632
EPS = 1e-6


def alloc_common(k):
    c = {}
    c["P"] = [k.ps(f"P{i}", [128, 512], F32) for i in range(8)]
    return c


def load_ident(k, c, ident_f_d, ident_b_d):
    c["identf"] = k.sb("identf", [128, 128], F32)
    c["identb"] = k.sb("identb", [128, 128], BF16)
    c["epsc"] = k.sb("epsc", [128, 1], F32)
    k.memset("vector", c["epsc"][:, :], EPS)
    k.dma("sync", out=c["identf"][:, :], in_=ident_f_d[:, :])
    k.dma("sync", out=c["identb"][:, :], in_=ident_b_d[:, :])


def modcols_prepare(k, pfx, modc_d, normg_d, slot):
    raw = k.sb(pfx + "raw", [128, 3, 16], F32)
    ng = k.sb(pfx + "ng", [128, 16], F32)
    A = k.sb(pfx + "A", [128, 16], F32)
    G = k.sb(pfx + "G", [128, 16], F32)
    k.dma("sync", out=raw[:, :, :], in_=modc_d)
    k.dma("sync", out=ng[:, :], in_=normg_d)
    k.stt("vector", out=A[:, :], in0=raw[:, 1, :], scalar=1.0, in1=ng[:, :], op0=ALU.add, op1=ALU.mult)
    k.ts("vector", out=G[:, :], in0=raw[:, 2, :], s1=0.5, s2=None, op0=ALU.mult)
    return A, raw, G


def norm_mod_T(k, c, pfx, x_tile, tr, A, Braw, hT_view, bufs):
    junk, ss, rstd, xh = bufs["junk"], bufs["ss"], bufs["rstd"], bufs["xh"]
    P = c["P"]
    k.memset("vector", ss[:tr, :], 0.0)
    k.act(out=junk[:tr, :], in_=x_tile, func=AF.Square, accum_out=ss[:tr, :])
    k.act(out=rstd[:tr, :], in_=ss[:tr, :], func=AF.Sqrt, scale=1.0 / D, bias=c["epsc"][:tr, :])
    k.I("vector", "reciprocal", out=rstd[:tr, :], in_=rstd[:tr, :])
    k.act(out=xh[:tr, :], in_=x_tile, func=AF.Copy, scale=rstd[:tr, :])
    pb = [P[0].bitcast_view(BF16), P[1].bitcast_view(BF16)]
    for kt in range(16):
        pv = pb[kt // 8]
        k.tr(out=pv[:, (kt % 8) * 128:(kt % 8) * 128 + tr], in_=xh[:tr, kt * 128:(kt + 1) * 128], ident=c["identb"][:tr, :tr])
    for h in range(2):
        pv = pb[h].rearrange("p (k t) -> p k t", t=128)[:, :, :tr]
        a_b = A[:, h * 8:(h + 1) * 8].unsq_bcast(tr)
        b_b = Braw[:, 0, h * 8:(h + 1) * 8].unsq_bcast(tr)
        tmp = bufs["tmpT"]
        k.tt("vector", out=tmp[:, :, :tr], in0=pv, in1=a_b, op=ALU.mult)
        k.tt("gpsimd", out=hT_view[:, h * 8:(h + 1) * 8, :], in0=tmp[:, :, :tr], in1=b_b, op=ALU.add)


def ffn_stage(k, c, pfx, xin_d, xout_d, R, tr, A, Braw, G, w_in_d, w_out_d):
    P = c["P"]
    ntiles = R // tr
    bufs = c.setdefault("nm_bufs", None)
    if bufs is None:
        bufs = c["nm_bufs"] = dict(
            junk=k.sb("junk", [128, 2048], BF16), ss=k.sb("ss", [128, 1], F32), rstd=k.sb("rstd", [128, 1], F32),
            xh=k.sb("xh", [128, 2048], BF16), tmpT=k.sb("tmpT", [128, 8, 128], F32))
    if "ffn_bufs" not in c:
        c["ffn_bufs"] = dict(
            xres=k.sb("xres", [128, 4, 2048], F32), hT=k.sb("hT", [128, 16, 512], BF16), aT=k.sb("aT", [128, 44, 512], BF16),
            wg=[k.sb(f"wg{i}", [128, 16, 256], BF16) for i in range(3)],
            wo=[k.sb(f"wo{i}", [128, 44, 128], BF16) for i in range(3)],
            sg=[k.sb(f"sg{i}", [128, 512], F32) for i in range(2)],
            oTs=[k.sb(f"oTs{i}", [128, 512], F32) for i in range(2)])
    fb = c["ffn_bufs"]
    xres, hT, aT, wg, wo, sg, oTs = fb["xres"], fb["hT"], fb["aT"], fb["wg"], fb["wo"], fb["sg"], fb["oTs"]
    w_in_v = w_in_d.ap().rearrange("(kt p) c -> p kt c", p=128)
    w_out_v = w_out_d.ap().rearrange("(j p) c -> p j c", p=128)
    nblk = (ntiles + 3) // 4
    for blk in range(nblk):
        t0 = blk * 4
        nt = min(4, ntiles - t0)
        nb = nt * tr
        for t in range(nt):
            r0 = (t0 + t) * tr
            k.dma("sync", out=xres[:tr, t, :], in_=xin_d[r0:r0 + tr, :])
            norm_mod_T(k, c, pfx, xres[:tr, t, :], tr, A, Braw, hT[:, :, t * tr:(t + 1) * tr], bufs)

        def load_wg(j):
            b = wg[j % 3]
            k.dma("gpsimd", out=b[:, :, 0:128], in_=w_in_v[:, :, j * 128:(j + 1) * 128])
            k.dma("sync", out=b[:, :, 128:256], in_=w_in_v[:, :, DFF + j * 128:DFF + (j + 1) * 128])

        load_wg(0)
        load_wg(1)
        for j in range(44):
            if j + 2 < 44:
                load_wg(j + 2)
            b = wg[j % 3]
            gp, up = P[2 + 2 * (j % 2)], P[3 + 2 * (j % 2)]
            for kt in range(16):
                k.mm(out=gp[:, :nb], lhsT=b[:, kt, 0:128], rhs=hT[:, kt, :nb], start=(kt == 0), stop=(kt == 15))
            for kt in range(16):
                k.mm(out=up[:, :nb], lhsT=b[:, kt, 128:256], rhs=hT[:, kt, :nb], start=(kt == 0), stop=(kt == 15))
            s = sg[j % 2]
            k.act(out=s[:, :nb], in_=gp[:, :nb], func=AF.Silu)
            k.tt("vector", out=aT[:, j, :nb], in0=s[:, :nb], in1=up[:, :nb], op=ALU.mult)

        def load_wo(m):
            k.dma("gpsimd" if m % 2 == 0 else "sync", out=wo[m % 3][:, :, :], in_=w_out_v[:, :, m * 128:(m + 1) * 128])

        load_wo(0)
        load_wo(1)
        for m in range(16):
            if m + 2 < 16:
                load_wo(m + 2)
            b = wo[m % 3]
            op_ = P[6 + (m % 2)]
            for j in range(44):
                k.mm(out=op_[:, :nb], lhsT=b[:, j, :], rhs=aT[:, j, :nb], start=(j == 0), stop=(j == 43))
            o = oTs[m % 2]
            k.act(out=o[:, :nb], in_=op_[:, :nb], func=AF.Copy, scale=G[:, m:m + 1])
            tb = P[m % 2]
            for t in range(nt):
                k.tr(out=tb[:tr, t * 128:(t + 1) * 128], in_=o[:, t * tr:(t + 1) * tr], ident=c["identf"][:, :])
            k.tt("vector", out=xres[:tr, :nt, m * 128:(m + 1) * 128], in0=xres[:tr, :nt, m * 128:(m + 1) * 128],
                 in1=tb[:tr, :nt * 128].rearrange("p (t f) -> p t f", f=128), op=ALU.add)
        for t in range(nt):
            r0 = (t0 + t) * tr
            k.dma("sync", out=xout_d[r0:r0 + tr, :], in_=xres[:tr, t, :])


def ada_stage(k, c, cc_d, w_d, b_d, out_d):
    P = c["P"]
    cc = k.sb("ada_cc", [128, 16, 2], F32)
    sc = k.sb("ada_sc", [128, 16, 2], F32)
    k.dma("sync", out=cc[:, :, :], in_=cc_d[:, :, :])
    k.act(out=sc[:, :, :], in_=cc[:, :, :], func=AF.Silu)
    wb = [k.sb(f"ada_w{i}", [128, 16, 512], F32) for i in range(2)]
    bb = k.sb("ada_b", [2, 2, 2304], F32)
    ob = k.sb("ada_o", [2, 2, 2304], F32)
    for l in range(2):
        k.dma("sync", out=bb[:, l, :], in_=b_d[l:l + 1, :].pbcast(2) if False else b_d[l, :].pbcast(2))
    blocks = [(0, 512), (512, 512), (1024, 512), (1536, 512), (2048, 256)]
    it = 0
    for l in range(2):
        wv = w_d[l].rearrange("(kt p) c -> p kt c", p=128)
        for (c0, cw) in blocks:
            w = wb[it % 2]
            k.dma("sync" if it % 2 == 0 else "gpsimd", out=w[:, :, :cw], in_=wv[:, :, c0:c0 + cw])
            ps = P[it % 2]
            for kt in range(16):
                k.mm(out=ps[:2, :cw], lhsT=sc[:, kt, :], rhs=w[:, kt, :cw], start=(kt == 0), stop=(kt == 15))
            k.tt("vector", out=ob[:, l, c0:c0 + cw], in0=ps[:2, :cw], in1=bb[:, l, c0:c0 + cw], op=ALU.add)
            it += 1
        k.dma("sync", out=out_d[l], in_=ob[:, l, :])


def rstd_from_ss(k, c, out, ss, dim):
    k.act(out=out, in_=ss, func=AF.Sqrt, scale=1.0 / dim, bias=c["epsc"][:out.ap.shape[0], :])
    k.I("vector", "reciprocal", out=out, in_=out)


def rope_apply(k, eng, dst, src, cos, sin, tmp, tr, nh):
    def tv(i):
        return tmp[:tr, i, :nh * 32].rearrange("p (h a f) -> p h a f", a=2, f=16)
    x1, x2 = src[:, :, :, 0, :], src[:, :, :, 1, :]
    k.tt(eng, out=tv(0), in0=x1, in1=cos, op=ALU.mult)
    k.tt(eng, out=tv(1), in0=x2, in1=sin, op=ALU.mult)
    k.tt(eng, out=dst[:, :, :, 0, :], in0=tv(0), in1=tv(1), op=ALU.subtract)
    k.tt(eng, out=tv(2), in0=x2, in1=cos, op=ALU.mult)
    k.tt(eng, out=tv(3), in0=x1, in1=sin, op=ALU.mult)
    k.tt(eng, out=dst[:, :, :, 1, :], in0=tv(2), in1=tv(3), op=ALU.add)


def evpre_stage(k, c, xin_d, R, tr, A, Braw, w_in_d, wuq_d, wukv_d, gqa_d, gkva_d, gq_d, gk_d, cos_d, sin_d,
                qT_d, kT_d, v_d, u_d, want_q=True, stop=9):
    P = c["P"]
    if c.get("nm_bufs") is None:
        c["nm_bufs"] = dict(
            junk=k.sb("junk", [128, 2048], BF16), ss=k.sb("ss", [128, 1], F32), rstd=k.sb("rstd", [128, 1], F32),
            xh=k.sb("xh", [128, 2048], BF16), tmpT=k.sb("tmpT", [128, 8, 128], F32))
    bufs = c["nm_bufs"]
    win = k.sb("ev_win", [128, 16, 1856], BF16)
    wuq = k.sb("ev_wuq", [128, 4, 1536], BF16)
    wukv = k.sb("ev_wukv", [128, 2, 2048], BF16)
    k.dma("gpsimd", out=win[:, :, :], in_=w_in_d.ap().rearrange("(kt p) c -> p kt c", p=128))
    k.dma("gpsimd", out=wuq[:, :, :], in_=wuq_d.ap().rearrange("(kt p) c -> p kt c", p=128))
    k.dma("gpsimd", out=wukv[:, :, :], in_=wukv_d.ap().rearrange("(kt p) c -> p kt c", p=128))
    gqa = k.sb("ev_gqa", [128, 4], F32)
    gkva = k.sb("ev_gkva", [128, 2], F32)
    gq = k.sb("ev_gq", [128, 192], F32)
    gk = k.sb("ev_gk", [128, 192], F32)
    k.dma("sync", out=gqa[:, :], in_=gqa_d[:, :])
    k.dma("sync", out=gkva[:, :], in_=gkva_d[:, :])
    k.dma("sync", out=gq[:, :], in_=gq_d[:].pbcast(128))
    k.dma("sync", out=gk[:, :], in_=gk_d[:].pbcast(128))
    k.ts("vector", out=gq[:, :], in0=gq[:, :], s1=float(192 ** -0.5), s2=None, op0=ALU.mult)
    xt = [k.sb(f"ev_x{i}", [128, 2048], F32) for i in range(2)]
    hT = k.sb("ev_hT", [128, 16, 128], BF16)
    zs = k.sb("ev_zs", [128, 1856], F32)
    qan = k.sb("ev_qan", [128, 768], BF16)
    qanT = k.sb("ev_qanT", [128, 6, 128], BF16)
    sq = k.sb("ev_sq", [128, 1536], F32)
    ss8 = k.sb("ev_ss8", [128, 8], F32)
    rs8 = k.sb("ev_rs8", [128, 8], F32)
    ss1 = k.sb("ev_ss1", [128, 1], F32)
    rs1 = k.sb("ev_rs1", [128, 1], F32)
    t1 = k.sb("ev_t1", [128, 8, 192], F32)
    t2 = k.sb("ev_t2", [128, 8, 192], F32)
    qf = k.sb("ev_qf", [128, 8, 192], BF16)
    kf = k.sb("ev_kf", [128, 8, 192], BF16)
    vt = k.sb("ev_vt", [128, 8, 128], BF16)
    kpe = k.sb("ev_kpe", [128, 64], F32)
    kpr = k.sb("ev_kpr", [128, 64], F32)
    rtmp = k.sb("ev_rtmp", [128, 4, 256], F32)
    cs = k.sb("ev_cos", [128, 32], F32)
    sn = k.sb("ev_sin", [128, 32], F32)
    oT = [k.sb(f"ev_oT{i}", [128, 8, 2, 128], BF16) for i in range(2)]
    pT = P[7].bitcast_view(BF16)
    idb = c["identb"]

    def rope_views(tile, nh):
        return tile[:tr, :nh, 128:192].rearrange("p h (a b f) -> p h a b f", a=2, b=2)

    def bc_heads(tab, nh):
        a = tab[:tr, :].ap
        return View(tab, bass.AP(a.tensor, a.offset, [list(a.ap[0]), [0, nh], [16, 2], [1, 16]]))

    def heads_T(src, dst_d, it):
        o = oT[it % 2]
        for half in range(2):
            for h4 in range(4):
                h = half * 4 + h4
                k.tr(out=pT[:, h4 * 256:h4 * 256 + tr], in_=src[:tr, h, 0:128], ident=idb[:tr, :tr])
                k.tr(out=pT[:64, h4 * 256 + 128:h4 * 256 + 128 + tr], in_=src[:tr, h, 128:192], ident=idb[:tr, :tr])
            pv = pT.rearrange("p (h a t) -> p h a t", a=2, t=128)
            k.cp("vector", out=o[:, half * 4:half * 4 + 4, 0, :tr], in_=pv[:, :, 0, :tr])
            k.cp("vector", out=o[:64, half * 4:half * 4 + 4, 1, :tr], in_=pv[:64, :, 1, :tr])
        return o

    ntiles = R // tr
    for t in range(ntiles):
        r0 = t * tr
        x = xt[t % 2]
        k.dma("sync", out=x[:tr, :], in_=xin_d[r0:r0 + tr, :])
        k.dma("sync", out=cs[:tr, :], in_=cos_d[r0:r0 + tr, :])
        k.dma("sync", out=sn[:tr, :], in_=sin_d[r0:r0 + tr, :])
        norm_mod_T(k, c, "ev", x[:tr, :], tr, A, Braw, hT[:, :, :tr], bufs)
        zb = [(0, 512), (512, 512), (1024, 512), (1536, 320)]
        for bi, (c0, cw) in enumerate(zb):
            for kt in range(16):
                k.mm(out=P[bi][:tr, :cw], lhsT=hT[:, kt, :tr], rhs=win[:, kt, c0:c0 + cw], start=(kt == 0), stop=(kt == 15))
            k.cp("scalar", out=zs[:tr, c0:c0 + cw], in_=P[bi][:tr, :cw])
        k.dma("sync", out=u_d[r0:r0 + tr, :], in_=zs[:tr, 832:1856])
        if stop <= 1:
            continue
        for (c0, cw, dim) in ((0, 512, 512), (512, 256, 256)):
            k.memset("vector", ss1[:tr, :], 0.0)
            k.act(out=bufs["junk"][:tr, :cw], in_=zs[:tr, c0:c0 + cw], func=AF.Square, accum_out=ss1[:tr, :])
            rstd_from_ss(k, c, rs1[:tr, :], ss1[:tr, :], dim)
            k.act(out=qan[:tr, c0:c0 + cw], in_=zs[:tr, c0:c0 + cw], func=AF.Copy, scale=rs1[:tr, :])
        for kt in range(6):
            k.tr(out=pT[:, kt * 128:kt * 128 + tr], in_=qan[:tr, kt * 128:(kt + 1) * 128], ident=idb[:tr, :tr])
        pv6 = pT[:, :768].rearrange("p (k t) -> p k t", t=128)[:, :, :tr]
        k.tt("vector", out=qanT[:, 0:4, :tr], in0=pv6[:, 0:4, :], in1=gqa[:, :].unsq_bcast(tr), op=ALU.mult)
        k.tt("vector", out=qanT[:, 4:6, :tr], in0=pv6[:, 4:6, :], in1=gkva[:, :].unsq_bcast(tr), op=ALU.mult)
        if stop <= 2:
            continue
        for bi in range(4):
            for kt in range(2):
                k.mm(out=P[bi][:tr, :], lhsT=qanT[:, 4 + kt, :tr], rhs=wukv[:, kt, bi * 512:(bi + 1) * 512], start=(kt == 0), stop=(kt == 1))
        for bi in range(4):
            kvv = P[bi][:tr, :].rearrange("p (h d) -> p h d", d=256)
            if stop != 34:
                k.cp("vector", out=vt[:tr, bi * 2:bi * 2 + 2, :], in_=kvv[:, :, 128:256])
            k.cp("vector", out=t1[:tr, bi * 2:bi * 2 + 2, 0:128], in_=kvv[:, :, 0:128])
        if stop != 33:
            k.dma("sync", out=v_d[r0:r0 + tr, :], in_=vt[:tr, :, :].rearrange("p h d -> p (h d)"))
        if stop <= 3 or stop in (33, 34):
            continue
        k.tt("gpsimd", out=kpe[:tr, :], in0=zs[:tr, 768:832], in1=gk[:tr, 128:192], op=ALU.mult)
        kp5 = kpe[:tr, :].rearrange("p (h a b f) -> p h a b f", h=1, a=2, b=2)
        kr5 = kpr[:tr, :].rearrange("p (h a b f) -> p h a b f", h=1, a=2, b=2)
        rope_apply(k, "gpsimd", kr5, kp5, bc_heads(cs, 1), bc_heads(sn, 1), rtmp, tr, 1)
        if stop <= 4:
            continue
        k.act(out=sq[:tr, :1024].rearrange("p (h d) -> p h d", d=128), in_=t1[:tr, :, 0:128], func=AF.Square)
        k.I("vector", "tensor_reduce", out=ss8[:tr, :], in_=sq[:tr, :1024].rearrange("p (h d) -> p h d", d=128), axis=AX.X, op=ALU.add)
        k.memset("vector", ss1[:tr, :], 0.0)
        k.act(out=bufs["junk"][:tr, :64], in_=zs[:tr, 768:832], func=AF.Square, accum_out=ss1[:tr, :])
        k.ts("vector", out=ss8[:tr, :], in0=ss8[:tr, :], s1=ss1[:tr, :], s2=None, op0=ALU.add)
        rstd_from_ss(k, c, rs8[:tr, :], ss8[:tr, :], 192)
        k.tt("vector", out=t2[:tr, :, 0:128], in0=t1[:tr, :, 0:128], in1=rs8[:tr, :].unsq_bcast(128), op=ALU.mult)
        gkn = View(gk, bass.AP(gk[:tr, 0:128].ap.tensor, gk[:tr, 0:128].ap.offset, [list(gk[:tr, 0:128].ap.ap[0]), [0, 8], [1, 128]]))
        k.tt("gpsimd", out=kf[:tr, :, 0:128], in0=t2[:tr, :, 0:128], in1=gkn, op=ALU.mult)
        kprb = View(kpr, bass.AP(kpr[:tr, :].ap.tensor, kpr[:tr, :].ap.offset, [list(kpr[:tr, :].ap.ap[0]), [0, 8], [1, 64]]))
        k.tt("vector", out=kf[:tr, :, 128:192], in0=kprb, in1=rs8[:tr, :].unsq_bcast(64), op=ALU.mult)
        if stop <= 5:
            continue
        o = heads_T(kf, kT_d, 2 * t)
        k.dma("sync", out=kT_d[:, 0:128, r0:r0 + tr].rearrange("h p t -> p h t"), in_=o[:, :, 0, :tr])
        k.dma("sync", out=kT_d[:, 128:192, r0:r0 + tr].rearrange("h p t -> p h t"), in_=o[:64, :, 1, :tr])
        if want_q and stop > 6:
            for bi in range(3):
                for kt in range(4):
                    k.mm(out=P[4 + bi][:tr, :], lhsT=qanT[:, kt, :tr], rhs=wuq[:, kt, bi * 512:(bi + 1) * 512], start=(kt == 0), stop=(kt == 3))
                k.cp("scalar", out=t1[:tr, :, :].rearrange("p h d -> p (h d)")[:, bi * 512:(bi + 1) * 512], in_=P[4 + bi][:tr, :])
            k.act(out=sq[:tr, :], in_=t1[:tr, :, :].rearrange("p h d -> p (h d)"), func=AF.Square)
            k.I("vector", "tensor_reduce", out=ss8[:tr, :], in_=sq[:tr, :].rearrange("p (h d) -> p h d", d=192), axis=AX.X, op=ALU.add)
            rstd_from_ss(k, c, rs8[:tr, :], ss8[:tr, :], 192)
            k.tt("vector", out=t2[:tr, :, :], in0=t1[:tr, :, :], in1=rs8[:tr, :].unsq_bcast(192), op=ALU.mult)
            gqn = View(gq, bass.AP(gq[:tr, :].ap.tensor, gq[:tr, :].ap.offset, [list(gq[:tr, :].ap.ap[0]), [0, 8], [1, 192]]))
            k.tt("gpsimd", out=t1[:tr, :, :], in0=t2[:tr, :, :], in1=gqn, op=ALU.mult)
            k.cp("scalar", out=qf[:tr, :, 0:128], in_=t1[:tr, :, 0:128])
            rope_apply(k, "vector", rope_views(qf, 8), rope_views(t1, 8), bc_heads(cs, 8), bc_heads(sn, 8), rtmp, tr, 8)
            o = heads_T(qf, qT_d, 2 * t + 1)
            k.dma("sync", out=qT_d[:, 0:128, r0:r0 + tr].rearrange("h p t -> p h t"), in_=o[:, :, 0, :tr])
            k.dma("sync", out=qT_d[:, 128:192, r0:r0 + tr].rearrange("h p t -> p h t"), in_=o[:64, :, 1, :tr])


SEG = 256
NSEG = 65
PI = 3.141592653589793


def sincos(k, dst, src, shift, w1, w2, kint):
    k.ts("vector", out=w1, in0=src, s1=shift + 8 * PI, s2=1.0 / (2 * PI), op0=ALU.add, op1=ALU.mult)
    k.cp("vector", out=kint, in_=w1)
    k.cp("vector", out=w1, in_=kint)
    k.ts("vector", out=w2, in0=src, s1=shift + 8 * PI, s2=None, op0=ALU.add)
    k.stt("vector", out=w1, in0=w1, scalar=-2 * PI, in1=w2, op0=ALU.mult, op1=ALU.add)
    k.ts("vector", out=w2, in0=w1, s1=PI, s2=2 * PI, op0=ALU.is_gt, op1=ALU.mult)
    k.tt("vector", out=w1, in0=w1, in1=w2, op=ALU.subtract)
    k.ts("vector", out=w1, in0=w1, s1=-PI, s2=PI, op0=ALU.max, op1=ALU.min)
    k.act(out=dst, in_=w1, func=AF.Sin)


def s5_stage(k, c, U_d, lamre_d, lamim_d, logdt_d, BTre_d, BTim_d, CTre_d, CTim_d, jtab_d, Y_d, nseg=NSEG):
    P = c["P"]
    S = SEG
    sb = lambda n, s: k.sb("s5_" + n, s, F32)
    lre, lim, ldt = sb("lre", [64, 16]), sb("lim", [64, 16]), sb("ldt", [64, 16])
    BTre, BTim = sb("BTre", [16, 16, 64]), sb("BTim", [16, 16, 64])
    CTre, CTim = sb("CTre", [64, 16, 16]), sb("CTim", [64, 16, 16])
    jt = sb("jt", [64, S + 1])
    for (b, d_) in ((lre, lamre_d), (lim, lamim_d)):
        k.dma("sync", out=b[:, :], in_=d_[:, :])
    k.dma("sync", out=ldt[:, :], in_=logdt_d[:].pbcast(64))
    k.dma("sync", out=BTre[:, :, :], in_=BTre_d[:, :, :])
    k.dma("sync", out=BTim[:, :, :], in_=BTim_d[:, :, :])
    k.dma("sync", out=CTre[:, :, :], in_=CTre_d[:, :, :])
    k.dma("sync", out=CTim[:, :, :], in_=CTim_d[:, :, :])
    k.dma("sync", out=jt[:, :], in_=jtab_d[:, :])
    negpi = sb("negpi", [64, 1])
    k.memset("vector", negpi[:, :], -PI)
    dt, th, mag = sb("dt", [64, 16]), sb("th", [64, 16]), sb("mag", [64, 16])
    k.act(out=dt[:, :], in_=ldt[:, :], func=AF.Exp)
    k.tt("vector", out=th[:, :], in0=lim[:, :], in1=dt[:, :], op=ALU.mult)
    k.tt("vector", out=mag[:, :], in0=lre[:, :], in1=dt[:, :], op=ALU.mult)
    k.act(out=mag[:, :], in_=mag[:, :], func=AF.Exp)
    g = sb("g", [64, 2, 16, S])
    h = sb("h", [64, 2, 16, S])
    ang = g[:, 0, :, :]
    cosT, sinT = sb("cosT", [64, 16, S]), sb("sinT", [64, 16, S])
    tA = sb("tA", [64, 16, S])
    ki = tA.bitcast_view(mybir.dt.int32)
    jv = jt[:, 0:S].ap
    jb = View(jt, bass.AP(jv.tensor, jv.offset, [list(jv.ap[0]), [0, 16], [1, S]]))
    k.tt("vector", out=ang, in0=jb, in1=th[:, :].unsq_bcast(S), op=ALU.mult)

    sincos(k, sinT[:, :, :], ang, 0.0, h[:, 0, :, :], h[:, 1, :, :], ki[:, :, :])
    sincos(k, cosT[:, :, :], ang, PI / 2, h[:, 0, :, :], h[:, 1, :, :], ki[:, :, :])
    psc, pss, pa = sb("psc", [64, 16]), sb("pss", [64, 16]), sb("pa", [64, 16])
    pw1, pw2 = sb("pw1", [64, 16]), sb("pw2", [64, 16])
    k.ts("vector", out=pa[:, :], in0=th[:, :], s1=float(S), s2=None, op0=ALU.mult)
    sincos(k, pss[:, :], pa[:, :], 0.0, pw1[:, :], pw2[:, :], ki[:, :, 0])
    sincos(k, psc[:, :], pa[:, :], PI / 2, pw1[:, :], pw2[:, :], ki[:, :, 0])
    are, aim = sb("are", [64, 16]), sb("aim", [64, 16])
    k.tt("vector", out=are[:, :], in0=mag[:, :], in1=cosT[:, :, 1], op=ALU.mult)
    k.tt("vector", out=aim[:, :], in0=mag[:, :], in1=sinT[:, :, 1], op=ALU.mult)
    den, w1, w2 = sb("den", [64, 16]), sb("w1", [64, 16]), sb("w2", [64, 16])
    kre, kim, nr = sb("kre", [64, 16]), sb("kim", [64, 16]), sb("nr", [64, 16])
    k.tt("vector", out=w1[:, :], in0=lre[:, :], in1=lre[:, :], op=ALU.mult)
    k.tt("vector", out=w2[:, :], in0=lim[:, :], in1=lim[:, :], op=ALU.mult)
    k.tt("vector", out=den[:, :], in0=w1[:, :], in1=w2[:, :], op=ALU.add)
    k.I("vector", "reciprocal", out=den[:, :], in_=den[:, :])
    k.ts("vector", out=nr[:, :], in0=are[:, :], s1=-1.0, s2=None, op0=ALU.add)
    k.tt("vector", out=w1[:, :], in0=nr[:, :], in1=lre[:, :], op=ALU.mult)
    k.tt("vector", out=w2[:, :], in0=aim[:, :], in1=lim[:, :], op=ALU.mult)
    k.tt("vector", out=kre[:, :], in0=w1[:, :], in1=w2[:, :], op=ALU.add)
    k.tt("vector", out=kre[:, :], in0=kre[:, :], in1=den[:, :], op=ALU.mult)
    k.tt("vector", out=w1[:, :], in0=aim[:, :], in1=lre[:, :], op=ALU.mult)
    k.tt("vector", out=w2[:, :], in0=nr[:, :], in1=lim[:, :], op=ALU.mult)
    k.tt("vector", out=kim[:, :], in0=w1[:, :], in1=w2[:, :], op=ALU.subtract)
    k.tt("vector", out=kim[:, :], in0=kim[:, :], in1=den[:, :], op=ALU.mult)
    PhR, PhI, magT = sb("PhR", [64, 16, S]), sb("PhI", [64, 16, S]), sb("magT", [64, 16, S])
    tB = h[:, 0, :, :]
    cS, sS = cosT[:, :, :], sinT[:, :, :]
    k.tt("vector", out=tA[:, :, :], in0=cS, in1=kre[:, :].unsq_bcast(S), op=ALU.mult)
    k.tt("gpsimd", out=tB, in0=sS, in1=kim[:, :].unsq_bcast(S), op=ALU.mult)
    k.tt("vector", out=PhR[:, :, :], in0=tA[:, :, :], in1=tB, op=ALU.add)
    k.tt("vector", out=tA[:, :, :], in0=cS, in1=kim[:, :].unsq_bcast(S), op=ALU.mult)
    k.tt("gpsimd", out=tB, in0=sS, in1=kre[:, :].unsq_bcast(S), op=ALU.mult)
    k.tt("vector", out=PhI[:, :, :], in0=tA[:, :, :], in1=tB, op=ALU.subtract)
    k.memset("vector", magT[:, :, :], 1.0)
    k.tt("vector", out=magT[:, :, :], in0=magT[:, :, :], in1=mag[:, :].unsq_bcast(S), op=ALU.mult)
    nCTim = sb("nCTim", [64, 16, 16])
    k.ts("vector", out=nCTim[:, :, :], in0=CTim[:, :, :], s1=-1.0, s2=None, op0=ALU.mult)
    car = sb("car", [64, 2, 16])
    k.memset("vector", car[:, :, :], 0.0)
    m = [sb(f"m{i}", [64, 2, S]) for i in range(2)]
    t4 = [sb(f"t4_{i}", [64, 4, S]) for i in range(2)]
    Ub = [sb("U0", [16, 16, S])] * 2
    Yb = [sb("Y0", [16, 4, S])] * 2
    cw = sb("cw", [64, 6, 16])
    Yv = Y_d.ap().rearrange("l c t -> c l t")
    for sg in range(nseg):
        U = Ub[sg % 2]
        k.dma("sync", out=U[:, :, :], in_=U_d[:, :, sg * S:(sg + 1) * S])
        def modulate_lane(l):
            ps = P[l % 4]
            k.mm(out=ps[:64, 0:S], lhsT=BTre[:, l, :], rhs=U[:, l, :])
            k.mm(out=ps[:64, S:2 * S], lhsT=BTim[:, l, :], rhs=U[:, l, :])
            bre, bim = ps[:64, 0:S], ps[:64, S:2 * S]
            tt_ = t4[l % 2]
            mm_ = m[l % 2]
            k.tt("vector", out=tt_[:, 0, :], in0=bre, in1=PhR[:, l, :], op=ALU.mult)
            k.tt("vector", out=tt_[:, 1, :], in0=bim, in1=PhI[:, l, :], op=ALU.mult)
            k.tt("vector", out=tt_[:, 2, :], in0=bre, in1=PhI[:, l, :], op=ALU.mult)
            k.tt("vector", out=tt_[:, 3, :], in0=bim, in1=PhR[:, l, :], op=ALU.mult)
            k.tt("gpsimd", out=mm_[:, 0, :], in0=tt_[:, 0, :], in1=tt_[:, 1, :], op=ALU.subtract)
            k.tt("gpsimd", out=mm_[:, 1, :], in0=tt_[:, 2, :], in1=tt_[:, 3, :], op=ALU.add)

        modulate_lane(0)
        for l in range(16):
            if l + 1 < 16:
                modulate_lane(l + 1)
            mm_ = m[l % 2]
            for ri in range(2):
                k.I("vector", "tensor_tensor_scan", out=g[:, ri, l, :], data0=magT[:, l, :], data1=mm_[:, ri, :],
                    initial=car[:, ri, l:l + 1], op0=ALU.mult, op1=ALU.add)
        gl_re, gl_im = g[:, 0, :, S - 1], g[:, 1, :, S - 1]
        pc, psn = psc[:, :], pss[:, :]
        k.tt("vector", out=cw[:, 0, :], in0=gl_re, in1=pc, op=ALU.mult)
        k.tt("vector", out=cw[:, 1, :], in0=gl_im, in1=psn, op=ALU.mult)
        k.tt("vector", out=cw[:, 2, :], in0=gl_re, in1=psn, op=ALU.mult)
        k.tt("vector", out=cw[:, 3, :], in0=gl_im, in1=pc, op=ALU.mult)
        k.tt("vector", out=car[:, 0, :], in0=cw[:, 0, :], in1=cw[:, 1, :], op=ALU.subtract)
        k.tt("vector", out=car[:, 1, :], in0=cw[:, 2, :], in1=cw[:, 3, :], op=ALU.add)
        if sg == 0:
            continue
        k.tt("gpsimd", out=h[:, 0, :, :], in0=g[:, 0, :, :], in1=cS, op=ALU.mult)
        k.tt("vector", out=tA[:, :, :], in0=g[:, 1, :, :], in1=sS, op=ALU.mult)
        k.tt("vector", out=h[:, 0, :, :], in0=h[:, 0, :, :], in1=tA[:, :, :], op=ALU.subtract)
        k.tt("gpsimd", out=h[:, 1, :, :], in0=g[:, 0, :, :], in1=sS, op=ALU.mult)
        k.tt("vector", out=tA[:, :, :], in0=g[:, 1, :, :], in1=cS, op=ALU.mult)
        k.tt("vector", out=h[:, 1, :, :], in0=h[:, 1, :, :], in1=tA[:, :, :], op=ALU.add)
        Y = Yb[sg % 2]
        for l in range(16):
            ps = P[4 + (l // 2) % 4]
            o = ps[:16, (l % 2) * S:(l % 2 + 1) * S]
            k.mm(out=o, lhsT=CTre[:, l, :], rhs=h[:, 0, l, :], start=True, stop=False)
            k.mm(out=o, lhsT=nCTim[:, l, :], rhs=h[:, 1, l, :], start=False, stop=True)
            if l % 2 == 1:
                l8 = (l - 1) % 4
                k.cp("scalar", out=Y[:, l8:l8 + 2, :].rearrange("p l t -> p (l t)"), in_=ps[:16, :])
            if l % 4 == 3:
                k.dma("sync", out=Yv[:, l - 3:l + 1, (sg - 1) * S:sg * S], in_=Y[:, :, :])


def attn_stage(k, c, qT_d, kT_d, v_d, attT_d, R=2048, NK=16640, nheads=8):
    P = c["P"]
    nkt = NK // 128
    ka = [k.sb(f"at_ka{i}", [128, NK], BF16) for i in range(2)]
    kb_lo = k.sb("at_kb", [128, NK], BF16)
    kb_hi = Buf(k, "at_kb_hi", kb_lo.h, "sb")
    k.bufs.append(kb_hi)
    kbs = (kb_lo, kb_hi)
    vv = [k.sb(f"at_v{i}", [128, nkt, 128], BF16) for i in range(2)]
    qa = [k.sb(f"at_qa{i}", [128, R], BF16) for i in range(2)]
    qb_lo = k.sb("at_qb", [128, R], BF16)
    qb_hi = Buf(k, "at_qb_hi", qb_lo.h, "sb")
    k.bufs.append(qb_hi)
    qbs = (qb_lo, qb_hi)
    ones = k.sb("at_ones", [128, 128], BF16)
    k.memset("vector", ones[:, :], 1.0)
    PT = [k.sb(f"at_PT{i}", [128, 512], BF16) for i in range(3)]
    rden = k.sb("at_rden", [128, 512], F32)
    oT = [k.sb(f"at_oT{i}", [128, 512], BF16) for i in range(2)]
    it = 0

    def load_head(h):
        s_ = h % 2
        pr = slice(s_ * 64, s_ * 64 + 64)
        k.dma("sync", out=ka[s_][:, :], in_=kT_d[h, 0:128, :])
        k.dma("sync", out=kbs[s_][pr, :], in_=kT_d[h, 128:192, :])
        vsrc = v_d[:, h * 128:(h + 1) * 128].rearrange("(t p) e -> p t e", p=128)
        hk = nkt // 2
        k.dma("gpsimd", out=vv[s_][:, :hk, :], in_=vsrc[:, :hk, :])
        k.dma("gpsimd", out=vv[s_][:, hk:, :], in_=vsrc[:, hk:, :])
        k.dma("sync", out=qa[s_][:, :], in_=qT_d[h, 0:128, :])
        k.dma("sync", out=qbs[s_][pr, :], in_=qT_d[h, 128:192, :])

    load_head(0)
    for h in range(nheads):
        if h + 1 < nheads:
            load_head(h + 1)
        s_ = h % 2
        pr = slice(s_ * 64, s_ * 64 + 64)
        ka_h, vv_h, qa_h = ka[s_], vv[s_], qa[s_]
        for qb_i in range(R // 512):
            qs = slice(qb_i * 512, (qb_i + 1) * 512)
            Ops, Dps = P[4 + (qb_i % 2)], P[6 + (qb_i % 2)]
            def scores(kt_, it_):
                S_ = P[it_ % 3]
                ks = slice(kt_ * 128, (kt_ + 1) * 128)
                k.mm(out=S_[:, :], lhsT=ka_h[:, ks], rhs=qa_h[:, qs], start=True, stop=False)
                k.mm(out=S_[:, :], lhsT=kbs[s_][pr, ks], rhs=qbs[s_][pr, qs], start=False, stop=True)

            scores(0, it)
            for kt in range(nkt):
                S = P[it % 3]
                pt = PT[it % 3]
                if kt + 1 < nkt:
                    scores(kt + 1, it + 1)
                k.act(out=pt[:, :], in_=S[:, :], func=AF.Exp)
                k.mm(out=Ops[:, :], lhsT=vv_h[:, kt, :], rhs=pt[:, :], start=(kt == 0), stop=(kt == nkt - 1))
                k.mm(out=Dps[:, :], lhsT=ones[:, :], rhs=pt[:, :], start=(kt == 0), stop=(kt == nkt - 1))
                it += 1
            k.I("vector", "reciprocal", out=rden[:, :], in_=Dps[:, :])
            o = oT[qb_i % 2]
            k.tt("vector", out=o[:, :], in0=Ops[:, :], in1=rden[:, :], op=ALU.mult)
            k.dma("sync", out=attT_d[h * 128:(h + 1) * 128, qs], in_=o[:, :])


def evout_stage(k, c, xa_d, u_d, yf_d, yb_d, attT_d, d_d, gluw_d, glub_d, wout_d, gate_d, xm_d, R=2048):
    P = c["P"]
    idb = c["identb"]
    wout = k.sb("eo_wout", [128, 16, 2048], BF16)
    gluw = k.sb("eo_gluw", [128, 8, 1024], BF16)
    k.dma("gpsimd", out=wout[:, :, :], in_=wout_d.ap().rearrange("(kt p) c -> p kt c", p=128))
    k.dma("gpsimd", out=gluw[:, :, :], in_=gluw_d.ap().rearrange("(kt p) c -> p kt c", p=128))
    drow, brow, grow = k.sb("eo_drow", [128, 1024], F32), k.sb("eo_brow", [128, 1024], F32), k.sb("eo_grow", [128, 2048], F32)
    k.dma("sync", out=drow[:, :], in_=d_d[:].pbcast(128))
    k.dma("sync", out=brow[:, :], in_=glub_d[:].pbcast(128))
    k.dma("sync", out=grow[:, :], in_=gate_d[:].pbcast(128))
    ut, yft, ybt = k.sb("eo_u", [128, 1024], F32), k.sb("eo_yf", [128, 1024], F32), k.sb("eo_yb", [128, 1024], F32)
    y, w1, ge = k.sb("eo_y", [128, 1024], F32), k.sb("eo_w1", [128, 1024], F32), k.sb("eo_ge", [128, 1024], F32)
    geb, ssb = k.sb("eo_geb", [128, 1024], BF16), k.sb("eo_ssb", [128, 1024], BF16)
    geT, ssT = k.sb("eo_geT", [128, 8, 128], BF16), k.sb("eo_ssT", [128, 8, 128], BF16)
    att = k.sb("eo_att", [128, 8, 128], BF16)
    xt = k.sb("eo_x", [128, 2048], F32)
    pT = P[7].bitcast_view(BF16)
    attv = attT_d.ap().rearrange("(kt p) t -> p kt t", p=128)
    for t in range(R // 128):
        rs = slice(t * 128, (t + 1) * 128)
        k.dma("sync", out=ut[:, :], in_=u_d[rs, :])
        k.dma("sync", out=yft[:, :], in_=yf_d[rs, :])
        k.dma("sync", out=ybt[:, :], in_=yb_d[rs, :])
        k.dma("sync", out=xt[:, :], in_=xa_d[rs, :])
        k.dma("sync", out=att[:, :, :], in_=attv[:, :, rs])
        k.tt("vector", out=y[:, :], in0=ut[:, :], in1=drow[:, :], op=ALU.mult)
        k.tt("gpsimd", out=w1[:, :], in0=yft[:, :], in1=ybt[:, :], op=ALU.add)
        k.tt("vector", out=y[:, :], in0=y[:, :], in1=w1[:, :], op=ALU.add)
        k.tt("gpsimd", out=w1[:, :], in0=y[:, :], in1=y[:, :], op=ALU.mult)
        k.ts("vector", out=w1[:, :], in0=w1[:, :], s1=0.044715, s2=1.0, op0=ALU.mult, op1=ALU.add)
        k.tt("vector", out=w1[:, :], in0=w1[:, :], in1=y[:, :], op=ALU.mult)
        k.act(out=w1[:, :], in_=w1[:, :], func=AF.Sigmoid, scale=1.5957691216057308)
        k.tt("vector", out=ge[:, :], in0=y[:, :], in1=w1[:, :], op=ALU.mult)
        k.cp("gpsimd", out=geb[:, :], in_=ge[:, :])
        for kt in range(8):
            k.tr(out=pT[:, kt * 128:(kt + 1) * 128], in_=geb[:, kt * 128:(kt + 1) * 128], ident=idb[:, :])
        k.cp("vector", out=geT[:, :, :].rearrange("p k t -> p (k t)"), in_=pT[:, :])
        for bi in range(2):
            for kt in range(8):
                k.mm(out=P[bi][:, :], lhsT=geT[:, kt, :], rhs=gluw[:, kt, bi * 512:(bi + 1) * 512], start=(kt == 0), stop=(kt == 7))
            k.tt("vector", out=w1[:, bi * 512:(bi + 1) * 512], in0=P[bi][:, :], in1=brow[:, bi * 512:(bi + 1) * 512], op=ALU.add)
        k.act(out=w1[:, :], in_=w1[:, :], func=AF.Sigmoid)
        k.tt("vector", out=ssb[:, :], in0=ge[:, :], in1=w1[:, :], op=ALU.mult)
        for kt in range(8):
            k.tr(out=pT[:, kt * 128:(kt + 1) * 128], in_=ssb[:, kt * 128:(kt + 1) * 128], ident=idb[:, :])
        k.cp("vector", out=ssT[:, :, :].rearrange("p k t -> p (k t)"), in_=pT[:, :])
        for bi in range(4):
            ps = P[2 + bi]
            for kt in range(16):
                lt = att[:, kt, :] if kt < 8 else ssT[:, kt - 8, :]
                k.mm(out=ps[:, :], lhsT=lt, rhs=wout[:, kt, bi * 512:(bi + 1) * 512], start=(kt == 0), stop=(kt == 15))
            cs_ = slice(bi * 512, (bi + 1) * 512)
            k.tt("vector", out=y[:, 0:512], in0=ps[:, :], in1=grow[:, cs_], op=ALU.mult)
            k.tt("vector", out=xt[:, cs_], in0=xt[:, cs_], in1=y[:, 0:512], op=ALU.add)
        k.dma("sync", out=xm_d[rs, :], in_=xt[:, :])


def hypre_stage(k, c, xh_d, hmask_d, A, Braw, w_in_d, cw_d, cb_d, zc_d):
    P = c["P"]
    if c.get("nm_bufs") is None:
        c["nm_bufs"] = dict(
            junk=k.sb("junk", [128, 2048], BF16), ss=k.sb("ss", [128, 1], F32), rstd=k.sb("rstd", [128, 1], F32),
            xh=k.sb("xh", [128, 2048], BF16), tmpT=k.sb("tmpT", [128, 8, 128], F32))
    bufs = c["nm_bufs"]
    hT = k.sb("hp_hT", [128, 16, 2050], BF16)
    hTh = k.sb("hp_hTh", [128, 16, 2], BF16)
    hm = k.sb("hp_hm", [128, 2], F32)
    cw = k.sb("hp_cw", [128, 48, 3], F32)
    cb = k.sb("hp_cb", [128, 48], F32)
    k.dma("sync", out=hm[:, :], in_=hmask_d[:, :])
    k.dma("sync", out=cw[:, :, :], in_=cw_d[:, :, :])
    k.dma("sync", out=cb[:, :], in_=cb_d[:, :])
    xt = [k.sb(f"hp_x{i}", [128, 2048], F32) for i in range(2)]
    for t in range(16):
        x = xt[t % 2]
        k.dma("sync", out=x[:, :], in_=xh_d[1 + t * 128:1 + (t + 1) * 128, :])
        norm_mod_T(k, c, "hp", x[:, :], 128, A, Braw, hT[:, :, 1 + t * 128:1 + (t + 1) * 128], bufs)
    xhalo = k.sb("hp_xhalo", [2, 2048], F32)
    k.dma("sync", out=xhalo[0:1, :], in_=xh_d[0:1, :])
    k.dma("sync", out=xhalo[1:2, :], in_=xh_d[2049:2050, :])
    norm_mod_T(k, c, "hp", xhalo[:2, :], 2, A, Braw, hTh[:, :, :], bufs)
    k.ts("vector", out=hT[:, :, 0:1], in0=hTh[:, :, 0:1], s1=hm[:, 0:1], s2=None, op0=ALU.mult)
    k.ts("vector", out=hT[:, :, 2049:2050], in0=hTh[:, :, 1:2], s1=hm[:, 1:2], s2=None, op0=ALU.mult)
    wv = w_in_d.ap().rearrange("(kt p) c -> p kt c", p=128)
    wj = [k.sb(f"hp_w{i}", [128, 16, 128], BF16) for i in range(2)]
    osb = [k.sb(f"hp_o{i}", [128, 512], F32) for i in range(2)]
    k.dma("gpsimd", out=wj[0][:, :, :], in_=wv[:, :, 0:128])
    it = 0
    for j in range(48):
        if j + 1 < 48:
            k.dma("gpsimd", out=wj[(j + 1) % 2][:, :, :], in_=wv[:, :, (j + 1) * 128:(j + 2) * 128])
        w = wj[j % 2]
        for b in range(4):
            zp = P[b]
            zh = P[4 + b]
            for kt in range(16):
                k.mm(out=zp[:, :], lhsT=w[:, kt, :], rhs=hT[:, kt, 1 + 512 * b:1 + 512 * (b + 1)], start=(kt == 0), stop=(kt == 15))
            for kt in range(16):
                k.mm(out=zh[:, 0:2], lhsT=w[:, kt, :], rhs=hT[:, kt, 512 * b:512 * b + 514:513], start=(kt == 0), stop=(kt == 15))
            o = osb[it % 2]
            w0, w1, w2 = cw[:, j, 0:1], cw[:, j, 1:2], cw[:, j, 2:3]
            k.ts("vector", out=o[:, :], in0=zp[:, :], s1=w1, s2=cb[:, j:j + 1], op0=ALU.mult, op1=ALU.add)
            k.stt("vector", out=o[:, 1:512], in0=zp[:, 0:511], scalar=w0, in1=o[:, 1:512], op0=ALU.mult, op1=ALU.add)
            k.stt("vector", out=o[:, 0:511], in0=zp[:, 1:512], scalar=w2, in1=o[:, 0:511], op0=ALU.mult, op1=ALU.add)
            k.stt("vector", out=o[:, 0:1], in0=zh[:, 0:1], scalar=w0, in1=o[:, 0:1], op0=ALU.mult, op1=ALU.add)
            k.stt("vector", out=o[:, 511:512], in0=zh[:, 1:2], scalar=w2, in1=o[:, 511:512], op0=ALU.mult, op1=ALU.add)
            k.dma("sync", out=zc_d[j * 128:(j + 1) * 128, 512 * b:512 * (b + 1)], in_=o[:, :])
            it += 1


def projres_stage(k, c, xin_d, yT_d, wout_d, gate_d, xm_d, R=2048):
    P = c["P"]
    wout = k.sb("pr_wout", [128, 16, 2048], BF16)
    yT = k.sb("pr_yT", [128, 16, R], BF16)
    k.dma("gpsimd", out=wout[:, :, :], in_=wout_d.ap().rearrange("(kt p) c -> p kt c", p=128))
    yv = yT_d.ap().rearrange("(kt p) t -> p kt t", p=128)
    for kt in range(16):
        k.dma("gpsimd", out=yT[:, kt, :], in_=yv[:, kt, :])
    grow = k.sb("pr_grow", [128, 2048], F32)
    k.dma("sync", out=grow[:, :], in_=gate_d[:].pbcast(128))
    xt = [k.sb(f"pr_x{i}", [128, 2048], F32) for i in range(2)]
    tmp = k.sb("pr_tmp", [128, 512], F32)
    for t in range(R // 128):
        rs = slice(t * 128, (t + 1) * 128)
        x = xt[t % 2]
        k.dma("sync", out=x[:, :], in_=xin_d[rs, :])
        for bi in range(4):
            ps = P[(t % 2) * 4 + bi]
            for kt in range(16):
                k.mm(out=ps[:, :], lhsT=yT[:, kt, rs], rhs=wout[:, kt, bi * 512:(bi + 1) * 512], start=(kt == 0), stop=(kt == 15))
            cs_ = slice(bi * 512, (bi + 1) * 512)
            k.tt("vector", out=tmp[:, :], in0=ps[:, :], in1=grow[:, cs_], op=ALU.mult)
            k.tt("vector", out=x[:, cs_], in0=x[:, cs_], in1=tmp[:, :], op=ALU.add)
        k.dma("sync", out=xm_d[rs, :], in_=x[:, :])


NFFT = 32768
LSEQ = 16384
CB = 16


def hy_consts():
    f8 = np.float64
    ts = np.arange(64, dtype=f8)[:, None]
    kf = np.arange(128, dtype=f8)[None, :]
    a1 = 2 * np.pi * ts * kf / 128
    F1cat = np.concatenate([np.cos(a1), -np.sin(a1)], axis=1)
    tf = np.arange(256, dtype=f8)[:, None]
    aw = 2 * np.pi * tf * kf / NFFT
    Wre = np.cos(aw).reshape(2, 128, 128).transpose(1, 0, 2)
    Wim = (-np.sin(aw)).reshape(2, 128, 128).transpose(1, 0, 2)
    ks = np.arange(256, dtype=f8)[None, :]
    a2 = 2 * np.pi * tf * ks / 256
    Cos = np.cos(a2).reshape(2, 128, 256).transpose(1, 0, 2)
    Sin = np.sin(a2).reshape(2, 128, 256).transpose(1, 0, 2)
    a2t = a2.T
    cA1 = np.concatenate([np.cos(a2t), np.sin(a2t)], axis=1).reshape(2, 128, 512).transpose(1, 0, 2)
    cA2 = np.concatenate([-np.sin(a2t), np.cos(a2t)], axis=1).reshape(2, 128, 512).transpose(1, 0, 2)
    awt = aw.T
    WTre, WTim = np.cos(awt), np.sin(awt)
    a1t = a1.T
    C1 = np.cos(a1t) / NFFT
    S1n = -np.sin(a1t) / NFFT
    b = lambda a: np.ascontiguousarray(a).astype(ml_dtypes.bfloat16)
    f = lambda a: np.ascontiguousarray(a, dtype=np.float32)
    return dict(F1cat=b(F1cat), Wre=f(Wre), Wim=f(Wim), Cos=b(Cos), Sin=b(Sin), NSin=b(-Sin), cA1=b(cA1), cA2=b(cA2),
                WTre=f(WTre), WTim=f(WTim), C1=b(C1), S1n=b(S1n))


HY_CONST_SHAPES = dict(F1cat=([64, 256], BF16), Wre=([128, 2, 128], F32), Wim=([128, 2, 128], F32), Cos=([128, 2, 256], BF16),
                       Sin=([128, 2, 256], BF16), NSin=([128, 2, 256], BF16), cA1=([128, 2, 512], BF16), cA2=([128, 2, 512], BF16),
                       WTre=([128, 256], F32), WTim=([128, 256], F32), C1=([128, 64], BF16), S1n=([128, 64], BF16))


def hy_load_consts(k, cd):
    out = {}
    for n, (shp, dt) in HY_CONST_SHAPES.items():
        b = k.sb("hc_" + n, shp, dt)
        src = cd[n]
        k.dma("sync", out=b.ap(), in_=src.ap())
        out[n] = b
    return out


def fft_stageA(k, c, hc, ybf, nch, Ab_re, Ab_im, tw, it0=0):
    P = c["P"]
    it = it0
    for cp in range(nch // 2):
        for half in range(2):
            ps = P[it % 2]
            for ci in range(2):
                ch = cp * 2 + ci
                k.mm(out=ps[:, ci * 256:(ci + 1) * 256], lhsT=ybf[:, ch, half * 128:(half + 1) * 128], rhs=hc["F1cat"][:, :])
            pv = ps[:, :].rearrange("p (c r f) -> p c r f", c=2, r=2)
            Are, Aim = pv[:, :, 0, :], pv[:, :, 1, :]
            wre = hc["Wre"][:, half, :].ap
            wim = hc["Wim"][:, half, :].ap
            wre_b = View(hc["Wre"], bass.AP(wre.tensor, wre.offset, [list(wre.ap[0]), [0, 2], [1, 128]]))
            wim_b = View(hc["Wim"], bass.AP(wim.tensor, wim.offset, [list(wim.ap[0]), [0, 2], [1, 128]]))
            t = tw[it % 2]
            tv = lambda i: t[:, i, 0:256].rearrange("p (c f) -> p c f", c=2)
            k.tt("vector", out=tv(0), in0=Are, in1=wre_b, op=ALU.mult)
            k.tt("vector", out=tv(1), in0=Aim, in1=wim_b, op=ALU.mult)
            k.tt("vector", out=tv(2), in0=Are, in1=wim_b, op=ALU.mult)
            k.tt("vector", out=tv(3), in0=Aim, in1=wre_b, op=ALU.mult)
            k.tt("gpsimd", out=Ab_re[:, half, cp * 2:cp * 2 + 2, :], in0=tv(0), in1=tv(1), op=ALU.subtract)
            k.tt("gpsimd", out=Ab_im[:, half, cp * 2:cp * 2 + 2, :], in0=tv(2), in1=tv(3), op=ALU.add)
            it += 1
    return it


def fft_stageB_block(k, c, hc, Ab_re, Ab_im, hk, blk, want_re=True, want_im=True):
    P = c["P"]
    cs = slice(blk * 4, blk * 4 + 4)
    ksl = slice(hk * 128, (hk + 1) * 128)
    Xre, Xim = P[2 + (blk % 2) * 2], P[3 + (blk % 2) * 2]
    if want_re:
        n = 0
        for ht in range(2):
            for (lt, rb) in ((hc["Cos"], Ab_re), (hc["Sin"], Ab_im)):
                k.mm(out=Xre[:, :], lhsT=lt[:, ht, ksl], rhs=rb[:, ht, cs, :].rearrange("p c f -> p (c f)"), start=(n == 0), stop=(n == 3))
                n += 1
    if want_im:
        n = 0
        for ht in range(2):
            for (lt, rb) in ((hc["NSin"], Ab_re), (hc["Cos"], Ab_im)):
                k.mm(out=Xim[:, :], lhsT=lt[:, ht, ksl], rhs=rb[:, ht, cs, :].rearrange("p c f -> p (c f)"), start=(n == 0), stop=(n == 3))
                n += 1
    return Xre, Xim


def hyconv_stage(k, c, cd, x1_d, x2_d, v_d, skip_d, Gre_d, Gim_d, y2_d, ncb=16):
    P = c["P"]
    hc = hy_load_consts(k, cd)
    sk = k.sb("hv_sk", [64, 2, 256], F32)
    for o in range(2):
        k.dma("sync", out=sk[:, o, :], in_=skip_d[o, :].pbcast(64))
    yf = k.sb("hv_yf", [64, CB, 256], F32)
    gt = k.sb("hv_gt", [64, CB, 256], F32)
    ybf = k.sb("hv_ybf", [64, CB, 256], BF16)
    Ab_re, Ab_im = k.sb("hv_Abre", [128, 2, CB, 128], BF16), k.sb("hv_Abim", [128, 2, CB, 128], BF16)
    Zb_re, Zb_im = k.sb("hv_Zbre", [128, 2, CB, 128], BF16), k.sb("hv_Zbim", [128, 2, CB, 128], BF16)
    Bb_re, Bb_im = k.sb("hv_Bbre", [128, CB, 256], BF16), k.sb("hv_Bbim", [128, CB, 256], BF16)
    tw = [k.sb(f"hv_tw{i}", [128, 4, 512], F32) for i in range(2)]
    Gt = [(k.sb(f"hv_Gre{i}", [128, 512], F32), k.sb(f"hv_Gim{i}", [128, 512], F32)) for i in range(2)]
    wsk = k.sb("hv_wsk", [64, 2, 256], F32)
    w2 = k.sb("hv_w2", [64, 2, 256], F32)
    gates = (x1_d, x2_d)
    it = 0
    for cb in range(ncb):
        chs = slice(cb * CB, (cb + 1) * CB)
        k.dma("sync", out=yf[:, :, :], in_=v_d[chs, :].rearrange("c (s f) -> s c f", s=64))
        for o in range(2):
            k.dma("sync", out=gt[:, :, :], in_=gates[o][chs, :].rearrange("c (s f) -> s c f", s=64))
            k.cp("scalar", out=ybf[:, :, :].rearrange("p c f -> p (c f)"), in_=yf[:, :, :].rearrange("p c f -> p (c f)"))
            it = fft_stageA(k, c, hc, ybf, CB, Ab_re, Ab_im, tw, it)
            for hk in range(2):
                for blk in range(CB // 4):
                    Xre, Xim = fft_stageB_block(k, c, hc, Ab_re, Ab_im, hk, blk)
                    gre, gim = Gt[it % 2]
                    c0 = (cb * CB + blk * 4) * 128
                    k.dma("sync", out=gre[:, :], in_=Gre_d[o, hk, :, c0:c0 + 512])
                    k.dma("sync", out=gim[:, :], in_=Gim_d[o, hk, :, c0:c0 + 512])
                    t = tw[it % 2]
                    k.tt("vector", out=t[:, 0, :], in0=Xre[:, :], in1=gre[:, :], op=ALU.mult)
                    k.tt("vector", out=t[:, 1, :], in0=Xim[:, :], in1=gim[:, :], op=ALU.mult)
                    k.tt("vector", out=t[:, 2, :], in0=Xre[:, :], in1=gim[:, :], op=ALU.mult)
                    k.tt("vector", out=t[:, 3, :], in0=Xim[:, :], in1=gre[:, :], op=ALU.mult)
                    zs = slice(blk * 4, blk * 4 + 4)
                    k.tt("gpsimd", out=Zb_re[:, hk, zs, :].rearrange("p c f -> p (c f)"), in0=t[:, 0, :], in1=t[:, 1, :], op=ALU.subtract)
                    k.tt("gpsimd", out=Zb_im[:, hk, zs, :].rearrange("p c f -> p (c f)"), in0=t[:, 2, :], in1=t[:, 3, :], op=ALU.add)
                    it += 1
            for ch in range(CB):
                ps = P[6 + (ch % 2)]
                n = 0
                for hk in range(2):
                    for (zb, rt) in ((Zb_re, hc["cA1"]), (Zb_im, hc["cA2"])):
                        k.mm(out=ps[:, :], lhsT=zb[:, hk, ch, :], rhs=rt[:, hk, :], start=(n == 0), stop=(n == 3))
                        n += 1
                Bre, Bim = ps[:, 0:256], ps[:, 256:512]
                t = tw[it % 2]
                k.tt("vector", out=t[:, 0, 0:256], in0=Bre, in1=hc["WTre"][:, :], op=ALU.mult)
                k.tt("vector", out=t[:, 1, 0:256], in0=Bim, in1=hc["WTim"][:, :], op=ALU.mult)
                k.tt("vector", out=t[:, 2, 0:256], in0=Bre, in1=hc["WTim"][:, :], op=ALU.mult)
                k.tt("vector", out=t[:, 3, 0:256], in0=Bim, in1=hc["WTre"][:, :], op=ALU.mult)
                k.tt("gpsimd", out=Bb_re[:, ch, :], in0=t[:, 0, 0:256], in1=t[:, 1, 0:256], op=ALU.subtract)
                k.tt("gpsimd", out=Bb_im[:, ch, :], in0=t[:, 2, 0:256], in1=t[:, 3, 0:256], op=ALU.add)
                it += 1
            for pr in range(CB // 2):
                ps = P[pr % 2]
                cs2 = slice(pr * 2, pr * 2 + 2)
                k.mm(out=ps[:64, :], lhsT=hc["C1"][:, :], rhs=Bb_re[:, cs2, :].rearrange("p c f -> p (c f)"), start=True, stop=False)
                k.mm(out=ps[:64, :], lhsT=hc["S1n"][:, :], rhs=Bb_im[:, cs2, :].rearrange("p c f -> p (c f)"), start=False, stop=True)
                skb = sk[:, o, cb * CB + pr * 2:cb * CB + pr * 2 + 2].unsq_bcast(256)
                k.tt("gpsimd", out=wsk[:, :, :], in0=yf[:, cs2, :], in1=skb, op=ALU.mult)
                k.tt("vector", out=w2[:, :, :], in0=ps[:64, :].rearrange("p (c f) -> p c f", c=2), in1=wsk[:, :, :], op=ALU.add)
                k.tt("gpsimd", out=yf[:, cs2, :], in0=w2[:, :, :], in1=gt[:, cs2, :], op=ALU.mult)
        k.dma("sync", out=y2_d[chs, :].rearrange("c (s f) -> s c f", s=64), in_=yf[:, :, :])


def hy_filt_consts(ci):
    import math
    L = LSEQ
    t = np.arange(L, dtype=np.float32) / np.float32(L)
    bands = np.linspace(1e-4, 15, 16, dtype=np.float32)
    ang = (np.float32(2.0 * math.pi) * t[:, None] * bands[None, :]).astype(np.float32)
    feat = np.concatenate([t[:, None], np.cos(ang), -np.sin(ang)], axis=-1).astype(np.float32)
    dmin, dmax = math.log(1e-2) / 1.5, math.log(1e-2) / 0.3
    deltas = np.abs(np.linspace(dmin, dmax, 2048, dtype=np.float32))[ci * 256:(ci + 1) * 256].astype(np.float64)
    drow = np.broadcast_to(deltas[None, :], (64, 256))
    E1 = np.exp(-(256.0 * np.arange(64, dtype=np.float64)[:, None] / L) * deltas[None, :])
    tfrow = np.broadcast_to(np.arange(256, dtype=np.float32)[None, :], (64, 256))
    return dict(featT=np.ascontiguousarray(feat.T), drow=np.ascontiguousarray(drow, dtype=np.float32), E1=np.ascontiguousarray(E1, dtype=np.float32),
                tfrow=np.ascontiguousarray(tfrow))


def hyfilt_stage(k, c, cd, featT_d, w1_d, w2_d, w3_d, bf_d, wout_d, drow_d, E1_d, tfrow_d, Gre_d, Gim_d, nfb=16, norders=2):
    P = c["P"]
    hc = {}
    for n in ("F1cat", "Wre", "Wim", "Cos", "Sin", "NSin"):
        shp, dt = HY_CONST_SHAPES[n]
        hc[n] = k.sb("hc_" + n, shp, dt)
        k.dma("sync", out=hc[n].ap(), in_=cd[n].ap())
    sb = lambda n, s, dt=F32: k.sb("hf_" + n, s, dt)
    w1s, w2s, w3s, bf = sb("w1", [33, 64]), sb("w2", [64, 64]), sb("w3", [64, 64]), sb("bf", [64, 4])
    k.dma("sync", out=w1s[:, :], in_=w1_d[:, :])
    k.dma("sync", out=w2s[:, :], in_=w2_d[:, :])
    k.dma("sync", out=w3s[:, :], in_=w3_d[:, :])
    k.dma("sync", out=bf[:, :], in_=bf_d[:, :])
    frb = sb("frb", [64, 3])
    for l in range(3):
        k.tt("vector", out=frb[:, l:l + 1], in0=bf[:, l:l + 1], in1=bf[:, 3:4], op=ALU.mult)
    drow, E1 = sb("drow", [64, 256]), sb("E1", [64, 256])
    k.dma("sync", out=drow[:, :], in_=drow_d[:, :])
    k.dma("sync", out=E1[:, :], in_=E1_d[:, :])
    hidT = sb("hidT", [64, LSEQ])
    ft = [sb(f"ft{i}", [33, 512]) for i in range(2)]
    a_, s1_, s2_ = sb("a", [64, 512]), sb("s1", [64, 512]), sb("s2", [64, 512])
    ki = k.sb("hf_ki", [64, 512], mybir.dt.int32)
    hA, hB = sb("hA", [64, 512]), sb("hB", [64, 512])
    ws = (w1s, w2s, w3s)
    for blk in range(LSEQ // 512):
        f = ft[blk % 2]
        k.dma("sync", out=f[:, :], in_=featT_d[:, blk * 512:(blk + 1) * 512])
        src = f[:, :]
        dsts = (hA[:, :], hB[:, :], hidT[:, blk * 512:(blk + 1) * 512])
        for l in range(3):
            ps = P[(blk * 3 + l) % 2]
            k.mm(out=ps[:64, :], lhsT=ws[l][:, :], rhs=src)
            k.ts("vector", out=a_[:, :], in0=ps[:64, :], s1=bf[:, 3:4], s2=frb[:, l:l + 1], op0=ALU.mult, op1=ALU.add)
            sincos(k, dsts[l], a_[:, :], 0.0, s1_[:, :], s2_[:, :], ki[:, :])
            src = dsts[l]
    ones = sb("ones", [64, 128])
    k.memset("vector", ones[:, :], 1.0)
    Wo = [sb(f"Wo{i}", [64, 2, 16]) for i in range(2)]
    Wn = sb("Wn", [64, 16, 256])
    tfrow = sb("tfrow", [64, 256])
    k.dma("sync", out=tfrow[:, :], in_=tfrow_d[:, :])
    Hf = sb("Hf", [64, 2, 16, 256])
    Hsd = k.sb("hf_Hsd", [64, 2, 16, 256], BF16)
    absum, s2n, rn = sb("absum", [64, 32]), sb("s2n", [64, 16]), sb("rn", [128, 16])
    Ab_re, Ab_im = k.sb("hf_Abre", [128, 2, 16, 128], BF16), k.sb("hf_Abim", [128, 2, 16, 128], BF16)
    tw = [sb(f"tw{i}", [128, 4, 512]) for i in range(2)]
    go = [sb(f"go{i}", [128, 512]) for i in range(2)]
    it = 0
    ig = 0
    for o in range(norders):
        for fb in range(nfb):
            chs = slice(fb * 16, (fb + 1) * 16)
            wo = Wo[fb % 2]
            k.dma("sync", out=wo[:, :, :], in_=wout_d[:, o, :, chs])
            dv = drow[:, chs].ap
            dr_b = View(drow, bass.AP(dv.tensor, dv.offset, [list(dv.ap[0]), [1, 16], [0, 256]]))
            tv_ = tfrow[:, :].ap
            tf_b = View(tfrow, bass.AP(tv_.tensor, tv_.offset, [list(tv_.ap[0]), [0, 16], [1, 256]]))
            k.tt("gpsimd", out=Wn[:, :, :], in0=dr_b, in1=tf_b, op=ALU.mult)
            k.act(out=Wn[:, :, :], in_=Wn[:, :, :], func=AF.Exp, scale=-1.0 / LSEQ)
            k.tt("gpsimd", out=Wn[:, :, :], in0=Wn[:, :, :], in1=E1[:, chs].unsq_bcast(256), op=ALU.mult)
            wflat = wo[:, :, :].rearrange("p d c -> p (d c)")
            for tg in range(32):
                ps = P[2 + tg % 2]
                for j in range(8):
                    tfv = tg * 8 + j
                    k.mm(out=ps[:64, j * 32:(j + 1) * 32], lhsT=hidT[:, tfv:LSEQ:256], rhs=wflat)
                wv_ = Wn[:, :, tg * 8:(tg + 1) * 8].ap
                Wn_b = View(Wn, bass.AP(wv_.tensor, wv_.offset, [list(wv_.ap[0]), [0, 2], [256, 16], [1, 8]]))
                k.tt("vector", out=Hf[:, :, :, tg * 8:(tg + 1) * 8], in0=ps[:64, 0:256].rearrange("p (j d c) -> p d c j", j=8, d=2),
                     in1=Wn_b, op=ALU.mult)
            k.memset("vector", Hf[0:1, 1, :, 0:1], 0.0)
            k.I("vector", "tensor_reduce", out=absum[:, :], in_=Hf[:, :, :, :].rearrange("p d c f -> p (d c) f"), axis=AX.X, op=ALU.add,
                apply_absolute_value=True)
            k.tt("vector", out=s2n[:, :], in0=absum[:, 0:16], in1=absum[:, 16:32], op=ALU.add)
            psn = P[4]
            k.mm(out=psn[:, 0:16], lhsT=ones[:, :], rhs=s2n[:, :])
            k.I("vector", "reciprocal", out=rn[:, :], in_=psn[:, 0:16])
            k.tt("gpsimd", out=Hsd[:, 0, :, :], in0=Hf[:, 0, :, :], in1=Hf[:, 1, :, :], op=ALU.add)
            k.tt("gpsimd", out=Hsd[:, 1, :, :], in0=Hf[:, 0, :, :], in1=Hf[:, 1, :, :], op=ALU.subtract)
            for sd in range(2):
                it = fft_stageA(k, c, hc, Hsd[:, sd, :, :], 16, Ab_re, Ab_im, tw, it)
                for hk in range(2):
                    for blk in range(4):
                        Xre, Xim = fft_stageB_block(k, c, hc, Ab_re, Ab_im, hk, blk, want_re=(sd == 0), want_im=(sd == 1))
                        X = Xre if sd == 0 else Xim
                        g_ = go[ig % 2]
                        k.tt("vector", out=g_[:, :].rearrange("p (c f) -> p c f", c=4), in0=X[:, :].rearrange("p (c f) -> p c f", c=4),
                             in1=rn[:, blk * 4:blk * 4 + 4].unsq_bcast(128), op=ALU.mult)
                        c0 = (fb * 16 + blk * 4) * 128
                        dst = Gre_d if sd == 0 else Gim_d
                        k.dma("sync", out=dst[o, hk, :, c0:c0 + 512], in_=g_[:, :])
                        ig += 1


def wprep_stage(k, c, win_d, wout_d, winb_d, woutb_d, nf=4):
    bufs = [k.sb(f"wp_b{i}", [128, 5632], BF16) for i in range(3)]
    it = 0
    for f in range(nf):
        for (src, dst, n) in ((win_d, winb_d, 22528), (wout_d, woutb_d, 11264)):
            for c0 in range(0, n, 5632):
                b = bufs[it % 3]
                k.dma("gpsimd", out=b[:, :], in_=src[f, :, c0:c0 + 5632])
                k.dma("sync", out=dst[f, :, c0:c0 + 5632], in_=b[:, :])
                it += 1


def _cols(v, n=16):
    return np.ascontiguousarray(np.asarray(v, np.float32).reshape(n, 128).T)


def _modc(m, k0):
    return np.ascontiguousarray(np.stack([_cols(m[k0]), _cols(m[k0 + 1]), _cols(m[k0 + 2])], axis=1).astype(np.float32))


def _rope_tabs(n_tokens):
    t = np.arange(n_tokens)
    row = (t // 64).astype(np.float32)
    col = (t % 64).astype(np.float32)
    inv = (10000.0 ** (-np.arange(16, dtype=np.float32) / 16)).astype(np.float32)
    ar = row[:, None] * inv
    ac = col[:, None] * inv
    cos = np.concatenate([np.cos(ar), np.cos(ac)], axis=1).astype(np.float32)
    sin = np.concatenate([np.sin(ar), np.sin(ac)], axis=1).astype(np.float32)
    return cos, sin


_IDF = np.eye(128, dtype=np.float32)
_IDB = _IDF.astype(ml_dtypes.bfloat16)
_PROGS = {}
_DBG = {}


def _prog(key, fn):
    if key not in _PROGS:
        _PROGS[key] = fn()
    return _PROGS[key]


def _run(nc, maps):
    return run_bass_kernel_spmd(nc, maps, core_ids=list(range(8))).results


def _build_ada():
    k = KB()
    cc = k.dram("cc", [128, 16, 2], F32, kind="ExternalInput")
    w = k.dram("w", [2, 2048, 2304], F32, kind="ExternalInput")
    b = k.dram("b", [2, 2304], F32, kind="ExternalInput")
    out = k.dram("out", [2, 2, 2304], F32, kind="ExternalOutput")
    c = alloc_common(k)
    ada_stage(k, c, cc, w, b, out)
    return k.emit()


def _build_ffn(R):
    k = KB()
    xin = k.dram("xin", [R, D], F32, kind="ExternalInput")
    xout = k.dram("xout", [R, D], F32, kind="ExternalOutput")
    modc = k.dram("modc", [128, 3, 16], F32, kind="ExternalInput")
    normg = k.dram("normg", [128, 16], F32, kind="ExternalInput")
    w_in = k.dram("w_in", [D, 2 * DFF], BF16, kind="ExternalInput")
    w_out = k.dram("w_out", [DFF, D], BF16, kind="ExternalInput")
    idf = k.dram("idf", [128, 128], F32, kind="ExternalInput")
    idb = k.dram("idb", [128, 128], BF16, kind="ExternalInput")
    c = alloc_common(k)
    load_ident(k, c, idf, idb)
    A, Braw, G = modcols_prepare(k, "m0", modc[:, :, :], normg[:, :], 0)
    ffn_stage(k, c, "f0", xin, xout, R, 128, A, Braw, G, w_in, w_out)
    return k.emit()


def _build_evpre(R):
    k = KB()
    di = lambda n, s, dt=F32: k.dram(n, s, dt, kind="ExternalInput")
    do = lambda n, s, dt=F32: k.dram(n, s, dt, kind="ExternalOutput")
    xin = di("xin", [R, D]); modc = di("modc", [128, 3, 16]); normg = di("normg", [128, 16])
    w_in = di("w_in", [D, 1856]); wuq = di("wuq", [512, 1536]); wukv = di("wukv", [256, 2048])
    gqa = di("gqa", [128, 4]); gkva = di("gkva", [128, 2]); gq = di("gq", [192]); gk = di("gk", [192])
    cos = di("cos", [R, 32]); sin = di("sin", [R, 32])
    idf = di("idf", [128, 128]); idb = di("idb", [128, 128], BF16)
    qT = do("qT", [8, 192, R], BF16); kT = do("kT", [8, 192, R], BF16); v = do("v", [R, 1024], BF16); u = do("u", [R, 1024])
    c = alloc_common(k)
    load_ident(k, c, idf, idb)
    A, Braw, G = modcols_prepare(k, "m1", modc[:, :, :], normg[:, :], 0)
    evpre_stage(k, c, xin, R, 128, A, Braw, w_in, wuq, wukv, gqa, gkva, gq, gk, cos, sin, qT, kT, v, u, True)
    return k.emit()


def _build_attn():
    k = KB()
    di = lambda n, s, dt=F32: k.dram(n, s, dt, kind="ExternalInput")
    qT = di("qT", [8, 192, 2048], BF16); kT = di("kT", [8, 192, 16640], BF16); v = di("v", [16640, 1024], BF16)
    attT = k.dram("attT", [1024, 2048], BF16, kind="ExternalOutput")
    c = alloc_common(k)
    attn_stage(k, c, qT, kT, v, attT, 2048, 16640, 8)
    return k.emit()


def _build_s5():
    k = KB()
    di = lambda n, s, dt=F32: k.dram(n, s, dt, kind="ExternalInput")
    U = di("U", [16, 16, 16640]); lre = di("lre", [64, 16]); lim = di("lim", [64, 16]); ldt = di("ldt", [16])
    BTre = di("BTre", [16, 16, 64]); BTim = di("BTim", [16, 16, 64]); CTre = di("CTre", [64, 16, 16]); CTim = di("CTim", [64, 16, 16])
    jt = di("jt", [64, SEG + 1])
    Y = k.dram("Y", [16, 16, 16384], F32, kind="ExternalOutput")
    c = alloc_common(k)
    s5_stage(k, c, U, lre, lim, ldt, BTre, BTim, CTre, CTim, jt, Y, NSEG)
    return k.emit()


def _build_evout():
    k = KB()
    R = 2048
    di = lambda n, s, dt=F32: k.dram(n, s, dt, kind="ExternalInput")
    xa = di("xa", [R, D]); u = di("u", [R, 1024]); yf = di("yf", [R, 1024]); yb = di("yb", [R, 1024]); attT = di("attT", [1024, R], BF16)
    dd = di("dd", [1024]); gluw = di("gluw", [1024, 1024]); glub = di("glub", [1024]); wout = di("wout", [2048, 2048]); gate = di("gate", [2048])
    idf = di("idf", [128, 128]); idb = di("idb", [128, 128], BF16)
    xm = k.dram("xm", [R, D], F32, kind="ExternalOutput")
    c = alloc_common(k)
    load_ident(k, c, idf, idb)
    evout_stage(k, c, xa, u, yf, yb, attT, dd, gluw, glub, wout, gate, xm, R)
    return k.emit()


def _build_hypre():
    k = KB()
    di = lambda n, s, dt=F32: k.dram(n, s, dt, kind="ExternalInput")
    xh = di("xhin", [2050, D]); hmask = di("hmask", [128, 2]); modc = di("modc", [128, 3, 16]); normg = di("normg", [128, 16])
    w_in = di("w_in", [D, 6144]); cw = di("cw", [128, 48, 3]); cb = di("cb", [128, 48])
    idf = di("idf", [128, 128]); idb = di("idb", [128, 128], BF16)
    zc = k.dram("zc", [6144, 2048], F32, kind="ExternalOutput")
    c = alloc_common(k)
    load_ident(k, c, idf, idb)
    A, Braw, G = modcols_prepare(k, "m1", modc[:, :, :], normg[:, :], 0)
    hypre_stage(k, c, xh, hmask, A, Braw, w_in, cw, cb, zc)
    return k.emit()


_FILT_CONSTS = ("F1cat", "Wre", "Wim", "Cos", "Sin", "NSin")


def _build_hyfilt():
    k = KB()
    di = lambda n, s, dt=F32: k.dram(n, s, dt, kind="ExternalInput")
    cd = {n: di("k_" + n, HY_CONST_SHAPES[n][0], HY_CONST_SHAPES[n][1]) for n in _FILT_CONSTS}
    featT = di("featT", [33, LSEQ]); w1 = di("w1", [33, 64]); w2 = di("w2", [64, 64]); w3 = di("w3", [64, 64]); bf = di("bf", [64, 4])
    wout = di("wout", [64, 2, 2, 256]); drow = di("drow", [64, 256]); E1 = di("E1", [64, 256]); tfrow = di("tfrow", [64, 256])
    Gre = k.dram("Gre", [2, 2, 128, 256 * 128], F32, kind="ExternalOutput")
    Gim = k.dram("Gim", [2, 2, 128, 256 * 128], F32, kind="ExternalOutput")
    c = alloc_common(k)
    hyfilt_stage(k, c, cd, featT, w1, w2, w3, bf, wout, drow, E1, tfrow, Gre, Gim, 16, 2)
    return k.emit()


def _build_hyconv():
    k = KB()
    di = lambda n, s, dt=F32: k.dram(n, s, dt, kind="ExternalInput")
    cd = {n: di("k_" + n, shp, dt) for n, (shp, dt) in HY_CONST_SHAPES.items()}
    x1 = di("x1", [256, LSEQ]); x2 = di("x2", [256, LSEQ]); v = di("v", [256, LSEQ]); skip = di("skip", [2, 256])
    Gre = di("Gre", [2, 2, 128, 256 * 128]); Gim = di("Gim", [2, 2, 128, 256 * 128])
    y2 = k.dram("y2", [256, LSEQ], F32, kind="ExternalOutput")
    c = alloc_common(k)
    hyconv_stage(k, c, cd, x1, x2, v, skip, Gre, Gim, y2, 16)
    return k.emit()


def _build_projres():
    k = KB()
    di = lambda n, s, dt=F32: k.dram(n, s, dt, kind="ExternalInput")
    xin = di("xin", [2048, D]); yT = di("yT", [2048, 2048]); wout = di("wout", [2048, 2048]); gate = di("gate", [2048])
    xm = k.dram("xm", [2048, D], F32, kind="ExternalOutput")
    c = alloc_common(k)
    projres_stage(k, c, xin, yT, wout, gate, xm, 2048)
    return k.emit()


def _hyena_mixer(inp, xs, m1):
    x_full = np.concatenate(xs, axis=0)
    xp = np.concatenate([np.zeros((1, 2048), np.float32), x_full, np.zeros((1, 2048), np.float32)], axis=0)
    cw = np.ascontiguousarray(inp["hy_conv_w"][0].reshape(3, 48, 128).transpose(2, 1, 0))
    cb = np.ascontiguousarray(inp["hy_conv_b"][0].reshape(48, 128).T)
    maps = []
    for ci in range(8):
        hm = np.ones((128, 2), np.float32)
        if ci == 0:
            hm[:, 0] = 0
        if ci == 7:
            hm[:, 1] = 0
        maps.append(dict(xhin=np.ascontiguousarray(xp[ci * 2048:ci * 2048 + 2050]), hmask=hm, modc=_modc(m1, 3), normg=_cols(inp["norm_g"][1, 1]),
                         w_in=np.ascontiguousarray(inp["hy_w_in"][0]), cw=cw, cb=cb, idf=_IDF, idb=_IDB))
    rz = _run(_prog("hypre", _build_hypre), maps)
    z_all = np.concatenate([rz[ci]["zc"] for ci in range(8)], axis=1)
    consts = hy_consts()
    bf = np.ascontiguousarray(np.stack([inp["hy_filt_b1"][0], inp["hy_filt_b2"][0], inp["hy_filt_b3"][0], inp["hy_filt_freq"][0]], axis=1).astype(np.float32))
    maps = []
    for ci in range(8):
        fc = hy_filt_consts(ci)
        m = {"k_" + n: consts[n] for n in _FILT_CONSTS}
        m.update(featT=fc["featT"], drow=fc["drow"], E1=fc["E1"], tfrow=fc["tfrow"], w1=np.ascontiguousarray(inp["hy_filt_w1"][0]), w2=np.ascontiguousarray(inp["hy_filt_w2"][0]),
                 w3=np.ascontiguousarray(inp["hy_filt_w3"][0]), bf=bf, wout=np.ascontiguousarray(inp["hy_filt_w_out"][0][:, :, :, ci * 256:(ci + 1) * 256]))
        maps.append(m)
    rf = _run(_prog("hyfilt", _build_hyfilt), maps)
    maps = []
    for ci in range(8):
        cs = slice(ci * 256, (ci + 1) * 256)
        m = {"k_" + n: v for n, v in consts.items()}
        m.update(x1=np.ascontiguousarray(z_all[0:2048][cs]), x2=np.ascontiguousarray(z_all[2048:4096][cs]), v=np.ascontiguousarray(z_all[4096:6144][cs]),
                 skip=np.ascontiguousarray(inp["hy_skip"][0][:, cs]), Gre=rf[ci]["Gre"], Gim=rf[ci]["Gim"])
        maps.append(m)
    ry = _run(_prog("hyconv", _build_hyconv), maps)
    y_all = np.concatenate([ry[ci]["y2"] for ci in range(8)], axis=0)
    prc = dict(wout=np.ascontiguousarray(inp["hy_w_out"][0]), gate=np.ascontiguousarray(m1[5]))
    rp = _run(_prog("projres", _build_projres), [dict(prc, xin=np.ascontiguousarray(xs[ci]), yT=np.ascontiguousarray(y_all[:, ci * 2048:(ci + 1) * 2048]))
                                                  for ci in range(8)])
    return [rp[ci]["xm"] for ci in range(8)]


def _build_wprep():
    k = KB()
    win = k.dram("win", [4, 128, 22528], F32, kind="ExternalInput")
    wout = k.dram("wout", [4, 128, 11264], F32, kind="ExternalInput")
    winb = k.dram("winb", [4, 128, 22528], BF16, kind="ExternalOutput")
    woutb = k.dram("woutb", [4, 128, 11264], BF16, kind="ExternalOutput")
    c = {}
    wprep_stage(k, c, win, wout, winb, woutb, 4)
    return k.emit()


def _prep_ffn_weights(inp):
    win = inp["ffn_w_in"].reshape(4, 8, 128, 22528)
    wout = inp["ffn_w_out"].reshape(4, 8, 128, 11264)
    res = _run(_prog("wprep", _build_wprep), [dict(win=np.ascontiguousarray(win[:, ci]), wout=np.ascontiguousarray(wout[:, ci])) for ci in range(8)])
    winb = np.stack([res[ci]["winb"] for ci in range(8)], axis=1).reshape(2, 2, 2048, 11264)
    woutb = np.stack([res[ci]["woutb"] for ci in range(8)], axis=1).reshape(2, 2, 5632, 2048)
    return winb, woutb


def _s5_inmaps(inp, u_x, u_c):
    jt = np.broadcast_to(np.arange(SEG + 1, dtype=np.float32)[None, :], (64, SEG + 1)).copy()
    maps = []
    seq_f = np.concatenate([u_c, u_x], axis=0)
    seq_b = np.concatenate([u_x, u_c], axis=0)[::-1]
    for ci in range(8):
        gs = slice(ci * 8, ci * 8 + 8)
        U = np.empty((16, 16, 16640), np.float32)
        for di_, seq in enumerate((seq_f, seq_b)):
            blk = seq[:, ci * 128:(ci + 1) * 128].reshape(16640, 8, 16)
            U[:, di_ * 8:(di_ + 1) * 8, :] = blk.transpose(2, 1, 0)

        def lanes(a):
            return a[:, gs].reshape(16, *a.shape[2:])
        m = dict(U=U, lre=lanes(inp["s5_lam_re"][0]).T, lim=lanes(inp["s5_lam_im"][0]).T, ldt=lanes(inp["s5_log_dt"][0]),
                 BTre=lanes(inp["s5_b_re"][0]).transpose(2, 0, 1), BTim=lanes(inp["s5_b_im"][0]).transpose(2, 0, 1),
                 CTre=lanes(inp["s5_c_re"][0]).transpose(2, 0, 1), CTim=lanes(inp["s5_c_im"][0]).transpose(2, 0, 1), jt=jt)
        maps.append({kk: np.ascontiguousarray(v, dtype=np.float32) for kk, v in m.items()})
    return maps


def _ffn_launch(x_rows_per_core, m, k0, normg, w_in, w_out):
    R = x_rows_per_core[0].shape[0]
    nc = _prog(("ffn", R), lambda: _build_ffn(R))
    common = dict(modc=_modc(m, k0), normg=_cols(normg), w_in=np.ascontiguousarray(w_in), w_out=np.ascontiguousarray(w_out), idf=_IDF, idb=_IDB)
    res = _run(nc, [dict(common, xin=np.ascontiguousarray(xr)) for xr in x_rows_per_core])
    return [r["xout"] for r in res]


def kernel(**inp):
    inp = {kk: np.asarray(v) for kk, v in inp.items()}
    x = inp["x"][0]
    ctx = inp["ctx"][0]
    cvec = np.stack([inp["c"][0], inp["c_ctx"]], axis=1)
    cc = np.ascontiguousarray(cvec.reshape(16, 128, 2).transpose(1, 0, 2))
    nc = _prog("ada", _build_ada)
    res = _run(nc, [dict(cc=cc, w=np.ascontiguousarray(inp["ada_w"][:, :, ci * 2304:(ci + 1) * 2304]),
                         b=np.ascontiguousarray(inp["ada_b"][:, ci * 2304:(ci + 1) * 2304])) for ci in range(8)])
    mods = np.concatenate([res[ci]["out"] for ci in range(8)], axis=2)
    mx = [mods[l, 0].reshape(9, 2048) for l in range(2)]
    mc = [mods[l, 1].reshape(9, 2048) for l in range(2)]
    xs = [x[ci * 2048:(ci + 1) * 2048] for ci in range(8)]
    cs = [ctx[(ci % 2) * 128:(ci % 2) * 128 + 128] for ci in range(8)]
    winb, woutb = _prep_ffn_weights(inp)
    xs = _ffn_launch(xs, mx[0], 0, inp["norm_g"][0, 0], winb[0, 0], woutb[0, 0])
    cs = _ffn_launch(cs, mc[0], 0, inp["norm_g"][0, 0], winb[0, 0], woutb[0, 0])
    cos, sin = _rope_tabs(16384)
    evc = dict(normg=_cols(inp["norm_g"][0, 1]), w_in=inp["ev_w_in"][0], wuq=inp["mla_w_uq"][0], wukv=inp["mla_w_ukv"][0],
               gqa=_cols(inp["mla_q_a_norm_g"][0], 4), gkva=_cols(inp["mla_kv_a_norm_g"][0], 2), gq=inp["mla_q_head_g"][0], gk=inp["mla_k_head_g"][0],
               idf=_IDF, idb=_IDB)
    evc = {kk: np.ascontiguousarray(v) for kk, v in evc.items()}
    nc = _prog(("evpre", 2048), lambda: _build_evpre(2048))
    rx = _run(nc, [dict(evc, modc=_modc(mx[0], 3), xin=np.ascontiguousarray(xs[ci]), cos=np.ascontiguousarray(cos[ci * 2048:(ci + 1) * 2048]),
                        sin=np.ascontiguousarray(sin[ci * 2048:(ci + 1) * 2048])) for ci in range(8)])
    nc = _prog(("evpre", 128), lambda: _build_evpre(128))
    one = np.ones((128, 32), np.float32)
    zero = np.zeros((128, 32), np.float32)
    rc = _run(nc, [dict(evc, modc=_modc(mc[0], 3), xin=np.ascontiguousarray(cs[ci]), cos=one, sin=zero) for ci in range(8)])
    kT_all = np.ascontiguousarray(np.concatenate([rx[ci]["kT"] for ci in range(8)] + [rc[0]["kT"], rc[1]["kT"]], axis=2))
    v_all = np.ascontiguousarray(np.concatenate([rx[ci]["v"] for ci in range(8)] + [rc[0]["v"], rc[1]["v"]], axis=0))
    u_x = np.concatenate([rx[ci]["u"] for ci in range(8)], axis=0)
    u_c = np.concatenate([rc[0]["u"], rc[1]["u"]], axis=0)
    nc = _prog("attn", _build_attn)
    ra = _run(nc, [dict(qT=rx[ci]["qT"], kT=kT_all, v=v_all) for ci in range(8)])
    nc = _prog("s5", _build_s5)
    rs = _run(nc, _s5_inmaps(inp, u_x, u_c))
    Yf = np.empty((16384, 1024), np.float32)
    Yb = np.empty((16384, 1024), np.float32)
    for ci in range(8):
        Y = rs[ci]["Y"]
        Yf[:, ci * 128:(ci + 1) * 128] = Y[0:8].transpose(2, 0, 1).reshape(16384, 128)
        Yb[:, ci * 128:(ci + 1) * 128] = Y[8:16, :, ::-1].transpose(2, 0, 1).reshape(16384, 128)
    nc = _prog("evout", _build_evout)
    eoc = dict(dd=inp["s5_d"][0], gluw=inp["s5_glu_w"][0], glub=inp["s5_glu_b"][0], wout=inp["ev_w_out"][0], gate=mx[0][5], idf=_IDF, idb=_IDB)
    eoc = {kk: np.ascontiguousarray(v) for kk, v in eoc.items()}
    ro = _run(nc, [dict(eoc, xa=np.ascontiguousarray(xs[ci]), u=np.ascontiguousarray(u_x[ci * 2048:(ci + 1) * 2048]),
                        yf=np.ascontiguousarray(Yf[ci * 2048:(ci + 1) * 2048]), yb=np.ascontiguousarray(Yb[ci * 2048:(ci + 1) * 2048]),
                        attT=ra[ci]["attT"]) for ci in range(8)])
    xs = [ro[ci]["xm"] for ci in range(8)]
    _DBG["x_m0"] = xs
    xs = _ffn_launch(xs, mx[0], 6, inp["norm_g"][0, 2], winb[0, 1], woutb[0, 1])
    _DBG["x_b0"] = xs
    xs = _ffn_launch(xs, mx[1], 0, inp["norm_g"][1, 0], winb[1, 0], woutb[1, 0])
    _DBG["x_a1"] = xs
    xs = _hyena_mixer(inp, xs, mx[1])
    _DBG["x_m1"] = xs
    xs = _ffn_launch(xs, mx[1], 6, inp["norm_g"][1, 2], winb[1, 1], woutb[1, 1])
    return np.concatenate(xs, axis=0)[None].astype(np.float32)
```

```python
import numpy as np
import ml_dtypes
from contextlib import ExitStack
import concourse.bass as bass
import concourse.mybir as mybir
from concourse.bass_utils import run_bass_kernel_spmd

F32 = mybir.dt.float32
BF16 = mybir.dt.bfloat16
AF = mybir.ActivationFunctionType
ALU = mybir.AluOpType
AX = mybir.AxisListType
WRITE_KEYS = ("out", "accum_out")
SEM_ROLL = 30000


class View:
    __slots__ = ("buf", "ap")

    def __init__(self, buf, ap):
        self.buf = buf
        self.ap = ap

    def __getitem__(self, idx):
        return View(self.buf, self.ap[idx])

    def rearrange(self, pat, **kw):
        return View(self.buf, self.ap.rearrange(pat, **kw))

    def bitcast(self, dt):
        return View(self.buf, self.ap.bitcast(dt))

    def unsq_bcast(self, n):
        a = self.ap
        return View(self.buf, bass.AP(a.tensor, a.offset, [list(x) for x in a.ap] + [[0, n]]))

    def pbcast(self, n):
        return View(self.buf, self.ap.partition_broadcast(n))


class Buf:
    def __init__(self, k, name, h, space):
        self.k = k
        self.name = name
        self.h = h
        self.space = space
        self.last_w = None
        self.readers = []
        self.dma_sem = None
        self.dma_cnt = 0

    def __getitem__(self, idx):
        return View(self, self.h[idx])

    def ap(self):
        return View(self, self.h.ap() if hasattr(self.h, "ap") else self.h[:])

    def bitcast_view(self, dt):
        return View(self, self.h.bitcast(dt).ap())


class Op:
    __slots__ = ("eng", "meth", "args", "kwargs", "reads", "writes", "is_dma", "tok", "has_dep", "deps", "sbuf_side", "acc")


class KB:
    def __init__(self):
        self.nc = bass.Bass("TRN2", target_bir_lowering=False)
        self.ops = []
        self.es = ExitStack()
        self.bufs = []
        self.n = 0

    def sb(self, name, shape, dt=F32):
        h = self.es.enter_context(self.nc.sbuf_tensor(name, list(shape), dt))
        b = Buf(self, name, h, "sb")
        self.bufs.append(b)
        return b

    def ps(self, name, shape, dt=F32):
        h = self.es.enter_context(self.nc.psum_tensor(name, list(shape), dt))
        b = Buf(self, name, h, "ps")
        self.bufs.append(b)
        return b

    def dram(self, name, shape, dt=F32, kind=None):
        if kind is None:
            h = self.nc.dram_tensor(name, list(shape), dt)
        else:
            h = self.nc.dram_tensor(name, list(shape), dt, kind=kind)
        b = Buf(self, name, h, "dram")
        self.bufs.append(b)
        return b

    def I(self, eng, meth, *args, reads=(), writes=(), acc=False, **kwargs):
        o = Op()
        o.eng = eng
        o.meth = meth
        o.args = args
        o.kwargs = kwargs
        rd, wr = list(reads), list(writes)
        for a in args:
            if isinstance(a, View):
                rd.append(a.buf)
        for kk, v in kwargs.items():
            if isinstance(v, View):
                (wr if kk in WRITE_KEYS else rd).append(v.buf)
        o.reads = rd
        o.writes = wr
        o.is_dma = meth == "dma_start"
        o.tok = None
        o.has_dep = False
        o.deps = None
        o.sbuf_side = None
        o.acc = acc
        if o.is_dma:
            ob, ib = kwargs["out"].buf, kwargs["in_"].buf
            o.sbuf_side = ob if ob.space != "dram" else ib
            assert o.sbuf_side.space != "dram", "dram->dram dma unsupported"
        self.ops.append(o)
        return o

    def dma(self, eng, out, in_, **kw):
        return self.I(eng, "dma_start", out=out, in_=in_, **kw)

    def mm(self, out, lhsT, rhs, start=True, stop=True, **kw):
        return self.I("tensor", "matmul", out=out, lhsT=lhsT, rhs=rhs, start=start, stop=stop, acc=not start, **kw)

    def tr(self, out, in_, ident):
        return self.I("tensor", "transpose", out=out, in_=in_, identity=ident)

    def act(self, out, in_, func, eng="scalar", **kw):
        return self.I(eng, "activation", out=out, in_=in_, func=func, **kw)

    def tt(self, eng, out, in0, in1, op):
        return self.I(eng, "tensor_tensor", out=out, in0=in0, in1=in1, op=op)

    def ts(self, eng, out, in0, s1, s2, op0, op1=None, **kw):
        if op1 is None:
            return self.I(eng, "tensor_scalar", out=out, in0=in0, scalar1=s1, scalar2=None, op0=op0, **kw)
        return self.I(eng, "tensor_scalar", out=out, in0=in0, scalar1=s1, scalar2=s2, op0=op0, op1=op1, **kw)

    def stt(self, eng, out, in0, scalar, in1, op0, op1):
        return self.I(eng, "scalar_tensor_tensor", out=out, in0=in0, scalar=scalar, in1=in1, op0=op0, op1=op1)

    def cp(self, eng, out, in_):
        if eng == "scalar":
            return self.I(eng, "copy", out=out, in_=in_)
        return self.I(eng, "tensor_copy", out=out, in_=in_)

    def memset(self, eng, out, val):
        return self.I(eng, "memset", writes=[out.buf], ap=out, constant=val)

    def emit(self, final_wait_bufs=()):
        nc = self.nc
        ops = self.ops
        for i, o in enumerate(ops):
            deps = set()
            for b in o.reads:
                if b.last_w is not None:
                    deps.add(b.last_w)
            for b in o.writes:
                if b.last_w is not None:
                    deps.add(b.last_w)
                for r in b.readers:
                    deps.add(r)
            deps.discard(i)
            fd = []
            for d in deps:
                od = ops[d]
                same = (od.eng == o.eng) and not o.is_dma and not od.is_dma
                if same:
                    raw = any((b in od.writes) for b in o.reads)
                    if o.eng == "tensor":
                        raw = False
                    if not raw:
                        continue
                fd.append(d)
            o.deps = fd
            for d in fd:
                ops[d].has_dep = True
            for b in o.writes:
                b.last_w = i
                b.readers = []
            for b in o.reads:
                if b not in o.writes:
                    b.readers.append(i)
        engs = {"tensor": nc.tensor, "vector": nc.vector, "scalar": nc.scalar, "gpsimd": nc.gpsimd, "sync": nc.sync}
        esem = {}
        ecnt = {}
        known = {e: {} for e in engs}

        def new_sem(name):
            return self.es.enter_context(nc.semaphore(name))

        nsem = [0]
        for e in engs:
            esem[e] = new_sem(f"e_{e}_0")
            ecnt[e] = 0
            nsem[0] += 1
        for i, o in enumerate(ops):
            eng = engs[o.eng]
            kn = known[o.eng]
            for d in o.deps:
                sem, val = ops[d].tok
                if kn.get(id(sem), (None, 0))[1] < val:
                    eng.wait_ge(sem, val)
                    kn[id(sem)] = (sem, val)
            args = [a.ap if isinstance(a, View) else a for a in o.args]
            kwargs = {kk: (v.ap if isinstance(v, View) else v) for kk, v in o.kwargs.items()}
            inst = getattr(eng, o.meth)(*args, **kwargs)
            if o.is_dma:
                b = o.sbuf_side
                if b.dma_sem is None:
                    b.dma_sem = new_sem(f"d_{b.name}")
                    nsem[0] += 1
                b.dma_cnt += 16
                inst.then_inc(b.dma_sem, 16)
                o.tok = (b.dma_sem, b.dma_cnt)
            elif o.has_dep:
                if ecnt[o.eng] >= SEM_ROLL:
                    esem[o.eng] = new_sem(f"e_{o.eng}_{i}")
                    ecnt[o.eng] = 0
                    nsem[0] += 1
                ecnt[o.eng] += 1
                inst.then_inc(esem[o.eng], 1)
                o.tok = (esem[o.eng], ecnt[o.eng])
        for b in self.bufs:
            if b.dma_sem is not None:
                nc.sync.wait_ge(b.dma_sem, b.dma_cnt)
        self.nsem = nsem[0]
        self.es.close()
        return nc


def bf16_np(a):
    return a.astype(ml_dtypes.bfloat16)


D = 2048
DFF = 5632
EPS = 1e-6


def alloc_common(k):
    c = {}
    c["P"] = [k.ps(f"P{i}", [128, 512], F32) for i in range(8)]
    return c


def load_ident(k, c, ident_f_d, ident_b_d):
    c["identf"] = k.sb("identf", [128, 128], F32)
    c["identb"] = k.sb("identb", [128, 128], BF16)
    c["epsc"] = k.sb("epsc", [128, 1], F32)
    k.memset("vector", c["epsc"][:, :], EPS)
    k.dma("sync", out=c["identf"][:, :], in_=ident_f_d[:, :])
    k.dma("sync", out=c["identb"][:, :], in_=ident_b_d[:, :])


def modcols_prepare(k, pfx, modc_d, normg_d, slot):
    raw = k.sb(pfx + "raw", [128, 3, 16], F32)
    ng = k.sb(pfx + "ng", [128, 16], F32)
    A = k.sb(pfx + "A", [128, 16], F32)
    G = k.sb(pfx + "G", [128, 16], F32)
    k.dma("sync", out=raw[:, :, :], in_=modc_d)
    k.dma("sync", out=ng[:, :], in_=normg_d)
    k.stt("vector", out=A[:, :], in0=raw[:, 1, :], scalar=1.0, in1=ng[:, :], op0=ALU.add, op1=ALU.mult)
    k.ts("vector", out=G[:, :], in0=raw[:, 2, :], s1=0.5, s2=None, op0=ALU.mult)
    return A, raw, G


def norm_mod_T(k, c, pfx, x_tile, tr, A, Braw, hT_view, bufs):
    junk, ss, rstd, xh = bufs["junk"], bufs["ss"], bufs["rstd"], bufs["xh"]
    P = c["P"]
    k.memset("vector", ss[:tr, :], 0.0)
    k.act(out=junk[:tr, :], in_=x_tile, func=AF.Square, accum_out=ss[:tr, :])
    k.act(out=rstd[:tr, :], in_=ss[:tr, :], func=AF.Sqrt, scale=1.0 / D, bias=c["epsc"][:tr, :])
    k.I("vector", "reciprocal", out=rstd[:tr, :], in_=rstd[:tr, :])
    k.act(out=xh[:tr, :], in_=x_tile, func=AF.Copy, scale=rstd[:tr, :])
    pb = [P[0].bitcast_view(BF16), P[1].bitcast_view(BF16)]
    for kt in range(16):
        pv = pb[kt // 8]
        k.tr(out=pv[:, (kt % 8) * 128:(kt % 8) * 128 + tr], in_=xh[:tr, kt * 128:(kt + 1) * 128], ident=c["identb"][:tr, :tr])
    for h in range(2):
        pv = pb[h].rearrange("p (k t) -> p k t", t=128)[:, :, :tr]
        a_b = A[:, h * 8:(h + 1) * 8].unsq_bcast(tr)
        b_b = Braw[:, 0, h * 8:(h + 1) * 8].unsq_bcast(tr)
        tmp = bufs["tmpT"]
        k.tt("vector", out=tmp[:, :, :tr], in0=pv, in1=a_b, op=ALU.mult)
        k.tt("gpsimd", out=hT_view[:, h * 8:(h + 1) * 8, :], in0=tmp[:, :, :tr], in1=b_b, op=ALU.add)


def ffn_stage(k, c, pfx, xin_d, xout_d, R, tr, A, Braw, G, w_in_d, w_out_d):
    P = c["P"]
    ntiles = R // tr
    bufs = c.setdefault("nm_bufs", None)
    if bufs is None:
        bufs = c["nm_bufs"] = dict(
            junk=k.sb("junk", [128, 2048], BF16), ss=k.sb("ss", [128, 1], F32), rstd=k.sb("rstd", [128, 1], F32),
            xh=k.sb("xh", [128, 2048], BF16), tmpT=k.sb("tmpT", [128, 8, 128], F32))
    if "ffn_bufs" not in c:
        c["ffn_bufs"] = dict(
            xres=k.sb("xres", [128, 4, 2048], F32), hT=k.sb("hT", [128, 16, 512], BF16), aT=k.sb("aT", [128, 44, 512], BF16),
            wg=[k.sb(f"wg{i}", [128, 16, 256], BF16) for i in range(3)],
            wo=[k.sb(f"wo{i}", [128, 44, 128], BF16) for i in range(3)],
            sg=[k.sb(f"sg{i}", [128, 512], F32) for i in range(2)],
            oTs=[k.sb(f"oTs{i}", [128, 512], F32) for i in range(2)])
    fb = c["ffn_bufs"]
    xres, hT, aT, wg, wo, sg, oTs = fb["xres"], fb["hT"], fb["aT"], fb["wg"], fb["wo"], fb["sg"], fb["oTs"]
    w_in_v = w_in_d.ap().rearrange("(kt p) c -> p kt c", p=128)
    w_out_v = w_out_d.ap().rearrange("(j p) c -> p j c", p=128)
    nblk = (ntiles + 3) // 4
    for blk in range(nblk):
        t0 = blk * 4
        nt = min(4, ntiles - t0)
        nb = nt * tr
        for t in range(nt):
            r0 = (t0 + t) * tr
            k.dma("sync", out=xres[:tr, t, :], in_=xin_d[r0:r0 + tr, :])
            norm_mod_T(k, c, pfx, xres[:tr, t, :], tr, A, Braw, hT[:, :, t * tr:(t + 1) * tr], bufs)

        def load_wg(j):
            b = wg[j % 3]
            k.dma("gpsimd", out=b[:, :, 0:128], in_=w_in_v[:, :, j * 128:(j + 1) * 128])
            k.dma("sync", out=b[:, :, 128:256], in_=w_in_v[:, :, DFF + j * 128:DFF + (j + 1) * 128])

        load_wg(0)
        load_wg(1)
        for j in range(44):
            if j + 2 < 44:
                load_wg(j + 2)
            b = wg[j % 3]
            gp, up = P[2 + 2 * (j % 2)], P[3 + 2 * (j % 2)]
            for kt in range(16):
                k.mm(out=gp[:, :nb], lhsT=b[:, kt, 0:128], rhs=hT[:, kt, :nb], start=(kt == 0), stop=(kt == 15))
            for kt in range(16):
                k.mm(out=up[:, :nb], lhsT=b[:, kt, 128:256], rhs=hT[:, kt, :nb], start=(kt == 0), stop=(kt == 15))
            s = sg[j % 2]
            k.act(out=s[:, :nb], in_=gp[:, :nb], func=AF.Silu)
            k.tt("vector", out=aT[:, j, :nb], in0=s[:, :nb], in1=up[:, :nb], op=ALU.mult)

        def load_wo(m):
            k.dma("gpsimd" if m % 2 == 0 else "sync", out=wo[m % 3][:, :, :], in_=w_out_v[:, :, m * 128:(m + 1) * 128])

        load_wo(0)
        load_wo(1)
        for m in range(16):
            if m + 2 < 16:
                load_wo(m + 2)
            b = wo[m % 3]
            op_ = P[6 + (m % 2)]
            for j in range(44):
                k.mm(out=op_[:, :nb], lhsT=b[:, j, :], rhs=aT[:, j, :nb], start=(j == 0), stop=(j == 43))
            o = oTs[m % 2]
            k.act(out=o[:, :nb], in_=op_[:, :nb], func=AF.Copy, scale=G[:, m:m + 1])
            tb = P[m % 2]
            for t in range(nt):
                k.tr(out=tb[:tr, t * 128:(t + 1) * 128], in_=o[:, t * tr:(t + 1) * tr], ident=c["identf"][:, :])
            k.tt("vector", out=xres[:tr, :nt, m * 128:(m + 1) * 128], in0=xres[:tr, :nt, m * 128:(m + 1) * 128],
                 in1=tb[:tr, :nt * 128].rearrange("p (t f) -> p t f", f=128), op=ALU.add)
        for t in range(nt):
            r0 = (t0 + t) * tr
            k.dma("sync", out=xout_d[r0:r0 + tr, :], in_=xres[:tr, t, :])


def ada_stage(k, c, cc_d, w_d, b_d, out_d):
    P = c["P"]
    cc = k.sb("ada_cc", [128, 16, 2], F32)
    sc = k.sb("ada_sc", [128, 16, 2], F32)
    k.dma("sync", out=cc[:, :, :], in_=cc_d[:, :, :])
    k.act(out=sc[:, :, :], in_=cc[:, :, :], func=AF.Silu)
    wb = [k.sb(f"ada_w{i}", [128, 16, 512], F32) for i in range(2)]
    bb = k.sb("ada_b", [2, 2, 2304], F32)
    ob = k.sb("ada_o", [2, 2, 2304], F32)
    for l in range(2):
        k.dma("sync", out=bb[:, l, :], in_=b_d[l:l + 1, :].pbcast(2) if False else b_d[l, :].pbcast(2))
    blocks = [(0, 512), (512, 512), (1024, 512), (1536, 512), (2048, 256)]
    it = 0
    for l in range(2):
        wv = w_d[l].rearrange("(kt p) c -> p kt c", p=128)
        for (c0, cw) in blocks:
            w = wb[it % 2]
            k.dma("sync" if it % 2 == 0 else "gpsimd", out=w[:, :, :cw], in_=wv[:, :, c0:c0 + cw])
            ps = P[it % 2]
            for kt in range(16):
                k.mm(out=ps[:2, :cw], lhsT=sc[:, kt, :], rhs=w[:, kt, :cw], start=(kt == 0), stop=(kt == 15))
            k.tt("vector", out=ob[:, l, c0:c0 + cw], in0=ps[:2, :cw], in1=bb[:, l, c0:c0 + cw], op=ALU.add)
            it += 1
        k.dma("sync", out=out_d[l], in_=ob[:, l, :])


def rstd_from_ss(k, c, out, ss, dim):
    k.act(out=out, in_=ss, func=AF.Sqrt, scale=1.0 / dim, bias=c["epsc"][:out.ap.shape[0], :])
    k.I("vector", "reciprocal", out=out, in_=out)


def rope_apply(k, eng, dst, src, cos, sin, tmp, tr, nh):
    def tv(i):
        return tmp[:tr, i, :nh * 32].rearrange("p (h a f) -> p h a f", a=2, f=16)
    x1, x2 = src[:, :, :, 0, :], src[:, :, :, 1, :]
    k.tt(eng, out=tv(0), in0=x1, in1=cos, op=ALU.mult)
    k.tt(eng, out=tv(1), in0=x2, in1=sin, op=ALU.mult)
    k.tt(eng, out=dst[:, :, :, 0, :], in0=tv(0), in1=tv(1), op=ALU.subtract)
    k.tt(eng, out=tv(2), in0=x2, in1=cos, op=ALU.mult)
    k.tt(eng, out=tv(3), in0=x1, in1=sin, op=ALU.mult)
    k.tt(eng, out=dst[:, :, :, 1, :], in0=tv(2), in1=tv(3), op=ALU.add)


def evpre_stage(k, c, xin_d, R, tr, A, Braw, w_in_d, wuq_d, wukv_d, gqa_d, gkva_d, gq_d, gk_d, cos_d, sin_d,
                qT_d, kT_d, v_d, u_d, want_q=True, stop=9):
    P = c["P"]
    if c.get("nm_bufs") is None:
        c["nm_bufs"] = dict(
            junk=k.sb("junk", [128, 2048], BF16), ss=k.sb("ss", [128, 1], F32), rstd=k.sb("rstd", [128, 1], F32),
            xh=k.sb("xh", [128, 2048], BF16), tmpT=k.sb("tmpT", [128, 8, 128], F32))
    bufs = c["nm_bufs"]
    win = k.sb("ev_win", [128, 16, 1856], BF16)
    wuq = k.sb("ev_wuq", [128, 4, 1536], BF16)
    wukv = k.sb("ev_wukv", [128, 2, 2048], BF16)
    k.dma("gpsimd", out=win[:, :, :], in_=w_in_d.ap().rearrange("(kt p) c -> p kt c", p=128))
    k.dma("gpsimd", out=wuq[:, :, :], in_=wuq_d.ap().rearrange("(kt p) c -> p kt c", p=128))
    k.dma("gpsimd", out=wukv[:, :, :], in_=wukv_d.ap().rearrange("(kt p) c -> p kt c", p=128))
    gqa = k.sb("ev_gqa", [128, 4], F32)
    gkva = k.sb("ev_gkva", [128, 2], F32)
    gq = k.sb("ev_gq", [128, 192], F32)
    gk = k.sb("ev_gk", [128, 192], F32)
    k.dma("sync", out=gqa[:, :], in_=gqa_d[:, :])
    k.dma("sync", out=gkva[:, :], in_=gkva_d[:, :])
    k.dma("sync", out=gq[:, :], in_=gq_d[:].pbcast(128))
    k.dma("sync", out=gk[:, :], in_=gk_d[:].pbcast(128))
    k.ts("vector", out=gq[:, :], in0=gq[:, :], s1=float(192 ** -0.5), s2=None, op0=ALU.mult)
    xt = [k.sb(f"ev_x{i}", [128, 2048], F32) for i in range(2)]
    hT = k.sb("ev_hT", [128, 16, 128], BF16)
    zs = k.sb("ev_zs", [128, 1856], F32)
    qan = k.sb("ev_qan", [128, 768], BF16)
    qanT = k.sb("ev_qanT", [128, 6, 128], BF16)
    sq = k.sb("ev_sq", [128, 1536], F32)
    ss8 = k.sb("ev_ss8", [128, 8], F32)
    rs8 = k.sb("ev_rs8", [128, 8], F32)
    ss1 = k.sb("ev_ss1", [128, 1], F32)
    rs1 = k.sb("ev_rs1", [128, 1], F32)
    t1 = k.sb("ev_t1", [128, 8, 192], F32)
    t2 = k.sb("ev_t2", [128, 8, 192], F32)
    qf = k.sb("ev_qf", [128, 8, 192], BF16)
    kf = k.sb("ev_kf", [128, 8, 192], BF16)
    vt = k.sb("ev_vt", [128, 8, 128], BF16)
    kpe = k.sb("ev_kpe", [128, 64], F32)
    kpr = k.sb("ev_kpr", [128, 64], F32)
    rtmp = k.sb("ev_rtmp", [128, 4, 256], F32)
    cs = k.sb("ev_cos", [128, 32], F32)
    sn = k.sb("ev_sin", [128, 32], F32)
    oT = [k.sb(f"ev_oT{i}", [128, 8, 2, 128], BF16) for i in range(2)]
    pT = P[7].bitcast_view(BF16)
    idb = c["identb"]

    def rope_views(tile, nh):
        return tile[:tr, :nh, 128:192].rearrange("p h (a b f) -> p h a b f", a=2, b=2)

    def bc_heads(tab, nh):
        a = tab[:tr, :].ap
        return View(tab, bass.AP(a.tensor, a.offset, [list(a.ap[0]), [0, nh], [16, 2], [1, 16]]))

    def heads_T(src, dst_d, it):
        o = oT[it % 2]
        for half in range(2):
            for h4 in range(4):
                h = half * 4 + h4
                k.tr(out=pT[:, h4 * 256:h4 * 256 + tr], in_=src[:tr, h, 0:128], ident=idb[:tr, :tr])
                k.tr(out=pT[:64, h4 * 256 + 128:h4 * 256 + 128 + tr], in_=src[:tr, h, 128:192], ident=idb[:tr, :tr])
            pv = pT.rearrange("p (h a t) -> p h a t", a=2, t=128)
            k.cp("vector", out=o[:, half * 4:half * 4 + 4, 0, :tr], in_=pv[:, :, 0, :tr])
            k.cp("vector", out=o[:64, half * 4:half * 4 + 4, 1, :tr], in_=pv[:64, :, 1, :tr])
        return o

    ntiles = R // tr
    for t in range(ntiles):
        r0 = t * tr
        x = xt[t % 2]
        k.dma("sync", out=x[:tr, :], in_=xin_d[r0:r0 + tr, :])
        k.dma("sync", out=cs[:tr, :], in_=cos_d[r0:r0 + tr, :])
        k.dma("sync", out=sn[:tr, :], in_=sin_d[r0:r0 + tr, :])
        norm_mod_T(k, c, "ev", x[:tr, :], tr, A, Braw, hT[:, :, :tr], bufs)
        zb = [(0, 512), (512, 512), (1024, 512), (1536, 320)]
        for bi, (c0, cw) in enumerate(zb):
            for kt in range(16):
                k.mm(out=P[bi][:tr, :cw], lhsT=hT[:, kt, :tr], rhs=win[:, kt, c0:c0 + cw], start=(kt == 0), stop=(kt == 15))
            k.cp("scalar", out=zs[:tr, c0:c0 + cw], in_=P[bi][:tr, :cw])
        k.dma("sync", out=u_d[r0:r0 + tr, :], in_=zs[:tr, 832:1856])
        if stop <= 1:
            continue
        for (c0, cw, dim) in ((0, 512, 512), (512, 256, 256)):
            k.memset("vector", ss1[:tr, :], 0.0)
            k.act(out=bufs["junk"][:tr, :cw], in_=zs[:tr, c0:c0 + cw], func=AF.Square, accum_out=ss1[:tr, :])
            rstd_from_ss(k, c, rs1[:tr, :], ss1[:tr, :], dim)
            k.act(out=qan[:tr, c0:c0 + cw], in_=zs[:tr, c0:c0 + cw], func=AF.Copy, scale=rs1[:tr, :])
        for kt in range(6):
            k.tr(out=pT[:, kt * 128:kt * 128 + tr], in_=qan[:tr, kt * 128:(kt + 1) * 128], ident=idb[:tr, :tr])
        pv6 = pT[:, :768].rearrange("p (k t) -> p k t", t=128)[:, :, :tr]
        k.tt("vector", out=qanT[:, 0:4, :tr], in0=pv6[:, 0:4, :], in1=gqa[:, :].unsq_bcast(tr), op=ALU.mult)
        k.tt("vector", out=qanT[:, 4:6, :tr], in0=pv6[:, 4:6, :], in1=gkva[:, :].unsq_bcast(tr), op=ALU.mult)
        if stop <= 2:
            continue
        for bi in range(4):
            for kt in range(2):
                k.mm(out=P[bi][:tr, :], lhsT=qanT[:, 4 + kt, :tr], rhs=wukv[:, kt, bi * 512:(bi + 1) * 512], start=(kt == 0), stop=(kt == 1))
        for bi in range(4):
            kvv = P[bi][:tr, :].rearrange("p (h d) -> p h d", d=256)
            if stop != 34:
                k.cp("vector", out=vt[:tr, bi * 2:bi * 2 + 2, :], in_=kvv[:, :, 128:256])
            k.cp("vector", out=t1[:tr, bi * 2:bi * 2 + 2, 0:128], in_=kvv[:, :, 0:128])
        if stop != 33:
            k.dma("sync", out=v_d[r0:r0 + tr, :], in_=vt[:tr, :, :].rearrange("p h d -> p (h d)"))
        if stop <= 3 or stop in (33, 34):
            continue
        k.tt("gpsimd", out=kpe[:tr, :], in0=zs[:tr, 768:832], in1=gk[:tr, 128:192], op=ALU.mult)
        kp5 = kpe[:tr, :].rearrange("p (h a b f) -> p h a b f", h=1, a=2, b=2)
        kr5 = kpr[:tr, :].rearrange("p (h a b f) -> p h a b f", h=1, a=2, b=2)
        rope_apply(k, "gpsimd", kr5, kp5, bc_heads(cs, 1), bc_heads(sn, 1), rtmp, tr, 1)
        if stop <= 4:
            continue
        k.act(out=sq[:tr, :1024].rearrange("p (h d) -> p h d", d=128), in_=t1[:tr, :, 0:128], func=AF.Square)
        k.I("vector", "tensor_reduce", out=ss8[:tr, :], in_=sq[:tr, :1024].rearrange("p (h d) -> p h d", d=128), axis=AX.X, op=ALU.add)
        k.memset("vector", ss1[:tr, :], 0.0)
        k.act(out=bufs["junk"][:tr, :64], in_=zs[:tr, 768:832], func=AF.Square, accum_out=ss1[:tr, :])
        k.ts("vector", out=ss8[:tr, :], in0=ss8[:tr, :], s1=ss1[:tr, :], s2=None, op0=ALU.add)
        rstd_from_ss(k, c, rs8[:tr, :], ss8[:tr, :], 192)
        k.tt("vector", out=t2[:tr, :, 0:128], in0=t1[:tr, :, 0:128], in1=rs8[:tr, :].unsq_bcast(128), op=ALU.mult)
        gkn = View(gk, bass.AP(gk[:tr, 0:128].ap.tensor, gk[:tr, 0:128].ap.offset, [list(gk[:tr, 0:128].ap.ap[0]), [0, 8], [1, 128]]))
        k.tt("gpsimd", out=kf[:tr, :, 0:128], in0=t2[:tr, :, 0:128], in1=gkn, op=ALU.mult)
        kprb = View(kpr, bass.AP(kpr[:tr, :].ap.tensor, kpr[:tr, :].ap.offset, [list(kpr[:tr, :].ap.ap[0]), [0, 8], [1, 64]]))
        k.tt("vector", out=kf[:tr, :, 128:192], in0=kprb, in1=rs8[:tr, :].unsq_bcast(64), op=ALU.mult)
        if stop <= 5:
            continue
        o = heads_T(kf, kT_d, 2 * t)
        k.dma("sync", out=kT_d[:, 0:128, r0:r0 + tr].rearrange("h p t -> p h t"), in_=o[:, :, 0, :tr])
        k.dma("sync", out=kT_d[:, 128:192, r0:r0 + tr].rearrange("h p t -> p h t"), in_=o[:64, :, 1, :tr])
        if want_q and stop > 6:
            for bi in range(3):
                for kt in range(4):
                    k.mm(out=P[4 + bi][:tr, :], lhsT=qanT[:, kt, :tr], rhs=wuq[:, kt, bi * 512:(bi + 1) * 512], start=(kt == 0), stop=(kt == 3))
                k.cp("scalar", out=t1[:tr, :, :].rearrange("p h d -> p (h d)")[:, bi * 512:(bi + 1) * 512], in_=P[4 + bi][:tr, :])
            k.act(out=sq[:tr, :], in_=t1[:tr, :, :].rearrange("p h d -> p (h d)"), func=AF.Square)
            k.I("vector", "tensor_reduce", out=ss8[:tr, :], in_=sq[:tr, :].rearrange("p (h d) -> p h d", d=192), axis=AX.X, op=ALU.add)
            rstd_from_ss(k, c, rs8[:tr, :], ss8[:tr, :], 192)
            k.tt("vector", out=t2[:tr, :, :], in0=t1[:tr, :, :], in1=rs8[:tr, :].unsq_bcast(192), op=ALU.mult)
            gqn = View(gq, bass.AP(gq[:tr, :].ap.tensor, gq[:tr, :].ap.offset, [list(gq[:tr, :].ap.ap[0]), [0, 8], [1, 192]]))
            k.tt("gpsimd", out=t1[:tr, :, :], in0=t2[:tr, :, :], in1=gqn, op=ALU.mult)
            k.cp("scalar", out=qf[:tr, :, 0:128], in_=t1[:tr, :, 0:128])
            rope_apply(k, "vector", rope_views(qf, 8), rope_views(t1, 8), bc_heads(cs, 8), bc_heads(sn, 8), rtmp, tr, 8)
            o = heads_T(qf, qT_d, 2 * t + 1)
            k.dma("sync", out=qT_d[:, 0:128, r0:r0 + tr].rearrange("h p t -> p h t"), in_=o[:, :, 0, :tr])
            k.dma("sync", out=qT_d[:, 128:192, r0:r0 + tr].rearrange("h p t -> p h t"), in_=o[:64, :, 1, :tr])


SEG = 256
NSEG = 65
PI = 3.141592653589793


def sincos(k, dst, src, shift, w1, w2, kint):
    k.ts("vector", out=w1, in0=src, s1=shift + 8 * PI, s2=1.0 / (2 * PI), op0=ALU.add, op1=ALU.mult)
    k.cp("vector", out=kint, in_=w1)
    k.cp("vector", out=w1, in_=kint)
    k.ts("vector", out=w2, in0=src, s1=shift + 8 * PI, s2=None, op0=ALU.add)
    k.stt("vector", out=w1, in0=w1, scalar=-2 * PI, in1=w2, op0=ALU.mult, op1=ALU.add)
    k.ts("vector", out=w2, in0=w1, s1=PI, s2=2 * PI, op0=ALU.is_gt, op1=ALU.mult)
    k.tt("vector", out=w1, in0=w1, in1=w2, op=ALU.subtract)
    k.ts("vector", out=w1, in0=w1, s1=-PI, s2=PI, op0=ALU.max, op1=ALU.min)
    k.act(out=dst, in_=w1, func=AF.Sin)


def s5_stage(k, c, U_d, lamre_d, lamim_d, logdt_d, BTre_d, BTim_d, CTre_d, CTim_d, jtab_d, Y_d, nseg=NSEG):
    P = c["P"]
    S = SEG
    sb = lambda n, s: k.sb("s5_" + n, s, F32)
    lre, lim, ldt = sb("lre", [64, 16]), sb("lim", [64, 16]), sb("ldt", [64, 16])
    BTre, BTim = sb("BTre", [16, 16, 64]), sb("BTim", [16, 16, 64])
    CTre, CTim = sb("CTre", [64, 16, 16]), sb("CTim", [64, 16, 16])
    jt = sb("jt", [64, S + 1])
    for (b, d_) in ((lre, lamre_d), (lim, lamim_d)):
        k.dma("sync", out=b[:, :], in_=d_[:, :])
    k.dma("sync", out=ldt[:, :], in_=logdt_d[:].pbcast(64))
    k.dma("sync", out=BTre[:, :, :], in_=BTre_d[:, :, :])
    k.dma("sync", out=BTim[:, :, :], in_=BTim_d[:, :, :])
    k.dma("sync", out=CTre[:, :, :], in_=CTre_d[:, :, :])
    k.dma("sync", out=CTim[:, :, :], in_=CTim_d[:, :, :])
    k.dma("sync", out=jt[:, :], in_=jtab_d[:, :])
    negpi = sb("negpi", [64, 1])
    k.memset("vector", negpi[:, :], -PI)
    dt, th, mag = sb("dt", [64, 16]), sb("th", [64, 16]), sb("mag", [64, 16])
    k.act(out=dt[:, :], in_=ldt[:, :], func=AF.Exp)
    k.tt("vector", out=th[:, :], in0=lim[:, :], in1=dt[:, :], op=ALU.mult)
    k.tt("vector", out=mag[:, :], in0=lre[:, :], in1=dt[:, :], op=ALU.mult)
    k.act(out=mag[:, :], in_=mag[:, :], func=AF.Exp)
    g = sb("g", [64, 2, 16, S])
    h = sb("h", [64, 2, 16, S])
    ang = g[:, 0, :, :]
    cosT, sinT = sb("cosT", [64, 16, S]), sb("sinT", [64, 16, S])
    tA = sb("tA", [64, 16, S])
    ki = tA.bitcast_view(mybir.dt.int32)
    jv = jt[:, 0:S].ap
    jb = View(jt, bass.AP(jv.tensor, jv.offset, [list(jv.ap[0]), [0, 16], [1, S]]))
    k.tt("vector", out=ang, in0=jb, in1=th[:, :].unsq_bcast(S), op=ALU.mult)

    sincos(k, sinT[:, :, :], ang, 0.0, h[:, 0, :, :], h[:, 1, :, :], ki[:, :, :])
    sincos(k, cosT[:, :, :], ang, PI / 2, h[:, 0, :, :], h[:, 1, :, :], ki[:, :, :])
    psc, pss, pa = sb("psc", [64, 16]), sb("pss", [64, 16]), sb("pa", [64, 16])
    pw1, pw2 = sb("pw1", [64, 16]), sb("pw2", [64, 16])
    k.ts("vector", out=pa[:, :], in0=th[:, :], s1=float(S), s2=None, op0=ALU.mult)
    sincos(k, pss[:, :], pa[:, :], 0.0, pw1[:, :], pw2[:, :], ki[:, :, 0])
    sincos(k, psc[:, :], pa[:, :], PI / 2, pw1[:, :], pw2[:, :], ki[:, :, 0])
    are, aim = sb("are", [64, 16]), sb("aim", [64, 16])
    k.tt("vector", out=are[:, :], in0=mag[:, :], in1=cosT[:, :, 1], op=ALU.mult)
    k.tt("vector", out=aim[:, :], in0=mag[:, :], in1=sinT[:, :, 1], op=ALU.mult)
    den, w1, w2 = sb("den", [64, 16]), sb("w1", [64, 16]), sb("w2", [64, 16])
    kre, kim, nr = sb("kre", [64, 16]), sb("kim", [64, 16]), sb("nr", [64, 16])
    k.tt("vector", out=w1[:, :], in0=lre[:, :], in1=lre[:, :], op=ALU.mult)
    k.tt("vector", out=w2[:, :], in0=lim[:, :], in1=lim[:, :], op=ALU.mult)
    k.tt("vector", out=den[:, :], in0=w1[:, :], in1=w2[:, :], op=ALU.add)
    k.I("vector", "reciprocal", out=den[:, :], in_=den[:, :])
    k.ts("vector", out=nr[:, :], in0=are[:, :], s1=-1.0, s2=None, op0=ALU.add)
    k.tt("vector", out=w1[:, :], in0=nr[:, :], in1=lre[:, :], op=ALU.mult)
    k.tt("vector", out=w2[:, :], in0=aim[:, :], in1=lim[:, :], op=ALU.mult)
    k.tt("vector", out=kre[:, :], in0=w1[:, :], in1=w2[:, :], op=ALU.add)
    k.tt("vector", out=kre[:, :], in0=kre[:, :], in1=den[:, :], op=ALU.mult)
    k.tt("vector", out=w1[:, :], in0=aim[:, :], in1=lre[:, :], op=ALU.mult)
    k.tt("vector", out=w2[:, :], in0=nr[:, :], in1=lim[:, :], op=ALU.mult)
    k.tt("vector", out=kim[:, :], in0=w1[:, :], in1=w2[:, :], op=ALU.subtract)
    k.tt("vector", out=kim[:, :], in0=kim[:, :], in1=den[:, :], op=ALU.mult)
    PhR, PhI, magT = sb("PhR", [64, 16, S]), sb("PhI", [64, 16, S]), sb("magT", [64, 16, S])
    tB = h[:, 0, :, :]
    cS, sS = cosT[:, :, :], sinT[:, :, :]
    k.tt("vector", out=tA[:, :, :], in0=cS, in1=kre[:, :].unsq_bcast(S), op=ALU.mult)
    k.tt("gpsimd", out=tB, in0=sS, in1=kim[:, :].unsq_bcast(S), op=ALU.mult)
    k.tt("vector", out=PhR[:, :, :], in0=tA[:, :, :], in1=tB, op=ALU.add)
    k.tt("vector", out=tA[:, :, :], in0=cS, in1=kim[:, :].unsq_bcast(S), op=ALU.mult)
    k.tt("gpsimd", out=tB, in0=sS, in1=kre[:, :].unsq_bcast(S), op=ALU.mult)
    k.tt("vector", out=PhI[:, :, :], in0=tA[:, :, :], in1=tB, op=ALU.subtract)
    k.memset("vector", magT[:, :, :], 1.0)
    k.tt("vector", out=magT[:, :, :], in0=magT[:, :, :], in1=mag[:, :].unsq_bcast(S), op=ALU.mult)
    nCTim = sb("nCTim", [64, 16, 16])
    k.ts("vector", out=nCTim[:, :, :], in0=CTim[:, :, :], s1=-1.0, s2=None, op0=ALU.mult)
    car = sb("car", [64, 2, 16])
    k.memset("vector", car[:, :, :], 0.0)
    m = [sb(f"m{i}", [64, 2, S]) for i in range(2)]
    t4 = [sb(f"t4_{i}", [64, 4, S]) for i in range(2)]
    Ub = [sb("U0", [16, 16, S])] * 2
    Yb = [sb("Y0", [16, 4, S])] * 2
    cw = sb("cw", [64, 6, 16])
    Yv = Y_d.ap().rearrange("l c t -> c l t")
    for sg in range(nseg):
        U = Ub[sg % 2]
        k.dma("sync", out=U[:, :, :], in_=U_d[:, :, sg * S:(sg + 1) * S])
        def modulate_lane(l):
            ps = P[l % 4]
            k.mm(out=ps[:64, 0:S], lhsT=BTre[:, l, :], rhs=U[:, l, :])
            k.mm(out=ps[:64, S:2 * S], lhsT=BTim[:, l, :], rhs=U[:, l, :])
            bre, bim = ps[:64, 0:S], ps[:64, S:2 * S]
            tt_ = t4[l % 2]
            mm_ = m[l % 2]
            k.tt("vector", out=tt_[:, 0, :], in0=bre, in1=PhR[:, l, :], op=ALU.mult)
            k.tt("vector", out=tt_[:, 1, :], in0=bim, in1=PhI[:, l, :], op=ALU.mult)
            k.tt("vector", out=tt_[:, 2, :], in0=bre, in1=PhI[:, l, :], op=ALU.mult)
            k.tt("vector", out=tt_[:, 3, :], in0=bim, in1=PhR[:, l, :], op=ALU.mult)
            k.tt("gpsimd", out=mm_[:, 0, :], in0=tt_[:, 0, :], in1=tt_[:, 1, :], op=ALU.subtract)
            k.tt("gpsimd", out=mm_[:, 1, :], in0=tt_[:, 2, :], in1=tt_[:, 3, :], op=ALU.add)

        modulate_lane(0)
        for l in range(16):
            if l + 1 < 16:
                modulate_lane(l + 1)
            mm_ = m[l % 2]
            for ri in range(2):
                k.I("vector", "tensor_tensor_scan", out=g[:, ri, l, :], data0=magT[:, l, :], data1=mm_[:, ri, :],
                    initial=car[:, ri, l:l + 1], op0=ALU.mult, op1=ALU.add)
        gl_re, gl_im = g[:, 0, :, S - 1], g[:, 1, :, S - 1]
        pc, psn = psc[:, :], pss[:, :]
        k.tt("vector", out=cw[:, 0, :], in0=gl_re, in1=pc, op=ALU.mult)
        k.tt("vector", out=cw[:, 1, :], in0=gl_im, in1=psn, op=ALU.mult)
        k.tt("vector", out=cw[:, 2, :], in0=gl_re, in1=psn, op=ALU.mult)
        k.tt("vector", out=cw[:, 3, :], in0=gl_im, in1=pc, op=ALU.mult)
        k.tt("vector", out=car[:, 0, :], in0=cw[:, 0, :], in1=cw[:, 1, :], op=ALU.subtract)
        k.tt("vector", out=car[:, 1, :], in0=cw[:, 2, :], in1=cw[:, 3, :], op=ALU.add)
        if sg == 0:
            continue
        k.tt("gpsimd", out=h[:, 0, :, :], in0=g[:, 0, :, :], in1=cS, op=ALU.mult)
        k.tt("vector", out=tA[:, :, :], in0=g[:, 1, :, :], in1=sS, op=ALU.mult)
        k.tt("vector", out=h[:, 0, :, :], in0=h[:, 0, :, :], in1=tA[:, :, :], op=ALU.subtract)
        k.tt("gpsimd", out=h[:, 1, :, :], in0=g[:, 0, :, :], in1=sS, op=ALU.mult)
        k.tt("vector", out=tA[:, :, :], in0=g[:, 1, :, :], in1=cS, op=ALU.mult)
        k.tt("vector", out=h[:, 1, :, :], in0=h[:, 1, :, :], in1=tA[:, :, :], op=ALU.add)
        Y = Yb[sg % 2]
        for l in range(16):
            ps = P[4 + (l // 2) % 4]
            o = ps[:16, (l % 2) * S:(l % 2 + 1) * S]
            k.mm(out=o, lhsT=CTre[:, l, :], rhs=h[:, 0, l, :], start=True, stop=False)
            k.mm(out=o, lhsT=nCTim[:, l, :], rhs=h[:, 1, l, :], start=False, stop=True)
            if l % 2 == 1:
                l8 = (l - 1) % 4
                k.cp("scalar", out=Y[:, l8:l8 + 2, :].rearrange("p l t -> p (l t)"), in_=ps[:16, :])
            if l % 4 == 3:
                k.dma("sync", out=Yv[:, l - 3:l + 1, (sg - 1) * S:sg * S], in_=Y[:, :, :])


def attn_stage(k, c, qT_d, kT_d, v_d, attT_d, R=2048, NK=16640, nheads=8):
    P = c["P"]
    nkt = NK // 128
    ka = [k.sb(f"at_ka{i}", [128, NK], BF16) for i in range(2)]
    kb_lo = k.sb("at_kb", [128, NK], BF16)
    kb_hi = Buf(k, "at_kb_hi", kb_lo.h, "sb")
    k.bufs.append(kb_hi)
    kbs = (kb_lo, kb_hi)
    vv = [k.sb(f"at_v{i}", [128, nkt, 128], BF16) for i in range(2)]
    qa = [k.sb(f"at_qa{i}", [128, R], BF16) for i in range(2)]
    qb_lo = k.sb("at_qb", [128, R], BF16)
    qb_hi = Buf(k, "at_qb_hi", qb_lo.h, "sb")
    k.bufs.append(qb_hi)
    qbs = (qb_lo, qb_hi)
    ones = k.sb("at_ones", [128, 128], BF16)
    k.memset("vector", ones[:, :], 1.0)
    PT = [k.sb(f"at_PT{i}", [128, 512], BF16) for i in range(3)]
    rden = k.sb("at_rden", [128, 512], F32)
    oT = [k.sb(f"at_oT{i}", [128, 512], BF16) for i in range(2)]
    it = 0

    def load_head(h):
        s_ = h % 2
        pr = slice(s_ * 64, s_ * 64 + 64)
        k.dma("sync", out=ka[s_][:, :], in_=kT_d[h, 0:128, :])
        k.dma("sync", out=kbs[s_][pr, :], in_=kT_d[h, 128:192, :])
        vsrc = v_d[:, h * 128:(h + 1) * 128].rearrange("(t p) e -> p t e", p=128)
        hk = nkt // 2
        k.dma("gpsimd", out=vv[s_][:, :hk, :], in_=vsrc[:, :hk, :])
        k.dma("gpsimd", out=vv[s_][:, hk:, :], in_=vsrc[:, hk:, :])
        k.dma("sync", out=qa[s_][:, :], in_=qT_d[h, 0:128, :])
        k.dma("sync", out=qbs[s_][pr, :], in_=qT_d[h, 128:192, :])

    load_head(0)
    for h in range(nheads):
        if h + 1 < nheads:
            load_head(h + 1)
        s_ = h % 2
        pr = slice(s_ * 64, s_ * 64 + 64)
        ka_h, vv_h, qa_h = ka[s_], vv[s_], qa[s_]
        for qb_i in range(R // 512):
            qs = slice(qb_i * 512, (qb_i + 1) * 512)
            Ops, Dps = P[4 + (qb_i % 2)], P[6 + (qb_i % 2)]
            def scores(kt_, it_):
                S_ = P[it_ % 3]
                ks = slice(kt_ * 128, (kt_ + 1) * 128)
                k.mm(out=S_[:, :], lhsT=ka_h[:, ks], rhs=qa_h[:, qs], start=True, stop=False)
                k.mm(out=S_[:, :], lhsT=kbs[s_][pr, ks], rhs=qbs[s_][pr, qs], start=False, stop=True)

            scores(0, it)
            for kt in range(nkt):
                S = P[it % 3]
                pt = PT[it % 3]
                if kt + 1 < nkt:
                    scores(kt + 1, it + 1)
                k.act(out=pt[:, :], in_=S[:, :], func=AF.Exp)
                k.mm(out=Ops[:, :], lhsT=vv_h[:, kt, :], rhs=pt[:, :], start=(kt == 0), stop=(kt == nkt - 1))
                k.mm(out=Dps[:, :], lhsT=ones[:, :], rhs=pt[:, :], start=(kt == 0), stop=(kt == nkt - 1))
                it += 1
            k.I("vector", "reciprocal", out=rden[:, :], in_=Dps[:, :])
            o = oT[qb_i % 2]
            k.tt("vector", out=o[:, :], in0=Ops[:, :], in1=rden[:, :], op=ALU.mult)
            k.dma("sync", out=attT_d[h * 128:(h + 1) * 128, qs], in_=o[:, :])


def evout_stage(k, c, xa_d, u_d, yf_d, yb_d, attT_d, d_d, gluw_d, glub_d, wout_d, gate_d, xm_d, R=2048):
    P = c["P"]
    idb = c["identb"]
    wout = k.sb("eo_wout", [128, 16, 2048], BF16)
    gluw = k.sb("eo_gluw", [128, 8, 1024], BF16)
    k.dma("gpsimd", out=wout[:, :, :], in_=wout_d.ap().rearrange("(kt p) c -> p kt c", p=128))
    k.dma("gpsimd", out=gluw[:, :, :], in_=gluw_d.ap().rearrange("(kt p) c -> p kt c", p=128))
    drow, brow, grow = k.sb("eo_drow", [128, 1024], F32), k.sb("eo_brow", [128, 1024], F32), k.sb("eo_grow", [128, 2048], F32)
    k.dma("sync", out=drow[:, :], in_=d_d[:].pbcast(128))
    k.dma("sync", out=brow[:, :], in_=glub_d[:].pbcast(128))
    k.dma("sync", out=grow[:, :], in_=gate_d[:].pbcast(128))
    ut, yft, ybt = k.sb("eo_u", [128, 1024], F32), k.sb("eo_yf", [128, 1024], F32), k.sb("eo_yb", [128, 1024], F32)
    y, w1, ge = k.sb("eo_y", [128, 1024], F32), k.sb("eo_w1", [128, 1024], F32), k.sb("eo_ge", [128, 1024], F32)
    geb, ssb = k.sb("eo_geb", [128, 1024], BF16), k.sb("eo_ssb", [128, 1024], BF16)
    geT, ssT = k.sb("eo_geT", [128, 8, 128], BF16), k.sb("eo_ssT", [128, 8, 128], BF16)
    att = k.sb("eo_att", [128, 8, 128], BF16)
    xt = k.sb("eo_x", [128, 2048], F32)
    pT = P[7].bitcast_view(BF16)
    attv = attT_d.ap().rearrange("(kt p) t -> p kt t", p=128)
    for t in range(R // 128):
        rs = slice(t * 128, (t + 1) * 128)
        k.dma("sync", out=ut[:, :], in_=u_d[rs, :])
        k.dma("sync", out=yft[:, :], in_=yf_d[rs, :])
        k.dma("sync", out=ybt[:, :], in_=yb_d[rs, :])
        k.dma("sync", out=xt[:, :], in_=xa_d[rs, :])
        k.dma("sync", out=att[:, :, :], in_=attv[:, :, rs])
        k.tt("vector", out=y[:, :], in0=ut[:, :], in1=drow[:, :], op=ALU.mult)
        k.tt("gpsimd", out=w1[:, :], in0=yft[:, :], in1=ybt[:, :], op=ALU.add)
        k.tt("vector", out=y[:, :], in0=y[:, :], in1=w1[:, :], op=ALU.add)
        k.tt("gpsimd", out=w1[:, :], in0=y[:, :], in1=y[:, :], op=ALU.mult)
        k.ts("vector", out=w1[:, :], in0=w1[:, :], s1=0.044715, s2=1.0, op0=ALU.mult, op1=ALU.add)
        k.tt("vector", out=w1[:, :], in0=w1[:, :], in1=y[:, :], op=ALU.mult)
        k.act(out=w1[:, :], in_=w1[:, :], func=AF.Sigmoid, scale=1.5957691216057308)
        k.tt("vector", out=ge[:, :], in0=y[:, :], in1=w1[:, :], op=ALU.mult)
        k.cp("gpsimd", out=geb[:, :], in_=ge[:, :])
        for kt in range(8):
            k.tr(out=pT[:, kt * 128:(kt + 1) * 128], in_=geb[:, kt * 128:(kt + 1) * 128], ident=idb[:, :])
        k.cp("vector", out=geT[:, :, :].rearrange("p k t -> p (k t)"), in_=pT[:, :])
        for bi in range(2):
            for kt in range(8):
                k.mm(out=P[bi][:, :], lhsT=geT[:, kt, :], rhs=gluw[:, kt, bi * 512:(bi + 1) * 512], start=(kt == 0), stop=(kt == 7))
            k.tt("vector", out=w1[:, bi * 512:(bi + 1) * 512], in0=P[bi][:, :], in1=brow[:, bi * 512:(bi + 1) * 512], op=ALU.add)
        k.act(out=w1[:, :], in_=w1[:, :], func=AF.Sigmoid)
        k.tt("vector", out=ssb[:, :], in0=ge[:, :], in1=w1[:, :], op=ALU.mult)
        for kt in range(8):
            k.tr(out=pT[:, kt * 128:(kt + 1) * 128], in_=ssb[:, kt * 128:(kt + 1) * 128], ident=idb[:, :])
        k.cp("vector", out=ssT[:, :, :].rearrange("p k t -> p (k t)"), in_=pT[:, :])
        for bi in range(4):
            ps = P[2 + bi]
            for kt in range(16):
                lt = att[:, kt, :] if kt < 8 else ssT[:, kt - 8, :]
                k.mm(out=ps[:, :], lhsT=lt, rhs=wout[:, kt, bi * 512:(bi + 1) * 512], start=(kt == 0), stop=(kt == 15))
            cs_ = slice(bi * 512, (bi + 1) * 512)
            k.tt("vector", out=y[:, 0:512], in0=ps[:, :], in1=grow[:, cs_], op=ALU.mult)
            k.tt("vector", out=xt[:, cs_], in0=xt[:, cs_], in1=y[:, 0:512], op=ALU.add)
        k.dma("sync", out=xm_d[rs, :], in_=xt[:, :])


def hypre_stage(k, c, xh_d, hmask_d, A, Braw, w_in_d, cw_d, cb_d, zc_d):
    P = c["P"]
    if c.get("nm_bufs") is None:
        c["nm_bufs"] = dict(
            junk=k.sb("junk", [128, 2048], BF16), ss=k.sb("ss", [128, 1], F32), rstd=k.sb("rstd", [128, 1], F32),
            xh=k.sb("xh", [128, 2048], BF16), tmpT=k.sb("tmpT", [128, 8, 128], F32))
    bufs = c["nm_bufs"]
    hT = k.sb("hp_hT", [128, 16, 2050], BF16)
    hTh = k.sb("hp_hTh", [128, 16, 2], BF16)
    hm = k.sb("hp_hm", [128, 2], F32)
    cw = k.sb("hp_cw", [128, 48, 3], F32)
    cb = k.sb("hp_cb", [128, 48], F32)
    k.dma("sync", out=hm[:, :], in_=hmask_d[:, :])
    k.dma("sync", out=cw[:, :, :], in_=cw_d[:, :, :])
    k.dma("sync", out=cb[:, :], in_=cb_d[:, :])
    xt = [k.sb(f"hp_x{i}", [128, 2048], F32) for i in range(2)]
    for t in range(16):
        x = xt[t % 2]
        k.dma("sync", out=x[:, :], in_=xh_d[1 + t * 128:1 + (t + 1) * 128, :])
        norm_mod_T(k, c, "hp", x[:, :], 128, A, Braw, hT[:, :, 1 + t * 128:1 + (t + 1) * 128], bufs)
    xhalo = k.sb("hp_xhalo", [2, 2048], F32)
    k.dma("sync", out=xhalo[0:1, :], in_=xh_d[0:1, :])
    k.dma("sync", out=xhalo[1:2, :], in_=xh_d[2049:2050, :])
    norm_mod_T(k, c, "hp", xhalo[:2, :], 2, A, Braw, hTh[:, :, :], bufs)
    k.ts("vector", out=hT[:, :, 0:1], in0=hTh[:, :, 0:1], s1=hm[:, 0:1], s2=None, op0=ALU.mult)
    k.ts("vector", out=hT[:, :, 2049:2050], in0=hTh[:, :, 1:2], s1=hm[:, 1:2], s2=None, op0=ALU.mult)
    wv = w_in_d.ap().rearrange("(kt p) c -> p kt c", p=128)
    wj = [k.sb(f"hp_w{i}", [128, 16, 128], BF16) for i in range(2)]
    osb = [k.sb(f"hp_o{i}", [128, 512], F32) for i in range(2)]
    k.dma("gpsimd", out=wj[0][:, :, :], in_=wv[:, :, 0:128])
    it = 0
    for j in range(48):
        if j + 1 < 48:
            k.dma("gpsimd", out=wj[(j + 1) % 2][:, :, :], in_=wv[:, :, (j + 1) * 128:(j + 2) * 128])
        w = wj[j % 2]
        for b in range(4):
            zp = P[b]
            zh = P[4 + b]
            for kt in range(16):
                k.mm(out=zp[:, :], lhsT=w[:, kt, :], rhs=hT[:, kt, 1 + 512 * b:1 + 512 * (b + 1)], start=(kt == 0), stop=(kt == 15))
            for kt in range(16):
                k.mm(out=zh[:, 0:2], lhsT=w[:, kt, :], rhs=hT[:, kt, 512 * b:512 * b + 514:513], start=(kt == 0), stop=(kt == 15))
            o = osb[it % 2]
            w0, w1, w2 = cw[:, j, 0:1], cw[:, j, 1:2], cw[:, j, 2:3]
            k.ts("vector", out=o[:, :], in0=zp[:, :], s1=w1, s2=cb[:, j:j + 1], op0=ALU.mult, op1=ALU.add)
            k.stt("vector", out=o[:, 1:512], in0=zp[:, 0:511], scalar=w0, in1=o[:, 1:512], op0=ALU.mult, op1=ALU.add)
            k.stt("vector", out=o[:, 0:511], in0=zp[:, 1:512], scalar=w2, in1=o[:, 0:511], op0=ALU.mult, op1=ALU.add)
            k.stt("vector", out=o[:, 0:1], in0=zh[:, 0:1], scalar=w0, in1=o[:, 0:1], op0=ALU.mult, op1=ALU.add)
            k.stt("vector", out=o[:, 511:512], in0=zh[:, 1:2], scalar=w2, in1=o[:, 511:512], op0=ALU.mult, op1=ALU.add)
            k.dma("sync", out=zc_d[j * 128:(j + 1) * 128, 512 * b:512 * (b + 1)], in_=o[:, :])
            it += 1


def projres_stage(k, c, xin_d, yT_d, wout_d, gate_d, xm_d, R=2048):
    P = c["P"]
    wout = k.sb("pr_wout", [128, 16, 2048], BF16)
    yT = k.sb("pr_yT", [128, 16, R], BF16)
    k.dma("gpsimd", out=wout[:, :, :], in_=wout_d.ap().rearrange("(kt p) c -> p kt c", p=128))
    yv = yT_d.ap().rearrange("(kt p) t -> p kt t", p=128)
    for kt in range(16):
        k.dma("gpsimd", out=yT[:, kt, :], in_=yv[:, kt, :])
    grow = k.sb("pr_grow", [128, 2048], F32)
    k.dma("sync", out=grow[:, :], in_=gate_d[:].pbcast(128))
    xt = [k.sb(f"pr_x{i}", [128, 2048], F32) for i in range(2)]
    tmp = k.sb("pr_tmp", [128, 512], F32)
    for t in range(R // 128):
        rs = slice(t * 128, (t + 1) * 128)
        x = xt[t % 2]
        k.dma("sync", out=x[:, :], in_=xin_d[rs, :])
        for bi in range(4):
            ps = P[(t % 2) * 4 + bi]
            for kt in range(16):
                k.mm(out=ps[:, :], lhsT=yT[:, kt, rs], rhs=wout[:, kt, bi * 512:(bi + 1) * 512], start=(kt == 0), stop=(kt == 15))
            cs_ = slice(bi * 512, (bi + 1) * 512)
            k.tt("vector", out=tmp[:, :], in0=ps[:, :], in1=grow[:, cs_], op=ALU.mult)
            k.tt("vector", out=x[:, cs_], in0=x[:, cs_], in1=tmp[:, :], op=ALU.add)
        k.dma("sync", out=xm_d[rs, :], in_=x[:, :])


NFFT = 32768
LSEQ = 16384
CB = 16


def hy_consts():
    f8 = np.float64
    ts = np.arange(64, dtype=f8)[:, None]
    kf = np.arange(128, dtype=f8)[None, :]
    a1 = 2 * np.pi * ts * kf / 128
    F1cat = np.concatenate([np.cos(a1), -np.sin(a1)], axis=1)
    tf = np.arange(256, dtype=f8)[:, None]
    aw = 2 * np.pi * tf * kf / NFFT
    Wre = np.cos(aw).reshape(2, 128, 128).transpose(1, 0, 2)
    Wim = (-np.sin(aw)).reshape(2, 128, 128).transpose(1, 0, 2)
    ks = np.arange(256, dtype=f8)[None, :]
    a2 = 2 * np.pi * tf * ks / 256
    Cos = np.cos(a2).reshape(2, 128, 256).transpose(1, 0, 2)
    Sin = np.sin(a2).reshape(2, 128, 256).transpose(1, 0, 2)
    a2t = a2.T
    cA1 = np.concatenate([np.cos(a2t), np.sin(a2t)], axis=1).reshape(2, 128, 512).transpose(1, 0, 2)
    cA2 = np.concatenate([-np.sin(a2t), np.cos(a2t)], axis=1).reshape(2, 128, 512).transpose(1, 0, 2)
    awt = aw.T
    WTre, WTim = np.cos(awt), np.sin(awt)
    a1t = a1.T
    C1 = np.cos(a1t) / NFFT
    S1n = -np.sin(a1t) / NFFT
    b = lambda a: np.ascontiguousarray(a).astype(ml_dtypes.bfloat16)
    f = lambda a: np.ascontiguousarray(a, dtype=np.float32)
    return dict(F1cat=b(F1cat), Wre=f(Wre), Wim=f(Wim), Cos=b(Cos), Sin=b(Sin), NSin=b(-Sin), cA1=b(cA1), cA2=b(cA2),
                WTre=f(WTre), WTim=f(WTim), C1=b(C1), S1n=b(S1n))


HY_CONST_SHAPES = dict(F1cat=([64, 256], BF16), Wre=([128, 2, 128], F32), Wim=([128, 2, 128], F32), Cos=([128, 2, 256], BF16),
                       Sin=([128, 2, 256], BF16), NSin=([128, 2, 256], BF16), cA1=([128, 2, 512], BF16), cA2=([128, 2, 512], BF16),
                       WTre=([128, 256], F32), WTim=([128, 256], F32), C1=([128, 64], BF16), S1n=([128, 64], BF16))


def hy_load_consts(k, cd):
    out = {}
    for n, (shp, dt) in HY_CONST_SHAPES.items():
        b = k.sb("hc_" + n, shp, dt)
        src = cd[n]
        k.dma("sync", out=b.ap(), in_=src.ap())
        out[n] = b
    return out


def fft_stageA(k, c, hc, ybf, nch, Ab_re, Ab_im, tw, it0=0):
    P = c["P"]
    it = it0
    for cp in range(nch // 2):
        for half in range(2):
            ps = P[it % 2]
            for ci in range(2):
                ch = cp * 2 + ci
                k.mm(out=ps[:, ci * 256:(ci + 1) * 256], lhsT=ybf[:, ch, half * 128:(half + 1) * 128], rhs=hc["F1cat"][:, :])
            pv = ps[:, :].rearrange("p (c r f) -> p c r f", c=2, r=2)
            Are, Aim = pv[:, :, 0, :], pv[:, :, 1, :]
            wre = hc["Wre"][:, half, :].ap
            wim = hc["Wim"][:, half, :].ap
            wre_b = View(hc["Wre"], bass.AP(wre.tensor, wre.offset, [list(wre.ap[0]), [0, 2], [1, 128]]))
            wim_b = View(hc["Wim"], bass.AP(wim.tensor, wim.offset, [list(wim.ap[0]), [0, 2], [1, 128]]))
            t = tw[it % 2]
            tv = lambda i: t[:, i, 0:256].rearrange("p (c f) -> p c f", c=2)
            k.tt("vector", out=tv(0), in0=Are, in1=wre_b, op=ALU.mult)
            k.tt("vector", out=tv(1), in0=Aim, in1=wim_b, op=ALU.mult)
            k.tt("vector", out=tv(2), in0=Are, in1=wim_b, op=ALU.mult)
            k.tt("vector", out=tv(3), in0=Aim, in1=wre_b, op=ALU.mult)
            k.tt("gpsimd", out=Ab_re[:, half, cp * 2:cp * 2 + 2, :], in0=tv(0), in1=tv(1), op=ALU.subtract)
            k.tt("gpsimd", out=Ab_im[:, half, cp * 2:cp * 2 + 2, :], in0=tv(2), in1=tv(3), op=ALU.add)
            it += 1
    return it


def fft_stageB_block(k, c, hc, Ab_re, Ab_im, hk, blk, want_re=True, want_im=True):
    P = c["P"]
    cs = slice(blk * 4, blk * 4 + 4)
    ksl = slice(hk * 128, (hk + 1) * 128)
    Xre, Xim = P[2 + (blk % 2) * 2], P[3 + (blk % 2) * 2]
    if want_re:
        n = 0
        for ht in range(2):
            for (lt, rb) in ((hc["Cos"], Ab_re), (hc["Sin"], Ab_im)):
                k.mm(out=Xre[:, :], lhsT=lt[:, ht, ksl], rhs=rb[:, ht, cs, :].rearrange("p c f -> p (c f)"), start=(n == 0), stop=(n == 3))
                n += 1
    if want_im:
        n = 0
        for ht in range(2):
            for (lt, rb) in ((hc["NSin"], Ab_re), (hc["Cos"], Ab_im)):
                k.mm(out=Xim[:, :], lhsT=lt[:, ht, ksl], rhs=rb[:, ht, cs, :].rearrange("p c f -> p (c f)"), start=(n == 0), stop=(n == 3))
                n += 1
    return Xre, Xim


def hyconv_stage(k, c, cd, x1_d, x2_d, v_d, skip_d, Gre_d, Gim_d, y2_d, ncb=16):
    P = c["P"]
    hc = hy_load_consts(k, cd)
    sk = k.sb("hv_sk", [64, 2, 256], F32)
    for o in range(2):
        k.dma("sync", out=sk[:, o, :], in_=skip_d[o, :].pbcast(64))
    yf = k.sb("hv_yf", [64, CB, 256], F32)
    gt = k.sb("hv_gt", [64, CB, 256], F32)
    ybf = k.sb("hv_ybf", [64, CB, 256], BF16)
    Ab_re, Ab_im = k.sb("hv_Abre", [128, 2, CB, 128], BF16), k.sb("hv_Abim", [128, 2, CB, 128], BF16)
    Zb_re, Zb_im = k.sb("hv_Zbre", [128, 2, CB, 128], BF16), k.sb("hv_Zbim", [128, 2, CB, 128], BF16)
    Bb_re, Bb_im = k.sb("hv_Bbre", [128, CB, 256], BF16), k.sb("hv_Bbim", [128, CB, 256], BF16)
    tw = [k.sb(f"hv_tw{i}", [128, 4, 512], F32) for i in range(2)]
    Gt = [(k.sb(f"hv_Gre{i}", [128, 512], F32), k.sb(f"hv_Gim{i}", [128, 512], F32)) for i in range(2)]
    wsk = k.sb("hv_wsk", [64, 2, 256], F32)
    w2 = k.sb("hv_w2", [64, 2, 256], F32)
    gates = (x1_d, x2_d)
    it = 0
    for cb in range(ncb):
        chs = slice(cb * CB, (cb + 1) * CB)
        k.dma("sync", out=yf[:, :, :], in_=v_d[chs, :].rearrange("c (s f) -> s c f", s=64))
        for o in range(2):
            k.dma("sync", out=gt[:, :, :], in_=gates[o][chs, :].rearrange("c (s f) -> s c f", s=64))
            k.cp("scalar", out=ybf[:, :, :].rearrange("p c f -> p (c f)"), in_=yf[:, :, :].rearrange("p c f -> p (c f)"))
            it = fft_stageA(k, c, hc, ybf, CB, Ab_re, Ab_im, tw, it)
            for hk in range(2):
                for blk in range(CB // 4):
                    Xre, Xim = fft_stageB_block(k, c, hc, Ab_re, Ab_im, hk, blk)
                    gre, gim = Gt[it % 2]
                    c0 = (cb * CB + blk * 4) * 128
                    k.dma("sync", out=gre[:, :], in_=Gre_d[o, hk, :, c0:c0 + 512])
                    k.dma("sync", out=gim[:, :], in_=Gim_d[o, hk, :, c0:c0 + 512])
                    t = tw[it % 2]
                    k.tt("vector", out=t[:, 0, :], in0=Xre[:, :], in1=gre[:, :], op=ALU.mult)
                    k.tt("vector", out=t[:, 1, :], in0=Xim[:, :], in1=gim[:, :], op=ALU.mult)
                    k.tt("vector", out=t[:, 2, :], in0=Xre[:, :], in1=gim[:, :], op=ALU.mult)
                    k.tt("vector", out=t[:, 3, :], in0=Xim[:, :], in1=gre[:, :], op=ALU.mult)
                    zs = slice(blk * 4, blk * 4 + 4)
                    k.tt("gpsimd", out=Zb_re[:, hk, zs, :].rearrange("p c f -> p (c f)"), in0=t[:, 0, :], in1=t[:, 1, :], op=ALU.subtract)
                    k.tt("gpsimd", out=Zb_im[:, hk, zs, :].rearrange("p c f -> p (c f)"), in0=t[:, 2, :], in1=t[:, 3, :], op=ALU.add)
                    it += 1
            for ch in range(CB):
                ps = P[6 + (ch % 2)]
                n = 0
                for hk in range(2):
                    for (zb, rt) in ((Zb_re, hc["cA1"]), (Zb_im, hc["cA2"])):
                        k.mm(out=ps[:, :], lhsT=zb[:, hk, ch, :], rhs=rt[:, hk, :], start=(n == 0), stop=(n == 3))
                        n += 1
                Bre, Bim = ps[:, 0:256], ps[:, 256:512]
                t = tw[it % 2]
                k.tt("vector", out=t[:, 0, 0:256], in0=Bre, in1=hc["WTre"][:, :], op=ALU.mult)
                k.tt("vector", out=t[:, 1, 0:256], in0=Bim, in1=hc["WTim"][:, :], op=ALU.mult)
                k.tt("vector", out=t[:, 2, 0:256], in0=Bre, in1=hc["WTim"][:, :], op=ALU.mult)
                k.tt("vector", out=t[:, 3, 0:256], in0=Bim, in1=hc["WTre"][:, :], op=ALU.mult)
                k.tt("gpsimd", out=Bb_re[:, ch, :], in0=t[:, 0, 0:256], in1=t[:, 1, 0:256], op=ALU.subtract)
                k.tt("gpsimd", out=Bb_im[:, ch, :], in0=t[:, 2, 0:256], in1=t[:, 3, 0:256], op=ALU.add)
                it += 1
            for pr in range(CB // 2):
                ps = P[pr % 2]
                cs2 = slice(pr * 2, pr * 2 + 2)
                k.mm(out=ps[:64, :], lhsT=hc["C1"][:, :], rhs=Bb_re[:, cs2, :].rearrange("p c f -> p (c f)"), start=True, stop=False)
                k.mm(out=ps[:64, :], lhsT=hc["S1n"][:, :], rhs=Bb_im[:, cs2, :].rearrange("p c f -> p (c f)"), start=False, stop=True)
                skb = sk[:, o, cb * CB + pr * 2:cb * CB + pr * 2 + 2].unsq_bcast(256)
                k.tt("gpsimd", out=wsk[:, :, :], in0=yf[:, cs2, :], in1=skb, op=ALU.mult)
                k.tt("vector", out=w2[:, :, :], in0=ps[:64, :].rearrange("p (c f) -> p c f", c=2), in1=wsk[:, :, :], op=ALU.add)
                k.tt("gpsimd", out=yf[:, cs2, :], in0=w2[:, :, :], in1=gt[:, cs2, :], op=ALU.mult)
        k.dma("sync", out=y2_d[chs, :].rearrange("c (s f) -> s c f", s=64), in_=yf[:, :, :])


def hy_filt_consts(ci):
    import math
    L = LSEQ
    t = np.arange(L, dtype=np.float32) / np.float32(L)
    bands = np.linspace(1e-4, 15, 16, dtype=np.float32)
    ang = (np.float32(2.0 * math.pi) * t[:, None] * bands[None, :]).astype(np.float32)
    feat = np.concatenate([t[:, None], np.cos(ang), -np.sin(ang)], axis=-1).astype(np.float32)
    dmin, dmax = math.log(1e-2) / 1.5, math.log(1e-2) / 0.3
    deltas = np.abs(np.linspace(dmin, dmax, 2048, dtype=np.float32))[ci * 256:(ci + 1) * 256].astype(np.float64)
    drow = np.broadcast_to(deltas[None, :], (64, 256))
    E1 = np.exp(-(256.0 * np.arange(64, dtype=np.float64)[:, None] / L) * deltas[None, :])
    tfrow = np.broadcast_to(np.arange(256, dtype=np.float32)[None, :], (64, 256))
    return dict(featT=np.ascontiguousarray(feat.T), drow=np.ascontiguousarray(drow, dtype=np.float32), E1=np.ascontiguousarray(E1, dtype=np.float32),
                tfrow=np.ascontiguousarray(tfrow))


def hyfilt_stage(k, c, cd, featT_d, w1_d, w2_d, w3_d, bf_d, wout_d, drow_d, E1_d, tfrow_d, Gre_d, Gim_d, nfb=16, norders=2):
    P = c["P"]
    hc = {}
    for n in ("F1cat", "Wre", "Wim", "Cos", "Sin", "NSin"):
        shp, dt = HY_CONST_SHAPES[n]
        hc[n] = k.sb("hc_" + n, shp, dt)
        k.dma("sync", out=hc[n].ap(), in_=cd[n].ap())
    sb = lambda n, s, dt=F32: k.sb("hf_" + n, s, dt)
    w1s, w2s, w3s, bf = sb("w1", [33, 64]), sb("w2", [64, 64]), sb("w3", [64, 64]), sb("bf", [64, 4])
    k.dma("sync", out=w1s[:, :], in_=w1_d[:, :])
    k.dma("sync", out=w2s[:, :], in_=w2_d[:, :])
    k.dma("sync", out=w3s[:, :], in_=w3_d[:, :])
    k.dma("sync", out=bf[:, :], in_=bf_d[:, :])
    frb = sb("frb", [64, 3])
    for l in range(3):
        k.tt("vector", out=frb[:, l:l + 1], in0=bf[:, l:l + 1], in1=bf[:, 3:4], op=ALU.mult)
    drow, E1 = sb("drow", [64, 256]), sb("E1", [64, 256])
    k.dma("sync", out=drow[:, :], in_=drow_d[:, :])
    k.dma("sync", out=E1[:, :], in_=E1_d[:, :])
    hidT = k.sb("hf_hidT", [64, LSEQ], BF16)
    ft = [sb(f"ft{i}", [33, 512]) for i in range(2)]
    a_, s1_, s2_ = sb("a", [64, 512]), sb("s1", [64, 512]), sb("s2", [64, 512])
    ki = k.sb("hf_ki", [64, 512], mybir.dt.int32)
    hA, hB = sb("hA", [64, 512]), sb("hB", [64, 512])
    ws = (w1s, w2s, w3s)
    for blk in range(LSEQ // 512):
        f = ft[blk % 2]
        k.dma("sync", out=f[:, :], in_=featT_d[:, blk * 512:(blk + 1) * 512])
        src = f[:, :]
        dsts = (hA[:, :], hB[:, :], hidT[:, blk * 512:(blk + 1) * 512])
        for l in range(3):
            ps = P[(blk * 3 + l) % 2]
            k.mm(out=ps[:64, :], lhsT=ws[l][:, :], rhs=src)
            k.ts("vector", out=a_[:, :], in0=ps[:64, :], s1=bf[:, 3:4], s2=frb[:, l:l + 1], op0=ALU.mult, op1=ALU.add)
            sincos(k, dsts[l], a_[:, :], 0.0, s1_[:, :], s2_[:, :], ki[:, :])
            src = dsts[l]
    ones = sb("ones", [64, 128])
    k.memset("vector", ones[:, :], 1.0)
    Wo = [k.sb(f"hf_Wo{i}", [64, 2, 16], BF16) for i in range(2)]
    Wn = sb("Wn", [64, 16, 256])
    tfrow = sb("tfrow", [64, 256])
    k.dma("sync", out=tfrow[:, :], in_=tfrow_d[:, :])
    Hf = sb("Hf", [64, 2, 16, 256])
    Hsd = k.sb("hf_Hsd", [64, 2, 16, 256], BF16)
    absum, s2n, rn = sb("absum", [64, 32]), sb("s2n", [64, 16]), sb("rn", [128, 16])
    Ab_re, Ab_im = k.sb("hf_Abre", [128, 2, 16, 128], BF16), k.sb("hf_Abim", [128, 2, 16, 128], BF16)
    tw = [sb(f"tw{i}", [128, 4, 512]) for i in range(2)]
    go = [sb(f"go{i}", [128, 512]) for i in range(2)]
    it = 0
    ig = 0
    for o in range(norders):
        for fb in range(nfb):
            chs = slice(fb * 16, (fb + 1) * 16)
            wo = Wo[fb % 2]
            k.dma("gpsimd", out=wo[:, :, :], in_=wout_d[:, o, :, chs])
            dv = drow[:, chs].ap
            dr_b = View(drow, bass.AP(dv.tensor, dv.offset, [list(dv.ap[0]), [1, 16], [0, 256]]))
            tv_ = tfrow[:, :].ap
            tf_b = View(tfrow, bass.AP(tv_.tensor, tv_.offset, [list(tv_.ap[0]), [0, 16], [1, 256]]))
            k.tt("gpsimd", out=Wn[:, :, :], in0=dr_b, in1=tf_b, op=ALU.mult)
            k.act(out=Wn[:, :, :], in_=Wn[:, :, :], func=AF.Exp, scale=-1.0 / LSEQ)
            k.tt("gpsimd", out=Wn[:, :, :], in0=Wn[:, :, :], in1=E1[:, chs].unsq_bcast(256), op=ALU.mult)
            wflat = wo[:, :, :].rearrange("p d c -> p (d c)")
            for tg in range(32):
                ps = P[2 + tg % 2]
                for j in range(8):
                    tfv = tg * 8 + j
                    k.mm(out=ps[:64, j * 32:(j + 1) * 32], lhsT=hidT[:, tfv:LSEQ:256], rhs=wflat)
                wv_ = Wn[:, :, tg * 8:(tg + 1) * 8].ap
                Wn_b = View(Wn, bass.AP(wv_.tensor, wv_.offset, [list(wv_.ap[0]), [0, 2], [256, 16], [1, 8]]))
                k.tt("vector", out=Hf[:, :, :, tg * 8:(tg + 1) * 8], in0=ps[:64, 0:256].rearrange("p (j d c) -> p d c j", j=8, d=2),
                     in1=Wn_b, op=ALU.mult)
            k.memset("vector", Hf[0:1, 1, :, 0:1], 0.0)
            k.I("vector", "tensor_reduce", out=absum[:, :], in_=Hf[:, :, :, :].rearrange("p d c f -> p (d c) f"), axis=AX.X, op=ALU.add,
                apply_absolute_value=True)
            k.tt("vector", out=s2n[:, :], in0=absum[:, 0:16], in1=absum[:, 16:32], op=ALU.add)
            psn = P[4]
            k.mm(out=psn[:, 0:16], lhsT=ones[:, :], rhs=s2n[:, :])
            k.I("vector", "reciprocal", out=rn[:, :], in_=psn[:, 0:16])
            k.tt("gpsimd", out=Hsd[:, 0, :, :], in0=Hf[:, 0, :, :], in1=Hf[:, 1, :, :], op=ALU.add)
            k.tt("gpsimd", out=Hsd[:, 1, :, :], in0=Hf[:, 0, :, :], in1=Hf[:, 1, :, :], op=ALU.subtract)
            for sd in range(2):
                it = fft_stageA(k, c, hc, Hsd[:, sd, :, :], 16, Ab_re, Ab_im, tw, it)
                for hk in range(2):
                    for blk in range(4):
                        Xre, Xim = fft_stageB_block(k, c, hc, Ab_re, Ab_im, hk, blk, want_re=(sd == 0), want_im=(sd == 1))
                        X = Xre if sd == 0 else Xim
                        g_ = go[ig % 2]
                        k.tt("vector", out=g_[:, :].rearrange("p (c f) -> p c f", c=4), in0=X[:, :].rearrange("p (c f) -> p c f", c=4),
                             in1=rn[:, blk * 4:blk * 4 + 4].unsq_bcast(128), op=ALU.mult)
                        c0 = (fb * 16 + blk * 4) * 128
                        dst = Gre_d if sd == 0 else Gim_d
                        k.dma("sync", out=dst[o, hk, :, c0:c0 + 512], in_=g_[:, :])
                        ig += 1


def wprep_stage(k, c, win_d, wout_d, winb_d, woutb_d, nf=4):
    bufs = [k.sb(f"wp_b{i}", [128, 5632], BF16) for i in range(3)]
    it = 0
    for f in range(nf):
        for (src, dst, n) in ((win_d, winb_d, 22528), (wout_d, woutb_d, 11264)):
            for c0 in range(0, n, 5632):
                b = bufs[it % 3]
                k.dma("gpsimd", out=b[:, :], in_=src[f, :, c0:c0 + 5632])
                k.dma("sync", out=dst[f, :, c0:c0 + 5632], in_=b[:, :])
                it += 1


def _cols(v, n=16):
    return np.ascontiguousarray(np.asarray(v, np.float32).reshape(n, 128).T)


def _modc(m, k0):
    return np.ascontiguousarray(np.stack([_cols(m[k0]), _cols(m[k0 + 1]), _cols(m[k0 + 2])], axis=1).astype(np.float32))


def _rope_tabs(n_tokens):
    t = np.arange(n_tokens)
    row = (t // 64).astype(np.float32)
    col = (t % 64).astype(np.float32)
    inv = (10000.0 ** (-np.arange(16, dtype=np.float32) / 16)).astype(np.float32)
    ar = row[:, None] * inv
    ac = col[:, None] * inv
    cos = np.concatenate([np.cos(ar), np.cos(ac)], axis=1).astype(np.float32)
    sin = np.concatenate([np.sin(ar), np.sin(ac)], axis=1).astype(np.float32)
    return cos, sin


_IDF = np.eye(128, dtype=np.float32)
_IDB = _IDF.astype(ml_dtypes.bfloat16)
_PROGS = {}
_DBG = {}


def _prog(key, fn):
    if key not in _PROGS:
        _PROGS[key] = fn()
    return _PROGS[key]


def _run(nc, maps):
    return run_bass_kernel_spmd(nc, maps, core_ids=list(range(8))).results


def _build_ada():
    k = KB()
    cc = k.dram("cc", [128, 16, 2], F32, kind="ExternalInput")
    w = k.dram("w", [2, 2048, 2304], F32, kind="ExternalInput")
    b = k.dram("b", [2, 2304], F32, kind="ExternalInput")
    out = k.dram("out", [2, 2, 2304], F32, kind="ExternalOutput")
    c = alloc_common(k)
    ada_stage(k, c, cc, w, b, out)
    return k.emit()


def _build_ffn(R):
    k = KB()
    xin = k.dram("xin", [R, D], F32, kind="ExternalInput")
    xout = k.dram("xout", [R, D], F32, kind="ExternalOutput")
    modc = k.dram("modc", [128, 3, 16], F32, kind="ExternalInput")
    normg = k.dram("normg", [128, 16], F32, kind="ExternalInput")
    w_in = k.dram("w_in", [D, 2 * DFF], BF16, kind="ExternalInput")
    w_out = k.dram("w_out", [DFF, D], BF16, kind="ExternalInput")
    idf = k.dram("idf", [128, 128], F32, kind="ExternalInput")
    idb = k.dram("idb", [128, 128], BF16, kind="ExternalInput")
    c = alloc_common(k)
    load_ident(k, c, idf, idb)
    A, Braw, G = modcols_prepare(k, "m0", modc[:, :, :], normg[:, :], 0)
    ffn_stage(k, c, "f0", xin, xout, R, 128, A, Braw, G, w_in, w_out)
    return k.emit()


def _build_evpre(R):
    k = KB()
    di = lambda n, s, dt=F32: k.dram(n, s, dt, kind="ExternalInput")
    do = lambda n, s, dt=F32: k.dram(n, s, dt, kind="ExternalOutput")
    xin = di("xin", [R, D]); modc = di("modc", [128, 3, 16]); normg = di("normg", [128, 16])
    w_in = di("w_in", [D, 1856]); wuq = di("wuq", [512, 1536]); wukv = di("wukv", [256, 2048])
    gqa = di("gqa", [128, 4]); gkva = di("gkva", [128, 2]); gq = di("gq", [192]); gk = di("gk", [192])
    cos = di("cos", [R, 32]); sin = di("sin", [R, 32])
    idf = di("idf", [128, 128]); idb = di("idb", [128, 128], BF16)
    qT = do("qT", [8, 192, R], BF16); kT = do("kT", [8, 192, R], BF16); v = do("v", [R, 1024], BF16); u = do("u", [R, 1024])
    c = alloc_common(k)
    load_ident(k, c, idf, idb)
    A, Braw, G = modcols_prepare(k, "m1", modc[:, :, :], normg[:, :], 0)
    evpre_stage(k, c, xin, R, 128, A, Braw, w_in, wuq, wukv, gqa, gkva, gq, gk, cos, sin, qT, kT, v, u, True)
    return k.emit()


def _build_attn():
    k = KB()
    di = lambda n, s, dt=F32: k.dram(n, s, dt, kind="ExternalInput")
    qT = di("qT", [8, 192, 2048], BF16); kT = di("kT", [8, 192, 16640], BF16); v = di("v", [16640, 1024], BF16)
    attT = k.dram("attT", [1024, 2048], BF16, kind="ExternalOutput")
    c = alloc_common(k)
    attn_stage(k, c, qT, kT, v, attT, 2048, 16640, 8)
    return k.emit()


def _build_s5():
    k = KB()
    di = lambda n, s, dt=F32: k.dram(n, s, dt, kind="ExternalInput")
    U = di("U", [16, 16, 16640]); lre = di("lre", [64, 16]); lim = di("lim", [64, 16]); ldt = di("ldt", [16])
    BTre = di("BTre", [16, 16, 64]); BTim = di("BTim", [16, 16, 64]); CTre = di("CTre", [64, 16, 16]); CTim = di("CTim", [64, 16, 16])
    jt = di("jt", [64, SEG + 1])
    Y = k.dram("Y", [16, 16, 16384], F32, kind="ExternalOutput")
    c = alloc_common(k)
    s5_stage(k, c, U, lre, lim, ldt, BTre, BTim, CTre, CTim, jt, Y, NSEG)
    return k.emit()


def _build_evout():
    k = KB()
    R = 2048
    di = lambda n, s, dt=F32: k.dram(n, s, dt, kind="ExternalInput")
    xa = di("xa", [R, D]); u = di("u", [R, 1024]); yf = di("yf", [R, 1024]); yb = di("yb", [R, 1024]); attT = di("attT", [1024, R], BF16)
    dd = di("dd", [1024]); gluw = di("gluw", [1024, 1024]); glub = di("glub", [1024]); wout = di("wout", [2048, 2048]); gate = di("gate", [2048])
    idf = di("idf", [128, 128]); idb = di("idb", [128, 128], BF16)
    xm = k.dram("xm", [R, D], F32, kind="ExternalOutput")
    c = alloc_common(k)
    load_ident(k, c, idf, idb)
    evout_stage(k, c, xa, u, yf, yb, attT, dd, gluw, glub, wout, gate, xm, R)
    return k.emit()


def _build_hypre():
    k = KB()
    di = lambda n, s, dt=F32: k.dram(n, s, dt, kind="ExternalInput")
    xh = di("xhin", [2050, D]); hmask = di("hmask", [128, 2]); modc = di("modc", [128, 3, 16]); normg = di("normg", [128, 16])
    w_in = di("w_in", [D, 6144]); cw = di("cw", [128, 48, 3]); cb = di("cb", [128, 48])
    idf = di("idf", [128, 128]); idb = di("idb", [128, 128], BF16)
    zc = k.dram("zc", [6144, 2048], F32, kind="ExternalOutput")
    c = alloc_common(k)
    load_ident(k, c, idf, idb)
    A, Braw, G = modcols_prepare(k, "m1", modc[:, :, :], normg[:, :], 0)
    hypre_stage(k, c, xh, hmask, A, Braw, w_in, cw, cb, zc)
    return k.emit()


_FILT_CONSTS = ("F1cat", "Wre", "Wim", "Cos", "Sin", "NSin")


def _build_hyfilt():
    k = KB()
    di = lambda n, s, dt=F32: k.dram(n, s, dt, kind="ExternalInput")
    cd = {n: di("k_" + n, HY_CONST_SHAPES[n][0], HY_CONST_SHAPES[n][1]) for n in _FILT_CONSTS}
    featT = di("featT", [33, LSEQ]); w1 = di("w1", [33, 64]); w2 = di("w2", [64, 64]); w3 = di("w3", [64, 64]); bf = di("bf", [64, 4])
    wout = di("wout", [64, 2, 2, 256]); drow = di("drow", [64, 256]); E1 = di("E1", [64, 256]); tfrow = di("tfrow", [64, 256])
    Gre = k.dram("Gre", [2, 2, 128, 256 * 128], F32, kind="ExternalOutput")
    Gim = k.dram("Gim", [2, 2, 128, 256 * 128], F32, kind="ExternalOutput")
    c = alloc_common(k)
    hyfilt_stage(k, c, cd, featT, w1, w2, w3, bf, wout, drow, E1, tfrow, Gre, Gim, 16, 2)
    return k.emit()


def _build_hyconv():
    k = KB()
    di = lambda n, s, dt=F32: k.dram(n, s, dt, kind="ExternalInput")
    cd = {n: di("k_" + n, shp, dt) for n, (shp, dt) in HY_CONST_SHAPES.items()}
    x1 = di("x1", [256, LSEQ]); x2 = di("x2", [256, LSEQ]); v = di("v", [256, LSEQ]); skip = di("skip", [2, 256])
    Gre = di("Gre", [2, 2, 128, 256 * 128]); Gim = di("Gim", [2, 2, 128, 256 * 128])
    y2 = k.dram("y2", [256, LSEQ], F32, kind="ExternalOutput")
    c = alloc_common(k)
    hyconv_stage(k, c, cd, x1, x2, v, skip, Gre, Gim, y2, 16)
    return k.emit()


def _build_projres():
    k = KB()
    di = lambda n, s, dt=F32: k.dram(n, s, dt, kind="ExternalInput")
    xin = di("xin", [2048, D]); yT = di("yT", [2048, 2048]); wout = di("wout", [2048, 2048]); gate = di("gate", [2048])
    xm = k.dram("xm", [2048, D], F32, kind="ExternalOutput")
    c = alloc_common(k)
    projres_stage(k, c, xin, yT, wout, gate, xm, 2048)
    return k.emit()


def _hyena_mixer(inp, xs, m1):
    x_full = np.concatenate(xs, axis=0)
    xp = np.concatenate([np.zeros((1, 2048), np.float32), x_full, np.zeros((1, 2048), np.float32)], axis=0)
    cw = np.ascontiguousarray(inp["hy_conv_w"][0].reshape(3, 48, 128).transpose(2, 1, 0))
    cb = np.ascontiguousarray(inp["hy_conv_b"][0].reshape(48, 128).T)
    maps = []
    for ci in range(8):
        hm = np.ones((128, 2), np.float32)
        if ci == 0:
            hm[:, 0] = 0
        if ci == 7:
            hm[:, 1] = 0
        maps.append(dict(xhin=np.ascontiguousarray(xp[ci * 2048:ci * 2048 + 2050]), hmask=hm, modc=_modc(m1, 3), normg=_cols(inp["norm_g"][1, 1]),
                         w_in=np.ascontiguousarray(inp["hy_w_in"][0]), cw=cw, cb=cb, idf=_IDF, idb=_IDB))
    rz = _run(_prog("hypre", _build_hypre), maps)
    z_all = np.concatenate([rz[ci]["zc"] for ci in range(8)], axis=1)
    consts = hy_consts()
    bf = np.ascontiguousarray(np.stack([inp["hy_filt_b1"][0], inp["hy_filt_b2"][0], inp["hy_filt_b3"][0], inp["hy_filt_freq"][0]], axis=1).astype(np.float32))
    maps = []
    for ci in range(8):
        fc = hy_filt_consts(ci)
        m = {"k_" + n: consts[n] for n in _FILT_CONSTS}
        m.update(featT=fc["featT"], drow=fc["drow"], E1=fc["E1"], tfrow=fc["tfrow"], w1=np.ascontiguousarray(inp["hy_filt_w1"][0]), w2=np.ascontiguousarray(inp["hy_filt_w2"][0]),
                 w3=np.ascontiguousarray(inp["hy_filt_w3"][0]), bf=bf, wout=np.ascontiguousarray(inp["hy_filt_w_out"][0][:, :, :, ci * 256:(ci + 1) * 256]))
        maps.append(m)
    rf = _run(_prog("hyfilt", _build_hyfilt), maps)
    maps = []
    for ci in range(8):
        cs = slice(ci * 256, (ci + 1) * 256)
        m = {"k_" + n: v for n, v in consts.items()}
        m.update(x1=np.ascontiguousarray(z_all[0:2048][cs]), x2=np.ascontiguousarray(z_all[2048:4096][cs]), v=np.ascontiguousarray(z_all[4096:6144][cs]),
                 skip=np.ascontiguousarray(inp["hy_skip"][0][:, cs]), Gre=rf[ci]["Gre"], Gim=rf[ci]["Gim"])
        maps.append(m)
    ry = _run(_prog("hyconv", _build_hyconv), maps)
    y_all = np.concatenate([ry[ci]["y2"] for ci in range(8)], axis=0)
    prc = dict(wout=np.ascontiguousarray(inp["hy_w_out"][0]), gate=np.ascontiguousarray(m1[5]))
    rp = _run(_prog("projres", _build_projres), [dict(prc, xin=np.ascontiguousarray(xs[ci]), yT=np.ascontiguousarray(y_all[:, ci * 2048:(ci + 1) * 2048]))
                                                  for ci in range(8)])
    return [rp[ci]["xm"] for ci in range(8)]


def _build_wprep():
    k = KB()
    win = k.dram("win", [4, 128, 22528], F32, kind="ExternalInput")
    wout = k.dram("wout", [4, 128, 11264], F32, kind="ExternalInput")
    winb = k.dram("winb", [4, 128, 22528], BF16, kind="ExternalOutput")
    woutb = k.dram("woutb", [4, 128, 11264], BF16, kind="ExternalOutput")
    c = {}
    wprep_stage(k, c, win, wout, winb, woutb, 4)
    return k.emit()


def _prep_ffn_weights(inp):
    win = inp["ffn_w_in"].reshape(4, 8, 128, 22528)
    wout = inp["ffn_w_out"].reshape(4, 8, 128, 11264)
    res = _run(_prog("wprep", _build_wprep), [dict(win=np.ascontiguousarray(win[:, ci]), wout=np.ascontiguousarray(wout[:, ci])) for ci in range(8)])
    winb = np.stack([res[ci]["winb"] for ci in range(8)], axis=1).reshape(2, 2, 2048, 11264)
    woutb = np.stack([res[ci]["woutb"] for ci in range(8)], axis=1).reshape(2, 2, 5632, 2048)
    return winb, woutb


def _s5_inmaps(inp, u_x, u_c):
    jt = np.broadcast_to(np.arange(SEG + 1, dtype=np.float32)[None, :], (64, SEG + 1)).copy()
    maps = []
    seq_f = np.concatenate([u_c, u_x], axis=0)
    seq_b = np.concatenate([u_x, u_c], axis=0)[::-1]
    for ci in range(8):
        gs = slice(ci * 8, ci * 8 + 8)
        U = np.empty((16, 16, 16640), np.float32)
        for di_, seq in enumerate((seq_f, seq_b)):
            blk = seq[:, ci * 128:(ci + 1) * 128].reshape(16640, 8, 16)
            U[:, di_ * 8:(di_ + 1) * 8, :] = blk.transpose(2, 1, 0)

        def lanes(a):
            return a[:, gs].reshape(16, *a.shape[2:])
        m = dict(U=U, lre=lanes(inp["s5_lam_re"][0]).T, lim=lanes(inp["s5_lam_im"][0]).T, ldt=lanes(inp["s5_log_dt"][0]),
                 BTre=lanes(inp["s5_b_re"][0]).transpose(2, 0, 1), BTim=lanes(inp["s5_b_im"][0]).transpose(2, 0, 1),
                 CTre=lanes(inp["s5_c_re"][0]).transpose(2, 0, 1), CTim=lanes(inp["s5_c_im"][0]).transpose(2, 0, 1), jt=jt)
        maps.append({kk: np.ascontiguousarray(v, dtype=np.float32) for kk, v in m.items()})
    return maps


def _ffn_launch(x_rows_per_core, m, k0, normg, w_in, w_out):
    R = x_rows_per_core[0].shape[0]
    nc = _prog(("ffn", R), lambda: _build_ffn(R))
    common = dict(modc=_modc(m, k0), normg=_cols(normg), w_in=np.ascontiguousarray(w_in), w_out=np.ascontiguousarray(w_out), idf=_IDF, idb=_IDB)
    res = _run(nc, [dict(common, xin=np.ascontiguousarray(xr)) for xr in x_rows_per_core])
    return [r["xout"] for r in res]


def kernel(**inp):
    inp = {kk: np.asarray(v) for kk, v in inp.items()}
    x = inp["x"][0]
    ctx = inp["ctx"][0]
    cvec = np.stack([inp["c"][0], inp["c_ctx"]], axis=1)
    cc = np.ascontiguousarray(cvec.reshape(16, 128, 2).transpose(1, 0, 2))
    nc = _prog("ada", _build_ada)
    res = _run(nc, [dict(cc=cc, w=np.ascontiguousarray(inp["ada_w"][:, :, ci * 2304:(ci + 1) * 2304]),
                         b=np.ascontiguousarray(inp["ada_b"][:, ci * 2304:(ci + 1) * 2304])) for ci in range(8)])
    mods = np.concatenate([res[ci]["out"] for ci in range(8)], axis=2)
    mx = [mods[l, 0].reshape(9, 2048) for l in range(2)]
    mc = [mods[l, 1].reshape(9, 2048) for l in range(2)]
    xs = [x[ci * 2048:(ci + 1) * 2048] for ci in range(8)]
    cs = [ctx[(ci % 2) * 128:(ci % 2) * 128 + 128] for ci in range(8)]
    winb, woutb = _prep_ffn_weights(inp)
    xs = _ffn_launch(xs, mx[0], 0, inp["norm_g"][0, 0], winb[0, 0], woutb[0, 0])
    cs = _ffn_launch(cs, mc[0], 0, inp["norm_g"][0, 0], winb[0, 0], woutb[0, 0])
    cos, sin = _rope_tabs(16384)
    evc = dict(normg=_cols(inp["norm_g"][0, 1]), w_in=inp["ev_w_in"][0], wuq=inp["mla_w_uq"][0], wukv=inp["mla_w_ukv"][0],
               gqa=_cols(inp["mla_q_a_norm_g"][0], 4), gkva=_cols(inp["mla_kv_a_norm_g"][0], 2), gq=inp["mla_q_head_g"][0], gk=inp["mla_k_head_g"][0],
               idf=_IDF, idb=_IDB)
    evc = {kk: np.ascontiguousarray(v) for kk, v in evc.items()}
    nc = _prog(("evpre", 2048), lambda: _build_evpre(2048))
    rx = _run(nc, [dict(evc, modc=_modc(mx[0], 3), xin=np.ascontiguousarray(xs[ci]), cos=np.ascontiguousarray(cos[ci * 2048:(ci + 1) * 2048]),
                        sin=np.ascontiguousarray(sin[ci * 2048:(ci + 1) * 2048])) for ci in range(8)])
    nc = _prog(("evpre", 128), lambda: _build_evpre(128))
    one = np.ones((128, 32), np.float32)
    zero = np.zeros((128, 32), np.float32)
    rc = _run(nc, [dict(evc, modc=_modc(mc[0], 3), xin=np.ascontiguousarray(cs[ci]), cos=one, sin=zero) for ci in range(8)])
    kT_all = np.ascontiguousarray(np.concatenate([rx[ci]["kT"] for ci in range(8)] + [rc[0]["kT"], rc[1]["kT"]], axis=2))
    v_all = np.ascontiguousarray(np.concatenate([rx[ci]["v"] for ci in range(8)] + [rc[0]["v"], rc[1]["v"]], axis=0))
    u_x = np.concatenate([rx[ci]["u"] for ci in range(8)], axis=0)
    u_c = np.concatenate([rc[0]["u"], rc[1]["u"]], axis=0)
    nc = _prog("attn", _build_attn)
    ra = _run(nc, [dict(qT=rx[ci]["qT"], kT=kT_all, v=v_all) for ci in range(8)])
    nc = _prog("s5", _build_s5)
    rs = _run(nc, _s5_inmaps(inp, u_x, u_c))
    Yf = np.empty((16384, 1024), np.float32)
    Yb = np.empty((16384, 1024), np.float32)
    for ci in range(8):
        Y = rs[ci]["Y"]
        Yf[:, ci * 128:(ci + 1) * 128] = Y[0:8].transpose(2, 0, 1).reshape(16384, 128)
        Yb[:, ci * 128:(ci + 1) * 128] = Y[8:16, :, ::-1].transpose(2, 0, 1).reshape(16384, 128)
    nc = _prog("evout", _build_evout)
    eoc = dict(dd=inp["s5_d"][0], gluw=inp["s5_glu_w"][0], glub=inp["s5_glu_b"][0], wout=inp["ev_w_out"][0], gate=mx[0][5], idf=_IDF, idb=_IDB)
    eoc = {kk: np.ascontiguousarray(v) for kk, v in eoc.items()}
    ro = _run(nc, [dict(eoc, xa=np.ascontiguousarray(xs[ci]), u=np.ascontiguousarray(u_x[ci * 2048:(ci + 1) * 2048]),
                        yf=np.ascontiguousarray(Yf[ci * 2048:(ci + 1) * 2048]), yb=np.ascontiguousarray(Yb[ci * 2048:(ci + 1) * 2048]),
                        attT=ra[ci]["attT"]) for ci in range(8)])
    xs = [ro[ci]["xm"] for ci in range(8)]
    _DBG["x_m0"] = xs
    xs = _ffn_launch(xs, mx[0], 6, inp["norm_g"][0, 2], winb[0, 1], woutb[0, 1])
    _DBG["x_b0"] = xs
    xs = _ffn_launch(xs, mx[1], 0, inp["norm_g"][1, 0], winb[1, 0], woutb[1, 0])
    _DBG["x_a1"] = xs
    xs = _hyena_mixer(inp, xs, mx[1])
    _DBG["x_m1"] = xs
    xs = _ffn_launch(xs, mx[1], 6, inp["norm_g"][1, 2], winb[1, 1], woutb[1, 1])
    return np.concatenate(xs, axis=0)[None].astype(np.float32)
```
